# Optimizing a Trainium2 kernel written in Bass

```python
import math
import jax
import jax.numpy as jnp
from jax import lax
import numpy as np

D_MODEL = 1024
BATCH = 8
SEQ = 8192
DEPTH = 2

MIX_W = D_MODEL // 2
NSA_HEAD_DIM = 64
NSA_HEADS = MIX_W // NSA_HEAD_DIM
NSA_KV_GROUPS = 2
NSA_HPG = NSA_HEADS // NSA_KV_GROUPS
NSA_KV_W = NSA_KV_GROUPS * NSA_HEAD_DIM
NSA_CMP_BLOCK = 32
NSA_CMP_STRIDE = 16
NSA_CMP_HIDDEN = 256
NSA_SEL_BLOCK = 64
NSA_N_SEL = 16
NSA_WINDOW = 512
Q_BLOCK = 128
NSA_IN = MIX_W + 6 * NSA_KV_W + 3 * NSA_HEADS
RWKV_HEAD_DIM = 64
RWKV_HEADS = MIX_W // RWKV_HEAD_DIM
RWKV_DECAY_RANK = 64
RWKV_AAA_RANK = 64
RWKV_GATE_RANK = 128
RWKV_VRES_RANK = 32
RWKV_LN_EPS = 64e-5
RWKV_IN = 3 * MIX_W + RWKV_DECAY_RANK + RWKV_AAA_RANK + RWKV_GATE_RANK
GDN_HEAD_DIM = 128
GDN_HEADS = MIX_W // GDN_HEAD_DIM
GDN_CONV = 4
GDN_CHUNK = 64
GDN_IN = 4 * MIX_W + 2 * GDN_HEADS
GATE_IN = 3 * D_MODEL
D_IN = NSA_IN + RWKV_IN + GDN_IN + GATE_IN
D_FF = 4 * D_MODEL
EPS = 1e-6
NEG = -1e30
FORCE = 1e4

kernel_name = 'hybrid_nsa_rwkv7_gdn_block'


def _rms(x, g):
    xf = x.astype(jnp.float32)
    y = xf * lax.rsqrt(jnp.mean(xf * xf, axis=-1, keepdims=True) + EPS)
    return (y * g.astype(jnp.float32)).astype(x.dtype)


def _l2n(x):
    xf = x.astype(jnp.float32)
    return (xf * lax.rsqrt(jnp.sum(xf * xf, axis=-1, keepdims=True) + EPS)).astype(x.dtype)


def _shift(x):
    return jnp.pad(x, ((0, 0), (1, 0), (0, 0)))[:, :-1]


def _masked_softmax(s, valid):
    s = jnp.where(valid, s.astype(jnp.float32), NEG)
    p = jax.nn.softmax(s, axis=-1)
    return jnp.where(valid, p, 0.0)


def _nsa(q, kv, gates, q_g, k_g, cmp_pos, cmp_w1, cmp_w2):
    B, S, _ = q.shape
    G, HG, Dh = NSA_KV_GROUPS, NSA_HPG, NSA_HEAD_DIM
    nb = S // Q_BLOCK
    scale = Dh ** -0.5
    qb = _rms(q.reshape(B, S, G, HG, Dh), q_g)
    qb = qb.reshape(B, nb, Q_BLOCK, G, HG, Dh).transpose(1, 0, 3, 4, 2, 5)
    gb = jax.nn.sigmoid(gates.astype(jnp.float32)).reshape(B, nb, Q_BLOCK, G, HG, 3).transpose(1, 0, 3, 4, 2, 5)
    k_c, v_c, k_s, v_s, k_w, v_w = kv.reshape(B, S, 6, G, Dh).transpose(2, 0, 3, 1, 4)

    n_cmp = (S - NSA_CMP_BLOCK) // NSA_CMP_STRIDE + 1
    cmp_start = np.arange(n_cmp) * NSA_CMP_STRIDE
    cmp_idx = cmp_start[:, None] + np.arange(NSA_CMP_BLOCK)[None, :]
    cmp_end = jnp.asarray(cmp_start + NSA_CMP_BLOCK - 1)

    def compress(t, j):
        blk = t[:, :, cmp_idx, :] + cmp_pos[j]
        hid = jax.nn.silu(blk.reshape(B, G, n_cmp, NSA_CMP_BLOCK * Dh) @ cmp_w1[j])
        return hid @ cmp_w2[j]

    kc = _rms(compress(k_c, 0), k_g[0])
    vc = compress(v_c, 1)

    n_slc = S // NSA_SEL_BLOCK
    n_sel = min(NSA_N_SEL, n_slc)
    slc_start = np.arange(n_slc) * NSA_SEL_BLOCK
    overlap = np.clip(np.minimum(cmp_start[:, None] + NSA_CMP_BLOCK, slc_start[None, :] + NSA_SEL_BLOCK)
                      - np.maximum(cmp_start[:, None], slc_start[None, :]), 0, None)
    cmp_to_slc = jnp.asarray(overlap / NSA_CMP_BLOCK, dtype=jnp.float32)
    ks = _rms(k_s, k_g[1]).reshape(B, G, n_slc, NSA_SEL_BLOCK, Dh)
    vs = v_s.reshape(B, G, n_slc, NSA_SEL_BLOCK, Dh)

    pad = ((0, 0), (0, 0), (NSA_WINDOW, 0), (0, 0))
    kw = jnp.pad(_rms(k_w, k_g[2]), pad)
    vw = jnp.pad(v_w, pad)

    bi = jnp.arange(B)[:, None, None]
    gi = jnp.arange(G)[None, :, None]
    slc_ids = jnp.arange(n_slc)
    win_off = jnp.arange(NSA_WINDOW + Q_BLOCK) - NSA_WINDOW
    sel_off = jnp.arange(NSA_SEL_BLOCK)

    def block(args):
        qq, gg, blk = args
        t = blk * Q_BLOCK + jnp.arange(Q_BLOCK)
        s = jnp.einsum('bghqd,bgcd->bghqc', qq, kc) * scale
        p_c = _masked_softmax(s, cmp_end[None, :] <= t[:, None])
        o_c = jnp.einsum('bghqc,bgcd->bghqd', p_c, vc)
        imp = jnp.einsum('bghqc,cj->bgqj', p_c, cmp_to_slc)
        cur = (t // NSA_SEL_BLOCK)[:, None]
        forced = (slc_ids == 0) | (slc_ids == cur) | (slc_ids == cur - 1)
        causal = slc_ids * NSA_SEL_BLOCK <= t[:, None]
        imp = jnp.where(forced, FORCE, jnp.where(causal, imp, -FORCE))
        _, sel = lax.top_k(imp, n_sel)
        flat = sel.reshape(B, G, Q_BLOCK * n_sel)
        ksel = ks[bi, gi, flat].reshape(B, G, Q_BLOCK, n_sel * NSA_SEL_BLOCK, Dh)
        vsel = vs[bi, gi, flat].reshape(B, G, Q_BLOCK, n_sel * NSA_SEL_BLOCK, Dh)
        kpos = (sel[..., None] * NSA_SEL_BLOCK + sel_off).reshape(B, G, Q_BLOCK, n_sel * NSA_SEL_BLOCK)
        s = jnp.einsum('bghqd,bgqkd->bghqk', qq, ksel) * scale
        p_s = _masked_softmax(s, (kpos <= t[:, None])[:, :, None])
        o_s = jnp.einsum('bghqk,bgqkd->bghqd', p_s, vsel)
        start = blk * Q_BLOCK
        kwb = lax.dynamic_slice_in_dim(kw, start, NSA_WINDOW + Q_BLOCK, axis=2)
        vwb = lax.dynamic_slice_in_dim(vw, start, NSA_WINDOW + Q_BLOCK, axis=2)
        kp = start + win_off
        dist = t[:, None] - kp[None, :]
        valid = (dist >= 0) & (dist < NSA_WINDOW) & (kp[None, :] >= 0)
        s = jnp.einsum('bghqd,bgkd->bghqk', qq, kwb) * scale
        p_w = _masked_softmax(s, valid)
        o_w = jnp.einsum('bghqk,bgkd->bghqd', p_w, vwb)
        o = gg[..., 0:1] * o_c + gg[..., 1:2] * o_s + gg[..., 2:3] * o_w
        return o.astype(qq.dtype)

    o = lax.map(block, (qb, gb, jnp.arange(nb)))
    return o.transpose(1, 0, 4, 2, 3, 5).reshape(B, S, MIX_W)


def _rwkv7(cols, xn, v_first, vres, mu, w0, w_up, a0, a_up, g_up, k_k, k_a, r_k, ln_w, ln_b):
    B, S, _ = cols.shape
    H, N = RWKV_HEADS, RWKV_HEAD_DIM
    f32 = jnp.float32
    c = cols.astype(f32)
    c = c + (_shift(c) - c) * mu
    splits = np.cumsum([MIX_W, MIX_W, MIX_W, RWKV_DECAY_RANK, RWKV_AAA_RANK]).tolist()
    r, k, v, wd, ad, gd = jnp.split(c, splits, axis=-1)
    w = -jax.nn.softplus(-(w0 + jnp.tanh(wd) @ w_up)) - 0.5
    decay = jnp.exp(-jnp.exp(w))
    a = jax.nn.sigmoid(a0 + ad @ a_up)
    g = jax.nn.sigmoid(gd) @ g_up
    if vres is None:
        v_first = v
    else:
        v0, vd, vu = vres
        v = v + (v_first - v) * jax.nn.sigmoid(v0 + (xn.astype(f32) @ vd) @ vu)
    heads = lambda t: t.reshape(B, S, H, N)
    kk = _l2n(heads(k * k_k))
    k = k * (1.0 + (a - 1.0) * k_a)
    r, k, v, a, decay = (heads(t) for t in (r, k, v, a, decay))

    def step(state, xs):
        r_t, w_t, k_t, v_t, kk_t, a_t = xs
        sa = jnp.einsum('bhij,bhj->bhi', state, -kk_t)
        state = (state * w_t[:, :, None, :] + sa[..., :, None] * (kk_t * a_t)[..., None, :]
                 + v_t[..., :, None] * k_t[..., None, :])
        return state, jnp.einsum('bhij,bhj->bhi', state, r_t)

    tm = lambda t: jnp.swapaxes(t, 0, 1)
    state0 = jnp.zeros((B, H, N, N), f32)
    _, y = lax.scan(step, state0, (tm(r), tm(decay), tm(k), tm(v), tm(kk), tm(a)))
    y = tm(y)
    mean = jnp.mean(y, axis=-1, keepdims=True)
    var = jnp.mean(jnp.square(y - mean), axis=-1, keepdims=True)
    y = (y - mean) * lax.rsqrt(var + RWKV_LN_EPS) * ln_w.reshape(H, N) + ln_b.reshape(H, N)
    y = y + jnp.sum(r * k * r_k.reshape(H, N), axis=-1, keepdims=True) * v
    y = y.reshape(B, S, MIX_W) * g
    return y.astype(cols.dtype), v_first


def _causal_conv(x, w):
    K, C = w.shape
    return lax.conv_general_dilated(x, w[:, None, :], (1,), [(K - 1, 0)],
                                    dimension_numbers=('NWC', 'WIO', 'NWC'), feature_group_count=C)


def _chunk_gated_delta(q, k, v, g, beta):
    B, H, S, Dk = q.shape
    Dv = v.shape[-1]
    C = GDN_CHUNK
    n = S // C
    q = q.reshape(B, H, n, C, Dk)
    k = k.reshape(B, H, n, C, Dk)
    v = v.reshape(B, H, n, C, Dv)
    g = g.reshape(B, H, n, C)
    beta = beta.reshape(B, H, n, C)
    gam = jnp.cumsum(g, axis=-1)
    idx = jnp.arange(C)
    lower = idx[:, None] >= idx[None, :]
    strict = idx[:, None] > idx[None, :]
    decay = jnp.exp(jnp.where(lower, gam[..., :, None] - gam[..., None, :], -jnp.inf))
    kb = k * beta[..., None]
    a_mat = jnp.eye(C, dtype=q.dtype) + jnp.where(strict, jnp.einsum('bhncd,bhnsd->bhncs', kb, k) * decay, 0.0)
    u = lax.linalg.triangular_solve(a_mat, v * beta[..., None], left_side=True, lower=True, unit_diagonal=True)
    w = lax.linalg.triangular_solve(a_mat, kb * jnp.exp(gam)[..., None], left_side=True, lower=True,
                                    unit_diagonal=True)
    qk = jnp.einsum('bhncd,bhnsd->bhncs', q, k) * decay
    qg = q * jnp.exp(gam)[..., None]
    kg = k * jnp.exp(gam[..., -1:] - gam)[..., None]
    g_last = jnp.exp(gam[..., -1])

    def step(state, xs):
        qg_c, kg_c, u_c, w_c, qk_c, gl_c = xs
        v_new = u_c - jnp.einsum('bhcd,bhdv->bhcv', w_c, state)
        o = jnp.einsum('bhcd,bhdv->bhcv', qg_c, state) + jnp.einsum('bhcs,bhsv->bhcv', qk_c, v_new)
        state = state * gl_c[..., None, None] + jnp.einsum('bhcd,bhcv->bhdv', kg_c, v_new)
        return state, o

    mv = lambda t: jnp.moveaxis(t, 2, 0)
    state0 = jnp.zeros((B, H, Dk, Dv), q.dtype)
    _, o = lax.scan(step, state0, (mv(qg), mv(kg), mv(u), mv(w), mv(qk), mv(g_last)))
    return jnp.moveaxis(o, 0, 2).reshape(B, H, S, Dv)


def _gdn(cols, conv_w, a_log, dt_bias, norm_w):
    B, S, _ = cols.shape
    H, Dh = GDN_HEADS, GDN_HEAD_DIM
    f32 = jnp.float32
    qkv = jax.nn.silu(_causal_conv(cols[..., :3 * MIX_W], conv_w)).astype(f32)
    z = cols[..., 3 * MIX_W:4 * MIX_W].astype(f32)
    a = cols[..., 4 * MIX_W:4 * MIX_W + H].astype(f32)
    b = cols[..., 4 * MIX_W + H:].astype(f32)
    q, k, v = (t.reshape(B, S, H, Dh).transpose(0, 2, 1, 3) for t in jnp.split(qkv, 3, axis=-1))
    q = _l2n(q) * (Dh ** -0.5)
    k = _l2n(k)
    beta = jax.nn.sigmoid(b).transpose(0, 2, 1)
    g = (-jnp.exp(a_log) * jax.nn.softplus(a + dt_bias)).transpose(0, 2, 1)
    o = _chunk_gated_delta(q, k, v, g, beta).transpose(0, 2, 1, 3)
    o = _rms(o, norm_w) * jax.nn.silu(z.reshape(B, S, H, Dh))
    return o.reshape(B, S, MIX_W).astype(cols.dtype)


def setup_inputs(seed: int = 0) -> dict:
    key = jax.random.key(seed)
    keys = iter(jax.random.split(key, 40))
    f32 = jnp.float32

    def nrm(shape, scale):
        return jax.random.normal(next(keys), shape, f32) * scale

    def uni(shape, lo, hi):
        return jax.random.uniform(next(keys), shape, f32, lo, hi)

    L, Lv = DEPTH, DEPTH - 1
    dt = jnp.exp(uni((L, GDN_HEADS), math.log(1e-3), math.log(1e-1)))
    return {
        'x': nrm((BATCH, SEQ, D_MODEL), 1.0),
        'norm_mix_g': 1.0 + nrm((L, D_MODEL), 0.02),
        'w_in': nrm((L, D_MODEL, D_IN), D_MODEL ** -0.5),
        'nsa_q_norm': 1.0 + nrm((L, NSA_HEAD_DIM), 0.02),
        'nsa_k_norm': 1.0 + nrm((L, 3, NSA_HEAD_DIM), 0.02),
        'nsa_cmp_pos': nrm((L, 2, NSA_CMP_BLOCK, NSA_HEAD_DIM), 0.02),
        'nsa_cmp_w1': nrm((L, 2, NSA_CMP_BLOCK * NSA_HEAD_DIM, NSA_CMP_HIDDEN), (NSA_CMP_BLOCK * NSA_HEAD_DIM) ** -0.5),
        'nsa_cmp_w2': nrm((L, 2, NSA_CMP_HIDDEN, NSA_HEAD_DIM), NSA_CMP_HIDDEN ** -0.5),
        'rwkv_mu': uni((L, RWKV_IN), 0.0, 1.0),
        'rwkv_w0': uni((L, MIX_W), -6.0, -1.0),
        'rwkv_w_up': nrm((L, RWKV_DECAY_RANK, MIX_W), 0.1 * RWKV_DECAY_RANK ** -0.5),
        'rwkv_a0': nrm((L, MIX_W), 0.1),
        'rwkv_a_up': nrm((L, RWKV_AAA_RANK, MIX_W), 0.1 * RWKV_AAA_RANK ** -0.5),
        'rwkv_g_up': nrm((L, RWKV_GATE_RANK, MIX_W), RWKV_GATE_RANK ** -0.5),
        'rwkv_k_k': 0.85 + nrm((L, MIX_W), 0.02),
        'rwkv_k_a': 1.0 + nrm((L, MIX_W), 0.02),
        'rwkv_r_k': nrm((L, MIX_W), 0.1),
        'rwkv_ln_w': 1.0 + nrm((L, MIX_W), 0.02),
        'rwkv_ln_b': nrm((L, MIX_W), 0.02),
        'rwkv_v0': nrm((Lv, MIX_W), 0.1),
        'rwkv_vres_down': nrm((Lv, D_MODEL, RWKV_VRES_RANK), D_MODEL ** -0.5),
        'rwkv_vres_up': nrm((Lv, RWKV_VRES_RANK, MIX_W), 0.1 * RWKV_VRES_RANK ** -0.5),
        'gdn_conv_w': nrm((L, GDN_CONV, 3 * MIX_W), GDN_CONV ** -0.5),
        'gdn_a_log': jnp.log(uni((L, GDN_HEADS), 1.0, 16.0)),
        'gdn_dt_bias': dt + jnp.log(-jnp.expm1(-dt)),
        'gdn_norm_w': 1.0 + nrm((L, GDN_HEAD_DIM), 0.02),
        'w_branch': nrm((L, 3, MIX_W, D_MODEL), MIX_W ** -0.5),
        'w_out': nrm((L, D_MODEL, D_MODEL), D_MODEL ** -0.5),
        'norm_ffn_g': 1.0 + nrm((L, D_MODEL), 0.02),
        'w_ff1': nrm((L, D_MODEL, D_FF), D_MODEL ** -0.5),
        'w_ff2': nrm((L, D_FF, D_MODEL), D_FF ** -0.5),
    }


def reference(x, norm_mix_g, w_in, nsa_q_norm, nsa_k_norm, nsa_cmp_pos, nsa_cmp_w1, nsa_cmp_w2,
              rwkv_mu, rwkv_w0, rwkv_w_up, rwkv_a0, rwkv_a_up, rwkv_g_up, rwkv_k_k, rwkv_k_a, rwkv_r_k,
              rwkv_ln_w, rwkv_ln_b, rwkv_v0, rwkv_vres_down, rwkv_vres_up,
              gdn_conv_w, gdn_a_log, gdn_dt_bias, gdn_norm_w,
              w_branch, w_out, norm_ffn_g, w_ff1, w_ff2):
    B, S, D = x.shape
    v_first = None
    for i in range(DEPTH):
        h = _rms(x, norm_mix_g[i])
        proj = h @ w_in[i]
        p_nsa, p_rwkv, p_gdn, p_gate = jnp.split(
            proj, [NSA_IN, NSA_IN + RWKV_IN, NSA_IN + RWKV_IN + GDN_IN], axis=-1)
        o_a = _nsa(p_nsa[..., :MIX_W], p_nsa[..., MIX_W:MIX_W + 6 * NSA_KV_W], p_nsa[..., MIX_W + 6 * NSA_KV_W:],
                   nsa_q_norm[i], nsa_k_norm[i], nsa_cmp_pos[i], nsa_cmp_w1[i], nsa_cmp_w2[i])
        vres = None if i == 0 else (rwkv_v0[i - 1], rwkv_vres_down[i - 1], rwkv_vres_up[i - 1])
        o_b, v_first = _rwkv7(p_rwkv, h, v_first, vres, rwkv_mu[i], rwkv_w0[i], rwkv_w_up[i], rwkv_a0[i],
                              rwkv_a_up[i], rwkv_g_up[i], rwkv_k_k[i], rwkv_k_a[i], rwkv_r_k[i],
                              rwkv_ln_w[i], rwkv_ln_b[i])
        o_c = _gdn(p_gdn, gdn_conv_w[i], gdn_a_log[i], gdn_dt_bias[i], gdn_norm_w[i])
        gate = jax.nn.sigmoid(p_gate.reshape(B, S, 3, D))
        merged = (gate[:, :, 0] * (o_a @ w_branch[i, 0]) + gate[:, :, 1] * (o_b @ w_branch[i, 1])
                  + gate[:, :, 2] * (o_c @ w_branch[i, 2]))
        x = x + merged @ w_out[i]
        h = _rms(x, norm_ffn_g[i])
        x = x + jnp.square(jax.nn.relu(h @ w_ff1[i])) @ w_ff2[i]
    return x
```

```python
import numpy as np
from contextlib import ExitStack
import concourse.bass as bass
import concourse.mybir as mybir
from concourse.bass_utils import run_bass_kernel_spmd

F32 = mybir.dt.float32
BF16 = mybir.dt.bfloat16
AF = mybir.ActivationFunctionType
ALU = mybir.AluOpType
AX = mybir.AxisListType

D = 1024
MIX = 512
NSA_IN = 1304
RWKV_IN = 1792
GDN_IN = 2056
D_IN = 8224
D_FF = 4096
DP = D_IN + 32
EPS = 1e-6
OFF_NSA = 0
OFF_RWKV = NSA_IN
OFF_GDN = NSA_IN + RWKV_IN
OFF_GATE = NSA_IN + RWKV_IN + GDN_IN
NEGM = -30000.0


class _Op:
    __slots__ = ("eng", "fn", "deps", "is_dma", "need_sig", "sem", "val", "pos")


class KB:
    ENGS = ("pe", "act", "dve", "pool", "sp")

    def __init__(self, nc, stack):
        self.nc = nc
        self.stack = stack
        self.esem = {e: stack.enter_context(nc.semaphore("es_" + e)) for e in ("pe", "act", "dve", "pool")}
        self.ecnt = {e: 0 for e in self.esem}
        self.dsem = {"sp": [stack.enter_context(nc.semaphore("dsp%d" % i)) for i in range(16)],
                     "pool": [stack.enter_context(nc.semaphore("dpl%d" % i)) for i in range(8)]}
        self.dcnt = {"sp": 0, "pool": 0}
        self.dlast = {"sp": {}, "pool": {}}
        self.seen = {e: {} for e in self.ENGS}
        self.begin()

    def begin(self):
        self.ops = {e: [] for e in self.ENGS}
        self.res = {}

    def add(self, eng, fn, reads=(), writes=(), dma=False):
        op = _Op()
        op.eng = eng
        op.fn = fn
        op.is_dma = dma
        op.need_sig = dma
        op.sem = None
        op.val = 0
        op.pos = len(self.ops[eng])
        deps = set()
        rl, wl = [], []
        reads = [x.split("__u")[0] for x in reads]
        writes = [x.split("__u")[0] for x in writes]
        for r in reads:
            (wl if r.startswith("ps") else rl).append(r)
        wl.extend(writes)
        for r in rl:
            st = self.res.setdefault(r, [None, []])
            if st[0] is not None:
                deps.add(st[0])
        for w in wl:
            st = self.res.setdefault(w, [None, []])
            if st[0] is not None:
                deps.add(st[0])
            deps.update(st[1])
        for r in rl:
            self.res[r][1].append(op)
        for w in wl:
            st = self.res[w]
            st[0] = op
            st[1] = []
        deps.discard(op)
        keep = []
        latest = {}
        for d in deps:
            if d.is_dma:
                keep.append(d)
                continue
            if d.eng == eng:
                if eng == "pe":
                    continue
                if op.pos - d.pos > 2:
                    continue
            cur = latest.get(d.eng)
            if cur is None or d.pos > cur.pos:
                latest[d.eng] = d
        for d in latest.values():
            d.need_sig = True
            keep.append(d)
        if dma:
            k = self.dcnt[eng]
            self.dcnt[eng] += 1
            P = len(self.dsem[eng])
            slot = k % P
            op.sem = self.dsem[eng][slot]
            op.val = 16 * (k // P + 1)
            prev = self.dlast[eng].get(slot)
            if prev is not None:
                keep.append(prev)
            self.dlast[eng][slot] = op
        op.deps = keep
        self.ops[eng].append(op)
        return op

    def flush(self):
        nc = self.nc
        for q in ("sp", "pool"):
            outstanding = [o for o in self.dlast[q].values()]
            if outstanding:
                op = self.add(q, None)
                op.deps = outstanding
        for e in ("pe", "act", "dve", "pool"):
            for op in self.ops[e]:
                if op.need_sig and not op.is_dma:
                    self.ecnt[e] += 1
                    op.sem = self.esem[e]
                    op.val = self.ecnt[e]
        ops = self.ops
        seen = self.seen

        def emit(e, eng):
            sn = seen[e]
            for op in ops[e]:
                for d in op.deps:
                    key = id(d.sem)
                    if sn.get(key, 0) >= d.val:
                        continue
                    eng.wait_ge(d.sem, d.val)
                    sn[key] = d.val
                if op.fn is None:
                    continue
                ins = op.fn(eng)
                if op.need_sig:
                    ins.then_inc(op.sem, 16 if op.is_dma else 1)

        with nc.Block() as block:
            if ops["pe"]:
                block.tensor(lambda t: emit("pe", t))
            if ops["act"]:
                block.scalar(lambda t: emit("act", t))
            if ops["dve"]:
                block.vector(lambda t: emit("dve", t))
            if ops["pool"]:
                block.gpsimd(lambda t: emit("pool", t))
            if ops["sp"]:
                block.sync(lambda t: emit("sp", t))
        self.begin()

    def dma(self, out, in_, r=(), w=(), q="sp"):
        return self.add(q, lambda e: e.dma_start(out=out, in_=in_), r, w, dma=True)

    def mm(self, out, lhsT, rhs, start, stop, r=(), w=()):
        return self.add("pe", lambda e: e.matmul(out, lhsT=lhsT, rhs=rhs, start=start, stop=stop), r, w)

    def tr(self, out, in_, ident, r=(), w=()):
        return self.add("pe", lambda e: e.transpose(out, in_, ident), r, w)

    def act(self, out, in_, func, r=(), w=(), bias=None, scale=1.0, accum=None):
        def fn(e):
            kw = {}
            if bias is not None:
                kw["bias"] = bias
            if accum is not None:
                kw["accum_out"] = accum
            return e.activation(out, in_, func, scale=scale, **kw)
        return self.add("act", fn, r, w)

    def ts(self, eng, out, in0, s1, s2, op0, op1=None, r=(), w=()):
        def fn(e):
            if op1 is None:
                return e.tensor_scalar(out, in0, s1, None, op0)
            return e.tensor_scalar(out, in0, s1, s2, op0, op1)
        return self.add(eng, fn, r, w)

    def tt(self, eng, out, in0, in1, op, r=(), w=()):
        return self.add(eng, lambda e: e.tensor_tensor(out, in0, in1, op), r, w)

    def stt(self, eng, out, in0, scalar, in1, op0, op1, r=(), w=()):
        return self.add(eng, lambda e: e.scalar_tensor_tensor(out, in0, scalar, in1, op0, op1), r, w)

    def cp(self, eng, out, in_, r=(), w=()):
        if eng == "act":
            return self.add(eng, lambda e: e.copy(out, in_), r, w)
        return self.add(eng, lambda e: e.tensor_copy(out, in_), r, w)

    def memset(self, eng, ap, val, w=()):
        return self.add(eng, lambda e: e.memset(ap, val), (), w)


class Prog:
    def __init__(self, S, n_layers=2, debug=False):
        self.S = S
        self.NT = S // 128
        self.L = n_layers
        self.debug = debug
        self.nc = bass.Bass("TRN2", target_bir_lowering=False)
        self.stack = ExitStack()
        self.kb = KB(self.nc, self.stack)
        self.din = {}
        self.dscr = {}

    def inp(self, name, shape, dt=F32):
        t = self.nc.dram_tensor(name, list(shape), dt, kind="ExternalInput")
        self.din[name] = t
        return t.ap()

    def outp(self, name, shape, dt=F32):
        return self.nc.dram_tensor(name, list(shape), dt, kind="ExternalOutput").ap()

    def scr(self, name, shape, dt=F32):
        if self.debug:
            return self.outp(name, shape, dt)
        return self.nc.dram_tensor(name, list(shape), dt, kind="Internal").ap()

    def sb(self, st, name, shape, dt=F32):
        self.uid = getattr(self, "uid", 0) + 1
        return st.enter_context(self.nc.sbuf_tensor("%s__u%d" % (name, self.uid), list(shape), dt))

    def psb(self, st, name, dt=F32, n=512):
        self.uid = getattr(self, "uid", 0) + 1
        return st.enter_context(self.nc.psum_tensor("%s__u%d" % (name, self.uid), [128, n], dt))

    def declare(self):
        L, S = self.L, self.S
        i = self.inp
        self.x_in = i("x", [S, D])
        self.out = self.outp("out", [S, D])
        self.norm_mix_g = i("norm_mix_g", [L, D])
        self.w_in = i("w_in", [L, D, DP])
        self.ident_bf = i("ident_bf", [128, 128], BF16)
        self.ident_f = i("ident_f", [128, 128])
        self.proj_tm = [self.scr("proj_tm%d" % l, [S, OFF_GATE]) for l in range(L)]
        self.proj_g = [self.scr("proj_g%d" % l, [S, DP - OFF_GATE]) for l in range(L)]

    def load_weight_bf16(self, st, name, src_ap, kc, ncols, chunk):
        kb = self.kb
        w = self.sb(st, name, [128, kc, ncols], BF16)
        src = src_ap.rearrange("(k p) n -> p k n", p=128)
        with ExitStack() as s2:
            stg = [self.sb(s2, "%s_stg%d" % (name, j), [128, kc, chunk], F32) for j in range(2)]
            engs = ["dve", "pool", "act"]
            n = 0
            for c0 in range(0, ncols, chunk):
                cw = min(chunk, ncols - c0)
                j = n % 2
                kb.dma(stg[j][:, :, 0:cw], src[:, :, c0:c0 + cw], w=[stg[j].name])
                kb.cp(engs[n % 3], w[:, :, c0:c0 + cw], stg[j][:, :, 0:cw], r=[stg[j].name], w=[name + ":%d" % n])
                n += 1
            kb.flush()
        return w

    def phase_proj(self, l, x_src):
        kb, S, NT = self.kb, self.S, self.NT
        with ExitStack() as st:
            wsb = self.load_weight_bf16(st, "w_in_sb", self.w_in[l], 8, DP, 516)
            gb = self.sb(st, "gmix", [128, D])
            idb = self.sb(st, "idb", [128, 128], BF16)
            kb.dma(gb[:], self.norm_mix_g[l:l + 1, :].to_broadcast([128, D]), w=["gmix"])
            kb.dma(idb[:], self.ident_bf[:, :], w=["idb"])
            xt = [self.sb(st, "xt%d" % j, [128, D]) for j in range(2)]
            junk = self.sb(st, "junk", [128, D])
            ss = [self.sb(st, "ss%d" % j, [128, 1]) for j in range(2)]
            rstd = [self.sb(st, "rstd%d" % j, [128, 1]) for j in range(2)]
            hb = [self.sb(st, "hb%d" % j, [128, D], BF16) for j in range(2)]
            hT = [self.sb(st, "hT%d" % j, [128, 8, 128], BF16) for j in range(2)]
            osb = [self.sb(st, "osb%d" % j, [128, 2048]) for j in range(2)]
            pst = self.psb(st, "ps_tr", BF16, 1024)
            pso = [self.psb(st, "ps_o%d" % j) for j in range(4)]
            nblk = 0
            nog = 0
            for t in range(NT):
                j = t % 2
                rows = slice(t * 128, (t + 1) * 128)
                self.rms_to_hT(xt[j], x_src[rows, :], junk, ss[j], rstd[j], gb, hb[j], pst, idb, hT[j])
                for og, oe in ((0, 2048), (2048, 4096), (4096, OFF_GATE), (OFF_GATE, OFF_GATE + 2048), (OFF_GATE + 2048, DP)):
                    o = osb[nog % 2]
                    nog += 1
                    ow = oe - og
                    for c0 in range(og, oe, 512):
                        cw = min(512, oe - c0)
                        ps = pso[nblk % 4]
                        for k in range(8):
                            kb.mm(ps[:, 0:cw], hT[j][:, k, :], wsb[:, k, c0:c0 + cw], k == 0, k == 7,
                                  r=[hT[j].name, "w_in_sb"], w=[ps.name])
                        kb.cp("act" if nblk % 2 else "dve", o[:, c0 - og:c0 - og + cw], ps[:, 0:cw],
                              r=[ps.name], w=[o.name])
                        nblk += 1
                    if og < OFF_GATE:
                        kb.dma(self.proj_tm[l][rows, og:oe], o[:, 0:ow], r=[o.name], q="pool")
                    else:
                        kb.dma(self.proj_g[l][rows, og - OFF_GATE:oe - OFF_GATE], o[:, 0:ow], r=[o.name], q="pool")
            kb.flush()

    def rms_to_hT(self, xt, x_rows, junk, ss, rstd, gb, hb, pst, idb, hT):
        kb = self.kb
        kb.dma(xt[:], x_rows, w=[xt.name])
        kb.act(junk[:], xt[:], AF.Square, r=[xt.name], w=[junk.name, ss.name], accum=ss[:])
        self.rsqrt(rstd[:], ss[:], 1.0 / D, EPS, [ss.name], [rstd.name])
        kb.stt("dve", hb[:], xt[:], rstd[:, 0:1], gb[:], ALU.mult, ALU.mult,
               r=[xt.name, rstd.name, gb.name], w=[hb.name])
        for k in range(8):
            kb.tr(pst[:, k * 128:(k + 1) * 128], hb[:, k * 128:(k + 1) * 128], idb[:],
                  r=[hb.name, idb.name], w=[pst.name])
        kb.cp("act", hT[:].rearrange("p k t -> p (k t)"), pst[:], r=[pst.name], w=[hT.name])

    def rsqrt(self, out, in_, scale, eps, r, w):
        kb = self.kb
        kb.ts("dve", out, in_, scale, eps, ALU.mult, ALU.add, r=r, w=w)
        kb.add("act", lambda e: e.sqrt(out, out), w, w)
        kb.add("dve", lambda e: e.reciprocal(out, out), w, w)

    def declare_rest(self):
        L, S = self.L, self.S
        i = self.inp
        self.w_branch = i("w_branch", [L, 3, MIX, D])
        self.w_out = i("w_out", [L, D, D])
        self.norm_ffn_g = i("norm_ffn_g", [L, D])
        self.w_ff1 = i("w_ff1", [L, D, D_FF])
        self.w_ff2 = i("w_ff2", [L, D_FF, D])
        self.o_mix = [[self.scr("o_mix%d_%d" % (l, b), [S, MIX]) for b in range(3)] for l in range(L)]
        self.x_mid = [self.scr("x_mid%d" % l, [S, D]) for l in range(L)]
        self.x_lay = [self.scr("x_lay%d" % l, [S, D]) for l in range(L - 1)] + [self.out]

    def transpose_bf(self, src_bf, nk, pst, idb, dstT, eng="act"):
        kb = self.kb
        for k in range(nk):
            kb.tr(pst[:, k * 128:(k + 1) * 128], src_bf[:, k * 128:(k + 1) * 128], idb[:],
                  r=[src_bf.name, idb.name], w=[pst.name])
        kb.cp(eng, dstT.rearrange("p k t -> p (k t)"), pst[:, 0:nk * 128], r=[pst.name], w=[dstT.name])

    def phase_merge(self, l, x_src):
        kb, S, NT = self.kb, self.S, self.NT
        with ExitStack() as st:
            wb = [self.load_weight_bf16(st, "wbr%d" % b, self.w_branch[l, b], 4, D, 1024) for b in range(3)]
            wo = self.load_weight_bf16(st, "wout_sb", self.w_out[l], 8, D, 512)
            idb = self.sb(st, "idb", [128, 128], BF16)
            kb.dma(idb[:], self.ident_bf[:, :], w=["idb"])
            ot = [self.sb(st, "ot%d" % j, [128, MIX]) for j in range(2)]
            ob = [self.sb(st, "ob%d" % j, [128, MIX], BF16) for j in range(2)]
            oT = [self.sb(st, "oT%d" % j, [128, 4, 128], BF16) for j in range(2)]
            gt = [self.sb(st, "gt%d" % j, [128, D]) for j in range(2)]
            tmp = self.sb(st, "tmpm", [128, D])
            mrg = self.sb(st, "mrg", [128, D])
            mb = self.sb(st, "mrgb", [128, D], BF16)
            mT = self.sb(st, "mT", [128, 8, 128], BF16)
            xt = [self.sb(st, "xt%d" % j, [128, D]) for j in range(2)]
            xo = [self.sb(st, "xo%d" % j, [128, D]) for j in range(2)]
            pst = self.psb(st, "ps_tr", BF16, 1024)
            pso = [self.psb(st, "ps_o%d" % j) for j in range(4)]
            n = 0
            nb = 0
            for t in range(NT):
                rows = slice(t * 128, (t + 1) * 128)
                for b in range(3):
                    j = n % 2
                    n += 1
                    kb.dma(ot[j][:], self.o_mix[l][b][rows, :], w=[ot[j].name])
                    kb.dma(gt[j][:], self.proj_g[l][rows, b * D:(b + 1) * D], w=[gt[j].name])
                    kb.cp("pool", ob[j][:], ot[j][:], r=[ot[j].name], w=[ob[j].name])
                    kb.act(gt[j][:], gt[j][:], AF.Sigmoid, r=[gt[j].name], w=[gt[j].name])
                    self.transpose_bf(ob[j], 4, pst, idb, oT[j][:])
                    for c in range(2):
                        ps = pso[nb % 4]
                        nb += 1
                        for k in range(4):
                            kb.mm(ps[:], oT[j][:, k, :], wb[b][:, k, c * 512:(c + 1) * 512], k == 0, k == 3,
                                  r=[oT[j].name], w=[ps.name])
                        cs = slice(c * 512, (c + 1) * 512)
                        if b == 0:
                            kb.tt("dve", mrg[:, cs], ps[:], gt[j][:, cs], ALU.mult, r=[ps.name, gt[j].name], w=["mrg"])
                        else:
                            kb.tt("dve", tmp[:, cs], ps[:], gt[j][:, cs], ALU.mult, r=[ps.name, gt[j].name], w=["tmpm"])
                            kb.tt("pool", mrg[:, cs], mrg[:, cs], tmp[:, cs], ALU.add, r=["tmpm", "mrg"], w=["mrg"])
                kb.cp("act", mb[:], mrg[:], r=["mrg"], w=["mrgb"])
                self.transpose_bf(mb, 8, pst, idb, mT[:])
                j = t % 2
                kb.dma(xt[j][:], x_src[rows, :], w=[xt[j].name])
                for c in range(2):
                    ps = pso[nb % 4]
                    nb += 1
                    for k in range(8):
                        kb.mm(ps[:], mT[:, k, :], wo[:, k, c * 512:(c + 1) * 512], k == 0, k == 7, r=["mT"], w=[ps.name])
                    cs = slice(c * 512, (c + 1) * 512)
                    kb.tt("dve", xo[j][:, cs], ps[:], xt[j][:, cs], ALU.add, r=[ps.name, xt[j].name], w=[xo[j].name])
                kb.dma(self.x_mid[l][rows, :], xo[j][:], r=[xo[j].name], q="pool")
            kb.flush()

    def phase_ffn(self, l):
        kb, S, NT = self.kb, self.S, self.NT
        with ExitStack() as st:
            w1 = self.load_weight_bf16(st, "w1_sb", self.w_ff1[l], 8, D_FF, 512)
            w2 = self.load_weight_bf16(st, "w2_sb", self.w_ff2[l], 32, D, 128)
            gb = self.sb(st, "gffn", [128, D])
            idb = self.sb(st, "idb", [128, 128], BF16)
            kb.dma(gb[:], self.norm_ffn_g[l:l + 1, :].to_broadcast([128, D]), w=["gffn"])
            kb.dma(idb[:], self.ident_bf[:, :], w=["idb"])
            xt = [self.sb(st, "xt%d" % j, [128, D]) for j in range(2)]
            junk = self.sb(st, "junk", [128, D])
            ss = [self.sb(st, "ss%d" % j, [128, 1]) for j in range(2)]
            rstd = [self.sb(st, "rstd%d" % j, [128, 1]) for j in range(2)]
            hb = [self.sb(st, "hb%d" % j, [128, D], BF16) for j in range(2)]
            hT = [self.sb(st, "hT%d" % j, [128, 8, 128], BF16) for j in range(2)]
            fr = self.sb(st, "fr", [128, 512])
            fb = self.sb(st, "fb", [128, D_FF], BF16)
            fT = self.sb(st, "fT", [128, 32, 128], BF16)
            xo = [self.sb(st, "xo%d" % j, [128, D]) for j in range(2)]
            pst = self.psb(st, "ps_tr", BF16, 1024)
            pso = [self.psb(st, "ps_o%d" % j) for j in range(4)]
            nb = 0
            for t in range(NT):
                j = t % 2
                rows = slice(t * 128, (t + 1) * 128)
                self.rms_to_hT(xt[j], self.x_mid[l][rows, :], junk, ss[j], rstd[j], gb, hb[j], pst, idb, hT[j])
                for c in range(8):
                    ps = pso[nb % 4]
                    nb += 1
                    for k in range(8):
                        kb.mm(ps[:], hT[j][:, k, :], w1[:, k, c * 512:(c + 1) * 512], k == 0, k == 7,
                              r=[hT[j].name], w=[ps.name])
                    kb.act(fr[:], ps[:], AF.Relu, r=[ps.name], w=["fr"])
                    kb.tt("dve", fb[:, c * 512:(c + 1) * 512], fr[:], fr[:], ALU.mult, r=["fr"], w=["fb"])
                for q4 in range(4):
                    self.transpose_bf_slice(fb, q4 * 8, 8, pst, idb, fT)
                for c in range(2):
                    ps = pso[nb % 4]
                    nb += 1
                    for k in range(32):
                        kb.mm(ps[:], fT[:, k, :], w2[:, k, c * 512:(c + 1) * 512], k == 0, k == 31, r=["fT"], w=[ps.name])
                    cs = slice(c * 512, (c + 1) * 512)
                    kb.tt("dve", xo[j][:, cs], ps[:], xt[j][:, cs], ALU.add, r=[ps.name, xt[j].name], w=[xo[j].name])
                kb.dma(self.x_lay[l][rows, :], xo[j][:], r=[xo[j].name], q="pool")
            kb.flush()

    def transpose_bf_slice(self, src_bf, k0, nk, pst, idb, dstT):
        kb = self.kb
        for k in range(nk):
            kb.tr(pst[:, k * 128:(k + 1) * 128], src_bf[:, (k0 + k) * 128:(k0 + k + 1) * 128], idb[:],
                  r=[src_bf.name, idb.name], w=[pst.name])
        kb.cp("act", dstT[:, k0:k0 + nk, :].rearrange("p k t -> p (k t)"), pst[:, 0:nk * 128],
              r=[pst.name], w=[dstT.name])

    def declare_gdn(self):
        L = self.L
        i = self.inp
        self.gdn_conv_w = i("gdn_conv_w", [L, 4, 3 * MIX])
        self.gdn_a_log = i("gdn_a_log", [L, 4])
        self.gdn_dt_bias = i("gdn_dt_bias", [L, 4])
        self.gdn_norm_w = i("gdn_norm_w", [L, 128])
        self.ones_f = i("ones_f", [128, 128])
        self.triu_incl = i("triu_incl", [128, 128])
        self.triu_strict = i("triu_strict", [128, 128])

    def next_ps(self):
        self.psn = getattr(self, "psn", -1) + 1
        return self.pss[self.psn % len(self.pss)]

    def tr_f32(self, dst, src, idf, eng="act"):
        kb = self.kb
        n = src.shape[-1] // 128
        ps = self.next_ps()
        for k in range(n):
            kb.tr(ps[:, k * 128:(k + 1) * 128], src[:, k * 128:(k + 1) * 128], idf[:],
                  r=[src.name, idf.name], w=[ps.name])
        kb.cp(eng, dst, ps[:, 0:n * 128], r=[ps.name], w=[dst.name])

    def phase_gdn(self, l):
        kb, S, NT = self.kb, self.S, self.NT
        H, N = 4, 128
        P = self.proj_tm[l]
        c0 = OFF_GDN
        with ExitStack() as st:
            sb = lambda name, shape, dt=F32: self.sb(st, name, shape, dt)
            self.pss = [self.psb(st, "ps_g%d" % j) for j in range(8)]
            idf = sb("idf", [128, 128]); ones = sb("ones", [128, 128])
            tin = sb("tin", [128, 128]); tst = sb("tst", [128, 128])
            kb.dma(idf[:], self.ident_f[:, :], w=[idf.name])
            kb.dma(ones[:], self.ones_f[:, :], w=[ones.name])
            kb.dma(tin[:], self.triu_incl[:, :], w=[tin.name])
            kb.dma(tst[:], self.triu_strict[:, :], w=[tst.name])
            cw = sb("cw", [128, 4, 1536])
            kb.dma(cw[:].rearrange("p k c -> p (k c)"),
                   self.gdn_conv_w[l:l + 1].rearrange("o k c -> o (k c)").to_broadcast([128, 4 * 1536]), w=[cw.name])
            alog = sb("alog", [128, 4]); dtb = sb("dtb", [128, 4]); nw = sb("nw", [128, 128])
            kb.dma(alog[:], self.gdn_a_log[l:l + 1, :].to_broadcast([128, 4]), w=[alog.name])
            kb.dma(dtb[:], self.gdn_dt_bias[l:l + 1, :].to_broadcast([128, 4]), w=[dtb.name])
            kb.dma(nw[:], self.gdn_norm_w[l:l + 1, :].to_broadcast([128, 128]), w=[nw.name])
            nea = sb("nea", [128, 4])
            kb.act(nea[:], alog[:], AF.Exp, r=[alog.name], w=[nea.name])
            kb.ts("dve", nea[:], nea[:], -1.0, None, ALU.mult, r=[nea.name], w=[nea.name])
            X = [sb("X%d" % k, [128, 1536]) for k in range(4)]
            zt = sb("zt", [128, 512]); ab = sb("ab", [128, 8])
            y = sb("y", [128, 1536]); t1 = sb("t1", [128, 1536])
            ssq = sb("ssq", [128, 8]); rs = sb("rs", [128, 8])
            qkn = sb("qkn", [128, 1024])
            beta = sb("beta", [128, 4]); g = sb("g", [128, 4]); G = sb("G", [128, 4]); Gt = sb("Gt", [128, 4])
            eG = sb("eG", [128, 4]); neG = sb("neG", [128, 4]); eGC = sb("eGC", [128, 4]); eGr = sb("eGr", [128, 4])
            kbm = sb("kbm", [128, 512]); kam = sb("kam", [128, 512]); qgm = sb("qgm", [128, 512])
            khat = sb("khat", [128, 512]); vp = sb("vp", [128, 512])
            kT = sb("kT", [128, 512]); kbT = sb("kbT", [128, 512]); qT = sb("qT", [128, 512])
            aT = sb("aT", [128, 512]); rT = sb("rT", [128, 512])
            dg = sb("dg", [128, 512]); gam = sb("gam", [128, 512])
            Nm = [sb("Nm%d" % k, [128, 512]) for k in range(2)]
            Lm = [sb("Lm%d" % k, [128, 512]) for k in range(2)]
            Pm = [sb("Pm%d" % k, [128, 512]) for k in range(2)]
            MT = sb("MT", [128, 512]); N0k = sb("N0k", [128, 512])
            X1 = sb("X1", [128, 512]); UV = sb("UV", [128, 512]); Y = sb("Y", [128, 512])
            T0 = sb("T0", [128, 512])
            o = sb("o", [128, 512]); sz = sb("sz", [128, 512]); ysq = sb("ysq", [128, 512])
            yss = sb("yss", [128, 4]); yrs = sb("yrs", [128, 4])
            kb.memset("dve", T0[:], 0.0, w=[T0.name])
            hv = lambda a: a.rearrange("p (h n) -> p h n", h=H)
            for t in range(NT):
                r0 = t * 128
                for k in range(4):
                    sh = 3 - k
                    if r0 - sh < 0:
                        kb.memset("pool", X[k][:], 0.0, w=[X[k].name])
                        kb.dma(X[k][sh:128, :], P[0:128 - sh, c0:c0 + 1536], w=[X[k].name])
                    else:
                        kb.dma(X[k][:], P[r0 - sh:r0 - sh + 128, c0:c0 + 1536], w=[X[k].name])
                kb.dma(zt[:], P[r0:r0 + 128, c0 + 1536:c0 + 2048], w=[zt.name])
                kb.dma(ab[:], P[r0:r0 + 128, c0 + 2048:c0 + 2056], w=[ab.name])
                kb.tt("dve", y[:], X[0][:], cw[:, 0, :], ALU.mult, r=[X[0].name, cw.name], w=[y.name])
                for k in range(1, 4):
                    kb.tt("pool", t1[:], X[k][:], cw[:, k, :], ALU.mult, r=[X[k].name, cw.name], w=[t1.name])
                    kb.tt("dve", y[:], y[:], t1[:], ALU.add, r=[y.name, t1.name], w=[y.name])
                kb.act(y[:], y[:], AF.Silu, r=[y.name], w=[y.name])
                kb.act(t1[:, 0:1024], y[:, 0:1024], AF.Square, r=[y.name], w=[t1.name])
                kb.add("dve", lambda e: e.tensor_reduce(ssq[:], t1[:, 0:1024].rearrange("p (h n) -> p h n", h=8), AX.X, ALU.add),
                       [t1.name], [ssq.name])
                self.rsqrt(rs[:], ssq[:], 1.0, EPS, [ssq.name], [rs.name])
                kb.ts("dve", rs[:, 0:4], rs[:, 0:4], float(N) ** -0.5, None, ALU.mult, r=[rs.name], w=[rs.name])
                kb.tt("dve", qkn[:].rearrange("p (h n) -> p h n", h=8), y[:, 0:1024].rearrange("p (h n) -> p h n", h=8),
                      rs[:].unsqueeze(2).to_broadcast([128, 8, 128]), ALU.mult, r=[y.name, rs.name], w=[qkn.name])
                qn = qkn[:, 0:512]; kn = qkn[:, 512:1024]; v = y[:, 1024:1536]
                kb.act(beta[:], ab[:, 4:8], AF.Sigmoid, r=[ab.name], w=[beta.name])
                kb.tt("dve", g[:], ab[:, 0:4], dtb[:], ALU.add, r=[ab.name, dtb.name], w=[g.name])
                kb.act(g[:], g[:], AF.Exp, r=[g.name], w=[g.name])
                kb.act(g[:], g[:], AF.Ln, r=[g.name], w=[g.name], bias=1.0)
                kb.tt("dve", g[:], g[:], nea[:], ALU.mult, r=[g.name, nea.name], w=[g.name])
                ps = self.next_ps()
                kb.mm(ps[:, 0:4], tin[:], g[:], True, True, r=[tin.name, g.name], w=[ps.name])
                kb.mm(ps[:, 4:8], ones[:], g[:], True, True, r=[ones.name, g.name], w=[ps.name])
                kb.cp("dve", G[:], ps[:, 0:4], r=[ps.name], w=[G.name])
                kb.cp("dve", Gt[:], ps[:, 4:8], r=[ps.name], w=[Gt.name])
                kb.act(eG[:], G[:], AF.Exp, r=[G.name], w=[eG.name])
                kb.act(eGC[:], Gt[:], AF.Exp, r=[Gt.name], w=[eGC.name])
                kb.tt("dve", eGr[:], Gt[:], G[:], ALU.subtract, r=[Gt.name, G.name], w=[eGr.name])
                kb.act(eGr[:], eGr[:], AF.Exp, r=[eGr.name], w=[eGr.name])
                bb = lambda a: a[:].unsqueeze(2).to_broadcast([128, 4, 128])
                kb.tt("dve", hv(kbm[:]), hv(kn), bb(beta), ALU.mult, r=[qkn.name, beta.name], w=[kbm.name])
                kb.tt("pool", hv(vp[:]), hv(v), bb(beta), ALU.mult, r=[y.name, beta.name], w=[vp.name])
                kb.tt("dve", hv(kam[:]), hv(kbm[:]), bb(eG), ALU.mult, r=[kbm.name, eG.name], w=[kam.name])
                kb.ts("pool", kam[:], kam[:], -1.0, None, ALU.mult, r=[kam.name], w=[kam.name])
                kb.tt("dve", hv(qgm[:]), hv(qn), bb(eG), ALU.mult, r=[qkn.name, eG.name], w=[qgm.name])
                kb.tt("pool", hv(khat[:]), hv(kn), bb(eGr), ALU.mult, r=[qkn.name, eGr.name], w=[khat.name])
                self.tr_f32(kT[:], kn, idf); self.tr_f32(kbT[:], kbm[:], idf, "dve")
                self.tr_f32(qT[:], qn, idf); self.tr_f32(aT[:], kam[:], idf, "dve"); self.tr_f32(rT[:], qgm[:], idf)
                for h in range(H):
                    kb.ts("dve", dg[:, h * 128:(h + 1) * 128], idf[:], G[:, h:h + 1], None, ALU.mult,
                          r=[idf.name, G.name], w=[dg.name])
                ps = self.next_ps()
                for h in range(H):
                    kb.mm(ps[:, h * 128:(h + 1) * 128], ones[:], dg[:, h * 128:(h + 1) * 128], True, True,
                          r=[ones.name, dg.name], w=[ps.name])
                for h in range(H):
                    kb.ts("dve", gam[:, h * 128:(h + 1) * 128], ps[:, h * 128:(h + 1) * 128], G[:, h:h + 1], 0.0,
                          ALU.subtract, ALU.min, r=[ps.name, G.name], w=[gam.name])
                kb.act(gam[:], gam[:], AF.Exp, r=[gam.name], w=[gam.name])
                ps = self.next_ps()
                ps2 = self.next_ps()
                for h in range(H):
                    hs = slice(h * 128, (h + 1) * 128)
                    kb.mm(ps[:, hs], kT[:, hs], kbT[:, hs], True, True, r=[kT.name, kbT.name], w=[ps.name])
                    kb.mm(ps2[:, hs], kT[:, hs], qT[:, hs], True, True, r=[kT.name, qT.name], w=[ps2.name])
                N0 = Nm[0]
                kb.tt("dve", N0[:], ps[:], gam[:], ALU.mult, r=[ps.name, gam.name], w=[N0.name])
                kb.stt("dve", hv(N0[:]), hv(N0[:]), -1.0, tst[:].unsqueeze(1).to_broadcast([128, 4, 128]), ALU.mult, ALU.mult,
                       r=[N0.name, tst.name], w=[N0.name])
                kb.tt("dve", MT[:], ps2[:], gam[:], ALU.mult, r=[ps2.name, gam.name], w=[MT.name])
                kb.tt("pool", hv(MT[:]), hv(MT[:]), tin[:].unsqueeze(1).to_broadcast([128, 4, 128]), ALU.mult,
                      r=[MT.name, tin.name], w=[MT.name])
                kb.cp("pool", N0k[:], N0[:], r=[N0.name], w=[N0k.name])
                L0 = Lm[0]
                self.tr_f32(L0[:], N0[:], idf)
                P0 = Pm[0]
                kb.tt("dve", hv(P0[:]), hv(N0[:]), idf[:].unsqueeze(1).to_broadcast([128, 4, 128]), ALU.add,
                      r=[N0.name, idf.name], w=[P0.name])
                cur = 0
                for lev in range(6):
                    Nc, Lc, Pc = Nm[cur], Lm[cur], Pm[cur]
                    Nn, Ln, Pn = Nm[1 - cur], Lm[1 - cur], Pm[1 - cur]
                    psl = self.next_ps(); psn = self.next_ps()
                    for h in range(H):
                        hs = slice(h * 128, (h + 1) * 128)
                        kb.mm(psl[:, hs], Nc[:, hs], Lc[:, hs], True, True, r=[Nc.name, Lc.name], w=[psl.name])
                    if lev < 5:
                        for h in range(H):
                            hs = slice(h * 128, (h + 1) * 128)
                            kb.mm(psn[:, hs], Lc[:, hs], Nc[:, hs], True, True, r=[Nc.name, Lc.name], w=[psn.name])
                    kb.cp("act", Ln[:], psl[:], r=[psl.name], w=[Ln.name])
                    if lev < 5:
                        kb.cp("dve", Nn[:], psn[:], r=[psn.name], w=[Nn.name])
                    psp = self.next_ps()
                    for h in range(H):
                        hs = slice(h * 128, (h + 1) * 128)
                        kb.mm(psp[:, hs], Ln[:, hs], Pc[:, hs], True, True, r=[Ln.name, Pc.name], w=[psp.name])
                    kb.tt("dve", Pn[:], psp[:], Pc[:], ALU.add, r=[psp.name, Pc.name], w=[Pn.name])
                    cur = 1 - cur
                WT = Pm[cur]
                N0 = Nm[0]
                ps = self.next_ps()
                for h in range(H):
                    hs = slice(h * 128, (h + 1) * 128)
                    kb.mm(ps[:, hs], aT[:, hs], T0[:, hs], True, False, r=[aT.name, T0.name], w=[ps.name])
                    kb.mm(ps[:, hs], N0k[:, hs], vp[:, hs], False, True, r=[N0k.name, vp.name], w=[ps.name])
                kb.cp("act", X1[:], ps[:], r=[ps.name], w=[X1.name])
                ps = self.next_ps()
                for h in range(H):
                    hs = slice(h * 128, (h + 1) * 128)
                    kb.mm(ps[:, hs], WT[:, hs], X1[:, hs], True, True, r=[WT.name, X1.name], w=[ps.name])
                kb.tt("dve", UV[:], ps[:], vp[:], ALU.add, r=[ps.name, vp.name], w=[UV.name])
                ps = self.next_ps()
                ps2 = self.next_ps()
                for h in range(H):
                    hs = slice(h * 128, (h + 1) * 128)
                    kb.mm(ps[:, hs], rT[:, hs], T0[:, hs], True, False, r=[rT.name, T0.name], w=[ps.name])
                    kb.mm(ps[:, hs], MT[:, hs], UV[:, hs], False, True, r=[MT.name, UV.name], w=[ps.name])
                    kb.mm(ps2[:, hs], khat[:, hs], UV[:, hs], True, True, r=[khat.name, UV.name], w=[ps2.name])
                kb.cp("act", Y[:], ps[:], r=[ps.name], w=[Y.name])
                for h in range(H):
                    hs = slice(h * 128, (h + 1) * 128)
                    kb.stt("dve", T0[:, hs], T0[:, hs], eGC[:, h:h + 1], ps2[:, hs], ALU.mult, ALU.add,
                           r=[T0.name, eGC.name, ps2.name], w=[T0.name])
                kb.act(ysq[:], Y[:], AF.Square, r=[Y.name], w=[ysq.name])
                kb.add("dve", lambda e: e.tensor_reduce(yss[:], ysq[:].rearrange("p (h n) -> p h n", h=4), AX.X, ALU.add),
                       [ysq.name], [yss.name])
                self.rsqrt(yrs[:], yss[:], 1.0 / N, EPS, [yss.name], [yrs.name])
                kb.act(sz[:], zt[:], AF.Silu, r=[zt.name], w=[sz.name])
                kb.tt("dve", hv(o[:]), hv(Y[:]), bb(yrs), ALU.mult, r=[Y.name, yrs.name], w=[o.name])
                kb.tt("pool", hv(o[:]), hv(o[:]), nw[:].unsqueeze(1).to_broadcast([128, 4, 128]), ALU.mult,
                      r=[o.name, nw.name], w=[o.name])
                kb.tt("dve", o[:], o[:], sz[:], ALU.mult, r=[o.name, sz.name], w=[o.name])
                kb.dma(self.o_mix[l][2][r0:r0 + 128, :], o[:], r=[o.name], q="pool")
            kb.flush()

    def declare_rwkv(self):
        L, S = self.L, self.S
        i = self.inp
        for nm, shp in (("rwkv_mu", [L, RWKV_IN]), ("rwkv_w0", [L, MIX]), ("rwkv_w_up", [L, 64, MIX]),
                        ("rwkv_a0", [L, MIX]), ("rwkv_a_up", [L, 64, MIX]), ("rwkv_g_up", [L, 128, MIX]),
                        ("rwkv_k_k", [L, MIX]), ("rwkv_k_a", [L, MIX]), ("rwkv_r_k", [L, MIX]),
                        ("rwkv_ln_w", [L, MIX]), ("rwkv_ln_b", [L, MIX]), ("rwkv_v0", [1, MIX]),
                        ("rwkv_vres_up", [1, 32, MIX])):
            setattr(self, nm, i(nm, shp))
        self.vfirst = self.scr("vfirst", [S, MIX])

    def phase_rwkv(self, l):
        kb, S, NT = self.kb, self.S, self.NT
        H, N = 8, 64
        P = self.proj_tm[l]
        c0 = OFF_RWKV
        with ExitStack() as st:
            sb = lambda name, shape, dt=F32: self.sb(st, name, shape, dt)
            self.pss = [self.psb(st, "ps_r%d" % j) for j in range(8)]
            idf = sb("idf", [128, 128]); ones = sb("ones", [128, 128])
            tin = sb("tin", [128, 128]); tst = sb("tst", [128, 128])
            kb.dma(idf[:], self.ident_f[:, :], w=[idf.name])
            kb.dma(ones[:], self.ones_f[:, :], w=[ones.name])
            kb.dma(tin[:], self.triu_incl[:, :], w=[tin.name])
            kb.dma(tst[:], self.triu_strict[:, :], w=[tst.name])

            def bvec(name, src, n):
                tl = sb(name, [128, n])
                kb.dma(tl[:], src[l:l + 1, :].to_broadcast([128, n]), w=[tl.name])
                return tl
            mu = bvec("mu", self.rwkv_mu, RWKV_IN)
            w0 = bvec("w0", self.rwkv_w0, MIX); a0 = bvec("a0", self.rwkv_a0, MIX)
            k_k = bvec("k_k", self.rwkv_k_k, MIX); k_a = bvec("k_a", self.rwkv_k_a, MIX)
            r_k = bvec("r_k", self.rwkv_r_k, MIX); ln_w = bvec("ln_w", self.rwkv_ln_w, MIX)
            ln_b = bvec("ln_b", self.rwkv_ln_b, MIX)
            wup = sb("wup", [64, MIX]); aup = sb("aup", [64, MIX]); gup = sb("gup", [128, MIX])
            kb.dma(wup[:], self.rwkv_w_up[l], w=[wup.name])
            kb.dma(aup[:], self.rwkv_a_up[l], w=[aup.name])
            kb.dma(gup[:], self.rwkv_g_up[l], w=[gup.name])
            if l > 0:
                v0 = sb("v0", [128, MIX]); vup = sb("vup", [32, MIX])
                kb.dma(v0[:], self.rwkv_v0[0:1, :].to_broadcast([128, MIX]), w=[v0.name])
                kb.dma(vup[:], self.rwkv_vres_up[0], w=[vup.name])
                vf = sb("vf", [128, MIX]); xvd = sb("xvd", [128, 32]); xvdT = sb("xvdT", [32, 128])
            X0 = sb("X0", [128, RWKV_IN]); X1s = sb("X1s", [128, RWKV_IN]); c = sb("c", [128, RWKV_IN])
            sm = sb("sm", [128, 256]); smT = sb("smT", [128, 256])
            wdT = sb("wdT", [64, 128]); adT = sb("adT", [64, 128])
            ld = sb("ld", [128, MIX]); a = sb("a", [128, MIX]); g = sb("g", [128, MIX])
            kk = sb("kk", [128, MIX]); t1 = sb("t1", [128, MIX]); t2 = sb("t2", [128, MIX])
            ssq = sb("ssq", [128, 8]); rs = sb("rs", [128, 8])
            kp = sb("kp", [128, MIX]); bet = sb("bet", [128, MIX])
            G = sb("G", [128, MIX]); eG = sb("eG", [128, MIX]); enG = sb("enG", [128, MIX])
            eGr = sb("eGr", [128, MIX]); eGm = sb("eGm", [128, MIX])
            at = sb("at", [128, MIX]); bt = sb("bt", [128, MIX]); kt = sb("kt", [128, MIX]); rt = sb("rt", [128, MIX])
            bh = sb("bh", [128, MIX]); kh = sb("kh", [128, MIX])
            aT = sb("aT", [64, 1024]); bT = sb("bT", [64, 1024]); kT = sb("kT", [64, 1024]); rT = sb("rT", [64, 1024])
            PCT = sb("PCT", [64, 8])
            Nm = [sb("Nm%d" % k, [128, 1024]) for k in range(2)]
            Lm = [sb("Lm%d" % k, [128, 1024]) for k in range(2)]
            Pm = [sb("Pm%d" % k, [128, 1024]) for k in range(2)]
            LakT = sb("LakT", [128, 1024]); MrbT = sb("MrbT", [128, 1024]); MrkT = sb("MrkT", [128, 1024])
            X1 = sb("X1", [128, MIX]); U = sb("U", [128, MIX]); Y = sb("Y", [128, MIX])
            T0 = sb("T0", [64, MIX])
            mean = sb("mean", [128, 8]); var = sb("var", [128, 8]); rkv = sb("rkv", [128, 8])
            kb.memset("dve", T0[:], 0.0, w=[T0.name])
            hv = lambda x: x.rearrange("p (h n) -> p h n", h=H)
            b8 = lambda x: x[:].unsqueeze(2).to_broadcast([128, 8, 64])
            msk = lambda m: m[:].unsqueeze(1).to_broadcast([128, 8, 128])
            hm = lambda x: x.rearrange("p (h n) -> p h n", h=H)

            def tr64(dst, src):
                for half in range(2):
                    ps = self.next_ps()
                    for hh in range(4):
                        h = half * 4 + hh
                        kb.tr(ps[0:64, hh * 128:(hh + 1) * 128], src[:, h * 64:(h + 1) * 64], idf[:],
                              r=[src.name, idf.name], w=[ps.name])
                    kb.cp("act" if half else "dve", dst[:, half * 512:(half + 1) * 512], ps[0:64, :],
                          r=[ps.name], w=[dst.name])

            def mat8(dst, lT, rT_, mask):
                for half in range(2):
                    ps = self.next_ps()
                    for hh in range(4):
                        h = half * 4 + hh
                        kb.mm(ps[:, hh * 128:(hh + 1) * 128], lT[:, h * 128:(h + 1) * 128], rT_[:, h * 128:(h + 1) * 128],
                              True, True, r=[lT.name, rT_.name], w=[ps.name])
                    kb.tt("dve", dst[:, half * 512:(half + 1) * 512].rearrange("p (h n) -> p h n", h=4),
                          ps[:].rearrange("p (h n) -> p h n", h=4),
                          mask[:].unsqueeze(1).to_broadcast([128, 4, 128]), ALU.mult,
                          r=[ps.name, mask.name], w=[dst.name])

            for t in range(NT):
                r0 = t * 128
                kb.dma(X0[:], P[r0:r0 + 128, c0:c0 + RWKV_IN], w=[X0.name])
                if t == 0:
                    kb.memset("pool", X1s[:], 0.0, w=[X1s.name])
                    kb.dma(X1s[1:128, :], P[0:127, c0:c0 + RWKV_IN], w=[X1s.name])
                else:
                    kb.dma(X1s[:], P[r0 - 1:r0 + 127, c0:c0 + RWKV_IN], w=[X1s.name])
                kb.tt("dve", c[:], X1s[:], X0[:], ALU.subtract, r=[X0.name, X1s.name], w=[c.name])
                kb.tt("pool", c[:], c[:], mu[:], ALU.mult, r=[c.name, mu.name], w=[c.name])
                kb.tt("dve", c[:], c[:], X0[:], ALU.add, r=[c.name, X0.name], w=[c.name])
                r_ = c[:, 0:512]; k_ = c[:, 512:1024]; v_ = c[:, 1024:1536]
                kb.act(sm[:, 0:64], c[:, 1536:1600], AF.Tanh, r=[c.name], w=[sm.name])
                kb.cp("dve", sm[:, 64:128], c[:, 1600:1664], r=[c.name], w=[sm.name])
                kb.act(sm[:, 128:256], c[:, 1664:1792], AF.Sigmoid, r=[c.name], w=[sm.name])
                ps = self.next_ps()
                kb.tr(ps[0:64, 0:128], sm[:, 0:64], idf[:], r=[sm.name, idf.name], w=[ps.name])
                kb.tr(ps[0:64, 128:256], sm[:, 64:128], idf[:], r=[sm.name, idf.name], w=[ps.name])
                kb.tr(ps[:, 256:384], sm[:, 128:256], idf[:], r=[sm.name, idf.name], w=[ps.name])
                kb.cp("act", wdT[:], ps[0:64, 0:128], r=[ps.name], w=[wdT.name])
                kb.cp("dve", adT[:], ps[0:64, 128:256], r=[ps.name], w=[adT.name])
                kb.cp("act", smT[:, 0:128], ps[:, 256:384], r=[ps.name], w=[smT.name])
                psw = self.next_ps(); psa = self.next_ps(); psg = self.next_ps()
                kb.mm(psw[:], wdT[:], wup[:], True, True, r=[wdT.name, wup.name], w=[psw.name])
                kb.mm(psa[:], adT[:], aup[:], True, True, r=[adT.name, aup.name], w=[psa.name])
                kb.mm(psg[:], smT[:, 0:128], gup[:], True, True, r=[smT.name, gup.name], w=[psg.name])
                kb.tt("dve", ld[:], psw[:], w0[:], ALU.add, r=[psw.name, w0.name], w=[ld.name])
                kb.act(ld[:], ld[:], AF.Sigmoid, r=[ld.name], w=[ld.name])
                kb.ts("dve", ld[:], ld[:], -float(np.exp(-0.5)), None, ALU.mult, r=[ld.name], w=[ld.name])
                kb.tt("dve", a[:], psa[:], a0[:], ALU.add, r=[psa.name, a0.name], w=[a.name])
                kb.act(a[:], a[:], AF.Sigmoid, r=[a.name], w=[a.name])
                kb.cp("act", g[:], psg[:], r=[psg.name], w=[g.name])
                if l == 0:
                    kb.dma(self.vfirst[r0:r0 + 128, :], v_, r=[c.name], q="pool")
                else:
                    kb.dma(vf[:], self.vfirst[r0:r0 + 128, :], w=[vf.name])
                    kb.dma(xvd[:], self.proj_g[l][r0:r0 + 128, 3 * D:3 * D + 32], w=[xvd.name])
                    ps = self.next_ps()
                    kb.tr(ps[0:32, 0:128], xvd[:], idf[:], r=[xvd.name, idf.name], w=[ps.name])
                    kb.cp("act", xvdT[:], ps[0:32, 0:128], r=[ps.name], w=[xvdT.name])
                    ps = self.next_ps()
                    kb.mm(ps[:], xvdT[:], vup[:], True, True, r=[xvdT.name, vup.name], w=[ps.name])
                    kb.tt("dve", t1[:], ps[:], v0[:], ALU.add, r=[ps.name, v0.name], w=[t1.name])
                    kb.act(t1[:], t1[:], AF.Sigmoid, r=[t1.name], w=[t1.name])
                    kb.tt("dve", t2[:], vf[:], v_, ALU.subtract, r=[vf.name, c.name], w=[t2.name])
                    kb.tt("dve", t2[:], t2[:], t1[:], ALU.mult, r=[t2.name, t1.name], w=[t2.name])
                    kb.tt("dve", v_, v_, t2[:], ALU.add, r=[c.name, t2.name], w=[c.name])
                kb.tt("dve", kk[:], k_, k_k[:], ALU.mult, r=[c.name, k_k.name], w=[kk.name])
                kb.act(t1[:], kk[:], AF.Square, r=[kk.name], w=[t1.name])
                kb.add("dve", lambda e: e.tensor_reduce(ssq[:], t1[:].rearrange("p (h n) -> p h n", h=8), AX.X, ALU.add),
                       [t1.name], [ssq.name])
                self.rsqrt(rs[:], ssq[:], 1.0, EPS, [ssq.name], [rs.name])
                kb.tt("dve", hv(kk[:]), hv(kk[:]), b8(rs), ALU.mult, r=[kk.name, rs.name], w=[kk.name])
                kb.ts("dve", t2[:], a[:], -1.0, None, ALU.add, r=[a.name], w=[t2.name])
                kb.tt("dve", t2[:], t2[:], k_a[:], ALU.mult, r=[t2.name, k_a.name], w=[t2.name])
                kb.ts("dve", t2[:], t2[:], 1.0, None, ALU.add, r=[t2.name], w=[t2.name])
                kb.tt("dve", kp[:], t2[:], k_, ALU.mult, r=[t2.name, c.name], w=[kp.name])
                kb.tt("pool", bet[:], kk[:], a[:], ALU.mult, r=[kk.name, a.name], w=[bet.name])
                ps = self.next_ps(); ps2 = self.next_ps()
                kb.mm(ps[:], tin[:], ld[:], True, True, r=[tin.name, ld.name], w=[ps.name])
                kb.mm(ps2[:], ones[:], ld[:], True, True, r=[ones.name, ld.name], w=[ps2.name])
                kb.cp("dve", G[:], ps[:], r=[ps.name], w=[G.name])
                kb.tt("dve", eGr[:], ps2[:], G[:], ALU.subtract, r=[ps2.name, G.name], w=[eGr.name])
                kb.act(eGr[:], eGr[:], AF.Exp, r=[eGr.name], w=[eGr.name])
                kb.act(eG[:], G[:], AF.Exp, r=[G.name], w=[eG.name])
                kb.act(enG[:], G[:], AF.Exp, r=[G.name], w=[enG.name], scale=-1.0)
                kb.tt("pool", eGm[:], G[:], ld[:], ALU.subtract, r=[G.name, ld.name], w=[eGm.name])
                kb.act(eGm[:], eGm[:], AF.Exp, r=[eGm.name], w=[eGm.name])
                ps = self.next_ps()
                for h in range(H):
                    kb.mm(ps[0:64, h:h + 1], ld[:, h * 64:(h + 1) * 64], ones[:, 0:1], True, True,
                          r=[ld.name, ones.name], w=[ps.name])
                kb.act(PCT[:], ps[0:64, 0:8], AF.Exp, r=[ps.name], w=[PCT.name])
                kb.tt("dve", at[:], kk[:], eGm[:], ALU.mult, r=[kk.name, eGm.name], w=[at.name])
                kb.ts("pool", at[:], at[:], -1.0, None, ALU.mult, r=[at.name], w=[at.name])
                kb.tt("dve", bt[:], bet[:], enG[:], ALU.mult, r=[bet.name, enG.name], w=[bt.name])
                kb.tt("pool", kt[:], kp[:], enG[:], ALU.mult, r=[kp.name, enG.name], w=[kt.name])
                kb.tt("dve", rt[:], r_, eG[:], ALU.mult, r=[c.name, eG.name], w=[rt.name])
                kb.tt("pool", bh[:], bet[:], eGr[:], ALU.mult, r=[bet.name, eGr.name], w=[bh.name])
                kb.tt("dve", kh[:], kp[:], eGr[:], ALU.mult, r=[kp.name, eGr.name], w=[kh.name])
                tr64(aT, at); tr64(bT, bt); tr64(kT, kt); tr64(rT, rt)
                N0 = Nm[0]
                mat8(N0, bT, aT, tst)
                mat8(LakT, kT, aT, tst)
                mat8(MrbT, bT, rT, tin)
                mat8(MrkT, kT, rT, tin)
                L0 = Lm[0]
                for half in range(2):
                    self.tr_f32(L0[:, half * 512:(half + 1) * 512], N0[:, half * 512:(half + 1) * 512], idf)
                P0 = Pm[0]
                kb.tt("dve", hm(P0[:]), hm(N0[:]), msk(idf), ALU.add, r=[N0.name, idf.name], w=[P0.name])
                cur = 0
                for lev in range(6):
                    Nc, Lc, Pc = Nm[cur], Lm[cur], Pm[cur]
                    Nn, Ln, Pn = Nm[1 - cur], Lm[1 - cur], Pm[1 - cur]
                    for half in range(2):
                        hsl = slice(half * 512, (half + 1) * 512)
                        psl = self.next_ps()
                        for hh in range(4):
                            hs = slice(half * 512 + hh * 128, half * 512 + (hh + 1) * 128)
                            kb.mm(psl[:, hh * 128:(hh + 1) * 128], Nc[:, hs], Lc[:, hs], True, True,
                                  r=[Nc.name, Lc.name], w=[psl.name])
                        kb.cp("act", Ln[:, hsl], psl[:], r=[psl.name], w=[Ln.name])
                        if lev < 5:
                            psn = self.next_ps()
                            for hh in range(4):
                                hs = slice(half * 512 + hh * 128, half * 512 + (hh + 1) * 128)
                                kb.mm(psn[:, hh * 128:(hh + 1) * 128], Lc[:, hs], Nc[:, hs], True, True,
                                      r=[Nc.name, Lc.name], w=[psn.name])
                            kb.cp("dve", Nn[:, hsl], psn[:], r=[psn.name], w=[Nn.name])
                    for half in range(2):
                        hsl = slice(half * 512, (half + 1) * 512)
                        psp = self.next_ps()
                        for hh in range(4):
                            hs = slice(half * 512 + hh * 128, half * 512 + (hh + 1) * 128)
                            kb.mm(psp[:, hh * 128:(hh + 1) * 128], Ln[:, hs], Pc[:, hs], True, True,
                                  r=[Ln.name, Pc.name], w=[psp.name])
                        kb.tt("dve", Pn[:, hsl], psp[:], Pc[:, hsl], ALU.add, r=[psp.name, Pc.name], w=[Pn.name])
                    cur = 1 - cur
                WT = Pm[cur]
                ps = self.next_ps()
                for h in range(H):
                    hs = slice(h * 64, (h + 1) * 64); ms = slice(h * 128, (h + 1) * 128)
                    kb.mm(ps[:, hs], aT[:, ms], T0[:, hs], True, False, r=[aT.name, T0.name], w=[ps.name])
                    kb.mm(ps[:, hs], LakT[:, ms], c[:, 1024 + h * 64:1024 + (h + 1) * 64], False, True,
                          r=[LakT.name, c.name], w=[ps.name])
                kb.cp("act", X1[:], ps[:], r=[ps.name], w=[X1.name])
                ps = self.next_ps()
                for h in range(H):
                    hs = slice(h * 64, (h + 1) * 64); ms = slice(h * 128, (h + 1) * 128)
                    kb.mm(ps[:, hs], WT[:, ms], X1[:, hs], True, True, r=[WT.name, X1.name], w=[ps.name])
                kb.cp("dve", U[:], ps[:], r=[ps.name], w=[U.name])
                ps = self.next_ps(); ps2 = self.next_ps()
                for h in range(H):
                    hs = slice(h * 64, (h + 1) * 64); ms = slice(h * 128, (h + 1) * 128)
                    vs = c[:, 1024 + h * 64:1024 + (h + 1) * 64]
                    kb.mm(ps[:, hs], rT[:, ms], T0[:, hs], True, False, r=[rT.name, T0.name], w=[ps.name])
                    kb.mm(ps[:, hs], MrbT[:, ms], U[:, hs], False, False, r=[MrbT.name, U.name], w=[ps.name])
                    kb.mm(ps[:, hs], MrkT[:, ms], vs, False, True, r=[MrkT.name, c.name], w=[ps.name])
                    kb.mm(ps2[0:64, hs], bh[:, hs], U[:, hs], True, False, r=[bh.name, U.name], w=[ps2.name])
                    kb.mm(ps2[0:64, hs], kh[:, hs], vs, False, True, r=[kh.name, c.name], w=[ps2.name])
                kb.cp("act", Y[:], ps[:], r=[ps.name], w=[Y.name])
                for h in range(H):
                    hs = slice(h * 64, (h + 1) * 64)
                    kb.stt("dve", T0[:, hs], T0[:, hs], PCT[:, h:h + 1], ps2[0:64, hs], ALU.mult, ALU.add,
                           r=[T0.name, PCT.name, ps2.name], w=[T0.name])
                kb.add("dve", lambda e: e.tensor_reduce(mean[:], Y[:].rearrange("p (h n) -> p h n", h=8), AX.X, ALU.add),
                       [Y.name], [mean.name])
                kb.ts("dve", mean[:], mean[:], 1.0 / N, None, ALU.mult, r=[mean.name], w=[mean.name])
                kb.tt("dve", hv(Y[:]), hv(Y[:]), b8(mean), ALU.subtract, r=[Y.name, mean.name], w=[Y.name])
                kb.act(t1[:], Y[:], AF.Square, r=[Y.name], w=[t1.name])
                kb.add("dve", lambda e: e.tensor_reduce(var[:], t1[:].rearrange("p (h n) -> p h n", h=8), AX.X, ALU.add),
                       [t1.name], [var.name])
                self.rsqrt(var[:], var[:], 1.0 / N, 64e-5, [var.name], [var.name])
                kb.tt("dve", hv(Y[:]), hv(Y[:]), b8(var), ALU.mult, r=[Y.name, var.name], w=[Y.name])
                kb.tt("pool", Y[:], Y[:], ln_w[:], ALU.mult, r=[Y.name, ln_w.name], w=[Y.name])
                kb.tt("dve", Y[:], Y[:], ln_b[:], ALU.add, r=[Y.name, ln_b.name], w=[Y.name])
                kb.tt("pool", t2[:], r_, kp[:], ALU.mult, r=[c.name, kp.name], w=[t2.name])
                kb.tt("pool", t2[:], t2[:], r_k[:], ALU.mult, r=[t2.name, r_k.name], w=[t2.name])
                kb.add("dve", lambda e: e.tensor_reduce(rkv[:], t2[:].rearrange("p (h n) -> p h n", h=8), AX.X, ALU.add),
                       [t2.name], [rkv.name])
                kb.tt("dve", hv(t2[:]), hv(v_), b8(rkv), ALU.mult, r=[c.name, rkv.name], w=[t2.name])
                kb.tt("dve", Y[:], Y[:], t2[:], ALU.add, r=[Y.name, t2.name], w=[Y.name])
                kb.tt("dve", Y[:], Y[:], g[:], ALU.mult, r=[Y.name, g.name], w=[Y.name])
                kb.dma(self.o_mix[l][1][r0:r0 + 128, :], Y[:], r=[Y.name], q="pool")
            kb.flush()

    def build(self):
        self.declare(); self.declare_rest(); self.declare_gdn(); self.declare_rwkv(); self.declare_nsa()
        x = self.x_in
        for l in range(self.L):
            self.phase_proj(l, x)
            self.phase_nsa_prep(l)
            self.phase_nsa_attn(l)
            self.phase_rwkv(l)
            self.phase_gdn(l)
            self.phase_merge(l, x)
            self.phase_ffn(l)
            x = self.x_lay[l]
        return self.nc


def _consts():
    import ml_dtypes
    f = np.float32
    return {"ident_bf": np.eye(128, dtype=ml_dtypes.bfloat16), "ident_f": np.eye(128, dtype=f),
            "ones_f": np.ones((128, 128), f), "triu_incl": np.triu(np.ones((128, 128), f)),
            "triu_strict": np.triu(np.ones((128, 128), f), 1)}


def kernel(**inputs):
    x = np.asarray(inputs["x"], np.float32)
    B, S, _ = x.shape
    L = inputs["w_in"].shape[0]
    prog = Prog(S, n_layers=L)
    nc = prog.build()
    w_in = np.asarray(inputs["w_in"], np.float32)
    ext = np.zeros((L, D, 32), np.float32)
    ext[1:] = np.asarray(inputs["rwkv_vres_down"], np.float32)
    shared = dict(_consts())
    shared.update(_nsa_consts(S))
    shared["w_in"] = np.ascontiguousarray(np.concatenate([w_in, ext], axis=2))
    for k in prog.din:
        if k not in shared and k != "x":
            shared[k] = np.ascontiguousarray(np.asarray(inputs[k], np.float32))
    in_maps = [dict(shared, x=np.ascontiguousarray(x[b])) for b in range(B)]
    res = run_bass_kernel_spmd(nc, in_maps, core_ids=list(range(B)))
    return np.stack([np.asarray(r["out"], np.float32) for r in res.results], axis=0)


def _nsa_consts(S):
    import ml_dtypes
    f = np.float32
    NB = S // 64
    n_cmp = (S - 32) // 16 + 1
    nch = (n_cmp + 127) // 128
    ex = np.zeros((128, S), f)
    ex[np.arange(S) // 64, np.arange(S)] = 1.0
    k = np.arange(128)[:, None]
    q = np.arange(128)[None, :]
    caus = np.where(k > q, NEGM, 0.0).astype(f)
    win = np.where(k <= q, NEGM, 0.0).astype(f)
    rr = np.arange(17)[None, :, None]
    cm = ((16 * k[:, :, None] + 31) <= (128 * rr + q[:, None, :])).astype(f)
    cs = np.arange(nch * 128) * 16
    ss = np.arange(NB) * 64
    ov = np.clip(np.minimum(cs[:, None] + 32, ss[None, :] + 64) - np.maximum(cs[:, None], ss[None, :]), 0, None) / 32.0
    ov[n_cmp:] = 0.0
    c2s = ov.reshape(nch, 128, NB).transpose(1, 0, 2).astype(f)
    return {"exall": ex.astype(ml_dtypes.bfloat16),
            "causneg": np.tile(caus, (1, 4)).astype(ml_dtypes.bfloat16),
            "winneg": np.tile(win, (1, 4)).astype(ml_dtypes.bfloat16),
            "cmask": np.ascontiguousarray(cm), "cmp2slc": np.ascontiguousarray(c2s)}


def _declare_nsa(self):
    L, S = self.L, self.S
    i = self.inp
    self.NB = S // 64
    self.n_cmp = (S - 32) // 16 + 1
    self.nch = (self.n_cmp + 127) // 128
    self.nsa_q_norm = i("nsa_q_norm", [L, 64])
    self.nsa_k_norm = i("nsa_k_norm", [L, 3, 64])
    self.nsa_cmp_pos = i("nsa_cmp_pos", [L, 2, 32, 64])
    self.nsa_cmp_w1 = i("nsa_cmp_w1", [L, 2, 2048, 256])
    self.nsa_cmp_w2 = i("nsa_cmp_w2", [L, 2, 256, 64])
    self.exall = i("exall", [128, S], BF16)
    self.causneg = i("causneg", [128, 512], BF16)
    self.winneg = i("winneg", [128, 512], BF16)
    self.cmask = i("cmask", [128, 17, 128])
    self.cmp2slc = i("cmp2slc", [128, self.nch, self.NB])
    self.qT_d = self.scr("qT_d", [8, 64, S])
    self.kswT_d = self.scr("kswT_d", [4, 64, S], BF16)
    self.kvcT_d = self.scr("kvcT_d", [4, 64, S])
    self.vaug_d = self.scr("vaug_d", [S, 4 * 65], BF16)


def _phase_nsa_prep(self, l):
    kb, S, NT = self.kb, self.S, self.NT
    P = self.proj_tm[l]
    with ExitStack() as st:
        sb = lambda name, shape, dt=F32: self.sb(st, name, shape, dt)
        pA, pB, pC, pD = [self.psb(st, "ps_n%d" % j) for j in range(4)]
        idf = sb("idf", [128, 128])
        kb.dma(idf[:], self.ident_f[:, :], w=[idf.name])
        gq = sb("gq", [128, 12, 64])
        for h in range(8):
            kb.dma(gq[:, h, :], self.nsa_q_norm[l:l + 1, :].to_broadcast([128, 64]), w=[gq.name])
        for h in range(2):
            kb.dma(gq[:, 8 + h, :], self.nsa_k_norm[l, 1:2, :].to_broadcast([128, 64]), w=[gq.name])
            kb.dma(gq[:, 10 + h, :], self.nsa_k_norm[l, 2:3, :].to_broadcast([128, 64]), w=[gq.name])
        xin = [sb("xin%d" % j, [128, NSA_IN]) for j in range(2)]
        nin = sb("nin", [128, 768]); sq = sb("sq", [128, 768]); ss = sb("ss", [128, 12]); rs = sb("rs", [128, 12])
        qo = [sb("qo%d" % j, [64, 8, 128]) for j in range(2)]
        ko = [sb("ko%d" % j, [64, 4, 128], BF16) for j in range(2)]
        co = [sb("co%d" % j, [64, 4, 128]) for j in range(2)]
        va = [sb("va%d" % j, [128, 4, 65], BF16) for j in range(2)]
        for j in range(2):
            kb.memset("dve", va[j][:], 1.0, w=[va[j].name])
        h12 = lambda x: x.rearrange("p (h n) -> p h n", h=12)
        for t in range(NT):
            j = t % 2
            rows = slice(t * 128, (t + 1) * 128)
            x = xin[j]
            kb.dma(x[:], P[rows, 0:NSA_IN], w=[x.name])
            kb.cp("pool", nin[:, 0:512], x[:, 0:512], r=[x.name], w=[nin.name])
            kb.cp("pool", nin[:, 512:640], x[:, 768:896], r=[x.name], w=[nin.name])
            kb.cp("pool", nin[:, 640:768], x[:, 1024:1152], r=[x.name], w=[nin.name])
            kb.act(sq[:], nin[:], AF.Square, r=[nin.name], w=[sq.name])
            kb.add("dve", lambda e: e.tensor_reduce(ss[:], sq[:].rearrange("p (h n) -> p h n", h=12), AX.X, ALU.add),
                   [sq.name], [ss.name])
            self.rsqrt(rs[:], ss[:], 1.0 / 64, EPS, [ss.name], [rs.name])
            kb.tt("dve", h12(nin[:]), h12(nin[:]), rs[:].unsqueeze(2).to_broadcast([128, 12, 64]), ALU.mult,
                  r=[nin.name, rs.name], w=[nin.name])
            kb.tt("pool", nin[:], nin[:], gq[:].rearrange("p h n -> p (h n)"), ALU.mult, r=[nin.name, gq.name], w=[nin.name])
            for blk in range(8):
                ps = pA if blk < 4 else pB
                kb.tr(ps[0:64, (blk % 4) * 128:(blk % 4 + 1) * 128], nin[:, blk * 64:(blk + 1) * 64], idf[:],
                      r=[nin.name, idf.name], w=[ps.name])
            for blk in range(4):
                kb.tr(pC[0:64, blk * 128:(blk + 1) * 128], nin[:, 512 + blk * 64:512 + (blk + 1) * 64], idf[:],
                      r=[nin.name, idf.name], w=[pC.name])
                kb.tr(pD[0:64, blk * 128:(blk + 1) * 128], x[:, 512 + blk * 64:512 + (blk + 1) * 64], idf[:],
                      r=[x.name, idf.name], w=[pD.name])
            kb.cp("act", qo[j][:, 0:4, :].rearrange("p h t -> p (h t)"), pA[0:64, :], r=[pA.name], w=[qo[j].name])
            kb.cp("dve", qo[j][:, 4:8, :].rearrange("p h t -> p (h t)"), pB[0:64, :], r=[pB.name], w=[qo[j].name])
            kb.cp("act", ko[j][:].rearrange("p h t -> p (h t)"), pC[0:64, :], r=[pC.name], w=[ko[j].name])
            kb.cp("dve", co[j][:].rearrange("p h t -> p (h t)"), pD[0:64, :], r=[pD.name], w=[co[j].name])
            kb.dma(self.qT_d[:, :, rows].rearrange("h d t -> d h t"), qo[j][:], r=[qo[j].name], q="pool")
            kb.dma(self.kswT_d[:, :, rows].rearrange("h d t -> d h t"), ko[j][:], r=[ko[j].name], q="pool")
            kb.dma(self.kvcT_d[:, :, rows].rearrange("h d t -> d h t"), co[j][:], r=[co[j].name], q="pool")
            kb.cp("pool", va[j][:, 0:2, 0:64], x[:, 896:1024].rearrange("p (g n) -> p g n", g=2), r=[x.name], w=[va[j].name])
            kb.cp("pool", va[j][:, 2:4, 0:64], x[:, 1152:1280].rearrange("p (g n) -> p g n", g=2), r=[x.name], w=[va[j].name])
            kb.dma(self.vaug_d[rows, :], va[j][:].rearrange("p g n -> p (g n)"), r=[va[j].name], q="pool")
        kb.flush()


Prog.declare_nsa = _declare_nsa
Prog.phase_nsa_prep = _phase_nsa_prep


def _phase_nsa_attn(self, l):
    kb, S, NT, NB, n_cmp, nch = self.kb, self.S, self.NT, self.NB, self.n_cmp, self.nch
    P = self.proj_tm[l]
    SC = 0.125
    with ExitStack() as st:
        sb = lambda name, shape, dt=F32: self.sb(st, name, shape, dt)
        psS = [self.psb(st, "ps_S%d" % j) for j in range(2)]
        psO = [self.psb(st, "ps_O%d" % j) for j in range(2)]
        psI = self.psb(st, "ps_I"); psT = self.psb(st, "ps_T")
        psX = [self.psb(st, "ps_X%d" % j) for j in range(2)]
        idf = sb("idf", [128, 128]); idb = sb("idb", [128, 128], BF16); ones = sb("ones", [128, 128])
        kb.dma(idf[:], self.ident_f[:, :], w=[idf.name])
        kb.dma(idb[:], self.ident_bf[:, :], w=[idb.name])
        kb.dma(ones[:], self.ones_f[:, :], w=[ones.name])
        ks = sb("ks", [64, 2, S], BF16); kw = sb("kw", [64, 2, S], BF16)
        kb.dma(ks[:], self.kswT_d[0:2].rearrange("g d s -> d g s"), w=[ks.name])
        kb.dma(kw[:], self.kswT_d[2:4].rearrange("g d s -> d g s"), w=[kw.name])
        vv = sb("vv", [128, NT, 4 * 65], BF16)
        kb.dma(vv[:], self.vaug_d.rearrange("(c p) n -> p c n", p=128), w=[vv.name])
        ex = sb("ex", [128, S], BF16)
        kb.dma(ex[:], self.exall[:, :], w=[ex.name])
        cneg = sb("cneg", [128, 512], BF16); wneg = sb("wneg", [128, 512], BF16)
        kb.dma(cneg[:], self.causneg[:, :], w=[cneg.name])
        kb.dma(wneg[:], self.winneg[:, :], w=[wneg.name])
        cmk = sb("cmk", [128, 17, 128]); c2s = sb("c2s", [128, nch, NB])
        kb.dma(cmk[:], self.cmask[:, :, :], w=[cmk.name])
        kb.dma(c2s[:], self.cmp2slc[:, :, :], w=[c2s.name])
        kcT = sb("kcT", [64, 2, nch * 128]); vc = sb("vc", [128, nch, 2, 65])
        kb.memset("dve", kcT[:], 0.0, w=[kcT.name])
        kb.memset("dve", vc[:], 0.0, w=[vc.name])
        kb.memset("dve", vc[:, :, :, 64:65], 1.0, w=[vc.name])
        with ExitStack() as s2:
            sb2 = lambda name, shape, dt=F32: self.sb(s2, name, shape, dt)
            w1 = sb2("w1c", [64, 32, 256]); w2 = sb2("w2c", [128, 2, 64]); pos = sb2("pos", [32, 64]); posT = sb2("posT", [64, 32])
            tT = sb2("tT", [64, S]); hid = sb2("hid", [128, 2, 512]); cb = sb2("cb", [128, 2])
            kg0r = sb2("kg0r", [1, 64]); kg0 = sb2("kg0", [64, 1]); sqc = sb2("sqc", [64, 512]); rsc = sb2("rsc", [64, 512])
            kcr = sb2("kcr", [64, 512])
            kb.dma(kg0r[:], self.nsa_k_norm[l, 0:1, :], w=[kg0r.name])
            kb.tr(psT[0:64, 0:1], kg0r[:], idf[0:1, 0:1], r=[kg0r.name, idf.name], w=[psT.name])
            kb.cp("dve", kg0[:], psT[0:64, 0:1], r=[psT.name], w=[kg0.name])
            for jj in range(2):
                kb.dma(w1[:], self.nsa_cmp_w1[l, jj].rearrange("(l d) n -> d l n", d=64), w=[w1.name])
                kb.dma(w2[:], self.nsa_cmp_w2[l, jj].rearrange("(k p) n -> p k n", p=128), w=[w2.name])
                kb.dma(pos[:], self.nsa_cmp_pos[l, jj], w=[pos.name])
                kb.tr(psT[0:64, 0:32], pos[:], idf[0:32, 0:32], r=[pos.name, idf.name], w=[psT.name])
                kb.cp("dve", posT[:], psT[0:64, 0:32], r=[psT.name], w=[posT.name])
                for half in range(2):
                    for li in range(32):
                        kb.mm(psT[:, 64 + half:65 + half], w1[:, li, half * 128:(half + 1) * 128], posT[:, li:li + 1],
                              li == 0, li == 31, r=[w1.name, posT.name], w=[psT.name])
                kb.cp("dve", cb[:], psT[:, 64:66], r=[psT.name], w=[cb.name])
                for g in range(2):
                    kb.dma(tT[:], self.kvcT_d[jj * 2 + g], w=[tT.name])
                    for half in range(2):
                        for li in range(32):
                            kb.mm(psX[half][:, 0:n_cmp], w1[:, li, half * 128:(half + 1) * 128],
                                  tT[:, li:li + 16 * (n_cmp - 1) + 1:16], li == 0, li == 31,
                                  r=[w1.name, tT.name], w=[psX[half].name])
                        kb.act(hid[:, half, 0:n_cmp], psX[half][:, 0:n_cmp], AF.Silu, bias=cb[:, half:half + 1],
                               r=[psX[half].name, cb.name], w=[hid.name])
                    if jj == 0:
                        for half in range(2):
                            kb.mm(psT[0:64, 0:n_cmp], w2[:, half, :], hid[:, half, 0:n_cmp], half == 0, half == 1,
                                  r=[w2.name, hid.name], w=[psT.name])
                        kb.act(sqc[:, 0:n_cmp], psT[0:64, 0:n_cmp], AF.Square, r=[psT.name], w=[sqc.name])
                        kb.cp("dve", kcr[:, 0:n_cmp], psT[0:64, 0:n_cmp], r=[psT.name], w=[kcr.name])
                        kb.mm(psI[0:64, 0:n_cmp], ones[0:64, 0:64], sqc[:, 0:n_cmp], True, True,
                              r=[ones.name, sqc.name], w=[psI.name])
                        self.rsqrt(rsc[:, 0:n_cmp], psI[0:64, 0:n_cmp], 1.0 / 64, EPS, [psI.name], [rsc.name])
                        kb.stt("dve", kcT[:, g, 0:n_cmp], kcr[:, 0:n_cmp], kg0[:, 0:1], rsc[:, 0:n_cmp], ALU.mult, ALU.mult,
                               r=[kcr.name, kg0.name, rsc.name], w=[kcT.name])
                    else:
                        for ch in range(nch):
                            cw = min(128, n_cmp - ch * 128)
                            for half in range(2):
                                kb.mm(psT[0:cw, 0:64], hid[:, half, ch * 128:ch * 128 + cw], w2[:, half, :],
                                      half == 0, half == 1, r=[hid.name, w2.name], w=[psT.name])
                            kb.cp("dve", vc[0:cw, ch, g, 0:64], psT[0:cw, 0:64], r=[psT.name], w=[vc.name])
            kb.flush()
        q32 = [sb("q32_%d" % j, [64, 4, 128]) for j in range(2)]
        q16 = [sb("q16_%d" % j, [64, 4, 128], BF16) for j in range(2)]
        gs = sb("gs", [128, 24])
        e32 = [sb("e32_%d" % j, [128, 512]) for j in range(2)]
        e16 = [sb("e16_%d" % j, [128, 512], BF16) for j in range(2)]
        den = sb("den", [128, 4]); rden = sb("rden", [128, 4]); coef = sb("coef", [128, 4])
        impm = sb("impm", [128, NB]); wk = sb("wk", [128, NB]); m1 = sb("m1", [128, 8]); m2 = sb("m2", [128, 8])
        sel = sb("sel", [128, NB]); negT = sb("negT", [128, 512], BF16)
        oacc = [sb("oacc%d" % j, [128, MIX]) for j in range(2)]
        kb.memset("dve", negT[:], 0.0, w=[negT.name])
        nS = 0
        nO = 0

        def finish_branch(pso, g, br, oa, first):
            kb.ts("dve", den[:], pso[:, 64:260:65], 1e-30, None, ALU.max, r=[pso.name], w=[den.name])
            kb.add("dve", lambda e: e.reciprocal(rden[:], den[:]), [den.name], [rden.name])
            kb.tt("dve", coef[:], rden[:], gs[:, g * 12 + br:g * 12 + 12:3], ALU.mult, r=[rden.name, gs.name], w=[coef.name])
            for h in range(4):
                osl = oa[:, (4 * g + h) * 64:(4 * g + h + 1) * 64]
                if first:
                    kb.ts("dve", osl, pso[:, h * 65:h * 65 + 64], coef[:, h:h + 1], None, ALU.mult,
                          r=[pso.name, coef.name], w=[oa.name])
                else:
                    kb.stt("dve", osl, pso[:, h * 65:h * 65 + 64], coef[:, h:h + 1], osl, ALU.mult, ALU.add,
                           r=[pso.name, coef.name, oa.name], w=[oa.name])

        for b in range(NT):
            rows = slice(b * 128, (b + 1) * 128)
            oa = oacc[b % 2]
            kb.dma(gs[:], P[rows, 1280:1304], w=[gs.name])
            kb.act(gs[:], gs[:], AF.Sigmoid, r=[gs.name], w=[gs.name])
            for g in range(2):
                qj = (b * 2 + g) % 2
                kb.dma(q32[qj][:], self.qT_d[4 * g:4 * g + 4, :, rows].rearrange("h d t -> d h t"), w=[q32[qj].name])
                kb.cp("pool", q16[qj][:], q32[qj][:], r=[q32[qj].name], w=[q16[qj].name])
                qf32 = q32[qj][:].rearrange("p h t -> p (h t)")
                qf16 = q16[qj][:].rearrange("p h t -> p (h t)")
                pso = psO[nO % 2]; nO += 1
                chunks = list(range(0, min(b // 16, nch - 1) + 1))
                for ci, kc in enumerate(chunks):
                    pS = psS[nS % 2]; e = e32[nS % 2]; nS += 1
                    kb.mm(pS[:], kcT[:, g, kc * 128:(kc + 1) * 128], qf32, True, True,
                          r=[kcT.name, q32[qj].name], w=[pS.name])
                    kb.act(e[:], pS[:], AF.Exp, scale=SC, r=[pS.name], w=[e.name])
                    rr = b - 16 * kc
                    if rr <= 16:
                        kb.tt("dve", e[:].rearrange("p (h t) -> p h t", h=4), e[:].rearrange("p (h t) -> p h t", h=4),
                              cmk[:, rr, :].unsqueeze(1).to_broadcast([128, 4, 128]), ALU.mult,
                              r=[e.name, cmk.name], w=[e.name])
                    for h in range(4):
                        kb.mm(pso[:, h * 65:(h + 1) * 65], e[:, h * 128:(h + 1) * 128], vc[:, kc, g, :],
                              ci == 0 and h == 0, False, r=[e.name, vc.name], w=[pso.name])
                    for h in range(4):
                        kb.mm(psI[:, h * 128:h * 128 + NB], e[:, h * 128:(h + 1) * 128], c2s[:, kc, :],
                              ci == 0 and h == 0, False, r=[e.name, c2s.name], w=[psI.name])
                finish_branch(pso, g, 0, oa, True)
                if NB > 16:
                    for h in range(4):
                        if h == 0:
                            kb.ts("dve", impm[:], psI[:, 0:NB], rden[:, 0:1], None, ALU.mult,
                                  r=[psI.name, rden.name], w=[impm.name])
                        else:
                            kb.stt("dve", impm[:], psI[:, h * 128:h * 128 + NB], rden[:, h:h + 1], impm[:], ALU.mult, ALU.add,
                                   r=[psI.name, rden.name, impm.name], w=[impm.name])
                    if 2 * b + 2 < NB:
                        kb.memset("pool", impm[:, 2 * b + 2:NB], -1.0, w=[impm.name])
                    kb.memset("pool", impm[0:64, 2 * b + 1:2 * b + 2], -1.0, w=[impm.name])
                    kb.ts("pool", impm[:, 0:1], impm[:, 0:1], 100.0, None, ALU.add, r=[impm.name], w=[impm.name])
                    kb.ts("pool", impm[:, 2 * b:2 * b + 1], impm[:, 2 * b:2 * b + 1], 100.0, None, ALU.add,
                          r=[impm.name], w=[impm.name])
                    if b > 0:
                        kb.ts("pool", impm[0:64, 2 * b - 1:2 * b], impm[0:64, 2 * b - 1:2 * b], 100.0, None, ALU.add,
                              r=[impm.name], w=[impm.name])
                    kb.ts("pool", impm[64:128, 2 * b + 1:2 * b + 2], impm[64:128, 2 * b + 1:2 * b + 2], 100.0, None, ALU.add,
                          r=[impm.name], w=[impm.name])
                    kb.add("dve", lambda e_: e_.max(m1[:], impm[:]), [impm.name], [m1.name])
                    kb.add("dve", lambda e_: e_.match_replace(wk[:], m1[:], impm[:], -1e9), [m1.name, impm.name], [wk.name])
                    kb.add("dve", lambda e_: e_.max(m2[:], wk[:]), [wk.name], [m2.name])
                    kb.ts("dve", sel[:], impm[:], m2[:, 7:8], None, ALU.is_ge, r=[impm.name, m2.name], w=[sel.name])
                    kb.ts("dve", sel[:], sel[:], -NEGM, NEGM, ALU.mult, ALU.add, r=[sel.name], w=[sel.name])
                    kb.tr(psT[0:NB, 0:128], sel[:], idf[:], r=[sel.name, idf.name], w=[psT.name])
                    kb.cp("dve", negT[0:NB, :].rearrange("p (h t) -> p h t", h=4),
                          psT[0:NB, 0:128].unsqueeze(1).to_broadcast([NB, 4, 128]), r=[psT.name], w=[negT.name])
                pso = psO[nO % 2]; nO += 1
                for kc in range(0, b + 1):
                    pS = psS[nS % 2]; e = e16[nS % 2]; nS += 1
                    diag = kc == b
                    kb.mm(pS[:], ks[:, g, kc * 128:(kc + 1) * 128], qf16, True, False, r=[ks.name, q16[qj].name], w=[pS.name])
                    kb.mm(pS[:], ex[0:NB, kc * 128:(kc + 1) * 128], negT[0:NB, :], False, not diag,
                          r=[ex.name, negT.name], w=[pS.name])
                    if diag:
                        kb.mm(pS[:], idb[:], cneg[:], False, True, r=[idb.name, cneg.name], w=[pS.name])
                    kb.act(e[:], pS[:], AF.Exp, scale=SC, r=[pS.name], w=[e.name])
                    for h in range(4):
                        kb.mm(pso[:, h * 65:(h + 1) * 65], e[:, h * 128:(h + 1) * 128], vv[:, kc, g * 65:(g + 1) * 65],
                              kc == 0 and h == 0, False, r=[e.name, vv.name], w=[pso.name])
                finish_branch(pso, g, 1, oa, False)
                pso = psO[nO % 2]; nO += 1
                wch = list(range(max(0, b - 4), b + 1))
                for ci, kc in enumerate(wch):
                    pS = psS[nS % 2]; e = e16[nS % 2]; nS += 1
                    diag = kc == b
                    edge = kc == b - 4
                    kb.mm(pS[:], kw[:, g, kc * 128:(kc + 1) * 128], qf16, True, not (diag or edge),
                          r=[kw.name, q16[qj].name], w=[pS.name])
                    if diag:
                        kb.mm(pS[:], idb[:], cneg[:], False, True, r=[idb.name, cneg.name], w=[pS.name])
                    if edge:
                        kb.mm(pS[:], idb[:], wneg[:], False, True, r=[idb.name, wneg.name], w=[pS.name])
                    kb.act(e[:], pS[:], AF.Exp, scale=SC, r=[pS.name], w=[e.name])
                    for h in range(4):
                        kb.mm(pso[:, h * 65:(h + 1) * 65], e[:, h * 128:(h + 1) * 128],
                              vv[:, kc, (2 + g) * 65:(3 + g) * 65], ci == 0 and h == 0, False,
                              r=[e.name, vv.name], w=[pso.name])
                finish_branch(pso, g, 2, oa, False)
            kb.dma(self.o_mix[l][0][rows, :], oa[:], r=[oa.name], q="pool")
        kb.flush()


Prog.phase_nsa_attn = _phase_nsa_attn
```

```python
import numpy as np
from contextlib import ExitStack
import concourse.bass as bass
import concourse.mybir as mybir
from concourse.bass_utils import run_bass_kernel_spmd

F32 = mybir.dt.float32
BF16 = mybir.dt.bfloat16
AF = mybir.ActivationFunctionType
ALU = mybir.AluOpType
AX = mybir.AxisListType

D = 1024
MIX = 512
NSA_IN = 1304
RWKV_IN = 1792
GDN_IN = 2056
D_IN = 8224
D_FF = 4096
DP = D_IN + 32
EPS = 1e-6
OFF_NSA = 0
OFF_RWKV = NSA_IN
OFF_GDN = NSA_IN + RWKV_IN
OFF_GATE = NSA_IN + RWKV_IN + GDN_IN
NEGM = -30000.0


class _Op:
    __slots__ = ("eng", "fn", "deps", "is_dma", "need_sig", "sem", "val", "pos")


class KB:
    ENGS = ("pe", "act", "dve", "pool", "sp")

    def __init__(self, nc, stack):
        self.nc = nc
        self.stack = stack
        self.esem = {e: stack.enter_context(nc.semaphore("es_" + e)) for e in ("pe", "act", "dve", "pool")}
        self.ecnt = {e: 0 for e in self.esem}
        self.dsem = {"sp": [stack.enter_context(nc.semaphore("dsp%d" % i)) for i in range(16)],
                     "pool": [stack.enter_context(nc.semaphore("dpl%d" % i)) for i in range(8)]}
        self.dcnt = {"sp": 0, "pool": 0}
        self.dlast = {"sp": {}, "pool": {}}
        self.seen = {e: {} for e in self.ENGS}
        self.begin()

    def begin(self):
        self.ops = {e: [] for e in self.ENGS}
        self.res = {}

    def add(self, eng, fn, reads=(), writes=(), dma=False):
        op = _Op()
        op.eng = eng
        op.fn = fn
        op.is_dma = dma
        op.need_sig = dma
        op.sem = None
        op.val = 0
        op.pos = len(self.ops[eng])
        deps = set()
        rl, wl = [], []
        reads = [x.split("__u")[0] for x in reads]
        writes = [x.split("__u")[0] for x in writes]
        for r in reads:
            (wl if r.startswith("ps") else rl).append(r)
        wl.extend(writes)
        for r in rl:
            st = self.res.setdefault(r, [None, []])
            if st[0] is not None:
                deps.add(st[0])
        for w in wl:
            st = self.res.setdefault(w, [None, []])
            if st[0] is not None:
                deps.add(st[0])
            deps.update(st[1])
        for r in rl:
            self.res[r][1].append(op)
        for w in wl:
            st = self.res[w]
            st[0] = op
            st[1] = []
        deps.discard(op)
        keep = []
        latest = {}
        for d in deps:
            if d.is_dma:
                keep.append(d)
                continue
            if d.eng == eng:
                if eng == "pe":
                    continue
                if op.pos - d.pos > 2:
                    continue
            cur = latest.get(d.eng)
            if cur is None or d.pos > cur.pos:
                latest[d.eng] = d
        for d in latest.values():
            d.need_sig = True
            keep.append(d)
        if dma:
            k = self.dcnt[eng]
            self.dcnt[eng] += 1
            P = len(self.dsem[eng])
            slot = k % P
            op.sem = self.dsem[eng][slot]
            op.val = 16 * (k // P + 1)
            prev = self.dlast[eng].get(slot)
            if prev is not None:
                keep.append(prev)
            self.dlast[eng][slot] = op
        op.deps = keep
        self.ops[eng].append(op)
        return op

    def flush(self):
        nc = self.nc
        for q in ("sp", "pool"):
            outstanding = [o for o in self.dlast[q].values()]
            if outstanding:
                op = self.add(q, None)
                op.deps = outstanding
        for e in ("pe", "act", "dve", "pool"):
            for op in self.ops[e]:
                if op.need_sig and not op.is_dma:
                    self.ecnt[e] += 1
                    op.sem = self.esem[e]
                    op.val = self.ecnt[e]
        ops = self.ops
        seen = self.seen

        def emit(e, eng):
            sn = seen[e]
            for op in ops[e]:
                for d in op.deps:
                    key = id(d.sem)
                    if sn.get(key, 0) >= d.val:
                        continue
                    eng.wait_ge(d.sem, d.val)
                    sn[key] = d.val
                if op.fn is None:
                    continue
                ins = op.fn(eng)
                if op.need_sig:
                    ins.then_inc(op.sem, 16 if op.is_dma else 1)

        with nc.Block() as block:
            if ops["pe"]:
                block.tensor(lambda t: emit("pe", t))
            if ops["act"]:
                block.scalar(lambda t: emit("act", t))
            if ops["dve"]:
                block.vector(lambda t: emit("dve", t))
            if ops["pool"]:
                block.gpsimd(lambda t: emit("pool", t))
            if ops["sp"]:
                block.sync(lambda t: emit("sp", t))
        self.begin()

    def dma(self, out, in_, r=(), w=(), q="sp"):
        return self.add(q, lambda e: e.dma_start(out=out, in_=in_), r, w, dma=True)

    def mm(self, out, lhsT, rhs, start, stop, r=(), w=()):
        return self.add("pe", lambda e: e.matmul(out, lhsT=lhsT, rhs=rhs, start=start, stop=stop), r, w)

    def tr(self, out, in_, ident, r=(), w=()):
        return self.add("pe", lambda e: e.transpose(out, in_, ident), r, w)

    def act(self, out, in_, func, r=(), w=(), bias=None, scale=1.0, accum=None):
        def fn(e):
            kw = {}
            if bias is not None:
                kw["bias"] = bias
            if accum is not None:
                kw["accum_out"] = accum
            return e.activation(out, in_, func, scale=scale, **kw)
        return self.add("act", fn, r, w)

    def ts(self, eng, out, in0, s1, s2, op0, op1=None, r=(), w=()):
        def fn(e):
            if op1 is None:
                return e.tensor_scalar(out, in0, s1, None, op0)
            return e.tensor_scalar(out, in0, s1, s2, op0, op1)
        return self.add(eng, fn, r, w)

    def tt(self, eng, out, in0, in1, op, r=(), w=()):
        return self.add(eng, lambda e: e.tensor_tensor(out, in0, in1, op), r, w)

    def stt(self, eng, out, in0, scalar, in1, op0, op1, r=(), w=()):
        return self.add(eng, lambda e: e.scalar_tensor_tensor(out, in0, scalar, in1, op0, op1), r, w)

    def cp(self, eng, out, in_, r=(), w=()):
        if eng == "act":
            return self.add(eng, lambda e: e.copy(out, in_), r, w)
        return self.add(eng, lambda e: e.tensor_copy(out, in_), r, w)

    def memset(self, eng, ap, val, w=()):
        return self.add(eng, lambda e: e.memset(ap, val), (), w)


class Prog:
    def __init__(self, S, n_layers=2, debug=False):
        self.S = S
        self.NT = S // 128
        self.L = n_layers
        self.debug = debug
        self.nc = bass.Bass("TRN2", target_bir_lowering=False)
        self.stack = ExitStack()
        self.kb = KB(self.nc, self.stack)
        self.din = {}
        self.dscr = {}

    def inp(self, name, shape, dt=F32):
        t = self.nc.dram_tensor(name, list(shape), dt, kind="ExternalInput")
        self.din[name] = t
        return t.ap()

    def outp(self, name, shape, dt=F32):
        return self.nc.dram_tensor(name, list(shape), dt, kind="ExternalOutput").ap()

    def scr(self, name, shape, dt=F32):
        if self.debug:
            return self.outp(name, shape, dt)
        return self.nc.dram_tensor(name, list(shape), dt, kind="Internal").ap()

    def sb(self, st, name, shape, dt=F32):
        self.uid = getattr(self, "uid", 0) + 1
        return st.enter_context(self.nc.sbuf_tensor("%s__u%d" % (name, self.uid), list(shape), dt))

    def psb(self, st, name, dt=F32, n=512):
        self.uid = getattr(self, "uid", 0) + 1
        return st.enter_context(self.nc.psum_tensor("%s__u%d" % (name, self.uid), [128, n], dt))

    def declare(self):
        L, S = self.L, self.S
        i = self.inp
        self.x_in = i("x", [S, D])
        self.out = self.outp("out", [S, D])
        self.norm_mix_g = i("norm_mix_g", [L, D])
        self.w_in = i("w_in", [L, D, DP])
        self.ident_bf = i("ident_bf", [128, 128], BF16)
        self.ident_f = i("ident_f", [128, 128])
        self.proj_tm = [self.scr("proj_tm%d" % l, [S, OFF_GATE]) for l in range(L)]
        self.proj_g = [self.scr("proj_g%d" % l, [S, DP - OFF_GATE]) for l in range(L)]

    def load_weight_bf16(self, st, name, src_ap, kc, ncols, chunk):
        kb = self.kb
        w = self.sb(st, name, [128, kc, ncols], BF16)
        src = src_ap.rearrange("(k p) n -> p k n", p=128)
        with ExitStack() as s2:
            stg = [self.sb(s2, "%s_stg%d" % (name, j), [128, kc, chunk], F32) for j in range(2)]
            engs = ["dve", "pool", "act"]
            n = 0
            for c0 in range(0, ncols, chunk):
                cw = min(chunk, ncols - c0)
                j = n % 2
                kb.dma(stg[j][:, :, 0:cw], src[:, :, c0:c0 + cw], w=[stg[j].name])
                kb.cp(engs[n % 3], w[:, :, c0:c0 + cw], stg[j][:, :, 0:cw], r=[stg[j].name], w=[name + ":%d" % n])
                n += 1
            kb.flush()
        return w

    def phase_proj(self, l, x_src):
        kb, S, NT = self.kb, self.S, self.NT
        with ExitStack() as st:
            wsb = self.load_weight_bf16(st, "w_in_sb", self.w_in[l], 8, DP, 516)
            gb = self.sb(st, "gmix", [128, D])
            idb = self.sb(st, "idb", [128, 128], BF16)
            kb.dma(gb[:], self.norm_mix_g[l:l + 1, :].to_broadcast([128, D]), w=["gmix"])
            kb.dma(idb[:], self.ident_bf[:, :], w=["idb"])
            xt = [self.sb(st, "xt%d" % j, [128, D]) for j in range(2)]
            junk = self.sb(st, "junk", [128, D])
            ss = [self.sb(st, "ss%d" % j, [128, 1]) for j in range(2)]
            rstd = [self.sb(st, "rstd%d" % j, [128, 1]) for j in range(2)]
            hb = [self.sb(st, "hb%d" % j, [128, D], BF16) for j in range(2)]
            hT = [self.sb(st, "hT%d" % j, [128, 8, 128], BF16) for j in range(2)]
            osb = [self.sb(st, "osb%d" % j, [128, 2048]) for j in range(2)]
            pst = self.psb(st, "ps_tr", BF16, 1024)
            pso = [self.psb(st, "ps_o%d" % j) for j in range(4)]
            nblk = 0
            nog = 0
            def prep(tt_, part):
                jj = tt_ % 2
                self.rms_to_hT(xt[jj], x_src[tt_ * 128:(tt_ + 1) * 128, :], junk, ss[jj], rstd[jj], gb, hb[jj], pst, idb,
                               hT[jj], part=part)
            prep(0, None)
            for t in range(NT):
                j = t % 2
                rows = slice(t * 128, (t + 1) * 128)
                for gi, (og, oe) in enumerate(((0, 2048), (2048, 4096), (4096, OFF_GATE), (OFF_GATE, OFF_GATE + 2048),
                                               (OFF_GATE + 2048, DP))):
                    if gi == 1 and t + 1 < NT:
                        prep(t + 1, 0)
                    o = osb[nog % 2]
                    nog += 1
                    ow = oe - og
                    for c0 in range(og, oe, 512):
                        cw = min(512, oe - c0)
                        ps = pso[nblk % 4]
                        for k in range(8):
                            kb.mm(ps[:, 0:cw], hT[j][:, k, :], wsb[:, k, c0:c0 + cw], k == 0, k == 7,
                                  r=[hT[j].name, "w_in_sb"], w=[ps.name])
                        kb.cp("act" if nblk % 2 else "dve", o[:, c0 - og:c0 - og + cw], ps[:, 0:cw],
                              r=[ps.name], w=[o.name])
                        nblk += 1
                    if og < OFF_GATE:
                        kb.dma(self.proj_tm[l][rows, og:oe], o[:, 0:ow], r=[o.name], q="pool")
                    else:
                        kb.dma(self.proj_g[l][rows, og - OFF_GATE:oe - OFF_GATE], o[:, 0:ow], r=[o.name], q="pool")
                if t + 1 < NT:
                    prep(t + 1, 1)
            kb.flush()

    def rms_to_hT(self, xt, x_rows, junk, ss, rstd, gb, hb, pst, idb, hT, part=None):
        kb = self.kb
        if part in (None, 0):
            kb.dma(xt[:], x_rows, w=[xt.name])
            kb.act(junk[:], xt[:], AF.Square, r=[xt.name], w=[junk.name, ss.name], accum=ss[:])
            self.rsqrt(rstd[:], ss[:], 1.0 / D, EPS, [ss.name], [rstd.name])
            kb.stt("dve", hb[:], xt[:], rstd[:, 0:1], gb[:], ALU.mult, ALU.mult,
                   r=[xt.name, rstd.name, gb.name], w=[hb.name])
        if part == 0:
            return
        for k in range(8):
            kb.tr(pst[:, k * 128:(k + 1) * 128], hb[:, k * 128:(k + 1) * 128], idb[:],
                  r=[hb.name, idb.name], w=[pst.name])
        kb.cp("act", hT[:].rearrange("p k t -> p (k t)"), pst[:], r=[pst.name], w=[hT.name])

    def rsqrt(self, out, in_, scale, eps, r, w):
        kb = self.kb
        kb.ts("dve", out, in_, scale, eps, ALU.mult, ALU.add, r=r, w=w)
        kb.add("act", lambda e: e.sqrt(out, out), w, w)
        kb.add("dve", lambda e: e.reciprocal(out, out), w, w)

    def declare_rest(self):
        L, S = self.L, self.S
        i = self.inp
        self.w_branch = i("w_branch", [L, 3, MIX, D])
        self.w_out = i("w_out", [L, D, D])
        self.norm_ffn_g = i("norm_ffn_g", [L, D])
        self.w_ff1 = i("w_ff1", [L, D, D_FF])
        self.w_ff2 = i("w_ff2", [L, D_FF, D])
        self.o_mix = [[self.scr("o_mix%d_%d" % (l, b), [S, MIX]) for b in range(3)] for l in range(L)]
        self.x_mid = [self.scr("x_mid%d" % l, [S, D]) for l in range(L)]
        self.x_lay = [self.scr("x_lay%d" % l, [S, D]) for l in range(L - 1)] + [self.out]

    def transpose_bf(self, src_bf, nk, pst, idb, dstT, eng="act"):
        kb = self.kb
        for k in range(nk):
            kb.tr(pst[:, k * 128:(k + 1) * 128], src_bf[:, k * 128:(k + 1) * 128], idb[:],
                  r=[src_bf.name, idb.name], w=[pst.name])
        kb.cp(eng, dstT.rearrange("p k t -> p (k t)"), pst[:, 0:nk * 128], r=[pst.name], w=[dstT.name])

    def phase_merge(self, l, x_src):
        kb, S, NT = self.kb, self.S, self.NT
        with ExitStack() as st:
            wb = [self.load_weight_bf16(st, "wbr%d" % b, self.w_branch[l, b], 4, D, 1024) for b in range(3)]
            wo = self.load_weight_bf16(st, "wout_sb", self.w_out[l], 8, D, 512)
            idb = self.sb(st, "idb", [128, 128], BF16)
            kb.dma(idb[:], self.ident_bf[:, :], w=["idb"])
            ot = [self.sb(st, "ot%d" % j, [128, MIX]) for j in range(2)]
            ob = [self.sb(st, "ob%d" % j, [128, MIX], BF16) for j in range(2)]
            oT = [self.sb(st, "oT%d" % j, [128, 4, 128], BF16) for j in range(2)]
            gt = [self.sb(st, "gt%d" % j, [128, D]) for j in range(2)]
            tmp = self.sb(st, "tmpm", [128, D])
            mrg = self.sb(st, "mrg", [128, D])
            mb = self.sb(st, "mrgb", [128, D], BF16)
            mT = self.sb(st, "mT", [128, 8, 128], BF16)
            xt = [self.sb(st, "xt%d" % j, [128, D]) for j in range(2)]
            xo = [self.sb(st, "xo%d" % j, [128, D]) for j in range(2)]
            pst = self.psb(st, "ps_tr", BF16, 1024)
            pso = [self.psb(st, "ps_o%d" % j) for j in range(4)]
            n = 0
            nb = 0
            for t in range(NT):
                rows = slice(t * 128, (t + 1) * 128)
                for b in range(3):
                    j = n % 2
                    n += 1
                    kb.dma(ot[j][:], self.o_mix[l][b][rows, :], w=[ot[j].name])
                    kb.dma(gt[j][:], self.proj_g[l][rows, b * D:(b + 1) * D], w=[gt[j].name])
                    kb.cp("pool", ob[j][:], ot[j][:], r=[ot[j].name], w=[ob[j].name])
                    kb.act(gt[j][:], gt[j][:], AF.Sigmoid, r=[gt[j].name], w=[gt[j].name])
                    self.transpose_bf(ob[j], 4, pst, idb, oT[j][:])
                    for c in range(2):
                        ps = pso[nb % 4]
                        nb += 1
                        for k in range(4):
                            kb.mm(ps[:], oT[j][:, k, :], wb[b][:, k, c * 512:(c + 1) * 512], k == 0, k == 3,
                                  r=[oT[j].name], w=[ps.name])
                        cs = slice(c * 512, (c + 1) * 512)
                        if b == 0:
                            kb.tt("dve", mrg[:, cs], ps[:], gt[j][:, cs], ALU.mult, r=[ps.name, gt[j].name], w=["mrg"])
                        else:
                            kb.tt("dve", tmp[:, cs], ps[:], gt[j][:, cs], ALU.mult, r=[ps.name, gt[j].name], w=["tmpm"])
                            kb.tt("pool", mrg[:, cs], mrg[:, cs], tmp[:, cs], ALU.add, r=["tmpm", "mrg"], w=["mrg"])
                kb.cp("act", mb[:], mrg[:], r=["mrg"], w=["mrgb"])
                self.transpose_bf(mb, 8, pst, idb, mT[:])
                j = t % 2
                kb.dma(xt[j][:], x_src[rows, :], w=[xt[j].name])
                for c in range(2):
                    ps = pso[nb % 4]
                    nb += 1
                    for k in range(8):
                        kb.mm(ps[:], mT[:, k, :], wo[:, k, c * 512:(c + 1) * 512], k == 0, k == 7, r=["mT"], w=[ps.name])
                    cs = slice(c * 512, (c + 1) * 512)
                    kb.tt("dve", xo[j][:, cs], ps[:], xt[j][:, cs], ALU.add, r=[ps.name, xt[j].name], w=[xo[j].name])
                kb.dma(self.x_mid[l][rows, :], xo[j][:], r=[xo[j].name], q="pool")
            kb.flush()

    def phase_ffn(self, l):
        kb, S, NT = self.kb, self.S, self.NT
        with ExitStack() as st:
            w1 = self.load_weight_bf16(st, "w1_sb", self.w_ff1[l], 8, D_FF, 512)
            w2 = self.load_weight_bf16(st, "w2_sb", self.w_ff2[l], 32, D, 128)
            gb = self.sb(st, "gffn", [128, D])
            idb = self.sb(st, "idb", [128, 128], BF16)
            kb.dma(gb[:], self.norm_ffn_g[l:l + 1, :].to_broadcast([128, D]), w=["gffn"])
            kb.dma(idb[:], self.ident_bf[:, :], w=["idb"])
            xt = [self.sb(st, "xt%d" % j, [128, D]) for j in range(2)]
            junk = self.sb(st, "junk", [128, D])
            ss = [self.sb(st, "ss%d" % j, [128, 1]) for j in range(2)]
            rstd = [self.sb(st, "rstd%d" % j, [128, 1]) for j in range(2)]
            hb = [self.sb(st, "hb%d" % j, [128, D], BF16) for j in range(2)]
            hT = [self.sb(st, "hT%d" % j, [128, 8, 128], BF16) for j in range(2)]
            fr = [self.sb(st, "fr%d" % j, [128, 512]) for j in range(2)]
            fb = self.sb(st, "fb", [128, D_FF], BF16)
            fT = self.sb(st, "fT", [128, 32, 128], BF16)
            xo = [self.sb(st, "xo%d" % j, [128, D]) for j in range(2)]
            pst = self.psb(st, "ps_tr", BF16, 1024)
            pso = [self.psb(st, "ps_o%d" % j) for j in range(4)]
            nb = 0
            def prep(tt_, part):
                jj = tt_ % 2
                self.rms_to_hT(xt[jj], self.x_mid[l][tt_ * 128:(tt_ + 1) * 128, :], junk, ss[jj], rstd[jj], gb, hb[jj], pst,
                               idb, hT[jj], part=part)
            prep(0, None)
            for t in range(NT):
                j = t % 2
                rows = slice(t * 128, (t + 1) * 128)
                for c in range(8):
                    ps = pso[nb % 4]
                    nb += 1
                    for k in range(8):
                        kb.mm(ps[:], hT[j][:, k, :], w1[:, k, c * 512:(c + 1) * 512], k == 0, k == 7,
                              r=[hT[j].name], w=[ps.name])
                    kb.act(fr[c % 2][:], ps[:], AF.Relu, r=[ps.name], w=[fr[c % 2].name])
                    kb.tt("dve", fb[:, c * 512:(c + 1) * 512], fr[c % 2][:], fr[c % 2][:], ALU.mult,
                          r=[fr[c % 2].name], w=["fb:%d" % c])
                    if c >= 2 and c % 2 == 0:
                        self.transpose_bf_slice(fb, (c - 2) * 4, 8, pst, idb, fT, rd=["fb:%d" % (c - 2), "fb:%d" % (c - 1)])
                    if c == 3 and t + 1 < NT:
                        prep(t + 1, 0)
                self.transpose_bf_slice(fb, 24, 8, pst, idb, fT, rd=["fb:6", "fb:7"])
                for c in range(2):
                    ps = pso[nb % 4]
                    nb += 1
                    for k in range(32):
                        kb.mm(ps[:], fT[:, k, :], w2[:, k, c * 512:(c + 1) * 512], k == 0, k == 31, r=["fT"], w=[ps.name])
                    cs = slice(c * 512, (c + 1) * 512)
                    kb.tt("dve", xo[j][:, cs], ps[:], xt[j][:, cs], ALU.add, r=[ps.name, xt[j].name], w=[xo[j].name])
                kb.dma(self.x_lay[l][rows, :], xo[j][:], r=[xo[j].name], q="pool")
                if t + 1 < NT:
                    prep(t + 1, 1)
            kb.flush()

    def transpose_bf_slice(self, src_bf, k0, nk, pst, idb, dstT, rd=None):
        kb = self.kb
        for k in range(nk):
            kb.tr(pst[:, k * 128:(k + 1) * 128], src_bf[:, (k0 + k) * 128:(k0 + k + 1) * 128], idb[:],
                  r=(rd if rd is not None else [src_bf.name]) + [idb.name], w=[pst.name])
        kb.cp("act", dstT[:, k0:k0 + nk, :].rearrange("p k t -> p (k t)"), pst[:, 0:nk * 128],
              r=[pst.name], w=[dstT.name])

    def declare_gdn(self):
        L = self.L
        i = self.inp
        self.gdn_conv_w = i("gdn_conv_w", [L, 4, 3 * MIX])
        self.gdn_a_log = i("gdn_a_log", [L, 4])
        self.gdn_dt_bias = i("gdn_dt_bias", [L, 4])
        self.gdn_norm_w = i("gdn_norm_w", [L, 128])
        self.ones_f = i("ones_f", [128, 128])
        self.triu_incl = i("triu_incl", [128, 128])
        self.triu_strict = i("triu_strict", [128, 128])

    def next_ps(self):
        self.psn = getattr(self, "psn", -1) + 1
        return self.pss[self.psn % len(self.pss)]

    def tr_f32(self, dst, src, idf, eng="act"):
        kb = self.kb
        n = src.shape[-1] // 128
        ps = self.next_ps()
        for k in range(n):
            kb.tr(ps[:, k * 128:(k + 1) * 128], src[:, k * 128:(k + 1) * 128], idf[:],
                  r=[src.name, idf.name], w=[ps.name])
        kb.cp(eng, dst, ps[:, 0:n * 128], r=[ps.name], w=[dst.name])

    def phase_gdn(self, l):
        kb, S, NT = self.kb, self.S, self.NT
        H, N = 4, 128
        P = self.proj_tm[l]
        c0 = OFF_GDN
        with ExitStack() as st:
            sb = lambda name, shape, dt=F32: self.sb(st, name, shape, dt)
            self.pss = [self.psb(st, "ps_g%d" % j) for j in range(8)]
            idf = sb("idf", [128, 128]); ones = sb("ones", [128, 128])
            tin = sb("tin", [128, 128]); tst = sb("tst", [128, 128])
            kb.dma(idf[:], self.ident_f[:, :], w=[idf.name])
            kb.dma(ones[:], self.ones_f[:, :], w=[ones.name])
            kb.dma(tin[:], self.triu_incl[:, :], w=[tin.name])
            kb.dma(tst[:], self.triu_strict[:, :], w=[tst.name])
            cw = sb("cw", [128, 4, 1536])
            kb.dma(cw[:].rearrange("p k c -> p (k c)"),
                   self.gdn_conv_w[l:l + 1].rearrange("o k c -> o (k c)").to_broadcast([128, 4 * 1536]), w=[cw.name])
            alog = sb("alog", [128, 4]); dtb = sb("dtb", [128, 4]); nw = sb("nw", [128, 128])
            kb.dma(alog[:], self.gdn_a_log[l:l + 1, :].to_broadcast([128, 4]), w=[alog.name])
            kb.dma(dtb[:], self.gdn_dt_bias[l:l + 1, :].to_broadcast([128, 4]), w=[dtb.name])
            kb.dma(nw[:], self.gdn_norm_w[l:l + 1, :].to_broadcast([128, 128]), w=[nw.name])
            nea = sb("nea", [128, 4])
            kb.act(nea[:], alog[:], AF.Exp, r=[alog.name], w=[nea.name])
            kb.ts("dve", nea[:], nea[:], -1.0, None, ALU.mult, r=[nea.name], w=[nea.name])
            X = [sb("X%d" % k, [128, 1536]) for k in range(4)]
            zt = sb("zt", [128, 512]); ab = sb("ab", [128, 8])
            y = sb("y", [128, 1536]); t1 = sb("t1", [128, 1536])
            ssq = sb("ssq", [128, 8]); rs = sb("rs", [128, 8])
            qkn = sb("qkn", [128, 1024])
            beta = sb("beta", [128, 4]); g = sb("g", [128, 4]); G = sb("G", [128, 4]); Gt = sb("Gt", [128, 4])
            eG = sb("eG", [128, 4]); neG = sb("neG", [128, 4]); eGC = sb("eGC", [128, 4]); eGr = sb("eGr", [128, 4])
            kbm = sb("kbm", [128, 512]); kam = sb("kam", [128, 512]); qgm = sb("qgm", [128, 512])
            khat = sb("khat", [128, 512]); vp = sb("vp", [128, 512])
            kT = sb("kT", [128, 512]); kbT = sb("kbT", [128, 512]); qT = sb("qT", [128, 512])
            aT = sb("aT", [128, 512]); rT = sb("rT", [128, 512])
            dg = sb("dg", [128, 512]); gam = sb("gam", [128, 512])
            Nm = [sb("Nm%d" % k, [128, 512]) for k in range(2)]
            Lm = [sb("Lm%d" % k, [128, 512]) for k in range(2)]
            Pm = [sb("Pm%d" % k, [128, 512]) for k in range(2)]
            MT = sb("MT", [128, 512]); N0k = sb("N0k", [128, 512])
            X1 = sb("X1", [128, 512]); UV = sb("UV", [128, 512]); Y = sb("Y", [128, 512])
            T0 = sb("T0", [128, 512])
            o = sb("o", [128, 512]); sz = sb("sz", [128, 512]); ysq = sb("ysq", [128, 512])
            yss = sb("yss", [128, 4]); yrs = sb("yrs", [128, 4])
            kb.memset("dve", T0[:], 0.0, w=[T0.name])
            hv = lambda a: a.rearrange("p (h n) -> p h n", h=H)
            for t in range(NT):
                r0 = t * 128
                for k in range(4):
                    sh = 3 - k
                    if r0 - sh < 0:
                        kb.memset("pool", X[k][:], 0.0, w=[X[k].name])
                        kb.dma(X[k][sh:128, :], P[0:128 - sh, c0:c0 + 1536], w=[X[k].name])
                    else:
                        kb.dma(X[k][:], P[r0 - sh:r0 - sh + 128, c0:c0 + 1536], w=[X[k].name])
                kb.dma(zt[:], P[r0:r0 + 128, c0 + 1536:c0 + 2048], w=[zt.name])
                kb.dma(ab[:], P[r0:r0 + 128, c0 + 2048:c0 + 2056], w=[ab.name])
                kb.tt("dve", y[:], X[0][:], cw[:, 0, :], ALU.mult, r=[X[0].name, cw.name], w=[y.name])
                for k in range(1, 4):
                    kb.tt("pool", t1[:], X[k][:], cw[:, k, :], ALU.mult, r=[X[k].name, cw.name], w=[t1.name])
                    kb.tt("dve", y[:], y[:], t1[:], ALU.add, r=[y.name, t1.name], w=[y.name])
                kb.act(y[:], y[:], AF.Silu, r=[y.name], w=[y.name])
                kb.act(t1[:, 0:1024], y[:, 0:1024], AF.Square, r=[y.name], w=[t1.name])
                kb.add("dve", lambda e: e.tensor_reduce(ssq[:], t1[:, 0:1024].rearrange("p (h n) -> p h n", h=8), AX.X, ALU.add),
                       [t1.name], [ssq.name])
                self.rsqrt(rs[:], ssq[:], 1.0, EPS, [ssq.name], [rs.name])
                kb.ts("dve", rs[:, 0:4], rs[:, 0:4], float(N) ** -0.5, None, ALU.mult, r=[rs.name], w=[rs.name])
                kb.tt("dve", qkn[:].rearrange("p (h n) -> p h n", h=8), y[:, 0:1024].rearrange("p (h n) -> p h n", h=8),
                      rs[:].unsqueeze(2).to_broadcast([128, 8, 128]), ALU.mult, r=[y.name, rs.name], w=[qkn.name])
                qn = qkn[:, 0:512]; kn = qkn[:, 512:1024]; v = y[:, 1024:1536]
                kb.act(beta[:], ab[:, 4:8], AF.Sigmoid, r=[ab.name], w=[beta.name])
                kb.tt("dve", g[:], ab[:, 0:4], dtb[:], ALU.add, r=[ab.name, dtb.name], w=[g.name])
                kb.act(g[:], g[:], AF.Exp, r=[g.name], w=[g.name])
                kb.act(g[:], g[:], AF.Ln, r=[g.name], w=[g.name], bias=1.0)
                kb.tt("dve", g[:], g[:], nea[:], ALU.mult, r=[g.name, nea.name], w=[g.name])
                ps = self.next_ps()
                kb.mm(ps[:, 0:4], tin[:], g[:], True, True, r=[tin.name, g.name], w=[ps.name])
                kb.mm(ps[:, 4:8], ones[:], g[:], True, True, r=[ones.name, g.name], w=[ps.name])
                kb.cp("dve", G[:], ps[:, 0:4], r=[ps.name], w=[G.name])
                kb.cp("dve", Gt[:], ps[:, 4:8], r=[ps.name], w=[Gt.name])
                kb.act(eG[:], G[:], AF.Exp, r=[G.name], w=[eG.name])
                kb.act(eGC[:], Gt[:], AF.Exp, r=[Gt.name], w=[eGC.name])
                kb.tt("dve", eGr[:], Gt[:], G[:], ALU.subtract, r=[Gt.name, G.name], w=[eGr.name])
                kb.act(eGr[:], eGr[:], AF.Exp, r=[eGr.name], w=[eGr.name])
                bb = lambda a: a[:].unsqueeze(2).to_broadcast([128, 4, 128])
                kb.tt("dve", hv(kbm[:]), hv(kn), bb(beta), ALU.mult, r=[qkn.name, beta.name], w=[kbm.name])
                kb.tt("pool", hv(vp[:]), hv(v), bb(beta), ALU.mult, r=[y.name, beta.name], w=[vp.name])
                kb.tt("dve", hv(kam[:]), hv(kbm[:]), bb(eG), ALU.mult, r=[kbm.name, eG.name], w=[kam.name])
                kb.ts("pool", kam[:], kam[:], -1.0, None, ALU.mult, r=[kam.name], w=[kam.name])
                kb.tt("dve", hv(qgm[:]), hv(qn), bb(eG), ALU.mult, r=[qkn.name, eG.name], w=[qgm.name])
                kb.tt("pool", hv(khat[:]), hv(kn), bb(eGr), ALU.mult, r=[qkn.name, eGr.name], w=[khat.name])
                self.tr_f32(kT[:], kn, idf); self.tr_f32(kbT[:], kbm[:], idf, "dve")
                self.tr_f32(qT[:], qn, idf); self.tr_f32(aT[:], kam[:], idf, "dve"); self.tr_f32(rT[:], qgm[:], idf)
                for h in range(H):
                    kb.ts("dve", dg[:, h * 128:(h + 1) * 128], idf[:], G[:, h:h + 1], None, ALU.mult,
                          r=[idf.name, G.name], w=[dg.name])
                ps = self.next_ps()
                for h in range(H):
                    kb.mm(ps[:, h * 128:(h + 1) * 128], ones[:], dg[:, h * 128:(h + 1) * 128], True, True,
                          r=[ones.name, dg.name], w=[ps.name])
                for h in range(H):
                    kb.ts("dve", gam[:, h * 128:(h + 1) * 128], ps[:, h * 128:(h + 1) * 128], G[:, h:h + 1], 0.0,
                          ALU.subtract, ALU.min, r=[ps.name, G.name], w=[gam.name])
                kb.act(gam[:], gam[:], AF.Exp, r=[gam.name], w=[gam.name])
                ps = self.next_ps()
                ps2 = self.next_ps()
                for h in range(H):
                    hs = slice(h * 128, (h + 1) * 128)
                    kb.mm(ps[:, hs], kT[:, hs], kbT[:, hs], True, True, r=[kT.name, kbT.name], w=[ps.name])
                    kb.mm(ps2[:, hs], kT[:, hs], qT[:, hs], True, True, r=[kT.name, qT.name], w=[ps2.name])
                N0 = Nm[0]
                kb.tt("dve", N0[:], ps[:], gam[:], ALU.mult, r=[ps.name, gam.name], w=[N0.name])
                kb.stt("dve", hv(N0[:]), hv(N0[:]), -1.0, tst[:].unsqueeze(1).to_broadcast([128, 4, 128]), ALU.mult, ALU.mult,
                       r=[N0.name, tst.name], w=[N0.name])
                kb.tt("dve", MT[:], ps2[:], gam[:], ALU.mult, r=[ps2.name, gam.name], w=[MT.name])
                kb.tt("pool", hv(MT[:]), hv(MT[:]), tin[:].unsqueeze(1).to_broadcast([128, 4, 128]), ALU.mult,
                      r=[MT.name, tin.name], w=[MT.name])
                kb.cp("pool", N0k[:], N0[:], r=[N0.name], w=[N0k.name])
                L0 = Lm[0]
                self.tr_f32(L0[:], N0[:], idf)
                P0 = Pm[0]
                kb.tt("dve", hv(P0[:]), hv(N0[:]), idf[:].unsqueeze(1).to_broadcast([128, 4, 128]), ALU.add,
                      r=[N0.name, idf.name], w=[P0.name])
                cur = 0
                for lev in range(6):
                    Nc, Lc, Pc = Nm[cur], Lm[cur], Pm[cur]
                    Nn, Ln, Pn = Nm[1 - cur], Lm[1 - cur], Pm[1 - cur]
                    psl = self.next_ps(); psn = self.next_ps()
                    for h in range(H):
                        hs = slice(h * 128, (h + 1) * 128)
                        kb.mm(psl[:, hs], Nc[:, hs], Lc[:, hs], True, True, r=[Nc.name, Lc.name], w=[psl.name])
                    if lev < 5:
                        for h in range(H):
                            hs = slice(h * 128, (h + 1) * 128)
                            kb.mm(psn[:, hs], Lc[:, hs], Nc[:, hs], True, True, r=[Nc.name, Lc.name], w=[psn.name])
                    kb.cp("act", Ln[:], psl[:], r=[psl.name], w=[Ln.name])
                    if lev < 5:
                        kb.cp("dve", Nn[:], psn[:], r=[psn.name], w=[Nn.name])
                    psp = self.next_ps()
                    for h in range(H):
                        hs = slice(h * 128, (h + 1) * 128)
                        kb.mm(psp[:, hs], Ln[:, hs], Pc[:, hs], True, True, r=[Ln.name, Pc.name], w=[psp.name])
                    kb.tt("dve", Pn[:], psp[:], Pc[:], ALU.add, r=[psp.name, Pc.name], w=[Pn.name])
                    cur = 1 - cur
                WT = Pm[cur]
                N0 = Nm[0]
                ps = self.next_ps()
                for h in range(H):
                    hs = slice(h * 128, (h + 1) * 128)
                    kb.mm(ps[:, hs], aT[:, hs], T0[:, hs], True, False, r=[aT.name, T0.name], w=[ps.name])
                    kb.mm(ps[:, hs], N0k[:, hs], vp[:, hs], False, True, r=[N0k.name, vp.name], w=[ps.name])
                kb.cp("act", X1[:], ps[:], r=[ps.name], w=[X1.name])
                ps = self.next_ps()
                for h in range(H):
                    hs = slice(h * 128, (h + 1) * 128)
                    kb.mm(ps[:, hs], WT[:, hs], X1[:, hs], True, True, r=[WT.name, X1.name], w=[ps.name])
                kb.tt("dve", UV[:], ps[:], vp[:], ALU.add, r=[ps.name, vp.name], w=[UV.name])
                ps = self.next_ps()
                ps2 = self.next_ps()
                for h in range(H):
                    hs = slice(h * 128, (h + 1) * 128)
                    kb.mm(ps[:, hs], rT[:, hs], T0[:, hs], True, False, r=[rT.name, T0.name], w=[ps.name])
                    kb.mm(ps[:, hs], MT[:, hs], UV[:, hs], False, True, r=[MT.name, UV.name], w=[ps.name])
                    kb.mm(ps2[:, hs], khat[:, hs], UV[:, hs], True, True, r=[khat.name, UV.name], w=[ps2.name])
                kb.cp("act", Y[:], ps[:], r=[ps.name], w=[Y.name])
                for h in range(H):
                    hs = slice(h * 128, (h + 1) * 128)
                    kb.stt("dve", T0[:, hs], T0[:, hs], eGC[:, h:h + 1], ps2[:, hs], ALU.mult, ALU.add,
                           r=[T0.name, eGC.name, ps2.name], w=[T0.name])
                kb.act(ysq[:], Y[:], AF.Square, r=[Y.name], w=[ysq.name])
                kb.add("dve", lambda e: e.tensor_reduce(yss[:], ysq[:].rearrange("p (h n) -> p h n", h=4), AX.X, ALU.add),
                       [ysq.name], [yss.name])
                self.rsqrt(yrs[:], yss[:], 1.0 / N, EPS, [yss.name], [yrs.name])
                kb.act(sz[:], zt[:], AF.Silu, r=[zt.name], w=[sz.name])
                kb.tt("dve", hv(o[:]), hv(Y[:]), bb(yrs), ALU.mult, r=[Y.name, yrs.name], w=[o.name])
                kb.tt("pool", hv(o[:]), hv(o[:]), nw[:].unsqueeze(1).to_broadcast([128, 4, 128]), ALU.mult,
                      r=[o.name, nw.name], w=[o.name])
                kb.tt("dve", o[:], o[:], sz[:], ALU.mult, r=[o.name, sz.name], w=[o.name])
                kb.dma(self.o_mix[l][2][r0:r0 + 128, :], o[:], r=[o.name], q="pool")
            kb.flush()

    def declare_rwkv(self):
        L, S = self.L, self.S
        i = self.inp
        for nm, shp in (("rwkv_mu", [L, RWKV_IN]), ("rwkv_w0", [L, MIX]), ("rwkv_w_up", [L, 64, MIX]),
                        ("rwkv_a0", [L, MIX]), ("rwkv_a_up", [L, 64, MIX]), ("rwkv_g_up", [L, 128, MIX]),
                        ("rwkv_k_k", [L, MIX]), ("rwkv_k_a", [L, MIX]), ("rwkv_r_k", [L, MIX]),
                        ("rwkv_ln_w", [L, MIX]), ("rwkv_ln_b", [L, MIX]), ("rwkv_v0", [1, MIX]),
                        ("rwkv_vres_up", [1, 32, MIX])):
            setattr(self, nm, i(nm, shp))
        self.vfirst = self.scr("vfirst", [S, MIX])

    def phase_rwkv(self, l):
        kb, S, NT = self.kb, self.S, self.NT
        H, N = 8, 64
        P = self.proj_tm[l]
        c0 = OFF_RWKV
        with ExitStack() as st:
            sb = lambda name, shape, dt=F32: self.sb(st, name, shape, dt)
            self.pss = [self.psb(st, "ps_r%d" % j) for j in range(8)]
            idf = sb("idf", [128, 128]); ones = sb("ones", [128, 128])
            tin = sb("tin", [128, 128]); tst = sb("tst", [128, 128])
            kb.dma(idf[:], self.ident_f[:, :], w=[idf.name])
            kb.dma(ones[:], self.ones_f[:, :], w=[ones.name])
            kb.dma(tin[:], self.triu_incl[:, :], w=[tin.name])
            kb.dma(tst[:], self.triu_strict[:, :], w=[tst.name])

            def bvec(name, src, n):
                tl = sb(name, [128, n])
                kb.dma(tl[:], src[l:l + 1, :].to_broadcast([128, n]), w=[tl.name])
                return tl
            mu = bvec("mu", self.rwkv_mu, RWKV_IN)
            w0 = bvec("w0", self.rwkv_w0, MIX); a0 = bvec("a0", self.rwkv_a0, MIX)
            k_k = bvec("k_k", self.rwkv_k_k, MIX); k_a = bvec("k_a", self.rwkv_k_a, MIX)
            r_k = bvec("r_k", self.rwkv_r_k, MIX); ln_w = bvec("ln_w", self.rwkv_ln_w, MIX)
            ln_b = bvec("ln_b", self.rwkv_ln_b, MIX)
            wup = sb("wup", [64, MIX]); aup = sb("aup", [64, MIX]); gup = sb("gup", [128, MIX])
            kb.dma(wup[:], self.rwkv_w_up[l], w=[wup.name])
            kb.dma(aup[:], self.rwkv_a_up[l], w=[aup.name])
            kb.dma(gup[:], self.rwkv_g_up[l], w=[gup.name])
            if l > 0:
                v0 = sb("v0", [128, MIX]); vup = sb("vup", [32, MIX])
                kb.dma(v0[:], self.rwkv_v0[0:1, :].to_broadcast([128, MIX]), w=[v0.name])
                kb.dma(vup[:], self.rwkv_vres_up[0], w=[vup.name])
                vf = sb("vf", [128, MIX]); xvd = sb("xvd", [128, 32]); xvdT = sb("xvdT", [32, 128])
            X0 = sb("X0", [128, RWKV_IN]); X1s = sb("X1s", [128, RWKV_IN]); c = sb("c", [128, RWKV_IN])
            sm = sb("sm", [128, 256]); smT = sb("smT", [128, 256])
            wdT = sb("wdT", [64, 128]); adT = sb("adT", [64, 128])
            ld = sb("ld", [128, MIX]); a = sb("a", [128, MIX]); g = sb("g", [128, MIX])
            kk = sb("kk", [128, MIX]); t1 = sb("t1", [128, MIX]); t2 = sb("t2", [128, MIX])
            ssq = sb("ssq", [128, 8]); rs = sb("rs", [128, 8])
            kp = sb("kp", [128, MIX]); bet = sb("bet", [128, MIX])
            G = sb("G", [128, MIX]); eG = sb("eG", [128, MIX]); enG = sb("enG", [128, MIX])
            eGr = sb("eGr", [128, MIX]); eGm = sb("eGm", [128, MIX])
            at = sb("at", [128, MIX]); bt = sb("bt", [128, MIX]); kt = sb("kt", [128, MIX]); rt = sb("rt", [128, MIX])
            bh = sb("bh", [128, MIX]); kh = sb("kh", [128, MIX])
            aT = sb("aT", [64, 1024]); bT = sb("bT", [64, 1024]); kT = sb("kT", [64, 1024]); rT = sb("rT", [64, 1024])
            PCT = sb("PCT", [64, 8])
            Nm = [sb("Nm%d" % k, [128, 1024]) for k in range(2)]
            Lm = [sb("Lm%d" % k, [128, 1024]) for k in range(2)]
            Pm = [sb("Pm%d" % k, [128, 1024]) for k in range(2)]
            LakT = sb("LakT", [128, 1024]); MrbT = sb("MrbT", [128, 1024]); MrkT = sb("MrkT", [128, 1024])
            X1 = sb("X1", [128, MIX]); U = sb("U", [128, MIX]); Y = sb("Y", [128, MIX])
            T0 = sb("T0", [64, MIX])
            mean = sb("mean", [128, 8]); var = sb("var", [128, 8]); rkv = sb("rkv", [128, 8])
            kb.memset("dve", T0[:], 0.0, w=[T0.name])
            hv = lambda x: x.rearrange("p (h n) -> p h n", h=H)
            b8 = lambda x: x[:].unsqueeze(2).to_broadcast([128, 8, 64])
            msk = lambda m: m[:].unsqueeze(1).to_broadcast([128, 8, 128])
            hm = lambda x: x.rearrange("p (h n) -> p h n", h=H)

            def tr64(dst, src):
                for half in range(2):
                    ps = self.next_ps()
                    for hh in range(4):
                        h = half * 4 + hh
                        kb.tr(ps[0:64, hh * 128:(hh + 1) * 128], src[:, h * 64:(h + 1) * 64], idf[:],
                              r=[src.name, idf.name], w=[ps.name])
                    kb.cp("act" if half else "dve", dst[:, half * 512:(half + 1) * 512], ps[0:64, :],
                          r=[ps.name], w=[dst.name])

            def mat8(dst, lT, rT_, mask):
                for half in range(2):
                    ps = self.next_ps()
                    for hh in range(4):
                        h = half * 4 + hh
                        kb.mm(ps[:, hh * 128:(hh + 1) * 128], lT[:, h * 128:(h + 1) * 128], rT_[:, h * 128:(h + 1) * 128],
                              True, True, r=[lT.name, rT_.name], w=[ps.name])
                    kb.tt("dve", dst[:, half * 512:(half + 1) * 512].rearrange("p (h n) -> p h n", h=4),
                          ps[:].rearrange("p (h n) -> p h n", h=4),
                          mask[:].unsqueeze(1).to_broadcast([128, 4, 128]), ALU.mult,
                          r=[ps.name, mask.name], w=[dst.name])

            for t in range(NT):
                r0 = t * 128
                kb.dma(X0[:], P[r0:r0 + 128, c0:c0 + RWKV_IN], w=[X0.name])
                if t == 0:
                    kb.memset("pool", X1s[:], 0.0, w=[X1s.name])
                    kb.dma(X1s[1:128, :], P[0:127, c0:c0 + RWKV_IN], w=[X1s.name])
                else:
                    kb.dma(X1s[:], P[r0 - 1:r0 + 127, c0:c0 + RWKV_IN], w=[X1s.name])
                kb.tt("dve", c[:], X1s[:], X0[:], ALU.subtract, r=[X0.name, X1s.name], w=[c.name])
                kb.tt("pool", c[:], c[:], mu[:], ALU.mult, r=[c.name, mu.name], w=[c.name])
                kb.tt("dve", c[:], c[:], X0[:], ALU.add, r=[c.name, X0.name], w=[c.name])
                r_ = c[:, 0:512]; k_ = c[:, 512:1024]; v_ = c[:, 1024:1536]
                kb.act(sm[:, 0:64], c[:, 1536:1600], AF.Tanh, r=[c.name], w=[sm.name])
                kb.cp("dve", sm[:, 64:128], c[:, 1600:1664], r=[c.name], w=[sm.name])
                kb.act(sm[:, 128:256], c[:, 1664:1792], AF.Sigmoid, r=[c.name], w=[sm.name])
                ps = self.next_ps()
                kb.tr(ps[0:64, 0:128], sm[:, 0:64], idf[:], r=[sm.name, idf.name], w=[ps.name])
                kb.tr(ps[0:64, 128:256], sm[:, 64:128], idf[:], r=[sm.name, idf.name], w=[ps.name])
                kb.tr(ps[:, 256:384], sm[:, 128:256], idf[:], r=[sm.name, idf.name], w=[ps.name])
                kb.cp("act", wdT[:], ps[0:64, 0:128], r=[ps.name], w=[wdT.name])
                kb.cp("dve", adT[:], ps[0:64, 128:256], r=[ps.name], w=[adT.name])
                kb.cp("act", smT[:, 0:128], ps[:, 256:384], r=[ps.name], w=[smT.name])
                psw = self.next_ps(); psa = self.next_ps(); psg = self.next_ps()
                kb.mm(psw[:], wdT[:], wup[:], True, True, r=[wdT.name, wup.name], w=[psw.name])
                kb.mm(psa[:], adT[:], aup[:], True, True, r=[adT.name, aup.name], w=[psa.name])
                kb.mm(psg[:], smT[:, 0:128], gup[:], True, True, r=[smT.name, gup.name], w=[psg.name])
                kb.tt("dve", ld[:], psw[:], w0[:], ALU.add, r=[psw.name, w0.name], w=[ld.name])
                kb.act(ld[:], ld[:], AF.Sigmoid, r=[ld.name], w=[ld.name])
                kb.ts("dve", ld[:], ld[:], -float(np.exp(-0.5)), None, ALU.mult, r=[ld.name], w=[ld.name])
                kb.tt("dve", a[:], psa[:], a0[:], ALU.add, r=[psa.name, a0.name], w=[a.name])
                kb.act(a[:], a[:], AF.Sigmoid, r=[a.name], w=[a.name])
                kb.cp("act", g[:], psg[:], r=[psg.name], w=[g.name])
                if l == 0:
                    kb.dma(self.vfirst[r0:r0 + 128, :], v_, r=[c.name], q="pool")
                else:
                    kb.dma(vf[:], self.vfirst[r0:r0 + 128, :], w=[vf.name])
                    kb.dma(xvd[:], self.proj_g[l][r0:r0 + 128, 3 * D:3 * D + 32], w=[xvd.name])
                    ps = self.next_ps()
                    kb.tr(ps[0:32, 0:128], xvd[:], idf[:], r=[xvd.name, idf.name], w=[ps.name])
                    kb.cp("act", xvdT[:], ps[0:32, 0:128], r=[ps.name], w=[xvdT.name])
                    ps = self.next_ps()
                    kb.mm(ps[:], xvdT[:], vup[:], True, True, r=[xvdT.name, vup.name], w=[ps.name])
                    kb.tt("dve", t1[:], ps[:], v0[:], ALU.add, r=[ps.name, v0.name], w=[t1.name])
                    kb.act(t1[:], t1[:], AF.Sigmoid, r=[t1.name], w=[t1.name])
                    kb.tt("dve", t2[:], vf[:], v_, ALU.subtract, r=[vf.name, c.name], w=[t2.name])
                    kb.tt("dve", t2[:], t2[:], t1[:], ALU.mult, r=[t2.name, t1.name], w=[t2.name])
                    kb.tt("dve", v_, v_, t2[:], ALU.add, r=[c.name, t2.name], w=[c.name])
                kb.tt("dve", kk[:], k_, k_k[:], ALU.mult, r=[c.name, k_k.name], w=[kk.name])
                kb.act(t1[:], kk[:], AF.Square, r=[kk.name], w=[t1.name])
                kb.add("dve", lambda e: e.tensor_reduce(ssq[:], t1[:].rearrange("p (h n) -> p h n", h=8), AX.X, ALU.add),
                       [t1.name], [ssq.name])
                self.rsqrt(rs[:], ssq[:], 1.0, EPS, [ssq.name], [rs.name])
                kb.tt("dve", hv(kk[:]), hv(kk[:]), b8(rs), ALU.mult, r=[kk.name, rs.name], w=[kk.name])
                kb.ts("dve", t2[:], a[:], -1.0, None, ALU.add, r=[a.name], w=[t2.name])
                kb.tt("dve", t2[:], t2[:], k_a[:], ALU.mult, r=[t2.name, k_a.name], w=[t2.name])
                kb.ts("dve", t2[:], t2[:], 1.0, None, ALU.add, r=[t2.name], w=[t2.name])
                kb.tt("dve", kp[:], t2[:], k_, ALU.mult, r=[t2.name, c.name], w=[kp.name])
                kb.tt("pool", bet[:], kk[:], a[:], ALU.mult, r=[kk.name, a.name], w=[bet.name])
                ps = self.next_ps(); ps2 = self.next_ps()
                kb.mm(ps[:], tin[:], ld[:], True, True, r=[tin.name, ld.name], w=[ps.name])
                kb.mm(ps2[:], ones[:], ld[:], True, True, r=[ones.name, ld.name], w=[ps2.name])
                kb.cp("dve", G[:], ps[:], r=[ps.name], w=[G.name])
                kb.tt("dve", eGr[:], ps2[:], G[:], ALU.subtract, r=[ps2.name, G.name], w=[eGr.name])
                kb.act(eGr[:], eGr[:], AF.Exp, r=[eGr.name], w=[eGr.name])
                kb.act(eG[:], G[:], AF.Exp, r=[G.name], w=[eG.name])
                kb.act(enG[:], G[:], AF.Exp, r=[G.name], w=[enG.name], scale=-1.0)
                kb.tt("pool", eGm[:], G[:], ld[:], ALU.subtract, r=[G.name, ld.name], w=[eGm.name])
                kb.act(eGm[:], eGm[:], AF.Exp, r=[eGm.name], w=[eGm.name])
                ps = self.next_ps()
                for h in range(H):
                    kb.mm(ps[0:64, h:h + 1], ld[:, h * 64:(h + 1) * 64], ones[:, 0:1], True, True,
                          r=[ld.name, ones.name], w=[ps.name])
                kb.act(PCT[:], ps[0:64, 0:8], AF.Exp, r=[ps.name], w=[PCT.name])
                kb.tt("dve", at[:], kk[:], eGm[:], ALU.mult, r=[kk.name, eGm.name], w=[at.name])
                kb.ts("pool", at[:], at[:], -1.0, None, ALU.mult, r=[at.name], w=[at.name])
                kb.tt("dve", bt[:], bet[:], enG[:], ALU.mult, r=[bet.name, enG.name], w=[bt.name])
                kb.tt("pool", kt[:], kp[:], enG[:], ALU.mult, r=[kp.name, enG.name], w=[kt.name])
                kb.tt("dve", rt[:], r_, eG[:], ALU.mult, r=[c.name, eG.name], w=[rt.name])
                kb.tt("pool", bh[:], bet[:], eGr[:], ALU.mult, r=[bet.name, eGr.name], w=[bh.name])
                kb.tt("dve", kh[:], kp[:], eGr[:], ALU.mult, r=[kp.name, eGr.name], w=[kh.name])
                tr64(aT, at); tr64(bT, bt); tr64(kT, kt); tr64(rT, rt)
                N0 = Nm[0]
                mat8(N0, bT, aT, tst)
                mat8(LakT, kT, aT, tst)
                mat8(MrbT, bT, rT, tin)
                mat8(MrkT, kT, rT, tin)
                L0 = Lm[0]
                for half in range(2):
                    self.tr_f32(L0[:, half * 512:(half + 1) * 512], N0[:, half * 512:(half + 1) * 512], idf)
                P0 = Pm[0]
                kb.tt("dve", hm(P0[:]), hm(N0[:]), msk(idf), ALU.add, r=[N0.name, idf.name], w=[P0.name])
                cur = 0
                for lev in range(6):
                    Nc, Lc, Pc = Nm[cur], Lm[cur], Pm[cur]
                    Nn, Ln, Pn = Nm[1 - cur], Lm[1 - cur], Pm[1 - cur]
                    for half in range(2):
                        hsl = slice(half * 512, (half + 1) * 512)
                        psl = self.next_ps()
                        for hh in range(4):
                            hs = slice(half * 512 + hh * 128, half * 512 + (hh + 1) * 128)
                            kb.mm(psl[:, hh * 128:(hh + 1) * 128], Nc[:, hs], Lc[:, hs], True, True,
                                  r=[Nc.name, Lc.name], w=[psl.name])
                        kb.cp("act", Ln[:, hsl], psl[:], r=[psl.name], w=[Ln.name])
                        if lev < 5:
                            psn = self.next_ps()
                            for hh in range(4):
                                hs = slice(half * 512 + hh * 128, half * 512 + (hh + 1) * 128)
                                kb.mm(psn[:, hh * 128:(hh + 1) * 128], Lc[:, hs], Nc[:, hs], True, True,
                                      r=[Nc.name, Lc.name], w=[psn.name])
                            kb.cp("dve", Nn[:, hsl], psn[:], r=[psn.name], w=[Nn.name])
                    for half in range(2):
                        hsl = slice(half * 512, (half + 1) * 512)
                        psp = self.next_ps()
                        for hh in range(4):
                            hs = slice(half * 512 + hh * 128, half * 512 + (hh + 1) * 128)
                            kb.mm(psp[:, hh * 128:(hh + 1) * 128], Ln[:, hs], Pc[:, hs], True, True,
                                  r=[Ln.name, Pc.name], w=[psp.name])
                        kb.tt("dve", Pn[:, hsl], psp[:], Pc[:, hsl], ALU.add, r=[psp.name, Pc.name], w=[Pn.name])
                    cur = 1 - cur
                WT = Pm[cur]
                ps = self.next_ps()
                for h in range(H):
                    hs = slice(h * 64, (h + 1) * 64); ms = slice(h * 128, (h + 1) * 128)
                    kb.mm(ps[:, hs], aT[:, ms], T0[:, hs], True, False, r=[aT.name, T0.name], w=[ps.name])
                    kb.mm(ps[:, hs], LakT[:, ms], c[:, 1024 + h * 64:1024 + (h + 1) * 64], False, True,
                          r=[LakT.name, c.name], w=[ps.name])
                kb.cp("act", X1[:], ps[:], r=[ps.name], w=[X1.name])
                ps = self.next_ps()
                for h in range(H):
                    hs = slice(h * 64, (h + 1) * 64); ms = slice(h * 128, (h + 1) * 128)
                    kb.mm(ps[:, hs], WT[:, ms], X1[:, hs], True, True, r=[WT.name, X1.name], w=[ps.name])
                kb.cp("dve", U[:], ps[:], r=[ps.name], w=[U.name])
                ps = self.next_ps(); ps2 = self.next_ps()
                for h in range(H):
                    hs = slice(h * 64, (h + 1) * 64); ms = slice(h * 128, (h + 1) * 128)
                    vs = c[:, 1024 + h * 64:1024 + (h + 1) * 64]
                    kb.mm(ps[:, hs], rT[:, ms], T0[:, hs], True, False, r=[rT.name, T0.name], w=[ps.name])
                    kb.mm(ps[:, hs], MrbT[:, ms], U[:, hs], False, False, r=[MrbT.name, U.name], w=[ps.name])
                    kb.mm(ps[:, hs], MrkT[:, ms], vs, False, True, r=[MrkT.name, c.name], w=[ps.name])
                    kb.mm(ps2[0:64, hs], bh[:, hs], U[:, hs], True, False, r=[bh.name, U.name], w=[ps2.name])
                    kb.mm(ps2[0:64, hs], kh[:, hs], vs, False, True, r=[kh.name, c.name], w=[ps2.name])
                kb.cp("act", Y[:], ps[:], r=[ps.name], w=[Y.name])
                for h in range(H):
                    hs = slice(h * 64, (h + 1) * 64)
                    kb.stt("dve", T0[:, hs], T0[:, hs], PCT[:, h:h + 1], ps2[0:64, hs], ALU.mult, ALU.add,
                           r=[T0.name, PCT.name, ps2.name], w=[T0.name])
                kb.add("dve", lambda e: e.tensor_reduce(mean[:], Y[:].rearrange("p (h n) -> p h n", h=8), AX.X, ALU.add),
                       [Y.name], [mean.name])
                kb.ts("dve", mean[:], mean[:], 1.0 / N, None, ALU.mult, r=[mean.name], w=[mean.name])
                kb.tt("dve", hv(Y[:]), hv(Y[:]), b8(mean), ALU.subtract, r=[Y.name, mean.name], w=[Y.name])
                kb.act(t1[:], Y[:], AF.Square, r=[Y.name], w=[t1.name])
                kb.add("dve", lambda e: e.tensor_reduce(var[:], t1[:].rearrange("p (h n) -> p h n", h=8), AX.X, ALU.add),
                       [t1.name], [var.name])
                self.rsqrt(var[:], var[:], 1.0 / N, 64e-5, [var.name], [var.name])
                kb.tt("dve", hv(Y[:]), hv(Y[:]), b8(var), ALU.mult, r=[Y.name, var.name], w=[Y.name])
                kb.tt("pool", Y[:], Y[:], ln_w[:], ALU.mult, r=[Y.name, ln_w.name], w=[Y.name])
                kb.tt("dve", Y[:], Y[:], ln_b[:], ALU.add, r=[Y.name, ln_b.name], w=[Y.name])
                kb.tt("pool", t2[:], r_, kp[:], ALU.mult, r=[c.name, kp.name], w=[t2.name])
                kb.tt("pool", t2[:], t2[:], r_k[:], ALU.mult, r=[t2.name, r_k.name], w=[t2.name])
                kb.add("dve", lambda e: e.tensor_reduce(rkv[:], t2[:].rearrange("p (h n) -> p h n", h=8), AX.X, ALU.add),
                       [t2.name], [rkv.name])
                kb.tt("dve", hv(t2[:]), hv(v_), b8(rkv), ALU.mult, r=[c.name, rkv.name], w=[t2.name])
                kb.tt("dve", Y[:], Y[:], t2[:], ALU.add, r=[Y.name, t2.name], w=[Y.name])
                kb.tt("dve", Y[:], Y[:], g[:], ALU.mult, r=[Y.name, g.name], w=[Y.name])
                kb.dma(self.o_mix[l][1][r0:r0 + 128, :], Y[:], r=[Y.name], q="pool")
            kb.flush()

    def build(self):
        self.declare(); self.declare_rest(); self.declare_gdn(); self.declare_rwkv(); self.declare_nsa()
        x = self.x_in
        for l in range(self.L):
            self.phase_proj(l, x)
            self.phase_nsa_prep(l)
            self.phase_nsa_attn(l)
            self.phase_rwkv(l)
            self.phase_gdn(l)
            self.phase_merge(l, x)
            self.phase_ffn(l)
            x = self.x_lay[l]
        return self.nc


def _consts():
    import ml_dtypes
    f = np.float32
    return {"ident_bf": np.eye(128, dtype=ml_dtypes.bfloat16), "ident_f": np.eye(128, dtype=f),
            "ones_f": np.ones((128, 128), f), "triu_incl": np.triu(np.ones((128, 128), f)),
            "triu_strict": np.triu(np.ones((128, 128), f), 1)}


def kernel(**inputs):
    x = np.asarray(inputs["x"], np.float32)
    B, S, _ = x.shape
    L = inputs["w_in"].shape[0]
    prog = Prog(S, n_layers=L)
    nc = prog.build()
    w_in = np.asarray(inputs["w_in"], np.float32)
    ext = np.zeros((L, D, 32), np.float32)
    ext[1:] = np.asarray(inputs["rwkv_vres_down"], np.float32)
    shared = dict(_consts())
    shared.update(_nsa_consts(S))
    shared["w_in"] = np.ascontiguousarray(np.concatenate([w_in, ext], axis=2))
    for k in prog.din:
        if k not in shared and k != "x":
            shared[k] = np.ascontiguousarray(np.asarray(inputs[k], np.float32))
    in_maps = [dict(shared, x=np.ascontiguousarray(x[b])) for b in range(B)]
    res = run_bass_kernel_spmd(nc, in_maps, core_ids=list(range(B)))
    return np.stack([np.asarray(r["out"], np.float32) for r in res.results], axis=0)


def _nsa_consts(S):
    import ml_dtypes
    f = np.float32
    NB = S // 64
    n_cmp = (S - 32) // 16 + 1
    nch = (n_cmp + 127) // 128
    ex = np.zeros((128, S), f)
    ex[np.arange(S) // 64, np.arange(S)] = 1.0
    k = np.arange(128)[:, None]
    q = np.arange(128)[None, :]
    caus = np.where(k > q, NEGM, 0.0).astype(f)
    win = np.where(k <= q, NEGM, 0.0).astype(f)
    rr = np.arange(17)[None, :, None]
    cm = ((16 * k[:, :, None] + 31) <= (128 * rr + q[:, None, :])).astype(f)
    cs = np.arange(nch * 128) * 16
    ss = np.arange(NB) * 64
    ov = np.clip(np.minimum(cs[:, None] + 32, ss[None, :] + 64) - np.maximum(cs[:, None], ss[None, :]), 0, None) / 32.0
    ov[n_cmp:] = 0.0
    c2s = ov.reshape(nch, 128, NB).transpose(1, 0, 2).astype(f)
    return {"exall": ex.astype(ml_dtypes.bfloat16),
            "causneg": np.tile(caus, (1, 4)).astype(ml_dtypes.bfloat16),
            "winneg": np.tile(win, (1, 4)).astype(ml_dtypes.bfloat16),
            "cmask": np.ascontiguousarray(cm), "cmp2slc": np.ascontiguousarray(c2s)}


def _declare_nsa(self):
    L, S = self.L, self.S
    i = self.inp
    self.NB = S // 64
    self.n_cmp = (S - 32) // 16 + 1
    self.nch = (self.n_cmp + 127) // 128
    self.nsa_q_norm = i("nsa_q_norm", [L, 64])
    self.nsa_k_norm = i("nsa_k_norm", [L, 3, 64])
    self.nsa_cmp_pos = i("nsa_cmp_pos", [L, 2, 32, 64])
    self.nsa_cmp_w1 = i("nsa_cmp_w1", [L, 2, 2048, 256])
    self.nsa_cmp_w2 = i("nsa_cmp_w2", [L, 2, 256, 64])
    self.exall = i("exall", [128, S], BF16)
    self.causneg = i("causneg", [128, 512], BF16)
    self.winneg = i("winneg", [128, 512], BF16)
    self.cmask = i("cmask", [128, 17, 128])
    self.cmp2slc = i("cmp2slc", [128, self.nch, self.NB])
    self.qT_d = self.scr("qT_d", [8, 64, S])
    self.kswT_d = self.scr("kswT_d", [4, 64, S], BF16)
    self.kvcT_d = self.scr("kvcT_d", [4, 64, S])
    self.vaug_d = self.scr("vaug_d", [S, 4 * 65], BF16)


def _phase_nsa_prep(self, l):
    kb, S, NT = self.kb, self.S, self.NT
    P = self.proj_tm[l]
    with ExitStack() as st:
        sb = lambda name, shape, dt=F32: self.sb(st, name, shape, dt)
        pA, pB, pC, pD = [self.psb(st, "ps_n%d" % j) for j in range(4)]
        idf = sb("idf", [128, 128])
        kb.dma(idf[:], self.ident_f[:, :], w=[idf.name])
        gq = sb("gq", [128, 12, 64])
        for h in range(8):
            kb.dma(gq[:, h, :], self.nsa_q_norm[l:l + 1, :].to_broadcast([128, 64]), w=[gq.name])
        for h in range(2):
            kb.dma(gq[:, 8 + h, :], self.nsa_k_norm[l, 1:2, :].to_broadcast([128, 64]), w=[gq.name])
            kb.dma(gq[:, 10 + h, :], self.nsa_k_norm[l, 2:3, :].to_broadcast([128, 64]), w=[gq.name])
        xin = [sb("xin%d" % j, [128, NSA_IN]) for j in range(2)]
        nin = sb("nin", [128, 768]); sq = sb("sq", [128, 768]); ss = sb("ss", [128, 12]); rs = sb("rs", [128, 12])
        qo = [sb("qo%d" % j, [64, 8, 128]) for j in range(2)]
        ko = [sb("ko%d" % j, [64, 4, 128], BF16) for j in range(2)]
        co = [sb("co%d" % j, [64, 4, 128]) for j in range(2)]
        va = [sb("va%d" % j, [128, 4, 65], BF16) for j in range(2)]
        for j in range(2):
            kb.memset("dve", va[j][:], 1.0, w=[va[j].name])
        h12 = lambda x: x.rearrange("p (h n) -> p h n", h=12)
        for t in range(NT):
            j = t % 2
            rows = slice(t * 128, (t + 1) * 128)
            x = xin[j]
            kb.dma(x[:], P[rows, 0:NSA_IN], w=[x.name])
            kb.cp("pool", nin[:, 0:512], x[:, 0:512], r=[x.name], w=[nin.name])
            kb.cp("pool", nin[:, 512:640], x[:, 768:896], r=[x.name], w=[nin.name])
            kb.cp("pool", nin[:, 640:768], x[:, 1024:1152], r=[x.name], w=[nin.name])
            kb.act(sq[:], nin[:], AF.Square, r=[nin.name], w=[sq.name])
            kb.add("dve", lambda e: e.tensor_reduce(ss[:], sq[:].rearrange("p (h n) -> p h n", h=12), AX.X, ALU.add),
                   [sq.name], [ss.name])
            self.rsqrt(rs[:], ss[:], 1.0 / 64, EPS, [ss.name], [rs.name])
            kb.tt("dve", h12(nin[:]), h12(nin[:]), rs[:].unsqueeze(2).to_broadcast([128, 12, 64]), ALU.mult,
                  r=[nin.name, rs.name], w=[nin.name])
            kb.tt("pool", nin[:], nin[:], gq[:].rearrange("p h n -> p (h n)"), ALU.mult, r=[nin.name, gq.name], w=[nin.name])
            for blk in range(8):
                ps = pA if blk < 4 else pB
                kb.tr(ps[0:64, (blk % 4) * 128:(blk % 4 + 1) * 128], nin[:, blk * 64:(blk + 1) * 64], idf[:],
                      r=[nin.name, idf.name], w=[ps.name])
            for blk in range(4):
                kb.tr(pC[0:64, blk * 128:(blk + 1) * 128], nin[:, 512 + blk * 64:512 + (blk + 1) * 64], idf[:],
                      r=[nin.name, idf.name], w=[pC.name])
                kb.tr(pD[0:64, blk * 128:(blk + 1) * 128], x[:, 512 + blk * 64:512 + (blk + 1) * 64], idf[:],
                      r=[x.name, idf.name], w=[pD.name])
            kb.cp("act", qo[j][:, 0:4, :].rearrange("p h t -> p (h t)"), pA[0:64, :], r=[pA.name], w=[qo[j].name])
            kb.cp("dve", qo[j][:, 4:8, :].rearrange("p h t -> p (h t)"), pB[0:64, :], r=[pB.name], w=[qo[j].name])
            kb.cp("act", ko[j][:].rearrange("p h t -> p (h t)"), pC[0:64, :], r=[pC.name], w=[ko[j].name])
            kb.cp("dve", co[j][:].rearrange("p h t -> p (h t)"), pD[0:64, :], r=[pD.name], w=[co[j].name])
            kb.dma(self.qT_d[:, :, rows].rearrange("h d t -> d h t"), qo[j][:], r=[qo[j].name], q="pool")
            kb.dma(self.kswT_d[:, :, rows].rearrange("h d t -> d h t"), ko[j][:], r=[ko[j].name], q="pool")
            kb.dma(self.kvcT_d[:, :, rows].rearrange("h d t -> d h t"), co[j][:], r=[co[j].name], q="pool")
            kb.cp("pool", va[j][:, 0:2, 0:64], x[:, 896:1024].rearrange("p (g n) -> p g n", g=2), r=[x.name], w=[va[j].name])
            kb.cp("pool", va[j][:, 2:4, 0:64], x[:, 1152:1280].rearrange("p (g n) -> p g n", g=2), r=[x.name], w=[va[j].name])
            kb.dma(self.vaug_d[rows, :], va[j][:].rearrange("p g n -> p (g n)"), r=[va[j].name], q="pool")
        kb.flush()


Prog.declare_nsa = _declare_nsa
Prog.phase_nsa_prep = _phase_nsa_prep


def _phase_nsa_attn(self, l):
    kb, S, NT, NB, n_cmp, nch = self.kb, self.S, self.NT, self.NB, self.n_cmp, self.nch
    P = self.proj_tm[l]
    SC = 0.125
    with ExitStack() as st:
        sb = lambda name, shape, dt=F32: self.sb(st, name, shape, dt)
        psS = [self.psb(st, "ps_S%d" % j) for j in range(2)]
        psO = [self.psb(st, "ps_O%d" % j) for j in range(2)]
        psI = self.psb(st, "ps_I"); psT = self.psb(st, "ps_T")
        psX = [self.psb(st, "ps_X%d" % j) for j in range(2)]
        idf = sb("idf", [128, 128]); idb = sb("idb", [128, 128], BF16); ones = sb("ones", [128, 128])
        kb.dma(idf[:], self.ident_f[:, :], w=[idf.name])
        kb.dma(idb[:], self.ident_bf[:, :], w=[idb.name])
        kb.dma(ones[:], self.ones_f[:, :], w=[ones.name])
        ks = sb("ks", [64, 2, S], BF16); kw = sb("kw", [64, 2, S], BF16)
        kb.dma(ks[:], self.kswT_d[0:2].rearrange("g d s -> d g s"), w=[ks.name])
        kb.dma(kw[:], self.kswT_d[2:4].rearrange("g d s -> d g s"), w=[kw.name])
        vv = sb("vv", [128, NT, 4 * 65], BF16)
        kb.dma(vv[:], self.vaug_d.rearrange("(c p) n -> p c n", p=128), w=[vv.name])
        ex = sb("ex", [128, S], BF16)
        kb.dma(ex[:], self.exall[:, :], w=[ex.name])
        cneg = sb("cneg", [128, 512], BF16); wneg = sb("wneg", [128, 512], BF16)
        kb.dma(cneg[:], self.causneg[:, :], w=[cneg.name])
        kb.dma(wneg[:], self.winneg[:, :], w=[wneg.name])
        cmk = sb("cmk", [128, 17, 128]); c2s = sb("c2s", [128, nch, NB])
        kb.dma(cmk[:], self.cmask[:, :, :], w=[cmk.name])
        kb.dma(c2s[:], self.cmp2slc[:, :, :], w=[c2s.name])
        kcT = sb("kcT", [64, 2, nch * 128]); vc = sb("vc", [128, nch, 2, 65])
        kb.memset("dve", kcT[:], 0.0, w=[kcT.name])
        kb.memset("dve", vc[:], 0.0, w=[vc.name])
        kb.memset("dve", vc[:, :, :, 64:65], 1.0, w=[vc.name])
        with ExitStack() as s2:
            sb2 = lambda name, shape, dt=F32: self.sb(s2, name, shape, dt)
            w1 = sb2("w1c", [64, 32, 256]); w2 = sb2("w2c", [128, 2, 64]); pos = sb2("pos", [32, 64]); posT = sb2("posT", [64, 32])
            tT = sb2("tT", [64, S]); hid = sb2("hid", [128, 2, 512]); cb = sb2("cb", [128, 2])
            kg0r = sb2("kg0r", [1, 64]); kg0 = sb2("kg0", [64, 1]); sqc = sb2("sqc", [64, 512]); rsc = sb2("rsc", [64, 512])
            kcr = sb2("kcr", [64, 512])
            kb.dma(kg0r[:], self.nsa_k_norm[l, 0:1, :], w=[kg0r.name])
            kb.tr(psT[0:64, 0:1], kg0r[:], idf[0:1, 0:1], r=[kg0r.name, idf.name], w=[psT.name])
            kb.cp("dve", kg0[:], psT[0:64, 0:1], r=[psT.name], w=[kg0.name])
            for jj in range(2):
                kb.dma(w1[:], self.nsa_cmp_w1[l, jj].rearrange("(l d) n -> d l n", d=64), w=[w1.name])
                kb.dma(w2[:], self.nsa_cmp_w2[l, jj].rearrange("(k p) n -> p k n", p=128), w=[w2.name])
                kb.dma(pos[:], self.nsa_cmp_pos[l, jj], w=[pos.name])
                kb.tr(psT[0:64, 0:32], pos[:], idf[0:32, 0:32], r=[pos.name, idf.name], w=[psT.name])
                kb.cp("dve", posT[:], psT[0:64, 0:32], r=[psT.name], w=[posT.name])
                for half in range(2):
                    for li in range(32):
                        kb.mm(psT[:, 64 + half:65 + half], w1[:, li, half * 128:(half + 1) * 128], posT[:, li:li + 1],
                              li == 0, li == 31, r=[w1.name, posT.name], w=[psT.name])
                kb.cp("dve", cb[:], psT[:, 64:66], r=[psT.name], w=[cb.name])
                for g in range(2):
                    kb.dma(tT[:], self.kvcT_d[jj * 2 + g], w=[tT.name])
                    for half in range(2):
                        for li in range(32):
                            kb.mm(psX[half][:, 0:n_cmp], w1[:, li, half * 128:(half + 1) * 128],
                                  tT[:, li:li + 16 * (n_cmp - 1) + 1:16], li == 0, li == 31,
                                  r=[w1.name, tT.name], w=[psX[half].name])
                        kb.act(hid[:, half, 0:n_cmp], psX[half][:, 0:n_cmp], AF.Silu, bias=cb[:, half:half + 1],
                               r=[psX[half].name, cb.name], w=[hid.name])
                    if jj == 0:
                        for half in range(2):
                            kb.mm(psT[0:64, 0:n_cmp], w2[:, half, :], hid[:, half, 0:n_cmp], half == 0, half == 1,
                                  r=[w2.name, hid.name], w=[psT.name])
                        kb.act(sqc[:, 0:n_cmp], psT[0:64, 0:n_cmp], AF.Square, r=[psT.name], w=[sqc.name])
                        kb.cp("dve", kcr[:, 0:n_cmp], psT[0:64, 0:n_cmp], r=[psT.name], w=[kcr.name])
                        kb.mm(psI[0:64, 0:n_cmp], ones[0:64, 0:64], sqc[:, 0:n_cmp], True, True,
                              r=[ones.name, sqc.name], w=[psI.name])
                        self.rsqrt(rsc[:, 0:n_cmp], psI[0:64, 0:n_cmp], 1.0 / 64, EPS, [psI.name], [rsc.name])
                        kb.stt("dve", kcT[:, g, 0:n_cmp], kcr[:, 0:n_cmp], kg0[:, 0:1], rsc[:, 0:n_cmp], ALU.mult, ALU.mult,
                               r=[kcr.name, kg0.name, rsc.name], w=[kcT.name])
                    else:
                        for ch in range(nch):
                            cw = min(128, n_cmp - ch * 128)
                            for half in range(2):
                                kb.mm(psT[0:cw, 0:64], hid[:, half, ch * 128:ch * 128 + cw], w2[:, half, :],
                                      half == 0, half == 1, r=[hid.name, w2.name], w=[psT.name])
                            kb.cp("dve", vc[0:cw, ch, g, 0:64], psT[0:cw, 0:64], r=[psT.name], w=[vc.name])
            kb.flush()
        q32 = [sb("q32_%d" % j, [64, 4, 128]) for j in range(2)]
        q16 = [sb("q16_%d" % j, [64, 4, 128], BF16) for j in range(2)]
        gs = sb("gs", [128, 24])
        e32 = [sb("e32_%d" % j, [128, 512]) for j in range(2)]
        e16 = [sb("e16_%d" % j, [128, 512], BF16) for j in range(2)]
        den = sb("den", [128, 4]); rden = sb("rden", [128, 4]); coef = sb("coef", [128, 4])
        impm = sb("impm", [128, NB]); wk = sb("wk", [128, NB]); m1 = sb("m1", [128, 8]); m2 = sb("m2", [128, 8])
        sel = sb("sel", [128, NB]); negT = sb("negT", [128, 512], BF16)
        oacc = [sb("oacc%d" % j, [128, MIX]) for j in range(2)]
        kb.memset("dve", negT[:], 0.0, w=[negT.name])
        nS = 0
        nO = 0

        def finish_branch(pso, g, br, oa, first):
            kb.ts("dve", den[:], pso[:, 64:260:65], 1e-30, None, ALU.max, r=[pso.name], w=[den.name])
            kb.add("dve", lambda e: e.reciprocal(rden[:], den[:]), [den.name], [rden.name])
            kb.tt("dve", coef[:], rden[:], gs[:, g * 12 + br:g * 12 + 12:3], ALU.mult, r=[rden.name, gs.name], w=[coef.name])
            for h in range(4):
                osl = oa[:, (4 * g + h) * 64:(4 * g + h + 1) * 64]
                if first:
                    kb.ts("dve", osl, pso[:, h * 65:h * 65 + 64], coef[:, h:h + 1], None, ALU.mult,
                          r=[pso.name, coef.name], w=[oa.name])
                else:
                    kb.stt("dve", osl, pso[:, h * 65:h * 65 + 64], coef[:, h:h + 1], osl, ALU.mult, ALU.add,
                           r=[pso.name, coef.name, oa.name], w=[oa.name])

        for b in range(NT):
            rows = slice(b * 128, (b + 1) * 128)
            oa = oacc[b % 2]
            kb.dma(gs[:], P[rows, 1280:1304], w=[gs.name])
            kb.act(gs[:], gs[:], AF.Sigmoid, r=[gs.name], w=[gs.name])
            for g in range(2):
                qj = (b * 2 + g) % 2
                kb.dma(q32[qj][:], self.qT_d[4 * g:4 * g + 4, :, rows].rearrange("h d t -> d h t"), w=[q32[qj].name])
                kb.cp("pool", q16[qj][:], q32[qj][:], r=[q32[qj].name], w=[q16[qj].name])
                qf32 = q32[qj][:].rearrange("p h t -> p (h t)")
                qf16 = q16[qj][:].rearrange("p h t -> p (h t)")
                pso = psO[nO % 2]; nO += 1
                chunks = list(range(0, min(b // 16, nch - 1) + 1))
                for ci, kc in enumerate(chunks):
                    pS = psS[nS % 2]; e = e32[nS % 2]; nS += 1
                    kb.mm(pS[:], kcT[:, g, kc * 128:(kc + 1) * 128], qf32, True, True,
                          r=[kcT.name, q32[qj].name], w=[pS.name])
                    kb.act(e[:], pS[:], AF.Exp, scale=SC, r=[pS.name], w=[e.name])
                    rr = b - 16 * kc
                    if rr <= 16:
                        kb.tt("dve", e[:].rearrange("p (h t) -> p h t", h=4), e[:].rearrange("p (h t) -> p h t", h=4),
                              cmk[:, rr, :].unsqueeze(1).to_broadcast([128, 4, 128]), ALU.mult,
                              r=[e.name, cmk.name], w=[e.name])
                    for h in range(4):
                        kb.mm(pso[:, h * 65:(h + 1) * 65], e[:, h * 128:(h + 1) * 128], vc[:, kc, g, :],
                              ci == 0 and h == 0, False, r=[e.name, vc.name], w=[pso.name])
                    for h in range(4):
                        kb.mm(psI[:, h * 128:h * 128 + NB], e[:, h * 128:(h + 1) * 128], c2s[:, kc, :],
                              ci == 0 and h == 0, False, r=[e.name, c2s.name], w=[psI.name])
                finish_branch(pso, g, 0, oa, True)
                if NB > 16:
                    for h in range(4):
                        if h == 0:
                            kb.ts("dve", impm[:], psI[:, 0:NB], rden[:, 0:1], None, ALU.mult,
                                  r=[psI.name, rden.name], w=[impm.name])
                        else:
                            kb.stt("dve", impm[:], psI[:, h * 128:h * 128 + NB], rden[:, h:h + 1], impm[:], ALU.mult, ALU.add,
                                   r=[psI.name, rden.name, impm.name], w=[impm.name])
                    if 2 * b + 2 < NB:
                        kb.memset("pool", impm[:, 2 * b + 2:NB], -1.0, w=[impm.name])
                    kb.memset("pool", impm[0:64, 2 * b + 1:2 * b + 2], -1.0, w=[impm.name])
                    kb.ts("pool", impm[:, 0:1], impm[:, 0:1], 100.0, None, ALU.add, r=[impm.name], w=[impm.name])
                    kb.ts("pool", impm[:, 2 * b:2 * b + 1], impm[:, 2 * b:2 * b + 1], 100.0, None, ALU.add,
                          r=[impm.name], w=[impm.name])
                    if b > 0:
                        kb.ts("pool", impm[0:64, 2 * b - 1:2 * b], impm[0:64, 2 * b - 1:2 * b], 100.0, None, ALU.add,
                              r=[impm.name], w=[impm.name])
                    kb.ts("pool", impm[64:128, 2 * b + 1:2 * b + 2], impm[64:128, 2 * b + 1:2 * b + 2], 100.0, None, ALU.add,
                          r=[impm.name], w=[impm.name])
                    kb.add("dve", lambda e_: e_.max(m1[:], impm[:]), [impm.name], [m1.name])
                    kb.add("dve", lambda e_: e_.match_replace(wk[:], m1[:], impm[:], -1e9), [m1.name, impm.name], [wk.name])
                    kb.add("dve", lambda e_: e_.max(m2[:], wk[:]), [wk.name], [m2.name])
                    kb.ts("dve", sel[:], impm[:], m2[:, 7:8], None, ALU.is_ge, r=[impm.name, m2.name], w=[sel.name])
                    kb.ts("dve", sel[:], sel[:], -NEGM, NEGM, ALU.mult, ALU.add, r=[sel.name], w=[sel.name])
                pso = psO[nO % 2]; nO += 1
                wch = list(range(max(0, b - 4), b + 1))
                for ci, kc in enumerate(wch):
                    pS = psS[nS % 2]; e = e16[nS % 2]; nS += 1
                    diag = kc == b
                    edge = kc == b - 4
                    kb.mm(pS[:], kw[:, g, kc * 128:(kc + 1) * 128], qf16, True, not (diag or edge),
                          r=[kw.name, q16[qj].name], w=[pS.name])
                    if diag:
                        kb.mm(pS[:], idb[:], cneg[:], False, True, r=[idb.name, cneg.name], w=[pS.name])
                    if edge:
                        kb.mm(pS[:], idb[:], wneg[:], False, True, r=[idb.name, wneg.name], w=[pS.name])
                    kb.act(e[:], pS[:], AF.Exp, scale=SC, r=[pS.name], w=[e.name])
                    for h in range(4):
                        kb.mm(pso[:, h * 65:(h + 1) * 65], e[:, h * 128:(h + 1) * 128],
                              vv[:, kc, (2 + g) * 65:(3 + g) * 65], ci == 0 and h == 0, False,
                              r=[e.name, vv.name], w=[pso.name])
                finish_branch(pso, g, 2, oa, False)
                if NB > 16:
                    kb.tr(psT[0:NB, 0:128], sel[:], idf[:], r=[sel.name, idf.name], w=[psT.name])
                    kb.cp("dve", negT[0:NB, :].rearrange("p (h t) -> p h t", h=4),
                          psT[0:NB, 0:128].unsqueeze(1).to_broadcast([NB, 4, 128]), r=[psT.name], w=[negT.name])
                pso = psO[nO % 2]; nO += 1
                for kc in range(0, b + 1):
                    pS = psS[nS % 2]; e = e16[nS % 2]; nS += 1
                    diag = kc == b
                    kb.mm(pS[:], ks[:, g, kc * 128:(kc + 1) * 128], qf16, True, False, r=[ks.name, q16[qj].name], w=[pS.name])
                    kb.mm(pS[:], ex[0:NB, kc * 128:(kc + 1) * 128], negT[0:NB, :], False, not diag,
                          r=[ex.name, negT.name], w=[pS.name])
                    if diag:
                        kb.mm(pS[:], idb[:], cneg[:], False, True, r=[idb.name, cneg.name], w=[pS.name])
                    kb.act(e[:], pS[:], AF.Exp, scale=SC, r=[pS.name], w=[e.name])
                    for h in range(4):
                        kb.mm(pso[:, h * 65:(h + 1) * 65], e[:, h * 128:(h + 1) * 128], vv[:, kc, g * 65:(g + 1) * 65],
                              kc == 0 and h == 0, False, r=[e.name, vv.name], w=[pso.name])
                finish_branch(pso, g, 1, oa, False)
            kb.dma(self.o_mix[l][0][rows, :], oa[:], r=[oa.name], q="pool")
        kb.flush()


Prog.phase_nsa_attn = _phase_nsa_attn
```

```python
import numpy as np
from contextlib import ExitStack
import concourse.bass as bass
import concourse.mybir as mybir
from concourse.bass_utils import run_bass_kernel_spmd

F32 = mybir.dt.float32
BF16 = mybir.dt.bfloat16
AF = mybir.ActivationFunctionType
ALU = mybir.AluOpType
AX = mybir.AxisListType

D = 1024
MIX = 512
NSA_IN = 1304
RWKV_IN = 1792
GDN_IN = 2056
D_IN = 8224
D_FF = 4096
DP = D_IN + 32
EPS = 1e-6
OFF_NSA = 0
OFF_RWKV = NSA_IN
OFF_GDN = NSA_IN + RWKV_IN
OFF_GATE = NSA_IN + RWKV_IN + GDN_IN
NEGM = -30000.0


class _Op:
    __slots__ = ("eng", "fn", "deps", "is_dma", "need_sig", "sem", "val", "pos")


class KB:
    ENGS = ("pe", "act", "dve", "pool", "sp")

    def __init__(self, nc, stack):
        self.nc = nc
        self.stack = stack
        self.esem = {e: stack.enter_context(nc.semaphore("es_" + e)) for e in ("pe", "act", "dve", "pool")}
        self.ecnt = {e: 0 for e in self.esem}
        self.dsem = {"sp": [stack.enter_context(nc.semaphore("dsp%d" % i)) for i in range(16)],
                     "pool": [stack.enter_context(nc.semaphore("dpl%d" % i)) for i in range(8)]}
        self.dcnt = {"sp": 0, "pool": 0}
        self.dlast = {"sp": {}, "pool": {}}
        self.seen = {e: {} for e in self.ENGS}
        self.begin()

    def begin(self):
        self.ops = {e: [] for e in self.ENGS}
        self.res = {}

    def add(self, eng, fn, reads=(), writes=(), dma=False):
        op = _Op()
        op.eng = eng
        op.fn = fn
        op.is_dma = dma
        op.need_sig = dma
        op.sem = None
        op.val = 0
        op.pos = len(self.ops[eng])
        deps = set()
        rl, wl = [], []
        reads = [x.split("__u")[0] for x in reads]
        writes = [x.split("__u")[0] for x in writes]
        for r in reads:
            (wl if r.startswith("ps") else rl).append(r)
        wl.extend(writes)
        for r in rl:
            st = self.res.setdefault(r, [None, []])
            if st[0] is not None:
                deps.add(st[0])
        for w in wl:
            st = self.res.setdefault(w, [None, []])
            if st[0] is not None:
                deps.add(st[0])
            deps.update(st[1])
        for r in rl:
            self.res[r][1].append(op)
        for w in wl:
            st = self.res[w]
            st[0] = op
            st[1] = []
        deps.discard(op)
        keep = []
        latest = {}
        for d in deps:
            if d.is_dma:
                keep.append(d)
                continue
            if d.eng == eng:
                if eng == "pe":
                    continue
                if op.pos - d.pos > 2:
                    continue
            cur = latest.get(d.eng)
            if cur is None or d.pos > cur.pos:
                latest[d.eng] = d
        for d in latest.values():
            d.need_sig = True
            keep.append(d)
        if dma:
            k = self.dcnt[eng]
            self.dcnt[eng] += 1
            P = len(self.dsem[eng])
            slot = k % P
            op.sem = self.dsem[eng][slot]
            op.val = 16 * (k // P + 1)
            prev = self.dlast[eng].get(slot)
            if prev is not None:
                keep.append(prev)
            self.dlast[eng][slot] = op
        op.deps = keep
        self.ops[eng].append(op)
        return op

    def flush(self):
        nc = self.nc
        for q in ("sp", "pool"):
            outstanding = [o for o in self.dlast[q].values()]
            if outstanding:
                op = self.add(q, None)
                op.deps = outstanding
        for e in ("pe", "act", "dve", "pool"):
            for op in self.ops[e]:
                if op.need_sig and not op.is_dma:
                    self.ecnt[e] += 1
                    op.sem = self.esem[e]
                    op.val = self.ecnt[e]
        ops = self.ops
        seen = self.seen

        def emit(e, eng):
            sn = seen[e]
            for op in ops[e]:
                for d in op.deps:
                    key = id(d.sem)
                    if sn.get(key, 0) >= d.val:
                        continue
                    eng.wait_ge(d.sem, d.val)
                    sn[key] = d.val
                if op.fn is None:
                    continue
                ins = op.fn(eng)
                if op.need_sig:
                    ins.then_inc(op.sem, 16 if op.is_dma else 1)

        with nc.Block() as block:
            if ops["pe"]:
                block.tensor(lambda t: emit("pe", t))
            if ops["act"]:
                block.scalar(lambda t: emit("act", t))
            if ops["dve"]:
                block.vector(lambda t: emit("dve", t))
            if ops["pool"]:
                block.gpsimd(lambda t: emit("pool", t))
            if ops["sp"]:
                block.sync(lambda t: emit("sp", t))
        self.begin()

    def dma(self, out, in_, r=(), w=(), q="sp"):
        return self.add(q, lambda e: e.dma_start(out=out, in_=in_), r, w, dma=True)

    def mm(self, out, lhsT, rhs, start, stop, r=(), w=()):
        return self.add("pe", lambda e: e.matmul(out, lhsT=lhsT, rhs=rhs, start=start, stop=stop), r, w)

    def tr(self, out, in_, ident, r=(), w=()):
        return self.add("pe", lambda e: e.transpose(out, in_, ident), r, w)

    def act(self, out, in_, func, r=(), w=(), bias=None, scale=1.0, accum=None):
        def fn(e):
            kw = {}
            if bias is not None:
                kw["bias"] = bias
            if accum is not None:
                kw["accum_out"] = accum
            return e.activation(out, in_, func, scale=scale, **kw)
        return self.add("act", fn, r, w)

    def ts(self, eng, out, in0, s1, s2, op0, op1=None, r=(), w=()):
        def fn(e):
            if op1 is None:
                return e.tensor_scalar(out, in0, s1, None, op0)
            return e.tensor_scalar(out, in0, s1, s2, op0, op1)
        return self.add(eng, fn, r, w)

    def tt(self, eng, out, in0, in1, op, r=(), w=()):
        return self.add(eng, lambda e: e.tensor_tensor(out, in0, in1, op), r, w)

    def stt(self, eng, out, in0, scalar, in1, op0, op1, r=(), w=()):
        return self.add(eng, lambda e: e.scalar_tensor_tensor(out, in0, scalar, in1, op0, op1), r, w)

    def cp(self, eng, out, in_, r=(), w=()):
        if eng == "act":
            return self.add(eng, lambda e: e.copy(out, in_), r, w)
        return self.add(eng, lambda e: e.tensor_copy(out, in_), r, w)

    def memset(self, eng, ap, val, w=()):
        return self.add(eng, lambda e: e.memset(ap, val), (), w)


class Prog:
    def __init__(self, S, n_layers=2, debug=False):
        self.S = S
        self.NT = S // 128
        self.L = n_layers
        self.debug = debug
        self.nc = bass.Bass("TRN2", target_bir_lowering=False)
        self.stack = ExitStack()
        self.kb = KB(self.nc, self.stack)
        self.din = {}
        self.dscr = {}

    def inp(self, name, shape, dt=F32):
        t = self.nc.dram_tensor(name, list(shape), dt, kind="ExternalInput")
        self.din[name] = t
        return t.ap()

    def outp(self, name, shape, dt=F32):
        return self.nc.dram_tensor(name, list(shape), dt, kind="ExternalOutput").ap()

    def scr(self, name, shape, dt=F32):
        if self.debug:
            return self.outp(name, shape, dt)
        return self.nc.dram_tensor(name, list(shape), dt, kind="Internal").ap()

    def sb(self, st, name, shape, dt=F32):
        self.uid = getattr(self, "uid", 0) + 1
        return st.enter_context(self.nc.sbuf_tensor("%s__u%d" % (name, self.uid), list(shape), dt))

    def psb(self, st, name, dt=F32, n=512):
        self.uid = getattr(self, "uid", 0) + 1
        return st.enter_context(self.nc.psum_tensor("%s__u%d" % (name, self.uid), [128, n], dt))

    def declare(self):
        L, S = self.L, self.S
        i = self.inp
        self.x_in = i("x", [S, D])
        self.out = self.outp("out", [S, D])
        self.norm_mix_g = i("norm_mix_g", [L, D])
        self.w_in = i("w_in", [L, D, DP])
        self.ident_bf = i("ident_bf", [128, 128], BF16)
        self.ident_f = i("ident_f", [128, 128])
        self.proj_tm = [self.scr("proj_tm%d" % l, [S, OFF_GATE]) for l in range(L)]
        self.proj_g = [self.scr("proj_g%d" % l, [S, DP - OFF_GATE]) for l in range(L)]

    def load_weight_bf16(self, st, name, src_ap, kc, ncols, chunk):
        kb = self.kb
        w = self.sb(st, name, [128, kc, ncols], BF16)
        src = src_ap.rearrange("(k p) n -> p k n", p=128)
        with ExitStack() as s2:
            stg = [self.sb(s2, "%s_stg%d" % (name, j), [128, kc, chunk], F32) for j in range(2)]
            engs = ["dve", "pool", "act"]
            n = 0
            for c0 in range(0, ncols, chunk):
                cw = min(chunk, ncols - c0)
                j = n % 2
                kb.dma(stg[j][:, :, 0:cw], src[:, :, c0:c0 + cw], w=[stg[j].name])
                kb.cp(engs[n % 3], w[:, :, c0:c0 + cw], stg[j][:, :, 0:cw], r=[stg[j].name], w=[name + ":%d" % n])
                n += 1
            kb.flush()
        return w

    def phase_proj(self, l, x_src):
        kb, S, NT = self.kb, self.S, self.NT
        with ExitStack() as st:
            wsb = self.load_weight_bf16(st, "w_in_sb", self.w_in[l], 8, DP, 516)
            gb = self.sb(st, "gmix", [128, D])
            idb = self.sb(st, "idb", [128, 128], BF16)
            kb.dma(gb[:], self.norm_mix_g[l:l + 1, :].to_broadcast([128, D]), w=["gmix"])
            kb.dma(idb[:], self.ident_bf[:, :], w=["idb"])
            xt = [self.sb(st, "xt%d" % j, [128, D]) for j in range(2)]
            junk = self.sb(st, "junk", [128, D])
            ss = [self.sb(st, "ss%d" % j, [128, 1]) for j in range(2)]
            rstd = [self.sb(st, "rstd%d" % j, [128, 1]) for j in range(2)]
            hb = [self.sb(st, "hb%d" % j, [128, D], BF16) for j in range(2)]
            hT = [self.sb(st, "hT%d" % j, [128, 8, 128], BF16) for j in range(2)]
            osb = [self.sb(st, "osb%d" % j, [128, 2048]) for j in range(2)]
            pst = self.psb(st, "ps_tr", BF16, 1024)
            pso = [self.psb(st, "ps_o%d" % j) for j in range(4)]
            nblk = 0
            nog = 0
            def prep(tt_, part):
                jj = tt_ % 2
                self.rms_to_hT(xt[jj], x_src[tt_ * 128:(tt_ + 1) * 128, :], junk, ss[jj], rstd[jj], gb, hb[jj], pst, idb,
                               hT[jj], part=part)
            prep(0, None)
            for t in range(NT):
                j = t % 2
                rows = slice(t * 128, (t + 1) * 128)
                for gi, (og, oe) in enumerate(((0, 2048), (2048, 4096), (4096, OFF_GATE), (OFF_GATE, OFF_GATE + 2048),
                                               (OFF_GATE + 2048, DP))):
                    if gi == 1 and t + 1 < NT:
                        prep(t + 1, 0)
                    o = osb[nog % 2]
                    nog += 1
                    ow = oe - og
                    for c0 in range(og, oe, 512):
                        cw = min(512, oe - c0)
                        ps = pso[nblk % 4]
                        for k in range(8):
                            kb.mm(ps[:, 0:cw], hT[j][:, k, :], wsb[:, k, c0:c0 + cw], k == 0, k == 7,
                                  r=[hT[j].name, "w_in_sb"], w=[ps.name])
                        kb.cp("act" if nblk % 2 else "dve", o[:, c0 - og:c0 - og + cw], ps[:, 0:cw],
                              r=[ps.name], w=[o.name])
                        nblk += 1
                    if og < OFF_GATE:
                        kb.dma(self.proj_tm[l][rows, og:oe], o[:, 0:ow], r=[o.name], q="pool")
                    else:
                        kb.dma(self.proj_g[l][rows, og - OFF_GATE:oe - OFF_GATE], o[:, 0:ow], r=[o.name], q="pool")
                if t + 1 < NT:
                    prep(t + 1, 1)
            kb.flush()

    def rms_to_hT(self, xt, x_rows, junk, ss, rstd, gb, hb, pst, idb, hT, part=None):
        kb = self.kb
        if part in (None, 0):
            kb.dma(xt[:], x_rows, w=[xt.name])
            kb.act(junk[:], xt[:], AF.Square, r=[xt.name], w=[junk.name, ss.name], accum=ss[:])
            self.rsqrt(rstd[:], ss[:], 1.0 / D, EPS, [ss.name], [rstd.name])
            kb.stt("dve", hb[:], xt[:], rstd[:, 0:1], gb[:], ALU.mult, ALU.mult,
                   r=[xt.name, rstd.name, gb.name], w=[hb.name])
        if part == 0:
            return
        for k in range(8):
            kb.tr(pst[:, k * 128:(k + 1) * 128], hb[:, k * 128:(k + 1) * 128], idb[:],
                  r=[hb.name, idb.name], w=[pst.name])
        kb.cp("act", hT[:].rearrange("p k t -> p (k t)"), pst[:], r=[pst.name], w=[hT.name])

    def rsqrt(self, out, in_, scale, eps, r, w):
        kb = self.kb
        kb.ts("dve", out, in_, scale, eps, ALU.mult, ALU.add, r=r, w=w)
        kb.add("act", lambda e: e.sqrt(out, out), w, w)
        kb.add("dve", lambda e: e.reciprocal(out, out), w, w)

    def declare_rest(self):
        L, S = self.L, self.S
        i = self.inp
        self.w_branch = i("w_branch", [L, 3, MIX, D])
        self.w_out = i("w_out", [L, D, D])
        self.norm_ffn_g = i("norm_ffn_g", [L, D])
        self.w_ff1 = i("w_ff1", [L, D, D_FF])
        self.w_ff2 = i("w_ff2", [L, D_FF, D])
        self.o_mix = [[self.scr("o_mix%d_%d" % (l, b), [S, MIX]) for b in range(3)] for l in range(L)]
        self.x_mid = [self.scr("x_mid%d" % l, [S, D]) for l in range(L)]
        self.x_lay = [self.scr("x_lay%d" % l, [S, D]) for l in range(L - 1)] + [self.out]

    def transpose_bf(self, src_bf, nk, pst, idb, dstT, eng="act"):
        kb = self.kb
        for k in range(nk):
            kb.tr(pst[:, k * 128:(k + 1) * 128], src_bf[:, k * 128:(k + 1) * 128], idb[:],
                  r=[src_bf.name, idb.name], w=[pst.name])
        kb.cp(eng, dstT.rearrange("p k t -> p (k t)"), pst[:, 0:nk * 128], r=[pst.name], w=[dstT.name])

    def phase_merge(self, l, x_src):
        kb, S, NT = self.kb, self.S, self.NT
        with ExitStack() as st:
            wb = [self.load_weight_bf16(st, "wbr%d" % b, self.w_branch[l, b], 4, D, 1024) for b in range(3)]
            wo = self.load_weight_bf16(st, "wout_sb", self.w_out[l], 8, D, 512)
            idb = self.sb(st, "idb", [128, 128], BF16)
            kb.dma(idb[:], self.ident_bf[:, :], w=["idb"])
            ot = [self.sb(st, "ot%d" % j, [128, MIX]) for j in range(2)]
            ob = [self.sb(st, "ob%d" % j, [128, MIX], BF16) for j in range(2)]
            oT = [self.sb(st, "oT%d" % j, [128, 4, 128], BF16) for j in range(2)]
            gt = [self.sb(st, "gt%d" % j, [128, D]) for j in range(2)]
            tmp = self.sb(st, "tmpm", [128, D])
            mrg = self.sb(st, "mrg", [128, D])
            mb = self.sb(st, "mrgb", [128, D], BF16)
            mT = self.sb(st, "mT", [128, 8, 128], BF16)
            xt = [self.sb(st, "xt%d" % j, [128, D]) for j in range(2)]
            xo = [self.sb(st, "xo%d" % j, [128, D]) for j in range(2)]
            pst = self.psb(st, "ps_tr", BF16, 1024)
            pso = [self.psb(st, "ps_o%d" % j) for j in range(4)]
            n = 0
            nb = 0
            for t in range(NT):
                rows = slice(t * 128, (t + 1) * 128)
                for b in range(3):
                    j = n % 2
                    n += 1
                    kb.dma(ot[j][:], self.o_mix[l][b][rows, :], w=[ot[j].name])
                    kb.dma(gt[j][:], self.proj_g[l][rows, b * D:(b + 1) * D], w=[gt[j].name])
                    kb.cp("pool", ob[j][:], ot[j][:], r=[ot[j].name], w=[ob[j].name])
                    kb.act(gt[j][:], gt[j][:], AF.Sigmoid, r=[gt[j].name], w=[gt[j].name])
                    self.transpose_bf(ob[j], 4, pst, idb, oT[j][:])
                    for c in range(2):
                        ps = pso[nb % 4]
                        nb += 1
                        for k in range(4):
                            kb.mm(ps[:], oT[j][:, k, :], wb[b][:, k, c * 512:(c + 1) * 512], k == 0, k == 3,
                                  r=[oT[j].name], w=[ps.name])
                        cs = slice(c * 512, (c + 1) * 512)
                        if b == 0:
                            kb.tt("dve", mrg[:, cs], ps[:], gt[j][:, cs], ALU.mult, r=[ps.name, gt[j].name], w=["mrg"])
                        else:
                            kb.tt("dve", tmp[:, cs], ps[:], gt[j][:, cs], ALU.mult, r=[ps.name, gt[j].name], w=["tmpm"])
                            kb.tt("pool", mrg[:, cs], mrg[:, cs], tmp[:, cs], ALU.add, r=["tmpm", "mrg"], w=["mrg"])
                kb.cp("act", mb[:], mrg[:], r=["mrg"], w=["mrgb"])
                self.transpose_bf(mb, 8, pst, idb, mT[:])
                j = t % 2
                kb.dma(xt[j][:], x_src[rows, :], w=[xt[j].name])
                for c in range(2):
                    ps = pso[nb % 4]
                    nb += 1
                    for k in range(8):
                        kb.mm(ps[:], mT[:, k, :], wo[:, k, c * 512:(c + 1) * 512], k == 0, k == 7, r=["mT"], w=[ps.name])
                    cs = slice(c * 512, (c + 1) * 512)
                    kb.tt("dve", xo[j][:, cs], ps[:], xt[j][:, cs], ALU.add, r=[ps.name, xt[j].name], w=[xo[j].name])
                kb.dma(self.x_mid[l][rows, :], xo[j][:], r=[xo[j].name], q="pool")
            kb.flush()

    def phase_ffn(self, l):
        kb, S, NT = self.kb, self.S, self.NT
        with ExitStack() as st:
            w1 = self.load_weight_bf16(st, "w1_sb", self.w_ff1[l], 8, D_FF, 512)
            w2 = self.load_weight_bf16(st, "w2_sb", self.w_ff2[l], 32, D, 128)
            gb = self.sb(st, "gffn", [128, D])
            idb = self.sb(st, "idb", [128, 128], BF16)
            kb.dma(gb[:], self.norm_ffn_g[l:l + 1, :].to_broadcast([128, D]), w=["gffn"])
            kb.dma(idb[:], self.ident_bf[:, :], w=["idb"])
            xt = [self.sb(st, "xt%d" % j, [128, D]) for j in range(2)]
            junk = self.sb(st, "junk", [128, D])
            ss = [self.sb(st, "ss%d" % j, [128, 1]) for j in range(2)]
            rstd = [self.sb(st, "rstd%d" % j, [128, 1]) for j in range(2)]
            hb = [self.sb(st, "hb%d" % j, [128, D], BF16) for j in range(2)]
            hT = [self.sb(st, "hT%d" % j, [128, 8, 128], BF16) for j in range(2)]
            fr = [self.sb(st, "fr%d" % j, [128, 512]) for j in range(2)]
            fb = self.sb(st, "fb", [128, D_FF], BF16)
            fT = self.sb(st, "fT", [128, 32, 128], BF16)
            xo = [self.sb(st, "xo%d" % j, [128, D]) for j in range(2)]
            pst = self.psb(st, "ps_tr", BF16, 1024)
            pso = [self.psb(st, "ps_o%d" % j) for j in range(4)]
            nb = 0
            def prep(tt_, part):
                jj = tt_ % 2
                self.rms_to_hT(xt[jj], self.x_mid[l][tt_ * 128:(tt_ + 1) * 128, :], junk, ss[jj], rstd[jj], gb, hb[jj], pst,
                               idb, hT[jj], part=part)
            prep(0, None)
            for t in range(NT):
                j = t % 2
                rows = slice(t * 128, (t + 1) * 128)
                for c in range(8):
                    ps = pso[nb % 4]
                    nb += 1
                    for k in range(8):
                        kb.mm(ps[:], hT[j][:, k, :], w1[:, k, c * 512:(c + 1) * 512], k == 0, k == 7,
                              r=[hT[j].name], w=[ps.name])
                    kb.act(fr[c % 2][:], ps[:], AF.Relu, r=[ps.name], w=[fr[c % 2].name])
                    kb.tt("dve", fb[:, c * 512:(c + 1) * 512], fr[c % 2][:], fr[c % 2][:], ALU.mult,
                          r=[fr[c % 2].name], w=["fb:%d" % c])
                    if c >= 2 and c % 2 == 0:
                        self.transpose_bf_slice(fb, (c - 2) * 4, 8, pst, idb, fT, rd=["fb:%d" % (c - 2), "fb:%d" % (c - 1)])
                    if c == 3 and t + 1 < NT:
                        prep(t + 1, 0)
                self.transpose_bf_slice(fb, 24, 8, pst, idb, fT, rd=["fb:6", "fb:7"])
                for c in range(2):
                    ps = pso[nb % 4]
                    nb += 1
                    for k in range(32):
                        kb.mm(ps[:], fT[:, k, :], w2[:, k, c * 512:(c + 1) * 512], k == 0, k == 31, r=["fT"], w=[ps.name])
                    cs = slice(c * 512, (c + 1) * 512)
                    kb.tt("dve", xo[j][:, cs], ps[:], xt[j][:, cs], ALU.add, r=[ps.name, xt[j].name], w=[xo[j].name])
                kb.dma(self.x_lay[l][rows, :], xo[j][:], r=[xo[j].name], q="pool")
                if t + 1 < NT:
                    prep(t + 1, 1)
            kb.flush()

    def transpose_bf_slice(self, src_bf, k0, nk, pst, idb, dstT, rd=None):
        kb = self.kb
        for k in range(nk):
            kb.tr(pst[:, k * 128:(k + 1) * 128], src_bf[:, (k0 + k) * 128:(k0 + k + 1) * 128], idb[:],
                  r=(rd if rd is not None else [src_bf.name]) + [idb.name], w=[pst.name])
        kb.cp("act", dstT[:, k0:k0 + nk, :].rearrange("p k t -> p (k t)"), pst[:, 0:nk * 128],
              r=[pst.name], w=[dstT.name])

    def declare_gdn(self):
        L = self.L
        i = self.inp
        self.gdn_conv_w = i("gdn_conv_w", [L, 4, 3 * MIX])
        self.gdn_a_log = i("gdn_a_log", [L, 4])
        self.gdn_dt_bias = i("gdn_dt_bias", [L, 4])
        self.gdn_norm_w = i("gdn_norm_w", [L, 128])
        self.ones_f = i("ones_f", [128, 128])
        self.triu_incl = i("triu_incl", [128, 128])
        self.triu_strict = i("triu_strict", [128, 128])

    def next_ps(self):
        self.psn = getattr(self, "psn", -1) + 1
        return self.pss[self.psn % len(self.pss)]

    def tr_f32(self, dst, src, idf, eng="act"):
        kb = self.kb
        n = src.shape[-1] // 128
        ps = self.next_ps()
        for k in range(n):
            kb.tr(ps[:, k * 128:(k + 1) * 128], src[:, k * 128:(k + 1) * 128], idf[:],
                  r=[src.name, idf.name], w=[ps.name])
        kb.cp(eng, dst, ps[:, 0:n * 128], r=[ps.name], w=[dst.name])

    def phase_gdn(self, l, side=False):
        kb, S, NT = self.kb, self.S, self.NT
        H, N = 4, 128
        P = self.proj_tm[l]
        c0 = OFF_GDN
        with ExitStack() as st:
            sb = lambda name, shape, dt=F32: self.sb(st, name, shape, dt)
            self.pss = [self.psb(st, "ps_g%d" % j) for j in range(6 if side else 8)]
            gen = None
            if side:
                gen = self.gen_nsa_prep(l, st, self.psb(st, "ps_npA"), self.psb(st, "ps_npB"))
            hook = (lambda: next(gen, None)) if gen is not None else (lambda: None)
            idf = sb("idf", [128, 128]); ones = sb("ones", [128, 128])
            tin = sb("tin", [128, 128]); tst = sb("tst", [128, 128])
            kb.dma(idf[:], self.ident_f[:, :], w=[idf.name])
            kb.dma(ones[:], self.ones_f[:, :], w=[ones.name])
            kb.dma(tin[:], self.triu_incl[:, :], w=[tin.name])
            kb.dma(tst[:], self.triu_strict[:, :], w=[tst.name])
            cw = sb("cw", [128, 4, 1536])
            kb.dma(cw[:].rearrange("p k c -> p (k c)"),
                   self.gdn_conv_w[l:l + 1].rearrange("o k c -> o (k c)").to_broadcast([128, 4 * 1536]), w=[cw.name])
            alog = sb("alog", [128, 4]); dtb = sb("dtb", [128, 4]); nw = sb("nw", [128, 128])
            kb.dma(alog[:], self.gdn_a_log[l:l + 1, :].to_broadcast([128, 4]), w=[alog.name])
            kb.dma(dtb[:], self.gdn_dt_bias[l:l + 1, :].to_broadcast([128, 4]), w=[dtb.name])
            kb.dma(nw[:], self.gdn_norm_w[l:l + 1, :].to_broadcast([128, 128]), w=[nw.name])
            nea = sb("nea", [128, 4])
            kb.act(nea[:], alog[:], AF.Exp, r=[alog.name], w=[nea.name])
            kb.ts("dve", nea[:], nea[:], -1.0, None, ALU.mult, r=[nea.name], w=[nea.name])
            X = [sb("X%d" % k, [128, 1536]) for k in range(4)]
            zt = sb("zt", [128, 512]); ab = sb("ab", [128, 8])
            y = sb("y", [128, 1536]); t1 = sb("t1", [128, 1536])
            ssq = sb("ssq", [128, 8]); rs = sb("rs", [128, 8])
            qkn = sb("qkn", [128, 1024])
            beta = sb("beta", [128, 4]); g = sb("g", [128, 4]); G = sb("G", [128, 4]); Gt = sb("Gt", [128, 4])
            eG = sb("eG", [128, 4]); neG = sb("neG", [128, 4]); eGC = sb("eGC", [128, 4]); eGr = sb("eGr", [128, 4])
            kbm = sb("kbm", [128, 512]); kam = sb("kam", [128, 512]); qgm = sb("qgm", [128, 512])
            khat = sb("khat", [128, 512]); vp = sb("vp", [128, 512])
            kT = sb("kT", [128, 512]); kbT = sb("kbT", [128, 512]); qT = sb("qT", [128, 512])
            aT = sb("aT", [128, 512]); rT = sb("rT", [128, 512])
            dg = sb("dg", [128, 512]); gam = sb("gam", [128, 512])
            Nm = [sb("Nm%d" % k, [128, 512]) for k in range(2)]
            Lm = [sb("Lm%d" % k, [128, 512]) for k in range(2)]
            Pm = [sb("Pm%d" % k, [128, 512]) for k in range(2)]
            MT = sb("MT", [128, 512]); N0k = sb("N0k", [128, 512])
            X1 = sb("X1", [128, 512]); UV = sb("UV", [128, 512]); Y = sb("Y", [128, 512])
            T0 = sb("T0", [128, 512])
            o = sb("o", [128, 512]); sz = sb("sz", [128, 512]); ysq = sb("ysq", [128, 512])
            yss = sb("yss", [128, 4]); yrs = sb("yrs", [128, 4])
            kb.memset("dve", T0[:], 0.0, w=[T0.name])
            hv = lambda a: a.rearrange("p (h n) -> p h n", h=H)
            for t in range(NT):
                r0 = t * 128
                for k in range(4):
                    sh = 3 - k
                    if r0 - sh < 0:
                        kb.memset("pool", X[k][:], 0.0, w=[X[k].name])
                        kb.dma(X[k][sh:128, :], P[0:128 - sh, c0:c0 + 1536], w=[X[k].name])
                    else:
                        kb.dma(X[k][:], P[r0 - sh:r0 - sh + 128, c0:c0 + 1536], w=[X[k].name])
                kb.dma(zt[:], P[r0:r0 + 128, c0 + 1536:c0 + 2048], w=[zt.name])
                kb.dma(ab[:], P[r0:r0 + 128, c0 + 2048:c0 + 2056], w=[ab.name])
                kb.tt("dve", y[:], X[0][:], cw[:, 0, :], ALU.mult, r=[X[0].name, cw.name], w=[y.name])
                for k in range(1, 4):
                    kb.tt("pool", t1[:], X[k][:], cw[:, k, :], ALU.mult, r=[X[k].name, cw.name], w=[t1.name])
                    kb.tt("dve", y[:], y[:], t1[:], ALU.add, r=[y.name, t1.name], w=[y.name])
                kb.act(y[:], y[:], AF.Silu, r=[y.name], w=[y.name])
                kb.act(t1[:, 0:1024], y[:, 0:1024], AF.Square, r=[y.name], w=[t1.name])
                kb.add("dve", lambda e: e.tensor_reduce(ssq[:], t1[:, 0:1024].rearrange("p (h n) -> p h n", h=8), AX.X, ALU.add),
                       [t1.name], [ssq.name])
                self.rsqrt(rs[:], ssq[:], 1.0, EPS, [ssq.name], [rs.name])
                kb.ts("dve", rs[:, 0:4], rs[:, 0:4], float(N) ** -0.5, None, ALU.mult, r=[rs.name], w=[rs.name])
                kb.tt("dve", qkn[:].rearrange("p (h n) -> p h n", h=8), y[:, 0:1024].rearrange("p (h n) -> p h n", h=8),
                      rs[:].unsqueeze(2).to_broadcast([128, 8, 128]), ALU.mult, r=[y.name, rs.name], w=[qkn.name])
                qn = qkn[:, 0:512]; kn = qkn[:, 512:1024]; v = y[:, 1024:1536]
                kb.act(beta[:], ab[:, 4:8], AF.Sigmoid, r=[ab.name], w=[beta.name])
                kb.tt("dve", g[:], ab[:, 0:4], dtb[:], ALU.add, r=[ab.name, dtb.name], w=[g.name])
                kb.act(g[:], g[:], AF.Exp, r=[g.name], w=[g.name])
                kb.act(g[:], g[:], AF.Ln, r=[g.name], w=[g.name], bias=1.0)
                kb.tt("dve", g[:], g[:], nea[:], ALU.mult, r=[g.name, nea.name], w=[g.name])
                ps = self.next_ps()
                kb.mm(ps[:, 0:4], tin[:], g[:], True, True, r=[tin.name, g.name], w=[ps.name])
                kb.mm(ps[:, 4:8], ones[:], g[:], True, True, r=[ones.name, g.name], w=[ps.name])
                kb.cp("dve", G[:], ps[:, 0:4], r=[ps.name], w=[G.name])
                kb.cp("dve", Gt[:], ps[:, 4:8], r=[ps.name], w=[Gt.name])
                kb.act(eG[:], G[:], AF.Exp, r=[G.name], w=[eG.name])
                kb.act(eGC[:], Gt[:], AF.Exp, r=[Gt.name], w=[eGC.name])
                kb.tt("dve", eGr[:], Gt[:], G[:], ALU.subtract, r=[Gt.name, G.name], w=[eGr.name])
                kb.act(eGr[:], eGr[:], AF.Exp, r=[eGr.name], w=[eGr.name])
                bb = lambda a: a[:].unsqueeze(2).to_broadcast([128, 4, 128])
                kb.tt("dve", hv(kbm[:]), hv(kn), bb(beta), ALU.mult, r=[qkn.name, beta.name], w=[kbm.name])
                kb.tt("pool", hv(vp[:]), hv(v), bb(beta), ALU.mult, r=[y.name, beta.name], w=[vp.name])
                kb.tt("dve", hv(kam[:]), hv(kbm[:]), bb(eG), ALU.mult, r=[kbm.name, eG.name], w=[kam.name])
                kb.ts("pool", kam[:], kam[:], -1.0, None, ALU.mult, r=[kam.name], w=[kam.name])
                kb.tt("dve", hv(qgm[:]), hv(qn), bb(eG), ALU.mult, r=[qkn.name, eG.name], w=[qgm.name])
                kb.tt("pool", hv(khat[:]), hv(kn), bb(eGr), ALU.mult, r=[qkn.name, eGr.name], w=[khat.name])
                hook()
                self.tr_f32(kT[:], kn, idf); self.tr_f32(kbT[:], kbm[:], idf, "dve")
                self.tr_f32(qT[:], qn, idf); self.tr_f32(aT[:], kam[:], idf, "dve"); self.tr_f32(rT[:], qgm[:], idf)
                for h in range(H):
                    kb.ts("dve", dg[:, h * 128:(h + 1) * 128], idf[:], G[:, h:h + 1], None, ALU.mult,
                          r=[idf.name, G.name], w=[dg.name])
                ps = self.next_ps()
                for h in range(H):
                    kb.mm(ps[:, h * 128:(h + 1) * 128], ones[:], dg[:, h * 128:(h + 1) * 128], True, True,
                          r=[ones.name, dg.name], w=[ps.name])
                for h in range(H):
                    kb.ts("dve", gam[:, h * 128:(h + 1) * 128], ps[:, h * 128:(h + 1) * 128], G[:, h:h + 1], 0.0,
                          ALU.subtract, ALU.min, r=[ps.name, G.name], w=[gam.name])
                kb.act(gam[:], gam[:], AF.Exp, r=[gam.name], w=[gam.name])
                ps = self.next_ps()
                ps2 = self.next_ps()
                for h in range(H):
                    hs = slice(h * 128, (h + 1) * 128)
                    kb.mm(ps[:, hs], kT[:, hs], kbT[:, hs], True, True, r=[kT.name, kbT.name], w=[ps.name])
                    kb.mm(ps2[:, hs], kT[:, hs], qT[:, hs], True, True, r=[kT.name, qT.name], w=[ps2.name])
                N0 = Nm[0]
                kb.tt("dve", N0[:], ps[:], gam[:], ALU.mult, r=[ps.name, gam.name], w=[N0.name])
                kb.stt("dve", hv(N0[:]), hv(N0[:]), -1.0, tst[:].unsqueeze(1).to_broadcast([128, 4, 128]), ALU.mult, ALU.mult,
                       r=[N0.name, tst.name], w=[N0.name])
                kb.tt("dve", MT[:], ps2[:], gam[:], ALU.mult, r=[ps2.name, gam.name], w=[MT.name])
                kb.tt("pool", hv(MT[:]), hv(MT[:]), tin[:].unsqueeze(1).to_broadcast([128, 4, 128]), ALU.mult,
                      r=[MT.name, tin.name], w=[MT.name])
                kb.cp("pool", N0k[:], N0[:], r=[N0.name], w=[N0k.name])
                L0 = Lm[0]
                self.tr_f32(L0[:], N0[:], idf)
                P0 = Pm[0]
                kb.tt("dve", hv(P0[:]), hv(N0[:]), idf[:].unsqueeze(1).to_broadcast([128, 4, 128]), ALU.add,
                      r=[N0.name, idf.name], w=[P0.name])
                cur = 0
                for lev in range(6):
                    Nc, Lc, Pc = Nm[cur], Lm[cur], Pm[cur]
                    Nn, Ln, Pn = Nm[1 - cur], Lm[1 - cur], Pm[1 - cur]
                    psl = self.next_ps(); psn = self.next_ps()
                    for h in range(H):
                        hs = slice(h * 128, (h + 1) * 128)
                        kb.mm(psl[:, hs], Nc[:, hs], Lc[:, hs], True, True, r=[Nc.name, Lc.name], w=[psl.name])
                    if lev < 5:
                        for h in range(H):
                            hs = slice(h * 128, (h + 1) * 128)
                            kb.mm(psn[:, hs], Lc[:, hs], Nc[:, hs], True, True, r=[Nc.name, Lc.name], w=[psn.name])
                    kb.cp("act", Ln[:], psl[:], r=[psl.name], w=[Ln.name])
                    if lev < 5:
                        kb.cp("dve", Nn[:], psn[:], r=[psn.name], w=[Nn.name])
                    psp = self.next_ps()
                    for h in range(H):
                        hs = slice(h * 128, (h + 1) * 128)
                        kb.mm(psp[:, hs], Ln[:, hs], Pc[:, hs], True, True, r=[Ln.name, Pc.name], w=[psp.name])
                    kb.tt("dve", Pn[:], psp[:], Pc[:], ALU.add, r=[psp.name, Pc.name], w=[Pn.name])
                    cur = 1 - cur
                    hook()
                WT = Pm[cur]
                N0 = Nm[0]
                ps = self.next_ps()
                for h in range(H):
                    hs = slice(h * 128, (h + 1) * 128)
                    kb.mm(ps[:, hs], aT[:, hs], T0[:, hs], True, False, r=[aT.name, T0.name], w=[ps.name])
                    kb.mm(ps[:, hs], N0k[:, hs], vp[:, hs], False, True, r=[N0k.name, vp.name], w=[ps.name])
                kb.cp("act", X1[:], ps[:], r=[ps.name], w=[X1.name])
                ps = self.next_ps()
                for h in range(H):
                    hs = slice(h * 128, (h + 1) * 128)
                    kb.mm(ps[:, hs], WT[:, hs], X1[:, hs], True, True, r=[WT.name, X1.name], w=[ps.name])
                kb.tt("dve", UV[:], ps[:], vp[:], ALU.add, r=[ps.name, vp.name], w=[UV.name])
                ps = self.next_ps()
                ps2 = self.next_ps()
                for h in range(H):
                    hs = slice(h * 128, (h + 1) * 128)
                    kb.mm(ps[:, hs], rT[:, hs], T0[:, hs], True, False, r=[rT.name, T0.name], w=[ps.name])
                    kb.mm(ps[:, hs], MT[:, hs], UV[:, hs], False, True, r=[MT.name, UV.name], w=[ps.name])
                    kb.mm(ps2[:, hs], khat[:, hs], UV[:, hs], True, True, r=[khat.name, UV.name], w=[ps2.name])
                kb.cp("act", Y[:], ps[:], r=[ps.name], w=[Y.name])
                for h in range(H):
                    hs = slice(h * 128, (h + 1) * 128)
                    kb.stt("dve", T0[:, hs], T0[:, hs], eGC[:, h:h + 1], ps2[:, hs], ALU.mult, ALU.add,
                           r=[T0.name, eGC.name, ps2.name], w=[T0.name])
                kb.act(ysq[:], Y[:], AF.Square, r=[Y.name], w=[ysq.name])
                kb.add("dve", lambda e: e.tensor_reduce(yss[:], ysq[:].rearrange("p (h n) -> p h n", h=4), AX.X, ALU.add),
                       [ysq.name], [yss.name])
                self.rsqrt(yrs[:], yss[:], 1.0 / N, EPS, [yss.name], [yrs.name])
                kb.act(sz[:], zt[:], AF.Silu, r=[zt.name], w=[sz.name])
                kb.tt("dve", hv(o[:]), hv(Y[:]), bb(yrs), ALU.mult, r=[Y.name, yrs.name], w=[o.name])
                kb.tt("pool", hv(o[:]), hv(o[:]), nw[:].unsqueeze(1).to_broadcast([128, 4, 128]), ALU.mult,
                      r=[o.name, nw.name], w=[o.name])
                kb.tt("dve", o[:], o[:], sz[:], ALU.mult, r=[o.name, sz.name], w=[o.name])
                kb.dma(self.o_mix[l][2][r0:r0 + 128, :], o[:], r=[o.name], q="pool")
                hook(); hook()
            if gen is not None:
                for _ in gen:
                    pass
            kb.flush()

    def declare_rwkv(self):
        L, S = self.L, self.S
        i = self.inp
        for nm, shp in (("rwkv_mu", [L, RWKV_IN]), ("rwkv_w0", [L, MIX]), ("rwkv_w_up", [L, 64, MIX]),
                        ("rwkv_a0", [L, MIX]), ("rwkv_a_up", [L, 64, MIX]), ("rwkv_g_up", [L, 128, MIX]),
                        ("rwkv_k_k", [L, MIX]), ("rwkv_k_a", [L, MIX]), ("rwkv_r_k", [L, MIX]),
                        ("rwkv_ln_w", [L, MIX]), ("rwkv_ln_b", [L, MIX]), ("rwkv_v0", [1, MIX]),
                        ("rwkv_vres_up", [1, 32, MIX])):
            setattr(self, nm, i(nm, shp))
        self.vfirst = self.scr("vfirst", [S, MIX])

    def phase_rwkv(self, l):
        kb, S, NT = self.kb, self.S, self.NT
        H, N = 8, 64
        P = self.proj_tm[l]
        c0 = OFF_RWKV
        with ExitStack() as st:
            sb = lambda name, shape, dt=F32: self.sb(st, name, shape, dt)
            self.pss = [self.psb(st, "ps_r%d" % j) for j in range(8)]
            idf = sb("idf", [128, 128]); ones = sb("ones", [128, 128])
            tin = sb("tin", [128, 128]); tst = sb("tst", [128, 128])
            kb.dma(idf[:], self.ident_f[:, :], w=[idf.name])
            kb.dma(ones[:], self.ones_f[:, :], w=[ones.name])
            kb.dma(tin[:], self.triu_incl[:, :], w=[tin.name])
            kb.dma(tst[:], self.triu_strict[:, :], w=[tst.name])

            def bvec(name, src, n):
                tl = sb(name, [128, n])
                kb.dma(tl[:], src[l:l + 1, :].to_broadcast([128, n]), w=[tl.name])
                return tl
            mu = bvec("mu", self.rwkv_mu, RWKV_IN)
            w0 = bvec("w0", self.rwkv_w0, MIX); a0 = bvec("a0", self.rwkv_a0, MIX)
            k_k = bvec("k_k", self.rwkv_k_k, MIX); k_a = bvec("k_a", self.rwkv_k_a, MIX)
            r_k = bvec("r_k", self.rwkv_r_k, MIX); ln_w = bvec("ln_w", self.rwkv_ln_w, MIX)
            ln_b = bvec("ln_b", self.rwkv_ln_b, MIX)
            wup = sb("wup", [64, MIX]); aup = sb("aup", [64, MIX]); gup = sb("gup", [128, MIX])
            kb.dma(wup[:], self.rwkv_w_up[l], w=[wup.name])
            kb.dma(aup[:], self.rwkv_a_up[l], w=[aup.name])
            kb.dma(gup[:], self.rwkv_g_up[l], w=[gup.name])
            if l > 0:
                v0 = sb("v0", [128, MIX]); vup = sb("vup", [32, MIX])
                kb.dma(v0[:], self.rwkv_v0[0:1, :].to_broadcast([128, MIX]), w=[v0.name])
                kb.dma(vup[:], self.rwkv_vres_up[0], w=[vup.name])
                vf = sb("vf", [128, MIX]); xvd = sb("xvd", [128, 32]); xvdT = sb("xvdT", [32, 128])
            X0 = sb("X0", [128, RWKV_IN]); X1s = sb("X1s", [128, RWKV_IN]); c = sb("c", [128, RWKV_IN])
            sm = sb("sm", [128, 256]); smT = sb("smT", [128, 256])
            wdT = sb("wdT", [64, 128]); adT = sb("adT", [64, 128])
            ld = sb("ld", [128, MIX]); a = sb("a", [128, MIX]); g = sb("g", [128, MIX])
            kk = sb("kk", [128, MIX]); t1 = sb("t1", [128, MIX]); t2 = sb("t2", [128, MIX])
            ssq = sb("ssq", [128, 8]); rs = sb("rs", [128, 8])
            kp = sb("kp", [128, MIX]); bet = sb("bet", [128, MIX])
            G = sb("G", [128, MIX]); eG = sb("eG", [128, MIX]); enG = sb("enG", [128, MIX])
            eGr = sb("eGr", [128, MIX]); eGm = sb("eGm", [128, MIX])
            at = sb("at", [128, MIX]); bt = sb("bt", [128, MIX]); kt = sb("kt", [128, MIX]); rt = sb("rt", [128, MIX])
            bh = sb("bh", [128, MIX]); kh = sb("kh", [128, MIX])
            aT = sb("aT", [64, 1024]); bT = sb("bT", [64, 1024]); kT = sb("kT", [64, 1024]); rT = sb("rT", [64, 1024])
            PCT = sb("PCT", [64, 8])
            Nm = [sb("Nm%d" % k, [128, 1024]) for k in range(2)]
            Lm = [sb("Lm%d" % k, [128, 1024]) for k in range(2)]
            Pm = [sb("Pm%d" % k, [128, 1024]) for k in range(2)]
            LakT = sb("LakT", [128, 1024]); MrbT = sb("MrbT", [128, 1024]); MrkT = sb("MrkT", [128, 1024])
            X1 = sb("X1", [128, MIX]); U = sb("U", [128, MIX]); Y = sb("Y", [128, MIX])
            T0 = sb("T0", [64, MIX])
            mean = sb("mean", [128, 8]); var = sb("var", [128, 8]); rkv = sb("rkv", [128, 8])
            kb.memset("dve", T0[:], 0.0, w=[T0.name])
            hv = lambda x: x.rearrange("p (h n) -> p h n", h=H)
            b8 = lambda x: x[:].unsqueeze(2).to_broadcast([128, 8, 64])
            msk = lambda m: m[:].unsqueeze(1).to_broadcast([128, 8, 128])
            hm = lambda x: x.rearrange("p (h n) -> p h n", h=H)

            def tr64(dst, src):
                for half in range(2):
                    ps = self.next_ps()
                    for hh in range(4):
                        h = half * 4 + hh
                        kb.tr(ps[0:64, hh * 128:(hh + 1) * 128], src[:, h * 64:(h + 1) * 64], idf[:],
                              r=[src.name, idf.name], w=[ps.name])
                    kb.cp("act" if half else "dve", dst[:, half * 512:(half + 1) * 512], ps[0:64, :],
                          r=[ps.name], w=[dst.name])

            def mat8(dst, lT, rT_, mask):
                for half in range(2):
                    ps = self.next_ps()
                    for hh in range(4):
                        h = half * 4 + hh
                        kb.mm(ps[:, hh * 128:(hh + 1) * 128], lT[:, h * 128:(h + 1) * 128], rT_[:, h * 128:(h + 1) * 128],
                              True, True, r=[lT.name, rT_.name], w=[ps.name])
                    kb.tt("dve", dst[:, half * 512:(half + 1) * 512].rearrange("p (h n) -> p h n", h=4),
                          ps[:].rearrange("p (h n) -> p h n", h=4),
                          mask[:].unsqueeze(1).to_broadcast([128, 4, 128]), ALU.mult,
                          r=[ps.name, mask.name], w=[dst.name])

            for t in range(NT):
                r0 = t * 128
                kb.dma(X0[:], P[r0:r0 + 128, c0:c0 + RWKV_IN], w=[X0.name])
                if t == 0:
                    kb.memset("pool", X1s[:], 0.0, w=[X1s.name])
                    kb.dma(X1s[1:128, :], P[0:127, c0:c0 + RWKV_IN], w=[X1s.name])
                else:
                    kb.dma(X1s[:], P[r0 - 1:r0 + 127, c0:c0 + RWKV_IN], w=[X1s.name])
                kb.tt("dve", c[:], X1s[:], X0[:], ALU.subtract, r=[X0.name, X1s.name], w=[c.name])
                kb.tt("pool", c[:], c[:], mu[:], ALU.mult, r=[c.name, mu.name], w=[c.name])
                kb.tt("dve", c[:], c[:], X0[:], ALU.add, r=[c.name, X0.name], w=[c.name])
                r_ = c[:, 0:512]; k_ = c[:, 512:1024]; v_ = c[:, 1024:1536]
                kb.act(sm[:, 0:64], c[:, 1536:1600], AF.Tanh, r=[c.name], w=[sm.name])
                kb.cp("dve", sm[:, 64:128], c[:, 1600:1664], r=[c.name], w=[sm.name])
                kb.act(sm[:, 128:256], c[:, 1664:1792], AF.Sigmoid, r=[c.name], w=[sm.name])
                ps = self.next_ps()
                kb.tr(ps[0:64, 0:128], sm[:, 0:64], idf[:], r=[sm.name, idf.name], w=[ps.name])
                kb.tr(ps[0:64, 128:256], sm[:, 64:128], idf[:], r=[sm.name, idf.name], w=[ps.name])
                kb.tr(ps[:, 256:384], sm[:, 128:256], idf[:], r=[sm.name, idf.name], w=[ps.name])
                kb.cp("act", wdT[:], ps[0:64, 0:128], r=[ps.name], w=[wdT.name])
                kb.cp("dve", adT[:], ps[0:64, 128:256], r=[ps.name], w=[adT.name])
                kb.cp("act", smT[:, 0:128], ps[:, 256:384], r=[ps.name], w=[smT.name])
                psw = self.next_ps(); psa = self.next_ps(); psg = self.next_ps()
                kb.mm(psw[:], wdT[:], wup[:], True, True, r=[wdT.name, wup.name], w=[psw.name])
                kb.mm(psa[:], adT[:], aup[:], True, True, r=[adT.name, aup.name], w=[psa.name])
                kb.mm(psg[:], smT[:, 0:128], gup[:], True, True, r=[smT.name, gup.name], w=[psg.name])
                kb.tt("dve", ld[:], psw[:], w0[:], ALU.add, r=[psw.name, w0.name], w=[ld.name])
                kb.act(ld[:], ld[:], AF.Sigmoid, r=[ld.name], w=[ld.name])
                kb.ts("dve", ld[:], ld[:], -float(np.exp(-0.5)), None, ALU.mult, r=[ld.name], w=[ld.name])
                kb.tt("dve", a[:], psa[:], a0[:], ALU.add, r=[psa.name, a0.name], w=[a.name])
                kb.act(a[:], a[:], AF.Sigmoid, r=[a.name], w=[a.name])
                kb.cp("act", g[:], psg[:], r=[psg.name], w=[g.name])
                if l == 0:
                    kb.dma(self.vfirst[r0:r0 + 128, :], v_, r=[c.name], q="pool")
                else:
                    kb.dma(vf[:], self.vfirst[r0:r0 + 128, :], w=[vf.name])
                    kb.dma(xvd[:], self.proj_g[l][r0:r0 + 128, 3 * D:3 * D + 32], w=[xvd.name])
                    ps = self.next_ps()
                    kb.tr(ps[0:32, 0:128], xvd[:], idf[:], r=[xvd.name, idf.name], w=[ps.name])
                    kb.cp("act", xvdT[:], ps[0:32, 0:128], r=[ps.name], w=[xvdT.name])
                    ps = self.next_ps()
                    kb.mm(ps[:], xvdT[:], vup[:], True, True, r=[xvdT.name, vup.name], w=[ps.name])
                    kb.tt("dve", t1[:], ps[:], v0[:], ALU.add, r=[ps.name, v0.name], w=[t1.name])
                    kb.act(t1[:], t1[:], AF.Sigmoid, r=[t1.name], w=[t1.name])
                    kb.tt("dve", t2[:], vf[:], v_, ALU.subtract, r=[vf.name, c.name], w=[t2.name])
                    kb.tt("dve", t2[:], t2[:], t1[:], ALU.mult, r=[t2.name, t1.name], w=[t2.name])
                    kb.tt("dve", v_, v_, t2[:], ALU.add, r=[c.name, t2.name], w=[c.name])
                kb.tt("dve", kk[:], k_, k_k[:], ALU.mult, r=[c.name, k_k.name], w=[kk.name])
                kb.act(t1[:], kk[:], AF.Square, r=[kk.name], w=[t1.name])
                kb.add("dve", lambda e: e.tensor_reduce(ssq[:], t1[:].rearrange("p (h n) -> p h n", h=8), AX.X, ALU.add),
                       [t1.name], [ssq.name])
                self.rsqrt(rs[:], ssq[:], 1.0, EPS, [ssq.name], [rs.name])
                kb.tt("dve", hv(kk[:]), hv(kk[:]), b8(rs), ALU.mult, r=[kk.name, rs.name], w=[kk.name])
                kb.ts("dve", t2[:], a[:], -1.0, None, ALU.add, r=[a.name], w=[t2.name])
                kb.tt("dve", t2[:], t2[:], k_a[:], ALU.mult, r=[t2.name, k_a.name], w=[t2.name])
                kb.ts("dve", t2[:], t2[:], 1.0, None, ALU.add, r=[t2.name], w=[t2.name])
                kb.tt("dve", kp[:], t2[:], k_, ALU.mult, r=[t2.name, c.name], w=[kp.name])
                kb.tt("pool", bet[:], kk[:], a[:], ALU.mult, r=[kk.name, a.name], w=[bet.name])
                ps = self.next_ps(); ps2 = self.next_ps()
                kb.mm(ps[:], tin[:], ld[:], True, True, r=[tin.name, ld.name], w=[ps.name])
                kb.mm(ps2[:], ones[:], ld[:], True, True, r=[ones.name, ld.name], w=[ps2.name])
                kb.cp("dve", G[:], ps[:], r=[ps.name], w=[G.name])
                kb.tt("dve", eGr[:], ps2[:], G[:], ALU.subtract, r=[ps2.name, G.name], w=[eGr.name])
                kb.act(eGr[:], eGr[:], AF.Exp, r=[eGr.name], w=[eGr.name])
                kb.act(eG[:], G[:], AF.Exp, r=[G.name], w=[eG.name])
                kb.act(enG[:], G[:], AF.Exp, r=[G.name], w=[enG.name], scale=-1.0)
                kb.tt("pool", eGm[:], G[:], ld[:], ALU.subtract, r=[G.name, ld.name], w=[eGm.name])
                kb.act(eGm[:], eGm[:], AF.Exp, r=[eGm.name], w=[eGm.name])
                ps = self.next_ps()
                for h in range(H):
                    kb.mm(ps[0:64, h:h + 1], ld[:, h * 64:(h + 1) * 64], ones[:, 0:1], True, True,
                          r=[ld.name, ones.name], w=[ps.name])
                kb.act(PCT[:], ps[0:64, 0:8], AF.Exp, r=[ps.name], w=[PCT.name])
                kb.tt("dve", at[:], kk[:], eGm[:], ALU.mult, r=[kk.name, eGm.name], w=[at.name])
                kb.ts("pool", at[:], at[:], -1.0, None, ALU.mult, r=[at.name], w=[at.name])
                kb.tt("dve", bt[:], bet[:], enG[:], ALU.mult, r=[bet.name, enG.name], w=[bt.name])
                kb.tt("pool", kt[:], kp[:], enG[:], ALU.mult, r=[kp.name, enG.name], w=[kt.name])
                kb.tt("dve", rt[:], r_, eG[:], ALU.mult, r=[c.name, eG.name], w=[rt.name])
                kb.tt("pool", bh[:], bet[:], eGr[:], ALU.mult, r=[bet.name, eGr.name], w=[bh.name])
                kb.tt("dve", kh[:], kp[:], eGr[:], ALU.mult, r=[kp.name, eGr.name], w=[kh.name])
                tr64(aT, at); tr64(bT, bt); tr64(kT, kt); tr64(rT, rt)
                N0 = Nm[0]
                mat8(N0, bT, aT, tst)
                mat8(LakT, kT, aT, tst)
                mat8(MrbT, bT, rT, tin)
                mat8(MrkT, kT, rT, tin)
                L0 = Lm[0]
                for half in range(2):
                    self.tr_f32(L0[:, half * 512:(half + 1) * 512], N0[:, half * 512:(half + 1) * 512], idf)
                P0 = Pm[0]
                kb.tt("dve", hm(P0[:]), hm(N0[:]), msk(idf), ALU.add, r=[N0.name, idf.name], w=[P0.name])
                cur = 0
                for lev in range(6):
                    Nc, Lc, Pc = Nm[cur], Lm[cur], Pm[cur]
                    Nn, Ln, Pn = Nm[1 - cur], Lm[1 - cur], Pm[1 - cur]
                    for half in range(2):
                        hsl = slice(half * 512, (half + 1) * 512)
                        psl = self.next_ps()
                        for hh in range(4):
                            hs = slice(half * 512 + hh * 128, half * 512 + (hh + 1) * 128)
                            kb.mm(psl[:, hh * 128:(hh + 1) * 128], Nc[:, hs], Lc[:, hs], True, True,
                                  r=[Nc.name, Lc.name], w=[psl.name])
                        kb.cp("act", Ln[:, hsl], psl[:], r=[psl.name], w=[Ln.name])
                        if lev < 5:
                            psn = self.next_ps()
                            for hh in range(4):
                                hs = slice(half * 512 + hh * 128, half * 512 + (hh + 1) * 128)
                                kb.mm(psn[:, hh * 128:(hh + 1) * 128], Lc[:, hs], Nc[:, hs], True, True,
                                      r=[Nc.name, Lc.name], w=[psn.name])
                            kb.cp("dve", Nn[:, hsl], psn[:], r=[psn.name], w=[Nn.name])
                    for half in range(2):
                        hsl = slice(half * 512, (half + 1) * 512)
                        psp = self.next_ps()
                        for hh in range(4):
                            hs = slice(half * 512 + hh * 128, half * 512 + (hh + 1) * 128)
                            kb.mm(psp[:, hh * 128:(hh + 1) * 128], Ln[:, hs], Pc[:, hs], True, True,
                                  r=[Ln.name, Pc.name], w=[psp.name])
                        kb.tt("dve", Pn[:, hsl], psp[:], Pc[:, hsl], ALU.add, r=[psp.name, Pc.name], w=[Pn.name])
                    cur = 1 - cur
                WT = Pm[cur]
                ps = self.next_ps()
                for h in range(H):
                    hs = slice(h * 64, (h + 1) * 64); ms = slice(h * 128, (h + 1) * 128)
                    kb.mm(ps[:, hs], aT[:, ms], T0[:, hs], True, False, r=[aT.name, T0.name], w=[ps.name])
                    kb.mm(ps[:, hs], LakT[:, ms], c[:, 1024 + h * 64:1024 + (h + 1) * 64], False, True,
                          r=[LakT.name, c.name], w=[ps.name])
                kb.cp("act", X1[:], ps[:], r=[ps.name], w=[X1.name])
                ps = self.next_ps()
                for h in range(H):
                    hs = slice(h * 64, (h + 1) * 64); ms = slice(h * 128, (h + 1) * 128)
                    kb.mm(ps[:, hs], WT[:, ms], X1[:, hs], True, True, r=[WT.name, X1.name], w=[ps.name])
                kb.cp("dve", U[:], ps[:], r=[ps.name], w=[U.name])
                ps = self.next_ps(); ps2 = self.next_ps()
                for h in range(H):
                    hs = slice(h * 64, (h + 1) * 64); ms = slice(h * 128, (h + 1) * 128)
                    vs = c[:, 1024 + h * 64:1024 + (h + 1) * 64]
                    kb.mm(ps[:, hs], rT[:, ms], T0[:, hs], True, False, r=[rT.name, T0.name], w=[ps.name])
                    kb.mm(ps[:, hs], MrbT[:, ms], U[:, hs], False, False, r=[MrbT.name, U.name], w=[ps.name])
                    kb.mm(ps[:, hs], MrkT[:, ms], vs, False, True, r=[MrkT.name, c.name], w=[ps.name])
                    kb.mm(ps2[0:64, hs], bh[:, hs], U[:, hs], True, False, r=[bh.name, U.name], w=[ps2.name])
                    kb.mm(ps2[0:64, hs], kh[:, hs], vs, False, True, r=[kh.name, c.name], w=[ps2.name])
                kb.cp("act", Y[:], ps[:], r=[ps.name], w=[Y.name])
                for h in range(H):
                    hs = slice(h * 64, (h + 1) * 64)
                    kb.stt("dve", T0[:, hs], T0[:, hs], PCT[:, h:h + 1], ps2[0:64, hs], ALU.mult, ALU.add,
                           r=[T0.name, PCT.name, ps2.name], w=[T0.name])
                kb.add("dve", lambda e: e.tensor_reduce(mean[:], Y[:].rearrange("p (h n) -> p h n", h=8), AX.X, ALU.add),
                       [Y.name], [mean.name])
                kb.ts("dve", mean[:], mean[:], 1.0 / N, None, ALU.mult, r=[mean.name], w=[mean.name])
                kb.tt("dve", hv(Y[:]), hv(Y[:]), b8(mean), ALU.subtract, r=[Y.name, mean.name], w=[Y.name])
                kb.act(t1[:], Y[:], AF.Square, r=[Y.name], w=[t1.name])
                kb.add("dve", lambda e: e.tensor_reduce(var[:], t1[:].rearrange("p (h n) -> p h n", h=8), AX.X, ALU.add),
                       [t1.name], [var.name])
                self.rsqrt(var[:], var[:], 1.0 / N, 64e-5, [var.name], [var.name])
                kb.tt("dve", hv(Y[:]), hv(Y[:]), b8(var), ALU.mult, r=[Y.name, var.name], w=[Y.name])
                kb.tt("pool", Y[:], Y[:], ln_w[:], ALU.mult, r=[Y.name, ln_w.name], w=[Y.name])
                kb.tt("dve", Y[:], Y[:], ln_b[:], ALU.add, r=[Y.name, ln_b.name], w=[Y.name])
                kb.tt("pool", t2[:], r_, kp[:], ALU.mult, r=[c.name, kp.name], w=[t2.name])
                kb.tt("pool", t2[:], t2[:], r_k[:], ALU.mult, r=[t2.name, r_k.name], w=[t2.name])
                kb.add("dve", lambda e: e.tensor_reduce(rkv[:], t2[:].rearrange("p (h n) -> p h n", h=8), AX.X, ALU.add),
                       [t2.name], [rkv.name])
                kb.tt("dve", hv(t2[:]), hv(v_), b8(rkv), ALU.mult, r=[c.name, rkv.name], w=[t2.name])
                kb.tt("dve", Y[:], Y[:], t2[:], ALU.add, r=[Y.name, t2.name], w=[Y.name])
                kb.tt("dve", Y[:], Y[:], g[:], ALU.mult, r=[Y.name, g.name], w=[Y.name])
                kb.dma(self.o_mix[l][1][r0:r0 + 128, :], Y[:], r=[Y.name], q="pool")
            kb.flush()

    def build(self):
        self.declare(); self.declare_rest(); self.declare_gdn(); self.declare_rwkv(); self.declare_nsa()
        x = self.x_in
        for l in range(self.L):
            self.phase_proj(l, x)
            self.phase_gdn(l, side=True)
            self.phase_nsa_attn(l)
            self.phase_rwkv(l)
            self.phase_merge(l, x)
            self.phase_ffn(l)
            x = self.x_lay[l]
        return self.nc


def _consts():
    import ml_dtypes
    f = np.float32
    return {"ident_bf": np.eye(128, dtype=ml_dtypes.bfloat16), "ident_f": np.eye(128, dtype=f),
            "ones_f": np.ones((128, 128), f), "triu_incl": np.triu(np.ones((128, 128), f)),
            "triu_strict": np.triu(np.ones((128, 128), f), 1)}


def kernel(**inputs):
    x = np.asarray(inputs["x"], np.float32)
    B, S, _ = x.shape
    L = inputs["w_in"].shape[0]
    prog = Prog(S, n_layers=L)
    nc = prog.build()
    w_in = np.asarray(inputs["w_in"], np.float32)
    ext = np.zeros((L, D, 32), np.float32)
    ext[1:] = np.asarray(inputs["rwkv_vres_down"], np.float32)
    shared = dict(_consts())
    shared.update(_nsa_consts(S))
    shared["w_in"] = np.ascontiguousarray(np.concatenate([w_in, ext], axis=2))
    for k in prog.din:
        if k not in shared and k != "x":
            shared[k] = np.ascontiguousarray(np.asarray(inputs[k], np.float32))
    in_maps = [dict(shared, x=np.ascontiguousarray(x[b])) for b in range(B)]
    res = run_bass_kernel_spmd(nc, in_maps, core_ids=list(range(B)))
    return np.stack([np.asarray(r["out"], np.float32) for r in res.results], axis=0)


def _nsa_consts(S):
    import ml_dtypes
    f = np.float32
    NB = S // 64
    n_cmp = (S - 32) // 16 + 1
    nch = (n_cmp + 127) // 128
    ex = np.zeros((128, S), f)
    ex[np.arange(S) // 64, np.arange(S)] = 1.0
    k = np.arange(128)[:, None]
    q = np.arange(128)[None, :]
    caus = np.where(k > q, NEGM, 0.0).astype(f)
    win = np.where(k <= q, NEGM, 0.0).astype(f)
    rr = np.arange(17)[None, :, None]
    cm = ((16 * k[:, :, None] + 31) <= (128 * rr + q[:, None, :])).astype(f)
    cs = np.arange(nch * 128) * 16
    ss = np.arange(NB) * 64
    ov = np.clip(np.minimum(cs[:, None] + 32, ss[None, :] + 64) - np.maximum(cs[:, None], ss[None, :]), 0, None) / 32.0
    ov[n_cmp:] = 0.0
    c2s = ov.reshape(nch, 128, NB).transpose(1, 0, 2).astype(f)
    return {"exall": ex.astype(ml_dtypes.bfloat16),
            "causneg": np.tile(caus, (1, 4)).astype(ml_dtypes.bfloat16),
            "winneg": np.tile(win, (1, 4)).astype(ml_dtypes.bfloat16),
            "cmask": np.ascontiguousarray(cm), "cmp2slc": np.ascontiguousarray(c2s)}


def _declare_nsa(self):
    L, S = self.L, self.S
    i = self.inp
    self.NB = S // 64
    self.n_cmp = (S - 32) // 16 + 1
    self.nch = (self.n_cmp + 127) // 128
    self.nsa_q_norm = i("nsa_q_norm", [L, 64])
    self.nsa_k_norm = i("nsa_k_norm", [L, 3, 64])
    self.nsa_cmp_pos = i("nsa_cmp_pos", [L, 2, 32, 64])
    self.nsa_cmp_w1 = i("nsa_cmp_w1", [L, 2, 2048, 256])
    self.nsa_cmp_w2 = i("nsa_cmp_w2", [L, 2, 256, 64])
    self.exall = i("exall", [128, S], BF16)
    self.causneg = i("causneg", [128, 512], BF16)
    self.winneg = i("winneg", [128, 512], BF16)
    self.cmask = i("cmask", [128, 17, 128])
    self.cmp2slc = i("cmp2slc", [128, self.nch, self.NB])
    self.qT_d = self.scr("qT_d", [8, 64, S])
    self.kswT_d = self.scr("kswT_d", [4, 64, S], BF16)
    self.kvcT_d = self.scr("kvcT_d", [4, 64, S])
    self.vaug_d = self.scr("vaug_d", [S, 4 * 65], BF16)


def _gen_nsa_prep(self, l, st, pA, pB):
    kb, S, NT = self.kb, self.S, self.NT
    P = self.proj_tm[l]
    sb = lambda name, shape, dt=F32: self.sb(st, "np_" + name, shape, dt)
    idf = sb("idf", [128, 128])
    kb.dma(idf[:], self.ident_f[:, :], w=[idf.name])
    gq = sb("gq", [128, 12, 64])
    for h in range(8):
        kb.dma(gq[:, h, :], self.nsa_q_norm[l:l + 1, :].to_broadcast([128, 64]), w=[gq.name])
    for h in range(2):
        kb.dma(gq[:, 8 + h, :], self.nsa_k_norm[l, 1:2, :].to_broadcast([128, 64]), w=[gq.name])
        kb.dma(gq[:, 10 + h, :], self.nsa_k_norm[l, 2:3, :].to_broadcast([128, 64]), w=[gq.name])
    xin = [sb("xin%d" % j, [128, NSA_IN]) for j in range(2)]
    nin = sb("nin", [128, 768]); sq = sb("sq", [128, 768]); ss = sb("ss", [128, 12]); rs = sb("rs", [128, 12])
    qo = [sb("qo%d" % j, [64, 8, 128]) for j in range(2)]
    ko = [sb("ko%d" % j, [64, 4, 128], BF16) for j in range(2)]
    co = [sb("co%d" % j, [64, 4, 128]) for j in range(2)]
    va = [sb("va%d" % j, [128, 4, 65], BF16) for j in range(2)]
    for j in range(2):
        kb.memset("dve", va[j][:], 1.0, w=[va[j].name])
    h12 = lambda x: x.rearrange("p (h n) -> p h n", h=12)
    yield
    for t in range(NT):
        j = t % 2
        rows = slice(t * 128, (t + 1) * 128)
        x = xin[j]
        kb.dma(x[:], P[rows, 0:NSA_IN], w=[x.name])
        kb.cp("pool", nin[:, 0:512], x[:, 0:512], r=[x.name], w=[nin.name])
        kb.cp("pool", nin[:, 512:640], x[:, 768:896], r=[x.name], w=[nin.name])
        kb.cp("pool", nin[:, 640:768], x[:, 1024:1152], r=[x.name], w=[nin.name])
        kb.act(sq[:], nin[:], AF.Square, r=[nin.name], w=[sq.name])
        yield
        kb.add("dve", lambda e: e.tensor_reduce(ss[:], sq[:].rearrange("p (h n) -> p h n", h=12), AX.X, ALU.add),
               [sq.name], [ss.name])
        self.rsqrt(rs[:], ss[:], 1.0 / 64, EPS, [ss.name], [rs.name])
        yield
        kb.tt("dve", h12(nin[:]), h12(nin[:]), rs[:].unsqueeze(2).to_broadcast([128, 12, 64]), ALU.mult,
              r=[nin.name, rs.name], w=[nin.name])
        kb.tt("pool", nin[:], nin[:], gq[:].rearrange("p h n -> p (h n)"), ALU.mult, r=[nin.name, gq.name], w=[nin.name])
        yield
        for half in range(2):
            for blk in range(4):
                b8 = half * 4 + blk
                kb.tr(pA[0:64, blk * 128:(blk + 1) * 128], nin[:, b8 * 64:(b8 + 1) * 64], idf[:],
                      r=[nin.name, idf.name], w=[pA.name])
            kb.cp("act" if half else "dve", qo[j][:, half * 4:half * 4 + 4, :].rearrange("p h t -> p (h t)"), pA[0:64, :],
                  r=[pA.name], w=[qo[j].name])
            yield
        for blk in range(4):
            kb.tr(pB[0:64, blk * 128:(blk + 1) * 128], nin[:, 512 + blk * 64:512 + (blk + 1) * 64], idf[:],
                  r=[nin.name, idf.name], w=[pB.name])
        kb.cp("act", ko[j][:].rearrange("p h t -> p (h t)"), pB[0:64, :], r=[pB.name], w=[ko[j].name])
        yield
        for blk in range(4):
            kb.tr(pB[0:64, blk * 128:(blk + 1) * 128], x[:, 512 + blk * 64:512 + (blk + 1) * 64], idf[:],
                  r=[x.name, idf.name], w=[pB.name])
        kb.cp("dve", co[j][:].rearrange("p h t -> p (h t)"), pB[0:64, :], r=[pB.name], w=[co[j].name])
        yield
        kb.dma(self.qT_d[:, :, rows].rearrange("h d t -> d h t"), qo[j][:], r=[qo[j].name], q="pool")
        kb.dma(self.kswT_d[:, :, rows].rearrange("h d t -> d h t"), ko[j][:], r=[ko[j].name], q="pool")
        kb.dma(self.kvcT_d[:, :, rows].rearrange("h d t -> d h t"), co[j][:], r=[co[j].name], q="pool")
        kb.cp("pool", va[j][:, 0:2, 0:64], x[:, 896:1024].rearrange("p (g n) -> p g n", g=2), r=[x.name], w=[va[j].name])
        kb.cp("pool", va[j][:, 2:4, 0:64], x[:, 1152:1280].rearrange("p (g n) -> p g n", g=2), r=[x.name], w=[va[j].name])
        kb.dma(self.vaug_d[rows, :], va[j][:].rearrange("p g n -> p (g n)"), r=[va[j].name], q="pool")
        yield


def _phase_nsa_prep(self, l):
    with ExitStack() as st:
        pA = self.psb(st, "ps_npA"); pB = self.psb(st, "ps_npB")
        for _ in self.gen_nsa_prep(l, st, pA, pB):
            pass
        self.kb.flush()


Prog.gen_nsa_prep = _gen_nsa_prep
Prog.declare_nsa = _declare_nsa
Prog.phase_nsa_prep = _phase_nsa_prep


def _phase_nsa_attn(self, l):
    kb, S, NT, NB, n_cmp, nch = self.kb, self.S, self.NT, self.NB, self.n_cmp, self.nch
    P = self.proj_tm[l]
    SC = 0.125
    with ExitStack() as st:
        sb = lambda name, shape, dt=F32: self.sb(st, name, shape, dt)
        psS = [self.psb(st, "ps_S%d" % j) for j in range(2)]
        psO = [self.psb(st, "ps_O%d" % j) for j in range(2)]
        psI = self.psb(st, "ps_I"); psT = self.psb(st, "ps_T")
        psX = [self.psb(st, "ps_X%d" % j) for j in range(2)]
        idf = sb("idf", [128, 128]); idb = sb("idb", [128, 128], BF16); ones = sb("ones", [128, 128])
        kb.dma(idf[:], self.ident_f[:, :], w=[idf.name])
        kb.dma(idb[:], self.ident_bf[:, :], w=[idb.name])
        kb.dma(ones[:], self.ones_f[:, :], w=[ones.name])
        ks = sb("ks", [64, 2, S], BF16); kw = sb("kw", [64, 2, S], BF16)
        kb.dma(ks[:], self.kswT_d[0:2].rearrange("g d s -> d g s"), w=[ks.name])
        kb.dma(kw[:], self.kswT_d[2:4].rearrange("g d s -> d g s"), w=[kw.name])
        vv = sb("vv", [128, NT, 4 * 65], BF16)
        kb.dma(vv[:], self.vaug_d.rearrange("(c p) n -> p c n", p=128), w=[vv.name])
        ex = sb("ex", [128, S], BF16)
        kb.dma(ex[:], self.exall[:, :], w=[ex.name])
        cneg = sb("cneg", [128, 512], BF16); wneg = sb("wneg", [128, 512], BF16)
        kb.dma(cneg[:], self.causneg[:, :], w=[cneg.name])
        kb.dma(wneg[:], self.winneg[:, :], w=[wneg.name])
        cmk = sb("cmk", [128, 17, 128]); c2s = sb("c2s", [128, nch, NB])
        kb.dma(cmk[:], self.cmask[:, :, :], w=[cmk.name])
        kb.dma(c2s[:], self.cmp2slc[:, :, :], w=[c2s.name])
        kcT = sb("kcT", [64, 2, nch * 128]); vc = sb("vc", [128, nch, 2, 65])
        kb.memset("dve", kcT[:], 0.0, w=[kcT.name])
        kb.memset("dve", vc[:], 0.0, w=[vc.name])
        kb.memset("dve", vc[:, :, :, 64:65], 1.0, w=[vc.name])
        with ExitStack() as s2:
            sb2 = lambda name, shape, dt=F32: self.sb(s2, name, shape, dt)
            w1 = sb2("w1c", [64, 32, 256]); w2 = sb2("w2c", [128, 2, 64]); pos = sb2("pos", [32, 64]); posT = sb2("posT", [64, 32])
            tT = sb2("tT", [64, S]); hid = sb2("hid", [128, 2, 512]); cb = sb2("cb", [128, 2])
            kg0r = sb2("kg0r", [1, 64]); kg0 = sb2("kg0", [64, 1]); sqc = sb2("sqc", [64, 512]); rsc = sb2("rsc", [64, 512])
            kcr = sb2("kcr", [64, 512])
            kb.dma(kg0r[:], self.nsa_k_norm[l, 0:1, :], w=[kg0r.name])
            kb.tr(psT[0:64, 0:1], kg0r[:], idf[0:1, 0:1], r=[kg0r.name, idf.name], w=[psT.name])
            kb.cp("dve", kg0[:], psT[0:64, 0:1], r=[psT.name], w=[kg0.name])
            for jj in range(2):
                kb.dma(w1[:], self.nsa_cmp_w1[l, jj].rearrange("(l d) n -> d l n", d=64), w=[w1.name])
                kb.dma(w2[:], self.nsa_cmp_w2[l, jj].rearrange("(k p) n -> p k n", p=128), w=[w2.name])
                kb.dma(pos[:], self.nsa_cmp_pos[l, jj], w=[pos.name])
                kb.tr(psT[0:64, 0:32], pos[:], idf[0:32, 0:32], r=[pos.name, idf.name], w=[psT.name])
                kb.cp("dve", posT[:], psT[0:64, 0:32], r=[psT.name], w=[posT.name])
                for half in range(2):
                    for li in range(32):
                        kb.mm(psT[:, 64 + half:65 + half], w1[:, li, half * 128:(half + 1) * 128], posT[:, li:li + 1],
                              li == 0, li == 31, r=[w1.name, posT.name], w=[psT.name])
                kb.cp("dve", cb[:], psT[:, 64:66], r=[psT.name], w=[cb.name])
                for g in range(2):
                    kb.dma(tT[:], self.kvcT_d[jj * 2 + g], w=[tT.name])
                    for half in range(2):
                        for li in range(32):
                            kb.mm(psX[half][:, 0:n_cmp], w1[:, li, half * 128:(half + 1) * 128],
                                  tT[:, li:li + 16 * (n_cmp - 1) + 1:16], li == 0, li == 31,
                                  r=[w1.name, tT.name], w=[psX[half].name])
                        kb.act(hid[:, half, 0:n_cmp], psX[half][:, 0:n_cmp], AF.Silu, bias=cb[:, half:half + 1],
                               r=[psX[half].name, cb.name], w=[hid.name])
                    if jj == 0:
                        for half in range(2):
                            kb.mm(psT[0:64, 0:n_cmp], w2[:, half, :], hid[:, half, 0:n_cmp], half == 0, half == 1,
                                  r=[w2.name, hid.name], w=[psT.name])
                        kb.act(sqc[:, 0:n_cmp], psT[0:64, 0:n_cmp], AF.Square, r=[psT.name], w=[sqc.name])
                        kb.cp("dve", kcr[:, 0:n_cmp], psT[0:64, 0:n_cmp], r=[psT.name], w=[kcr.name])
                        kb.mm(psI[0:64, 0:n_cmp], ones[0:64, 0:64], sqc[:, 0:n_cmp], True, True,
                              r=[ones.name, sqc.name], w=[psI.name])
                        self.rsqrt(rsc[:, 0:n_cmp], psI[0:64, 0:n_cmp], 1.0 / 64, EPS, [psI.name], [rsc.name])
                        kb.stt("dve", kcT[:, g, 0:n_cmp], kcr[:, 0:n_cmp], kg0[:, 0:1], rsc[:, 0:n_cmp], ALU.mult, ALU.mult,
                               r=[kcr.name, kg0.name, rsc.name], w=[kcT.name])
                    else:
                        for ch in range(nch):
                            cw = min(128, n_cmp - ch * 128)
                            for half in range(2):
                                kb.mm(psT[0:cw, 0:64], hid[:, half, ch * 128:ch * 128 + cw], w2[:, half, :],
                                      half == 0, half == 1, r=[hid.name, w2.name], w=[psT.name])
                            kb.cp("dve", vc[0:cw, ch, g, 0:64], psT[0:cw, 0:64], r=[psT.name], w=[vc.name])
            kb.flush()
        q32 = [sb("q32_%d" % j, [64, 4, 128]) for j in range(2)]
        q16 = [sb("q16_%d" % j, [64, 4, 128], BF16) for j in range(2)]
        gs = sb("gs", [128, 24])
        e32 = [sb("e32_%d" % j, [128, 512]) for j in range(2)]
        e16 = [sb("e16_%d" % j, [128, 512], BF16) for j in range(2)]
        den = sb("den", [128, 4]); rden = sb("rden", [128, 4]); coef = sb("coef", [128, 4])
        impm = sb("impm", [128, NB]); wk = sb("wk", [128, NB]); m1 = sb("m1", [128, 8]); m2 = sb("m2", [128, 8])
        sel = sb("sel", [128, NB]); negT = sb("negT", [128, 512], BF16)
        oacc = [sb("oacc%d" % j, [128, MIX]) for j in range(2)]
        kb.memset("dve", negT[:], 0.0, w=[negT.name])
        nS = 0
        nO = 0

        def finish_branch(pso, g, br, oa, first):
            kb.ts("dve", den[:], pso[:, 64:260:65], 1e-30, None, ALU.max, r=[pso.name], w=[den.name])
            kb.add("dve", lambda e: e.reciprocal(rden[:], den[:]), [den.name], [rden.name])
            kb.tt("dve", coef[:], rden[:], gs[:, g * 12 + br:g * 12 + 12:3], ALU.mult, r=[rden.name, gs.name], w=[coef.name])
            for h in range(4):
                osl = oa[:, (4 * g + h) * 64:(4 * g + h + 1) * 64]
                if first:
                    kb.ts("dve", osl, pso[:, h * 65:h * 65 + 64], coef[:, h:h + 1], None, ALU.mult,
                          r=[pso.name, coef.name], w=[oa.name])
                else:
                    kb.stt("dve", osl, pso[:, h * 65:h * 65 + 64], coef[:, h:h + 1], osl, ALU.mult, ALU.add,
                           r=[pso.name, coef.name, oa.name], w=[oa.name])

        for b in range(NT):
            rows = slice(b * 128, (b + 1) * 128)
            oa = oacc[b % 2]
            kb.dma(gs[:], P[rows, 1280:1304], w=[gs.name])
            kb.act(gs[:], gs[:], AF.Sigmoid, r=[gs.name], w=[gs.name])
            for g in range(2):
                qj = (b * 2 + g) % 2
                kb.dma(q32[qj][:], self.qT_d[4 * g:4 * g + 4, :, rows].rearrange("h d t -> d h t"), w=[q32[qj].name])
                kb.cp("pool", q16[qj][:], q32[qj][:], r=[q32[qj].name], w=[q16[qj].name])
                qf32 = q32[qj][:].rearrange("p h t -> p (h t)")
                qf16 = q16[qj][:].rearrange("p h t -> p (h t)")
                pso = psO[nO % 2]; nO += 1
                chunks = list(range(0, min(b // 16, nch - 1) + 1))
                for ci, kc in enumerate(chunks):
                    pS = psS[nS % 2]; e = e32[nS % 2]; nS += 1
                    kb.mm(pS[:], kcT[:, g, kc * 128:(kc + 1) * 128], qf32, True, True,
                          r=[kcT.name, q32[qj].name], w=[pS.name])
                    kb.act(e[:], pS[:], AF.Exp, scale=SC, r=[pS.name], w=[e.name])
                    rr = b - 16 * kc
                    if rr <= 16:
                        kb.tt("dve", e[:].rearrange("p (h t) -> p h t", h=4), e[:].rearrange("p (h t) -> p h t", h=4),
                              cmk[:, rr, :].unsqueeze(1).to_broadcast([128, 4, 128]), ALU.mult,
                              r=[e.name, cmk.name], w=[e.name])
                    for h in range(4):
                        kb.mm(pso[:, h * 65:(h + 1) * 65], e[:, h * 128:(h + 1) * 128], vc[:, kc, g, :],
                              ci == 0 and h == 0, False, r=[e.name, vc.name], w=[pso.name])
                    for h in range(4):
                        kb.mm(psI[:, h * 128:h * 128 + NB], e[:, h * 128:(h + 1) * 128], c2s[:, kc, :],
                              ci == 0 and h == 0, False, r=[e.name, c2s.name], w=[psI.name])
                finish_branch(pso, g, 0, oa, True)
                if NB > 16:
                    for h in range(4):
                        if h == 0:
                            kb.ts("dve", impm[:], psI[:, 0:NB], rden[:, 0:1], None, ALU.mult,
                                  r=[psI.name, rden.name], w=[impm.name])
                        else:
                            kb.stt("dve", impm[:], psI[:, h * 128:h * 128 + NB], rden[:, h:h + 1], impm[:], ALU.mult, ALU.add,
                                   r=[psI.name, rden.name, impm.name], w=[impm.name])
                    if 2 * b + 2 < NB:
                        kb.memset("pool", impm[:, 2 * b + 2:NB], -1.0, w=[impm.name])
                    kb.memset("pool", impm[0:64, 2 * b + 1:2 * b + 2], -1.0, w=[impm.name])
                    kb.ts("pool", impm[:, 0:1], impm[:, 0:1], 100.0, None, ALU.add, r=[impm.name], w=[impm.name])
                    kb.ts("pool", impm[:, 2 * b:2 * b + 1], impm[:, 2 * b:2 * b + 1], 100.0, None, ALU.add,
                          r=[impm.name], w=[impm.name])
                    if b > 0:
                        kb.ts("pool", impm[0:64, 2 * b - 1:2 * b], impm[0:64, 2 * b - 1:2 * b], 100.0, None, ALU.add,
                              r=[impm.name], w=[impm.name])
                    kb.ts("pool", impm[64:128, 2 * b + 1:2 * b + 2], impm[64:128, 2 * b + 1:2 * b + 2], 100.0, None, ALU.add,
                          r=[impm.name], w=[impm.name])
                    kb.add("dve", lambda e_: e_.max(m1[:], impm[:]), [impm.name], [m1.name])
                    kb.add("dve", lambda e_: e_.match_replace(wk[:], m1[:], impm[:], -1e9), [m1.name, impm.name], [wk.name])
                    kb.add("dve", lambda e_: e_.max(m2[:], wk[:]), [wk.name], [m2.name])
                    kb.ts("dve", sel[:], impm[:], m2[:, 7:8], None, ALU.is_ge, r=[impm.name, m2.name], w=[sel.name])
                    kb.ts("dve", sel[:], sel[:], -NEGM, NEGM, ALU.mult, ALU.add, r=[sel.name], w=[sel.name])
                pso = psO[nO % 2]; nO += 1
                wch = list(range(max(0, b - 4), b + 1))
                for ci, kc in enumerate(wch):
                    pS = psS[nS % 2]; e = e16[nS % 2]; nS += 1
                    diag = kc == b
                    edge = kc == b - 4
                    kb.mm(pS[:], kw[:, g, kc * 128:(kc + 1) * 128], qf16, True, not (diag or edge),
                          r=[kw.name, q16[qj].name], w=[pS.name])
                    if diag:
                        kb.mm(pS[:], idb[:], cneg[:], False, True, r=[idb.name, cneg.name], w=[pS.name])
                    if edge:
                        kb.mm(pS[:], idb[:], wneg[:], False, True, r=[idb.name, wneg.name], w=[pS.name])
                    kb.act(e[:], pS[:], AF.Exp, scale=SC, r=[pS.name], w=[e.name])
                    for h in range(4):
                        kb.mm(pso[:, h * 65:(h + 1) * 65], e[:, h * 128:(h + 1) * 128],
                              vv[:, kc, (2 + g) * 65:(3 + g) * 65], ci == 0 and h == 0, False,
                              r=[e.name, vv.name], w=[pso.name])
                finish_branch(pso, g, 2, oa, False)
                if NB > 16:
                    kb.tr(psT[0:NB, 0:128], sel[:], idf[:], r=[sel.name, idf.name], w=[psT.name])
                    kb.cp("dve", negT[0:NB, :].rearrange("p (h t) -> p h t", h=4),
                          psT[0:NB, 0:128].unsqueeze(1).to_broadcast([NB, 4, 128]), r=[psT.name], w=[negT.name])
                pso = psO[nO % 2]; nO += 1
                for kc in range(0, b + 1):
                    pS = psS[nS % 2]; e = e16[nS % 2]; nS += 1
                    diag = kc == b
                    kb.mm(pS[:], ks[:, g, kc * 128:(kc + 1) * 128], qf16, True, False, r=[ks.name, q16[qj].name], w=[pS.name])
                    kb.mm(pS[:], ex[0:NB, kc * 128:(kc + 1) * 128], negT[0:NB, :], False, not diag,
                          r=[ex.name, negT.name], w=[pS.name])
                    if diag:
                        kb.mm(pS[:], idb[:], cneg[:], False, True, r=[idb.name, cneg.name], w=[pS.name])
                    kb.act(e[:], pS[:], AF.Exp, scale=SC, r=[pS.name], w=[e.name])
                    for h in range(4):
                        kb.mm(pso[:, h * 65:(h + 1) * 65], e[:, h * 128:(h + 1) * 128], vv[:, kc, g * 65:(g + 1) * 65],
                              kc == 0 and h == 0, False, r=[e.name, vv.name], w=[pso.name])
                finish_branch(pso, g, 1, oa, False)
            kb.dma(self.o_mix[l][0][rows, :], oa[:], r=[oa.name], q="pool")
        kb.flush()


Prog.phase_nsa_attn = _phase_nsa_attn
```

```python
import numpy as np
from contextlib import ExitStack
import concourse.bass as bass
import concourse.mybir as mybir
from concourse.bass_utils import run_bass_kernel_spmd

F32 = mybir.dt.float32
BF16 = mybir.dt.bfloat16
AF = mybir.ActivationFunctionType
ALU = mybir.AluOpType
AX = mybir.AxisListType

D = 1024
MIX = 512
NSA_IN = 1304
RWKV_IN = 1792
GDN_IN = 2056
D_IN = 8224
D_FF = 4096
DP = D_IN + 32
EPS = 1e-6
OFF_NSA = 0
OFF_RWKV = NSA_IN
OFF_GDN = NSA_IN + RWKV_IN
OFF_GATE = NSA_IN + RWKV_IN + GDN_IN
NEGM = -30000.0


class _Op:
    __slots__ = ("eng", "fn", "deps", "is_dma", "need_sig", "sem", "val", "pos")


class KB:
    ENGS = ("pe", "act", "dve", "pool", "sp")

    def __init__(self, nc, stack):
        self.nc = nc
        self.stack = stack
        self.esem = {e: stack.enter_context(nc.semaphore("es_" + e)) for e in ("pe", "act", "dve", "pool")}
        self.ecnt = {e: 0 for e in self.esem}
        self.dsem = {"sp": [stack.enter_context(nc.semaphore("dsp%d" % i)) for i in range(16)],
                     "pool": [stack.enter_context(nc.semaphore("dpl%d" % i)) for i in range(8)]}
        self.dcnt = {"sp": 0, "pool": 0}
        self.dlast = {"sp": {}, "pool": {}}
        self.seen = {e: {} for e in self.ENGS}
        self.begin()

    def begin(self):
        self.ops = {e: [] for e in self.ENGS}
        self.res = {}

    def add(self, eng, fn, reads=(), writes=(), dma=False):
        op = _Op()
        op.eng = eng
        op.fn = fn
        op.is_dma = dma
        op.need_sig = dma
        op.sem = None
        op.val = 0
        op.pos = len(self.ops[eng])
        deps = set()
        rl, wl = [], []
        reads = [x.split("__u")[0] for x in reads]
        writes = [x.split("__u")[0] for x in writes]
        for r in reads:
            (wl if r.startswith("ps") else rl).append(r)
        wl.extend(writes)
        for r in rl:
            st = self.res.setdefault(r, [None, []])
            if st[0] is not None:
                deps.add(st[0])
        for w in wl:
            st = self.res.setdefault(w, [None, []])
            if st[0] is not None:
                deps.add(st[0])
            deps.update(st[1])
        for r in rl:
            self.res[r][1].append(op)
        for w in wl:
            st = self.res[w]
            st[0] = op
            st[1] = []
        deps.discard(op)
        keep = []
        latest = {}
        for d in deps:
            if d.is_dma:
                keep.append(d)
                continue
            if d.eng == eng:
                if eng == "pe":
                    continue
                if op.pos - d.pos > 2:
                    continue
            cur = latest.get(d.eng)
            if cur is None or d.pos > cur.pos:
                latest[d.eng] = d
        for d in latest.values():
            d.need_sig = True
            keep.append(d)
        if dma:
            k = self.dcnt[eng]
            self.dcnt[eng] += 1
            P = len(self.dsem[eng])
            slot = k % P
            op.sem = self.dsem[eng][slot]
            op.val = 16 * (k // P + 1)
            prev = self.dlast[eng].get(slot)
            if prev is not None:
                keep.append(prev)
            self.dlast[eng][slot] = op
        op.deps = keep
        self.ops[eng].append(op)
        return op

    def flush(self):
        nc = self.nc
        for q in ("sp", "pool"):
            outstanding = [o for o in self.dlast[q].values()]
            if outstanding:
                op = self.add(q, None)
                op.deps = outstanding
        for e in ("pe", "act", "dve", "pool"):
            for op in self.ops[e]:
                if op.need_sig and not op.is_dma:
                    self.ecnt[e] += 1
                    op.sem = self.esem[e]
                    op.val = self.ecnt[e]
        ops = self.ops
        seen = self.seen

        def emit(e, eng):
            sn = seen[e]
            for op in ops[e]:
                for d in op.deps:
                    key = id(d.sem)
                    if sn.get(key, 0) >= d.val:
                        continue
                    eng.wait_ge(d.sem, d.val)
                    sn[key] = d.val
                if op.fn is None:
                    continue
                ins = op.fn(eng)
                if op.need_sig:
                    ins.then_inc(op.sem, 16 if op.is_dma else 1)

        with nc.Block() as block:
            if ops["pe"]:
                block.tensor(lambda t: emit("pe", t))
            if ops["act"]:
                block.scalar(lambda t: emit("act", t))
            if ops["dve"]:
                block.vector(lambda t: emit("dve", t))
            if ops["pool"]:
                block.gpsimd(lambda t: emit("pool", t))
            if ops["sp"]:
                block.sync(lambda t: emit("sp", t))
        self.begin()

    def dma(self, out, in_, r=(), w=(), q="sp"):
        return self.add(q, lambda e: e.dma_start(out=out, in_=in_), r, w, dma=True)

    def mm(self, out, lhsT, rhs, start, stop, r=(), w=()):
        return self.add("pe", lambda e: e.matmul(out, lhsT=lhsT, rhs=rhs, start=start, stop=stop), r, w)

    def tr(self, out, in_, ident, r=(), w=()):
        return self.add("pe", lambda e: e.transpose(out, in_, ident), r, w)

    def act(self, out, in_, func, r=(), w=(), bias=None, scale=1.0, accum=None):
        def fn(e):
            kw = {}
            if bias is not None:
                kw["bias"] = bias
            if accum is not None:
                kw["accum_out"] = accum
            return e.activation(out, in_, func, scale=scale, **kw)
        return self.add("act", fn, r, w)

    def ts(self, eng, out, in0, s1, s2, op0, op1=None, r=(), w=()):
        def fn(e):
            if op1 is None:
                return e.tensor_scalar(out, in0, s1, None, op0)
            return e.tensor_scalar(out, in0, s1, s2, op0, op1)
        return self.add(eng, fn, r, w)

    def tt(self, eng, out, in0, in1, op, r=(), w=()):
        return self.add(eng, lambda e: e.tensor_tensor(out, in0, in1, op), r, w)

    def stt(self, eng, out, in0, scalar, in1, op0, op1, r=(), w=()):
        return self.add(eng, lambda e: e.scalar_tensor_tensor(out, in0, scalar, in1, op0, op1), r, w)

    def cp(self, eng, out, in_, r=(), w=()):
        if eng == "act":
            return self.add(eng, lambda e: e.copy(out, in_), r, w)
        return self.add(eng, lambda e: e.tensor_copy(out, in_), r, w)

    def memset(self, eng, ap, val, w=()):
        return self.add(eng, lambda e: e.memset(ap, val), (), w)


class Prog:
    def __init__(self, S, n_layers=2, debug=False):
        self.S = S
        self.NT = S // 128
        self.L = n_layers
        self.debug = debug
        self.nc = bass.Bass("TRN2", target_bir_lowering=False)
        self.stack = ExitStack()
        self.kb = KB(self.nc, self.stack)
        self.din = {}
        self.dscr = {}

    def inp(self, name, shape, dt=F32):
        t = self.nc.dram_tensor(name, list(shape), dt, kind="ExternalInput")
        self.din[name] = t
        return t.ap()

    def outp(self, name, shape, dt=F32):
        return self.nc.dram_tensor(name, list(shape), dt, kind="ExternalOutput").ap()

    def scr(self, name, shape, dt=F32):
        if self.debug:
            return self.outp(name, shape, dt)
        return self.nc.dram_tensor(name, list(shape), dt, kind="Internal").ap()

    def sb(self, st, name, shape, dt=F32):
        self.uid = getattr(self, "uid", 0) + 1
        return st.enter_context(self.nc.sbuf_tensor("%s__u%d" % (name, self.uid), list(shape), dt))

    def psb(self, st, name, dt=F32, n=512):
        self.uid = getattr(self, "uid", 0) + 1
        return st.enter_context(self.nc.psum_tensor("%s__u%d" % (name, self.uid), [128, n], dt))

    def declare(self):
        L, S = self.L, self.S
        i = self.inp
        self.x_in = i("x", [S, D])
        self.out = self.outp("out", [S, D])
        self.norm_mix_g = i("norm_mix_g", [L, D])
        self.w_in = i("w_in", [L, D, DP])
        self.ident_bf = i("ident_bf", [128, 128], BF16)
        self.ident_f = i("ident_f", [128, 128])
        self.proj_tm = [self.scr("proj_tm%d" % l, [S, OFF_GATE]) for l in range(L)]
        self.proj_g = [self.scr("proj_g%d" % l, [S, DP - OFF_GATE]) for l in range(L)]

    def load_weight_bf16(self, st, name, src_ap, kc, ncols, chunk):
        kb = self.kb
        w = self.sb(st, name, [128, kc, ncols], BF16)
        src = src_ap.rearrange("(k p) n -> p k n", p=128)
        with ExitStack() as s2:
            stg = [self.sb(s2, "%s_stg%d" % (name, j), [128, kc, chunk], F32) for j in range(2)]
            engs = ["dve", "pool", "act"]
            n = 0
            for c0 in range(0, ncols, chunk):
                cw = min(chunk, ncols - c0)
                j = n % 2
                kb.dma(stg[j][:, :, 0:cw], src[:, :, c0:c0 + cw], w=[stg[j].name])
                kb.cp(engs[n % 3], w[:, :, c0:c0 + cw], stg[j][:, :, 0:cw], r=[stg[j].name], w=[name + ":%d" % n])
                n += 1
            kb.flush()
        return w

    def phase_proj(self, l, x_src):
        kb, S, NT = self.kb, self.S, self.NT
        with ExitStack() as st:
            wsb = self.load_weight_bf16(st, "w_in_sb", self.w_in[l], 8, DP, 516)
            gb = self.sb(st, "gmix", [128, D])
            idb = self.sb(st, "idb", [128, 128], BF16)
            kb.dma(gb[:], self.norm_mix_g[l:l + 1, :].to_broadcast([128, D]), w=["gmix"])
            kb.dma(idb[:], self.ident_bf[:, :], w=["idb"])
            xt = [self.sb(st, "xt%d" % j, [128, D]) for j in range(2)]
            junk = self.sb(st, "junk", [128, D])
            ss = [self.sb(st, "ss%d" % j, [128, 1]) for j in range(2)]
            rstd = [self.sb(st, "rstd%d" % j, [128, 1]) for j in range(2)]
            hb = [self.sb(st, "hb%d" % j, [128, D], BF16) for j in range(2)]
            hT = [self.sb(st, "hT%d" % j, [128, 8, 128], BF16) for j in range(2)]
            osb = [self.sb(st, "osb%d" % j, [128, 2048]) for j in range(2)]
            pst = self.psb(st, "ps_tr", BF16, 1024)
            pso = [self.psb(st, "ps_o%d" % j) for j in range(4)]
            nblk = 0
            nog = 0
            def prep(tt_, part):
                jj = tt_ % 2
                self.rms_to_hT(xt[jj], x_src[tt_ * 128:(tt_ + 1) * 128, :], junk, ss[jj], rstd[jj], gb, hb[jj], pst, idb,
                               hT[jj], part=part)
            prep(0, None)
            for t in range(NT):
                j = t % 2
                rows = slice(t * 128, (t + 1) * 128)
                for gi, (og, oe) in enumerate(((0, 2048), (2048, 4096), (4096, OFF_GATE), (OFF_GATE, OFF_GATE + 2048),
                                               (OFF_GATE + 2048, DP))):
                    if gi == 1 and t + 1 < NT:
                        prep(t + 1, 0)
                    o = osb[nog % 2]
                    nog += 1
                    ow = oe - og
                    for c0 in range(og, oe, 512):
                        cw = min(512, oe - c0)
                        ps = pso[nblk % 4]
                        for k in range(8):
                            kb.mm(ps[:, 0:cw], hT[j][:, k, :], wsb[:, k, c0:c0 + cw], k == 0, k == 7,
                                  r=[hT[j].name, "w_in_sb"], w=[ps.name])
                        kb.cp("act" if nblk % 2 else "dve", o[:, c0 - og:c0 - og + cw], ps[:, 0:cw],
                              r=[ps.name], w=[o.name])
                        nblk += 1
                    if og < OFF_GATE:
                        kb.dma(self.proj_tm[l][rows, og:oe], o[:, 0:ow], r=[o.name], q="pool")
                    else:
                        kb.dma(self.proj_g[l][rows, og - OFF_GATE:oe - OFF_GATE], o[:, 0:ow], r=[o.name], q="pool")
                if t + 1 < NT:
                    prep(t + 1, 1)
            kb.flush()

    def rms_to_hT(self, xt, x_rows, junk, ss, rstd, gb, hb, pst, idb, hT, part=None):
        kb = self.kb
        if part in (None, 0):
            kb.dma(xt[:], x_rows, w=[xt.name])
            kb.act(junk[:], xt[:], AF.Square, r=[xt.name], w=[junk.name, ss.name], accum=ss[:])
            self.rsqrt(rstd[:], ss[:], 1.0 / D, EPS, [ss.name], [rstd.name])
            kb.stt("dve", hb[:], xt[:], rstd[:, 0:1], gb[:], ALU.mult, ALU.mult,
                   r=[xt.name, rstd.name, gb.name], w=[hb.name])
        if part == 0:
            return
        for k in range(8):
            kb.tr(pst[:, k * 128:(k + 1) * 128], hb[:, k * 128:(k + 1) * 128], idb[:],
                  r=[hb.name, idb.name], w=[pst.name])
        kb.cp("act", hT[:].rearrange("p k t -> p (k t)"), pst[:], r=[pst.name], w=[hT.name])

    def rsqrt(self, out, in_, scale, eps, r, w):
        kb = self.kb
        kb.ts("dve", out, in_, scale, eps, ALU.mult, ALU.add, r=r, w=w)
        kb.add("act", lambda e: e.sqrt(out, out), w, w)
        kb.add("dve", lambda e: e.reciprocal(out, out), w, w)

    def declare_rest(self):
        L, S = self.L, self.S
        i = self.inp
        self.w_branch = i("w_branch", [L, 3, MIX, D])
        self.w_out = i("w_out", [L, D, D])
        self.norm_ffn_g = i("norm_ffn_g", [L, D])
        self.w_ff1 = i("w_ff1", [L, D, D_FF])
        self.w_ff2 = i("w_ff2", [L, D_FF, D])
        self.o_mix = [[self.scr("o_mix%d_%d" % (l, b), [S, MIX]) for b in range(3)] for l in range(L)]
        self.x_mid = [self.scr("x_mid%d" % l, [S, D]) for l in range(L)]
        self.x_lay = [self.scr("x_lay%d" % l, [S, D]) for l in range(L - 1)] + [self.out]

    def transpose_bf(self, src_bf, nk, pst, idb, dstT, eng="act"):
        kb = self.kb
        for k in range(nk):
            kb.tr(pst[:, k * 128:(k + 1) * 128], src_bf[:, k * 128:(k + 1) * 128], idb[:],
                  r=[src_bf.name, idb.name], w=[pst.name])
        kb.cp(eng, dstT.rearrange("p k t -> p (k t)"), pst[:, 0:nk * 128], r=[pst.name], w=[dstT.name])

    def phase_merge(self, l, x_src):
        kb, S, NT = self.kb, self.S, self.NT
        with ExitStack() as st:
            wb = [self.load_weight_bf16(st, "wbr%d" % b, self.w_branch[l, b], 4, D, 1024) for b in range(3)]
            wo = self.load_weight_bf16(st, "wout_sb", self.w_out[l], 8, D, 512)
            idb = self.sb(st, "idb", [128, 128], BF16)
            kb.dma(idb[:], self.ident_bf[:, :], w=["idb"])
            ot = [self.sb(st, "ot%d" % j, [128, MIX]) for j in range(2)]
            ob = [self.sb(st, "ob%d" % j, [128, MIX], BF16) for j in range(2)]
            oT = [self.sb(st, "oT%d" % j, [128, 4, 128], BF16) for j in range(2)]
            gt = [self.sb(st, "gt%d" % j, [128, D]) for j in range(2)]
            tmp = self.sb(st, "tmpm", [128, D])
            mrg = self.sb(st, "mrg", [128, D])
            mb = self.sb(st, "mrgb", [128, D], BF16)
            mT = self.sb(st, "mT", [128, 8, 128], BF16)
            xt = [self.sb(st, "xt%d" % j, [128, D]) for j in range(2)]
            xo = [self.sb(st, "xo%d" % j, [128, D]) for j in range(2)]
            pst = self.psb(st, "ps_tr", BF16, 1024)
            pso = [self.psb(st, "ps_o%d" % j) for j in range(4)]
            n = 0
            nb = 0
            for t in range(NT):
                rows = slice(t * 128, (t + 1) * 128)
                for b in range(3):
                    j = n % 2
                    n += 1
                    kb.dma(ot[j][:], self.o_mix[l][b][rows, :], w=[ot[j].name])
                    kb.dma(gt[j][:], self.proj_g[l][rows, b * D:(b + 1) * D], w=[gt[j].name])
                    kb.cp("pool", ob[j][:], ot[j][:], r=[ot[j].name], w=[ob[j].name])
                    kb.act(gt[j][:], gt[j][:], AF.Sigmoid, r=[gt[j].name], w=[gt[j].name])
                    self.transpose_bf(ob[j], 4, pst, idb, oT[j][:])
                    for c in range(2):
                        ps = pso[nb % 4]
                        nb += 1
                        for k in range(4):
                            kb.mm(ps[:], oT[j][:, k, :], wb[b][:, k, c * 512:(c + 1) * 512], k == 0, k == 3,
                                  r=[oT[j].name], w=[ps.name])
                        cs = slice(c * 512, (c + 1) * 512)
                        if b == 0:
                            kb.tt("dve", mrg[:, cs], ps[:], gt[j][:, cs], ALU.mult, r=[ps.name, gt[j].name], w=["mrg"])
                        else:
                            kb.tt("dve", tmp[:, cs], ps[:], gt[j][:, cs], ALU.mult, r=[ps.name, gt[j].name], w=["tmpm"])
                            kb.tt("pool", mrg[:, cs], mrg[:, cs], tmp[:, cs], ALU.add, r=["tmpm", "mrg"], w=["mrg"])
                kb.cp("act", mb[:], mrg[:], r=["mrg"], w=["mrgb"])
                self.transpose_bf(mb, 8, pst, idb, mT[:])
                j = t % 2
                kb.dma(xt[j][:], x_src[rows, :], w=[xt[j].name])
                for c in range(2):
                    ps = pso[nb % 4]
                    nb += 1
                    for k in range(8):
                        kb.mm(ps[:], mT[:, k, :], wo[:, k, c * 512:(c + 1) * 512], k == 0, k == 7, r=["mT"], w=[ps.name])
                    cs = slice(c * 512, (c + 1) * 512)
                    kb.tt("dve", xo[j][:, cs], ps[:], xt[j][:, cs], ALU.add, r=[ps.name, xt[j].name], w=[xo[j].name])
                kb.dma(self.x_mid[l][rows, :], xo[j][:], r=[xo[j].name], q="pool")
            kb.flush()

    def phase_ffn(self, l):
        kb, S, NT = self.kb, self.S, self.NT
        with ExitStack() as st:
            w1 = self.load_weight_bf16(st, "w1_sb", self.w_ff1[l], 8, D_FF, 512)
            w2 = self.load_weight_bf16(st, "w2_sb", self.w_ff2[l], 32, D, 128)
            gb = self.sb(st, "gffn", [128, D])
            idb = self.sb(st, "idb", [128, 128], BF16)
            kb.dma(gb[:], self.norm_ffn_g[l:l + 1, :].to_broadcast([128, D]), w=["gffn"])
            kb.dma(idb[:], self.ident_bf[:, :], w=["idb"])
            xt = [self.sb(st, "xt%d" % j, [128, D]) for j in range(2)]
            junk = self.sb(st, "junk", [128, D])
            ss = [self.sb(st, "ss%d" % j, [128, 1]) for j in range(2)]
            rstd = [self.sb(st, "rstd%d" % j, [128, 1]) for j in range(2)]
            hb = [self.sb(st, "hb%d" % j, [128, D], BF16) for j in range(2)]
            hT = [self.sb(st, "hT%d" % j, [128, 8, 128], BF16) for j in range(2)]
            fr = [self.sb(st, "fr%d" % j, [128, 512]) for j in range(2)]
            fb = self.sb(st, "fb", [128, D_FF], BF16)
            fT = self.sb(st, "fT", [128, 32, 128], BF16)
            xo = [self.sb(st, "xo%d" % j, [128, D]) for j in range(2)]
            pst = self.psb(st, "ps_tr", BF16, 1024)
            pso = [self.psb(st, "ps_o%d" % j) for j in range(4)]
            nb = 0
            def prep(tt_, part):
                jj = tt_ % 2
                self.rms_to_hT(xt[jj], self.x_mid[l][tt_ * 128:(tt_ + 1) * 128, :], junk, ss[jj], rstd[jj], gb, hb[jj], pst,
                               idb, hT[jj], part=part)
            prep(0, None)
            for t in range(NT):
                j = t % 2
                rows = slice(t * 128, (t + 1) * 128)
                for c in range(8):
                    ps = pso[nb % 4]
                    nb += 1
                    for k in range(8):
                        kb.mm(ps[:], hT[j][:, k, :], w1[:, k, c * 512:(c + 1) * 512], k == 0, k == 7,
                              r=[hT[j].name], w=[ps.name])
                    kb.act(fr[c % 2][:], ps[:], AF.Relu, r=[ps.name], w=[fr[c % 2].name])
                    kb.tt("dve", fb[:, c * 512:(c + 1) * 512], fr[c % 2][:], fr[c % 2][:], ALU.mult,
                          r=[fr[c % 2].name], w=["fb:%d" % c])
                    if c >= 2 and c % 2 == 0:
                        self.transpose_bf_slice(fb, (c - 2) * 4, 8, pst, idb, fT, rd=["fb:%d" % (c - 2), "fb:%d" % (c - 1)])
                    if c == 3 and t + 1 < NT:
                        prep(t + 1, 0)
                self.transpose_bf_slice(fb, 24, 8, pst, idb, fT, rd=["fb:6", "fb:7"])
                for c in range(2):
                    ps = pso[nb % 4]
                    nb += 1
                    for k in range(32):
                        kb.mm(ps[:], fT[:, k, :], w2[:, k, c * 512:(c + 1) * 512], k == 0, k == 31, r=["fT"], w=[ps.name])
                    cs = slice(c * 512, (c + 1) * 512)
                    kb.tt("dve", xo[j][:, cs], ps[:], xt[j][:, cs], ALU.add, r=[ps.name, xt[j].name], w=[xo[j].name])
                kb.dma(self.x_lay[l][rows, :], xo[j][:], r=[xo[j].name], q="pool")
                if t + 1 < NT:
                    prep(t + 1, 1)
            kb.flush()

    def transpose_bf_slice(self, src_bf, k0, nk, pst, idb, dstT, rd=None):
        kb = self.kb
        for k in range(nk):
            kb.tr(pst[:, k * 128:(k + 1) * 128], src_bf[:, (k0 + k) * 128:(k0 + k + 1) * 128], idb[:],
                  r=(rd if rd is not None else [src_bf.name]) + [idb.name], w=[pst.name])
        kb.cp("act", dstT[:, k0:k0 + nk, :].rearrange("p k t -> p (k t)"), pst[:, 0:nk * 128],
              r=[pst.name], w=[dstT.name])

    def declare_gdn(self):
        L = self.L
        i = self.inp
        self.gdn_conv_w = i("gdn_conv_w", [L, 4, 3 * MIX])
        self.gdn_a_log = i("gdn_a_log", [L, 4])
        self.gdn_dt_bias = i("gdn_dt_bias", [L, 4])
        self.gdn_norm_w = i("gdn_norm_w", [L, 128])
        self.ones_f = i("ones_f", [128, 128])
        self.triu_incl = i("triu_incl", [128, 128])
        self.triu_strict = i("triu_strict", [128, 128])

    def next_ps(self):
        self.psn = getattr(self, "psn", -1) + 1
        return self.pss[self.psn % len(self.pss)]

    def tr_f32(self, dst, src, idf, eng="act"):
        kb = self.kb
        n = src.shape[-1] // 128
        ps = self.next_ps()
        for k in range(n):
            kb.tr(ps[:, k * 128:(k + 1) * 128], src[:, k * 128:(k + 1) * 128], idf[:],
                  r=[src.name, idf.name], w=[ps.name])
        kb.cp(eng, dst, ps[:, 0:n * 128], r=[ps.name], w=[dst.name])

    def phase_gdn(self, l, side=False):
        kb, S, NT = self.kb, self.S, self.NT
        H, N = 4, 128
        P = self.proj_tm[l]
        c0 = OFF_GDN
        with ExitStack() as st:
            sb = lambda name, shape, dt=F32: self.sb(st, name, shape, dt)
            self.pss = [self.psb(st, "ps_g%d" % j) for j in range(6 if side else 8)]
            gen = None
            if side:
                gen = self.gen_nsa_prep(l, st, self.psb(st, "ps_npA"), self.psb(st, "ps_npB"))
            hook = (lambda: next(gen, None)) if gen is not None else (lambda: None)
            idf = sb("idf", [128, 128]); ones = sb("ones", [128, 128])
            tin = sb("tin", [128, 128]); tst = sb("tst", [128, 128])
            kb.dma(idf[:], self.ident_f[:, :], w=[idf.name])
            kb.dma(ones[:], self.ones_f[:, :], w=[ones.name])
            kb.dma(tin[:], self.triu_incl[:, :], w=[tin.name])
            kb.dma(tst[:], self.triu_strict[:, :], w=[tst.name])
            cw = sb("cw", [128, 4, 1536])
            kb.dma(cw[:].rearrange("p k c -> p (k c)"),
                   self.gdn_conv_w[l:l + 1].rearrange("o k c -> o (k c)").to_broadcast([128, 4 * 1536]), w=[cw.name])
            alog = sb("alog", [128, 4]); dtb = sb("dtb", [128, 4]); nw = sb("nw", [128, 128])
            kb.dma(alog[:], self.gdn_a_log[l:l + 1, :].to_broadcast([128, 4]), w=[alog.name])
            kb.dma(dtb[:], self.gdn_dt_bias[l:l + 1, :].to_broadcast([128, 4]), w=[dtb.name])
            kb.dma(nw[:], self.gdn_norm_w[l:l + 1, :].to_broadcast([128, 128]), w=[nw.name])
            nea = sb("nea", [128, 4])
            kb.act(nea[:], alog[:], AF.Exp, r=[alog.name], w=[nea.name])
            kb.ts("dve", nea[:], nea[:], -1.0, None, ALU.mult, r=[nea.name], w=[nea.name])
            X = [sb("X%d" % k, [128, 1536]) for k in range(4)]
            zt = sb("zt", [128, 512]); ab = sb("ab", [128, 8])
            y = sb("y", [128, 1536]); t1 = sb("t1", [128, 1536])
            ssq = sb("ssq", [128, 8]); rs = sb("rs", [128, 8])
            qkn = sb("qkn", [128, 1024])
            beta = sb("beta", [128, 4]); g = sb("g", [128, 4]); G = sb("G", [128, 4]); Gt = sb("Gt", [128, 4])
            eG = sb("eG", [128, 4]); neG = sb("neG", [128, 4]); eGC = sb("eGC", [128, 4]); eGr = sb("eGr", [128, 4])
            kbm = sb("kbm", [128, 512]); kam = sb("kam", [128, 512]); qgm = sb("qgm", [128, 512])
            khat = sb("khat", [128, 512]); vp = sb("vp", [128, 512])
            kT = sb("kT", [128, 512]); kbT = sb("kbT", [128, 512]); qT = sb("qT", [128, 512])
            aT = sb("aT", [128, 512]); rT = sb("rT", [128, 512])
            dg = sb("dg", [128, 512]); gam = sb("gam", [128, 512])
            Nm = [sb("Nm%d" % k, [128, 512]) for k in range(2)]
            Lm = [sb("Lm%d" % k, [128, 512]) for k in range(2)]
            Pm = [sb("Pm%d" % k, [128, 512]) for k in range(2)]
            MT = sb("MT", [128, 512]); N0k = sb("N0k", [128, 512])
            X1 = sb("X1", [128, 512]); UV = sb("UV", [128, 512]); Y = sb("Y", [128, 512])
            T0 = sb("T0", [128, 512])
            o = sb("o", [128, 512]); sz = sb("sz", [128, 512]); ysq = sb("ysq", [128, 512])
            yss = sb("yss", [128, 4]); yrs = sb("yrs", [128, 4])
            kb.memset("dve", T0[:], 0.0, w=[T0.name])
            hv = lambda a: a.rearrange("p (h n) -> p h n", h=H)
            for t in range(NT):
                r0 = t * 128
                for k in range(4):
                    sh = 3 - k
                    if r0 - sh < 0:
                        kb.memset("pool", X[k][:], 0.0, w=[X[k].name])
                        kb.dma(X[k][sh:128, :], P[0:128 - sh, c0:c0 + 1536], w=[X[k].name])
                    else:
                        kb.dma(X[k][:], P[r0 - sh:r0 - sh + 128, c0:c0 + 1536], w=[X[k].name])
                kb.dma(zt[:], P[r0:r0 + 128, c0 + 1536:c0 + 2048], w=[zt.name])
                kb.dma(ab[:], P[r0:r0 + 128, c0 + 2048:c0 + 2056], w=[ab.name])
                kb.tt("dve", y[:], X[0][:], cw[:, 0, :], ALU.mult, r=[X[0].name, cw.name], w=[y.name])
                for k in range(1, 4):
                    kb.tt("pool", t1[:], X[k][:], cw[:, k, :], ALU.mult, r=[X[k].name, cw.name], w=[t1.name])
                    kb.tt("dve", y[:], y[:], t1[:], ALU.add, r=[y.name, t1.name], w=[y.name])
                kb.act(y[:], y[:], AF.Silu, r=[y.name], w=[y.name])
                kb.act(t1[:, 0:1024], y[:, 0:1024], AF.Square, r=[y.name], w=[t1.name])
                kb.add("dve", lambda e: e.tensor_reduce(ssq[:], t1[:, 0:1024].rearrange("p (h n) -> p h n", h=8), AX.X, ALU.add),
                       [t1.name], [ssq.name])
                self.rsqrt(rs[:], ssq[:], 1.0, EPS, [ssq.name], [rs.name])
                kb.ts("dve", rs[:, 0:4], rs[:, 0:4], float(N) ** -0.5, None, ALU.mult, r=[rs.name], w=[rs.name])
                kb.tt("dve", qkn[:].rearrange("p (h n) -> p h n", h=8), y[:, 0:1024].rearrange("p (h n) -> p h n", h=8),
                      rs[:].unsqueeze(2).to_broadcast([128, 8, 128]), ALU.mult, r=[y.name, rs.name], w=[qkn.name])
                qn = qkn[:, 0:512]; kn = qkn[:, 512:1024]; v = y[:, 1024:1536]
                kb.act(beta[:], ab[:, 4:8], AF.Sigmoid, r=[ab.name], w=[beta.name])
                kb.tt("dve", g[:], ab[:, 0:4], dtb[:], ALU.add, r=[ab.name, dtb.name], w=[g.name])
                kb.act(g[:], g[:], AF.Exp, r=[g.name], w=[g.name])
                kb.act(g[:], g[:], AF.Ln, r=[g.name], w=[g.name], bias=1.0)
                kb.tt("dve", g[:], g[:], nea[:], ALU.mult, r=[g.name, nea.name], w=[g.name])
                ps = self.next_ps()
                kb.mm(ps[:, 0:4], tin[:], g[:], True, True, r=[tin.name, g.name], w=[ps.name])
                kb.mm(ps[:, 4:8], ones[:], g[:], True, True, r=[ones.name, g.name], w=[ps.name])
                kb.cp("dve", G[:], ps[:, 0:4], r=[ps.name], w=[G.name])
                kb.cp("dve", Gt[:], ps[:, 4:8], r=[ps.name], w=[Gt.name])
                kb.act(eG[:], G[:], AF.Exp, r=[G.name], w=[eG.name])
                kb.act(eGC[:], Gt[:], AF.Exp, r=[Gt.name], w=[eGC.name])
                kb.tt("dve", eGr[:], Gt[:], G[:], ALU.subtract, r=[Gt.name, G.name], w=[eGr.name])
                kb.act(eGr[:], eGr[:], AF.Exp, r=[eGr.name], w=[eGr.name])
                bb = lambda a: a[:].unsqueeze(2).to_broadcast([128, 4, 128])
                kb.tt("dve", hv(kbm[:]), hv(kn), bb(beta), ALU.mult, r=[qkn.name, beta.name], w=[kbm.name])
                kb.tt("pool", hv(vp[:]), hv(v), bb(beta), ALU.mult, r=[y.name, beta.name], w=[vp.name])
                kb.tt("dve", hv(kam[:]), hv(kbm[:]), bb(eG), ALU.mult, r=[kbm.name, eG.name], w=[kam.name])
                kb.ts("pool", kam[:], kam[:], -1.0, None, ALU.mult, r=[kam.name], w=[kam.name])
                kb.tt("dve", hv(qgm[:]), hv(qn), bb(eG), ALU.mult, r=[qkn.name, eG.name], w=[qgm.name])
                kb.tt("pool", hv(khat[:]), hv(kn), bb(eGr), ALU.mult, r=[qkn.name, eGr.name], w=[khat.name])
                hook()
                self.tr_f32(kT[:], kn, idf); self.tr_f32(kbT[:], kbm[:], idf, "dve")
                self.tr_f32(qT[:], qn, idf); self.tr_f32(aT[:], kam[:], idf, "dve"); self.tr_f32(rT[:], qgm[:], idf)
                for h in range(H):
                    kb.ts("dve", dg[:, h * 128:(h + 1) * 128], idf[:], G[:, h:h + 1], None, ALU.mult,
                          r=[idf.name, G.name], w=[dg.name])
                ps = self.next_ps()
                for h in range(H):
                    kb.mm(ps[:, h * 128:(h + 1) * 128], ones[:], dg[:, h * 128:(h + 1) * 128], True, True,
                          r=[ones.name, dg.name], w=[ps.name])
                for h in range(H):
                    kb.ts("dve", gam[:, h * 128:(h + 1) * 128], ps[:, h * 128:(h + 1) * 128], G[:, h:h + 1], 0.0,
                          ALU.subtract, ALU.min, r=[ps.name, G.name], w=[gam.name])
                kb.act(gam[:], gam[:], AF.Exp, r=[gam.name], w=[gam.name])
                ps = self.next_ps()
                ps2 = self.next_ps()
                for h in range(H):
                    hs = slice(h * 128, (h + 1) * 128)
                    kb.mm(ps[:, hs], kT[:, hs], kbT[:, hs], True, True, r=[kT.name, kbT.name], w=[ps.name])
                    kb.mm(ps2[:, hs], kT[:, hs], qT[:, hs], True, True, r=[kT.name, qT.name], w=[ps2.name])
                N0 = Nm[0]
                kb.tt("dve", N0[:], ps[:], gam[:], ALU.mult, r=[ps.name, gam.name], w=[N0.name])
                kb.stt("dve", hv(N0[:]), hv(N0[:]), -1.0, tst[:].unsqueeze(1).to_broadcast([128, 4, 128]), ALU.mult, ALU.mult,
                       r=[N0.name, tst.name], w=[N0.name])
                kb.tt("dve", MT[:], ps2[:], gam[:], ALU.mult, r=[ps2.name, gam.name], w=[MT.name])
                kb.tt("pool", hv(MT[:]), hv(MT[:]), tin[:].unsqueeze(1).to_broadcast([128, 4, 128]), ALU.mult,
                      r=[MT.name, tin.name], w=[MT.name])
                kb.cp("pool", N0k[:], N0[:], r=[N0.name], w=[N0k.name])
                L0 = Lm[0]
                self.tr_f32(L0[:], N0[:], idf)
                P0 = Pm[0]
                kb.tt("dve", hv(P0[:]), hv(N0[:]), idf[:].unsqueeze(1).to_broadcast([128, 4, 128]), ALU.add,
                      r=[N0.name, idf.name], w=[P0.name])
                cur = 0
                for lev in range(6):
                    Nc, Lc, Pc = Nm[cur], Lm[cur], Pm[cur]
                    Nn, Ln, Pn = Nm[1 - cur], Lm[1 - cur], Pm[1 - cur]
                    psl = self.next_ps(); psn = self.next_ps()
                    for h in range(H):
                        hs = slice(h * 128, (h + 1) * 128)
                        kb.mm(psl[:, hs], Nc[:, hs], Lc[:, hs], True, True, r=[Nc.name, Lc.name], w=[psl.name])
                    if lev < 5:
                        for h in range(H):
                            hs = slice(h * 128, (h + 1) * 128)
                            kb.mm(psn[:, hs], Lc[:, hs], Nc[:, hs], True, True, r=[Nc.name, Lc.name], w=[psn.name])
                    kb.cp("act", Ln[:], psl[:], r=[psl.name], w=[Ln.name])
                    if lev < 5:
                        kb.cp("dve", Nn[:], psn[:], r=[psn.name], w=[Nn.name])
                    psp = self.next_ps()
                    for h in range(H):
                        hs = slice(h * 128, (h + 1) * 128)
                        kb.mm(psp[:, hs], Ln[:, hs], Pc[:, hs], True, True, r=[Ln.name, Pc.name], w=[psp.name])
                    kb.tt("dve", Pn[:], psp[:], Pc[:], ALU.add, r=[psp.name, Pc.name], w=[Pn.name])
                    cur = 1 - cur
                    hook()
                WT = Pm[cur]
                N0 = Nm[0]
                ps = self.next_ps()
                for h in range(H):
                    hs = slice(h * 128, (h + 1) * 128)
                    kb.mm(ps[:, hs], aT[:, hs], T0[:, hs], True, False, r=[aT.name, T0.name], w=[ps.name])
                    kb.mm(ps[:, hs], N0k[:, hs], vp[:, hs], False, True, r=[N0k.name, vp.name], w=[ps.name])
                kb.cp("act", X1[:], ps[:], r=[ps.name], w=[X1.name])
                ps = self.next_ps()
                for h in range(H):
                    hs = slice(h * 128, (h + 1) * 128)
                    kb.mm(ps[:, hs], WT[:, hs], X1[:, hs], True, True, r=[WT.name, X1.name], w=[ps.name])
                kb.tt("dve", UV[:], ps[:], vp[:], ALU.add, r=[ps.name, vp.name], w=[UV.name])
                ps = self.next_ps()
                ps2 = self.next_ps()
                for h in range(H):
                    hs = slice(h * 128, (h + 1) * 128)
                    kb.mm(ps[:, hs], rT[:, hs], T0[:, hs], True, False, r=[rT.name, T0.name], w=[ps.name])
                    kb.mm(ps[:, hs], MT[:, hs], UV[:, hs], False, True, r=[MT.name, UV.name], w=[ps.name])
                    kb.mm(ps2[:, hs], khat[:, hs], UV[:, hs], True, True, r=[khat.name, UV.name], w=[ps2.name])
                kb.cp("act", Y[:], ps[:], r=[ps.name], w=[Y.name])
                for h in range(H):
                    hs = slice(h * 128, (h + 1) * 128)
                    kb.stt("dve", T0[:, hs], T0[:, hs], eGC[:, h:h + 1], ps2[:, hs], ALU.mult, ALU.add,
                           r=[T0.name, eGC.name, ps2.name], w=[T0.name])
                kb.act(ysq[:], Y[:], AF.Square, r=[Y.name], w=[ysq.name])
                kb.add("dve", lambda e: e.tensor_reduce(yss[:], ysq[:].rearrange("p (h n) -> p h n", h=4), AX.X, ALU.add),
                       [ysq.name], [yss.name])
                self.rsqrt(yrs[:], yss[:], 1.0 / N, EPS, [yss.name], [yrs.name])
                kb.act(sz[:], zt[:], AF.Silu, r=[zt.name], w=[sz.name])
                kb.tt("dve", hv(o[:]), hv(Y[:]), bb(yrs), ALU.mult, r=[Y.name, yrs.name], w=[o.name])
                kb.tt("pool", hv(o[:]), hv(o[:]), nw[:].unsqueeze(1).to_broadcast([128, 4, 128]), ALU.mult,
                      r=[o.name, nw.name], w=[o.name])
                kb.tt("dve", o[:], o[:], sz[:], ALU.mult, r=[o.name, sz.name], w=[o.name])
                kb.dma(self.o_mix[l][2][r0:r0 + 128, :], o[:], r=[o.name], q="pool")
                hook(); hook()
            if gen is not None:
                for _ in gen:
                    pass
            kb.flush()

    def declare_rwkv(self):
        L, S = self.L, self.S
        i = self.inp
        for nm, shp in (("rwkv_mu", [L, RWKV_IN]), ("rwkv_w0", [L, MIX]), ("rwkv_w_up", [L, 64, MIX]),
                        ("rwkv_a0", [L, MIX]), ("rwkv_a_up", [L, 64, MIX]), ("rwkv_g_up", [L, 128, MIX]),
                        ("rwkv_k_k", [L, MIX]), ("rwkv_k_a", [L, MIX]), ("rwkv_r_k", [L, MIX]),
                        ("rwkv_ln_w", [L, MIX]), ("rwkv_ln_b", [L, MIX]), ("rwkv_v0", [1, MIX]),
                        ("rwkv_vres_up", [1, 32, MIX])):
            setattr(self, nm, i(nm, shp))
        self.vfirst = self.scr("vfirst", [S, MIX])

    def phase_rwkv(self, l):
        kb, S, NT = self.kb, self.S, self.NT
        H, N = 8, 64
        P = self.proj_tm[l]
        c0 = OFF_RWKV
        with ExitStack() as st:
            sb = lambda name, shape, dt=F32: self.sb(st, name, shape, dt)
            self.pss = [self.psb(st, "ps_r%d" % j) for j in range(8)]
            idf = sb("idf", [128, 128]); ones = sb("ones", [128, 128])
            tin = sb("tin", [128, 128]); tst = sb("tst", [128, 128])
            kb.dma(idf[:], self.ident_f[:, :], w=[idf.name])
            kb.dma(ones[:], self.ones_f[:, :], w=[ones.name])
            kb.dma(tin[:], self.triu_incl[:, :], w=[tin.name])
            kb.dma(tst[:], self.triu_strict[:, :], w=[tst.name])

            def bvec(name, src, n):
                tl = sb(name, [128, n])
                kb.dma(tl[:], src[l:l + 1, :].to_broadcast([128, n]), w=[tl.name])
                return tl
            mu = bvec("mu", self.rwkv_mu, RWKV_IN)
            w0 = bvec("w0", self.rwkv_w0, MIX); a0 = bvec("a0", self.rwkv_a0, MIX)
            k_k = bvec("k_k", self.rwkv_k_k, MIX); k_a = bvec("k_a", self.rwkv_k_a, MIX)
            r_k = bvec("r_k", self.rwkv_r_k, MIX); ln_w = bvec("ln_w", self.rwkv_ln_w, MIX)
            ln_b = bvec("ln_b", self.rwkv_ln_b, MIX)
            wup = sb("wup", [64, MIX]); aup = sb("aup", [64, MIX]); gup = sb("gup", [128, MIX])
            kb.dma(wup[:], self.rwkv_w_up[l], w=[wup.name])
            kb.dma(aup[:], self.rwkv_a_up[l], w=[aup.name])
            kb.dma(gup[:], self.rwkv_g_up[l], w=[gup.name])
            if l > 0:
                v0 = sb("v0", [128, MIX]); vup = sb("vup", [32, MIX])
                kb.dma(v0[:], self.rwkv_v0[0:1, :].to_broadcast([128, MIX]), w=[v0.name])
                kb.dma(vup[:], self.rwkv_vres_up[0], w=[vup.name])
                vf = sb("vf", [128, MIX]); xvd = sb("xvd", [128, 32]); xvdT = sb("xvdT", [32, 128])
            X0 = sb("X0", [128, RWKV_IN]); X1s = sb("X1s", [128, RWKV_IN]); c = sb("c", [128, RWKV_IN])
            sm = sb("sm", [128, 256]); smT = sb("smT", [128, 256])
            wdT = sb("wdT", [64, 128]); adT = sb("adT", [64, 128])
            ld = sb("ld", [128, MIX]); a = sb("a", [128, MIX]); g = sb("g", [128, MIX])
            kk = sb("kk", [128, MIX]); t1 = sb("t1", [128, MIX]); t2 = sb("t2", [128, MIX])
            ssq = sb("ssq", [128, 8]); rs = sb("rs", [128, 8])
            kp = sb("kp", [128, MIX]); bet = sb("bet", [128, MIX])
            G = sb("G", [128, MIX]); eG = sb("eG", [128, MIX]); enG = sb("enG", [128, MIX])
            eGr = sb("eGr", [128, MIX]); eGm = sb("eGm", [128, MIX])
            at = sb("at", [128, MIX]); bt = sb("bt", [128, MIX]); kt = sb("kt", [128, MIX]); rt = sb("rt", [128, MIX])
            bh = sb("bh", [128, MIX]); kh = sb("kh", [128, MIX])
            aT = sb("aT", [64, 1024]); bT = sb("bT", [64, 1024]); kT = sb("kT", [64, 1024]); rT = sb("rT", [64, 1024])
            PCT = sb("PCT", [64, 8])
            Nm = [sb("Nm%d" % k, [128, 1024]) for k in range(2)]
            Lm = [sb("Lm%d" % k, [128, 1024]) for k in range(2)]
            Pm = [sb("Pm%d" % k, [128, 1024]) for k in range(2)]
            LakT = sb("LakT", [128, 1024]); MrbT = sb("MrbT", [128, 1024]); MrkT = sb("MrkT", [128, 1024])
            X1 = sb("X1", [128, MIX]); U = sb("U", [128, MIX]); Y = sb("Y", [128, MIX])
            T0 = sb("T0", [64, MIX])
            mean = sb("mean", [128, 8]); var = sb("var", [128, 8]); rkv = sb("rkv", [128, 8])
            kb.memset("dve", T0[:], 0.0, w=[T0.name])
            hv = lambda x: x.rearrange("p (h n) -> p h n", h=H)
            b8 = lambda x: x[:].unsqueeze(2).to_broadcast([128, 8, 64])
            msk = lambda m: m[:].unsqueeze(1).to_broadcast([128, 8, 128])
            hm = lambda x: x.rearrange("p (h n) -> p h n", h=H)

            def tr64(dst, src):
                for half in range(2):
                    ps = self.next_ps()
                    for hh in range(4):
                        h = half * 4 + hh
                        kb.tr(ps[0:64, hh * 128:(hh + 1) * 128], src[:, h * 64:(h + 1) * 64], idf[:],
                              r=[src.name, idf.name], w=[ps.name])
                    kb.cp("act" if half else "dve", dst[:, half * 512:(half + 1) * 512], ps[0:64, :],
                          r=[ps.name], w=[dst.name])

            def mat8(dst, lT, rT_, mask):
                for half in range(2):
                    ps = self.next_ps()
                    for hh in range(4):
                        h = half * 4 + hh
                        kb.mm(ps[:, hh * 128:(hh + 1) * 128], lT[:, h * 128:(h + 1) * 128], rT_[:, h * 128:(h + 1) * 128],
                              True, True, r=[lT.name, rT_.name], w=[ps.name])
                    kb.tt("dve", dst[:, half * 512:(half + 1) * 512].rearrange("p (h n) -> p h n", h=4),
                          ps[:].rearrange("p (h n) -> p h n", h=4),
                          mask[:].unsqueeze(1).to_broadcast([128, 4, 128]), ALU.mult,
                          r=[ps.name, mask.name], w=[dst.name])

            for t in range(NT):
                r0 = t * 128
                kb.dma(X0[:], P[r0:r0 + 128, c0:c0 + RWKV_IN], w=[X0.name])
                if t == 0:
                    kb.memset("pool", X1s[:], 0.0, w=[X1s.name])
                    kb.dma(X1s[1:128, :], P[0:127, c0:c0 + RWKV_IN], w=[X1s.name])
                else:
                    kb.dma(X1s[:], P[r0 - 1:r0 + 127, c0:c0 + RWKV_IN], w=[X1s.name])
                kb.tt("dve", c[:], X1s[:], X0[:], ALU.subtract, r=[X0.name, X1s.name], w=[c.name])
                kb.tt("pool", c[:], c[:], mu[:], ALU.mult, r=[c.name, mu.name], w=[c.name])
                kb.tt("dve", c[:], c[:], X0[:], ALU.add, r=[c.name, X0.name], w=[c.name])
                r_ = c[:, 0:512]; k_ = c[:, 512:1024]; v_ = c[:, 1024:1536]
                kb.act(sm[:, 0:64], c[:, 1536:1600], AF.Tanh, r=[c.name], w=[sm.name])
                kb.cp("dve", sm[:, 64:128], c[:, 1600:1664], r=[c.name], w=[sm.name])
                kb.act(sm[:, 128:256], c[:, 1664:1792], AF.Sigmoid, r=[c.name], w=[sm.name])
                ps = self.next_ps()
                kb.tr(ps[0:64, 0:128], sm[:, 0:64], idf[:], r=[sm.name, idf.name], w=[ps.name])
                kb.tr(ps[0:64, 128:256], sm[:, 64:128], idf[:], r=[sm.name, idf.name], w=[ps.name])
                kb.tr(ps[:, 256:384], sm[:, 128:256], idf[:], r=[sm.name, idf.name], w=[ps.name])
                kb.cp("act", wdT[:], ps[0:64, 0:128], r=[ps.name], w=[wdT.name])
                kb.cp("dve", adT[:], ps[0:64, 128:256], r=[ps.name], w=[adT.name])
                kb.cp("act", smT[:, 0:128], ps[:, 256:384], r=[ps.name], w=[smT.name])
                psw = self.next_ps(); psa = self.next_ps(); psg = self.next_ps()
                kb.mm(psw[:], wdT[:], wup[:], True, True, r=[wdT.name, wup.name], w=[psw.name])
                kb.mm(psa[:], adT[:], aup[:], True, True, r=[adT.name, aup.name], w=[psa.name])
                kb.mm(psg[:], smT[:, 0:128], gup[:], True, True, r=[smT.name, gup.name], w=[psg.name])
                kb.tt("dve", ld[:], psw[:], w0[:], ALU.add, r=[psw.name, w0.name], w=[ld.name])
                kb.act(ld[:], ld[:], AF.Sigmoid, r=[ld.name], w=[ld.name])
                kb.ts("dve", ld[:], ld[:], -float(np.exp(-0.5)), None, ALU.mult, r=[ld.name], w=[ld.name])
                kb.tt("dve", a[:], psa[:], a0[:], ALU.add, r=[psa.name, a0.name], w=[a.name])
                kb.act(a[:], a[:], AF.Sigmoid, r=[a.name], w=[a.name])
                kb.cp("act", g[:], psg[:], r=[psg.name], w=[g.name])
                if l == 0:
                    kb.dma(self.vfirst[r0:r0 + 128, :], v_, r=[c.name], q="pool")
                else:
                    kb.dma(vf[:], self.vfirst[r0:r0 + 128, :], w=[vf.name])
                    kb.dma(xvd[:], self.proj_g[l][r0:r0 + 128, 3 * D:3 * D + 32], w=[xvd.name])
                    ps = self.next_ps()
                    kb.tr(ps[0:32, 0:128], xvd[:], idf[:], r=[xvd.name, idf.name], w=[ps.name])
                    kb.cp("act", xvdT[:], ps[0:32, 0:128], r=[ps.name], w=[xvdT.name])
                    ps = self.next_ps()
                    kb.mm(ps[:], xvdT[:], vup[:], True, True, r=[xvdT.name, vup.name], w=[ps.name])
                    kb.tt("dve", t1[:], ps[:], v0[:], ALU.add, r=[ps.name, v0.name], w=[t1.name])
                    kb.act(t1[:], t1[:], AF.Sigmoid, r=[t1.name], w=[t1.name])
                    kb.tt("dve", t2[:], vf[:], v_, ALU.subtract, r=[vf.name, c.name], w=[t2.name])
                    kb.tt("dve", t2[:], t2[:], t1[:], ALU.mult, r=[t2.name, t1.name], w=[t2.name])
                    kb.tt("dve", v_, v_, t2[:], ALU.add, r=[c.name, t2.name], w=[c.name])
                kb.tt("dve", kk[:], k_, k_k[:], ALU.mult, r=[c.name, k_k.name], w=[kk.name])
                kb.act(t1[:], kk[:], AF.Square, r=[kk.name], w=[t1.name])
                kb.add("dve", lambda e: e.tensor_reduce(ssq[:], t1[:].rearrange("p (h n) -> p h n", h=8), AX.X, ALU.add),
                       [t1.name], [ssq.name])
                self.rsqrt(rs[:], ssq[:], 1.0, EPS, [ssq.name], [rs.name])
                kb.tt("dve", hv(kk[:]), hv(kk[:]), b8(rs), ALU.mult, r=[kk.name, rs.name], w=[kk.name])
                kb.ts("dve", t2[:], a[:], -1.0, None, ALU.add, r=[a.name], w=[t2.name])
                kb.tt("dve", t2[:], t2[:], k_a[:], ALU.mult, r=[t2.name, k_a.name], w=[t2.name])
                kb.ts("dve", t2[:], t2[:], 1.0, None, ALU.add, r=[t2.name], w=[t2.name])
                kb.tt("dve", kp[:], t2[:], k_, ALU.mult, r=[t2.name, c.name], w=[kp.name])
                kb.tt("pool", bet[:], kk[:], a[:], ALU.mult, r=[kk.name, a.name], w=[bet.name])
                ps = self.next_ps(); ps2 = self.next_ps()
                kb.mm(ps[:], tin[:], ld[:], True, True, r=[tin.name, ld.name], w=[ps.name])
                kb.mm(ps2[:], ones[:], ld[:], True, True, r=[ones.name, ld.name], w=[ps2.name])
                kb.cp("dve", G[:], ps[:], r=[ps.name], w=[G.name])
                kb.tt("dve", eGr[:], ps2[:], G[:], ALU.subtract, r=[ps2.name, G.name], w=[eGr.name])
                kb.act(eGr[:], eGr[:], AF.Exp, r=[eGr.name], w=[eGr.name])
                kb.act(eG[:], G[:], AF.Exp, r=[G.name], w=[eG.name])
                kb.act(enG[:], G[:], AF.Exp, r=[G.name], w=[enG.name], scale=-1.0)
                kb.tt("pool", eGm[:], G[:], ld[:], ALU.subtract, r=[G.name, ld.name], w=[eGm.name])
                kb.act(eGm[:], eGm[:], AF.Exp, r=[eGm.name], w=[eGm.name])
                ps = self.next_ps()
                for h in range(H):
                    kb.mm(ps[0:64, h:h + 1], ld[:, h * 64:(h + 1) * 64], ones[:, 0:1], True, True,
                          r=[ld.name, ones.name], w=[ps.name])
                kb.act(PCT[:], ps[0:64, 0:8], AF.Exp, r=[ps.name], w=[PCT.name])
                kb.tt("dve", at[:], kk[:], eGm[:], ALU.mult, r=[kk.name, eGm.name], w=[at.name])
                kb.ts("pool", at[:], at[:], -1.0, None, ALU.mult, r=[at.name], w=[at.name])
                kb.tt("dve", bt[:], bet[:], enG[:], ALU.mult, r=[bet.name, enG.name], w=[bt.name])
                kb.tt("pool", kt[:], kp[:], enG[:], ALU.mult, r=[kp.name, enG.name], w=[kt.name])
                kb.tt("dve", rt[:], r_, eG[:], ALU.mult, r=[c.name, eG.name], w=[rt.name])
                kb.tt("pool", bh[:], bet[:], eGr[:], ALU.mult, r=[bet.name, eGr.name], w=[bh.name])
                kb.tt("dve", kh[:], kp[:], eGr[:], ALU.mult, r=[kp.name, eGr.name], w=[kh.name])
                tr64(aT, at); tr64(bT, bt); tr64(kT, kt); tr64(rT, rt)
                N0 = Nm[0]
                mat8(N0, bT, aT, tst)
                mat8(LakT, kT, aT, tst)
                mat8(MrbT, bT, rT, tin)
                mat8(MrkT, kT, rT, tin)
                L0 = Lm[0]
                for half in range(2):
                    self.tr_f32(L0[:, half * 512:(half + 1) * 512], N0[:, half * 512:(half + 1) * 512], idf)
                P0 = Pm[0]
                kb.tt("dve", hm(P0[:]), hm(N0[:]), msk(idf), ALU.add, r=[N0.name, idf.name], w=[P0.name])
                cur = 0
                for lev in range(6):
                    Nc, Lc, Pc = Nm[cur], Lm[cur], Pm[cur]
                    Nn, Ln, Pn = Nm[1 - cur], Lm[1 - cur], Pm[1 - cur]
                    for half in range(2):
                        hsl = slice(half * 512, (half + 1) * 512)
                        psl = self.next_ps()
                        for hh in range(4):
                            hs = slice(half * 512 + hh * 128, half * 512 + (hh + 1) * 128)
                            kb.mm(psl[:, hh * 128:(hh + 1) * 128], Nc[:, hs], Lc[:, hs], True, True,
                                  r=[Nc.name, Lc.name], w=[psl.name])
                        kb.cp("act", Ln[:, hsl], psl[:], r=[psl.name], w=[Ln.name])
                        if lev < 5:
                            psn = self.next_ps()
                            for hh in range(4):
                                hs = slice(half * 512 + hh * 128, half * 512 + (hh + 1) * 128)
                                kb.mm(psn[:, hh * 128:(hh + 1) * 128], Lc[:, hs], Nc[:, hs], True, True,
                                      r=[Nc.name, Lc.name], w=[psn.name])
                            kb.cp("dve", Nn[:, hsl], psn[:], r=[psn.name], w=[Nn.name])
                    for half in range(2):
                        hsl = slice(half * 512, (half + 1) * 512)
                        psp = self.next_ps()
                        for hh in range(4):
                            hs = slice(half * 512 + hh * 128, half * 512 + (hh + 1) * 128)
                            kb.mm(psp[:, hh * 128:(hh + 1) * 128], Ln[:, hs], Pc[:, hs], True, True,
                                  r=[Ln.name, Pc.name], w=[psp.name])
                        kb.tt("dve", Pn[:, hsl], psp[:], Pc[:, hsl], ALU.add, r=[psp.name, Pc.name], w=[Pn.name])
                    cur = 1 - cur
                WT = Pm[cur]
                ps = self.next_ps()
                for h in range(H):
                    hs = slice(h * 64, (h + 1) * 64); ms = slice(h * 128, (h + 1) * 128)
                    kb.mm(ps[:, hs], aT[:, ms], T0[:, hs], True, False, r=[aT.name, T0.name], w=[ps.name])
                    kb.mm(ps[:, hs], LakT[:, ms], c[:, 1024 + h * 64:1024 + (h + 1) * 64], False, True,
                          r=[LakT.name, c.name], w=[ps.name])
                kb.cp("act", X1[:], ps[:], r=[ps.name], w=[X1.name])
                ps = self.next_ps()
                for h in range(H):
                    hs = slice(h * 64, (h + 1) * 64); ms = slice(h * 128, (h + 1) * 128)
                    kb.mm(ps[:, hs], WT[:, ms], X1[:, hs], True, True, r=[WT.name, X1.name], w=[ps.name])
                kb.cp("dve", U[:], ps[:], r=[ps.name], w=[U.name])
                ps = self.next_ps(); ps2 = self.next_ps()
                for h in range(H):
                    hs = slice(h * 64, (h + 1) * 64); ms = slice(h * 128, (h + 1) * 128)
                    vs = c[:, 1024 + h * 64:1024 + (h + 1) * 64]
                    kb.mm(ps[:, hs], rT[:, ms], T0[:, hs], True, False, r=[rT.name, T0.name], w=[ps.name])
                    kb.mm(ps[:, hs], MrbT[:, ms], U[:, hs], False, False, r=[MrbT.name, U.name], w=[ps.name])
                    kb.mm(ps[:, hs], MrkT[:, ms], vs, False, True, r=[MrkT.name, c.name], w=[ps.name])
                    kb.mm(ps2[0:64, hs], bh[:, hs], U[:, hs], True, False, r=[bh.name, U.name], w=[ps2.name])
                    kb.mm(ps2[0:64, hs], kh[:, hs], vs, False, True, r=[kh.name, c.name], w=[ps2.name])
                kb.cp("act", Y[:], ps[:], r=[ps.name], w=[Y.name])
                for h in range(H):
                    hs = slice(h * 64, (h + 1) * 64)
                    kb.stt("dve", T0[:, hs], T0[:, hs], PCT[:, h:h + 1], ps2[0:64, hs], ALU.mult, ALU.add,
                           r=[T0.name, PCT.name, ps2.name], w=[T0.name])
                kb.add("dve", lambda e: e.tensor_reduce(mean[:], Y[:].rearrange("p (h n) -> p h n", h=8), AX.X, ALU.add),
                       [Y.name], [mean.name])
                kb.ts("dve", mean[:], mean[:], 1.0 / N, None, ALU.mult, r=[mean.name], w=[mean.name])
                kb.tt("dve", hv(Y[:]), hv(Y[:]), b8(mean), ALU.subtract, r=[Y.name, mean.name], w=[Y.name])
                kb.act(t1[:], Y[:], AF.Square, r=[Y.name], w=[t1.name])
                kb.add("dve", lambda e: e.tensor_reduce(var[:], t1[:].rearrange("p (h n) -> p h n", h=8), AX.X, ALU.add),
                       [t1.name], [var.name])
                self.rsqrt(var[:], var[:], 1.0 / N, 64e-5, [var.name], [var.name])
                kb.tt("dve", hv(Y[:]), hv(Y[:]), b8(var), ALU.mult, r=[Y.name, var.name], w=[Y.name])
                kb.tt("pool", Y[:], Y[:], ln_w[:], ALU.mult, r=[Y.name, ln_w.name], w=[Y.name])
                kb.tt("dve", Y[:], Y[:], ln_b[:], ALU.add, r=[Y.name, ln_b.name], w=[Y.name])
                kb.tt("pool", t2[:], r_, kp[:], ALU.mult, r=[c.name, kp.name], w=[t2.name])
                kb.tt("pool", t2[:], t2[:], r_k[:], ALU.mult, r=[t2.name, r_k.name], w=[t2.name])
                kb.add("dve", lambda e: e.tensor_reduce(rkv[:], t2[:].rearrange("p (h n) -> p h n", h=8), AX.X, ALU.add),
                       [t2.name], [rkv.name])
                kb.tt("dve", hv(t2[:]), hv(v_), b8(rkv), ALU.mult, r=[c.name, rkv.name], w=[t2.name])
                kb.tt("dve", Y[:], Y[:], t2[:], ALU.add, r=[Y.name, t2.name], w=[Y.name])
                kb.tt("dve", Y[:], Y[:], g[:], ALU.mult, r=[Y.name, g.name], w=[Y.name])
                kb.dma(self.o_mix[l][1][r0:r0 + 128, :], Y[:], r=[Y.name], q="pool")
            kb.flush()

    def build(self):
        self.declare(); self.declare_rest(); self.declare_gdn(); self.declare_rwkv(); self.declare_nsa()
        x = self.x_in
        for l in range(self.L):
            self.phase_proj(l, x)
            self.phase_gdn(l, side=True)
            self.phase_nsa_attn(l)
            self.phase_rwkv(l)
            self.phase_merge(l, x)
            self.phase_ffn(l)
            x = self.x_lay[l]
        return self.nc


def _consts():
    import ml_dtypes
    f = np.float32
    return {"ident_bf": np.eye(128, dtype=ml_dtypes.bfloat16), "ident_f": np.eye(128, dtype=f),
            "ones_f": np.ones((128, 128), f), "triu_incl": np.triu(np.ones((128, 128), f)),
            "triu_strict": np.triu(np.ones((128, 128), f), 1)}


def kernel(**inputs):
    x = np.asarray(inputs["x"], np.float32)
    B, S, _ = x.shape
    L = inputs["w_in"].shape[0]
    prog = Prog(S, n_layers=L)
    nc = prog.build()
    w_in = np.asarray(inputs["w_in"], np.float32)
    ext = np.zeros((L, D, 32), np.float32)
    ext[1:] = np.asarray(inputs["rwkv_vres_down"], np.float32)
    shared = dict(_consts())
    shared.update(_nsa_consts(S))
    shared["w_in"] = np.ascontiguousarray(np.concatenate([w_in, ext], axis=2))
    for k in prog.din:
        if k not in shared and k != "x":
            shared[k] = np.ascontiguousarray(np.asarray(inputs[k], np.float32))
    in_maps = [dict(shared, x=np.ascontiguousarray(x[b])) for b in range(B)]
    res = run_bass_kernel_spmd(nc, in_maps, core_ids=list(range(B)))
    return np.stack([np.asarray(r["out"], np.float32) for r in res.results], axis=0)


def _nsa_consts(S):
    import ml_dtypes
    f = np.float32
    NB = S // 64
    n_cmp = (S - 32) // 16 + 1
    nch = (n_cmp + 127) // 128
    ex = np.zeros((128, S), f)
    ex[np.arange(S) // 64, np.arange(S)] = 1.0
    k = np.arange(128)[:, None]
    q = np.arange(128)[None, :]
    caus = np.where(k > q, NEGM, 0.0).astype(f)
    win = np.where(k <= q, NEGM, 0.0).astype(f)
    rr = np.arange(17)[None, :, None]
    cm = ((16 * k[:, :, None] + 31) <= (128 * rr + q[:, None, :])).astype(f)
    cs = np.arange(nch * 128) * 16
    ss = np.arange(NB) * 64
    ov = np.clip(np.minimum(cs[:, None] + 32, ss[None, :] + 64) - np.maximum(cs[:, None], ss[None, :]), 0, None) / 32.0
    ov[n_cmp:] = 0.0
    c2s = ov.reshape(nch, 128, NB).transpose(1, 0, 2).astype(f)
    return {"exall": ex.astype(ml_dtypes.bfloat16),
            "causneg": np.tile(caus, (1, 4)).astype(ml_dtypes.bfloat16),
            "winneg": np.tile(win, (1, 4)).astype(ml_dtypes.bfloat16),
            "cmask": np.ascontiguousarray(cm), "cmp2slc": np.ascontiguousarray(c2s)}


def _declare_nsa(self):
    L, S = self.L, self.S
    i = self.inp
    self.NB = S // 64
    self.n_cmp = (S - 32) // 16 + 1
    self.nch = (self.n_cmp + 127) // 128
    self.nsa_q_norm = i("nsa_q_norm", [L, 64])
    self.nsa_k_norm = i("nsa_k_norm", [L, 3, 64])
    self.nsa_cmp_pos = i("nsa_cmp_pos", [L, 2, 32, 64])
    self.nsa_cmp_w1 = i("nsa_cmp_w1", [L, 2, 2048, 256])
    self.nsa_cmp_w2 = i("nsa_cmp_w2", [L, 2, 256, 64])
    self.exall = i("exall", [128, S], BF16)
    self.causneg = i("causneg", [128, 512], BF16)
    self.winneg = i("winneg", [128, 512], BF16)
    self.cmask = i("cmask", [128, 17, 128])
    self.cmp2slc = i("cmp2slc", [128, self.nch, self.NB])
    self.qT_d = self.scr("qT_d", [8, 64, S])
    self.kswT_d = self.scr("kswT_d", [4, 64, S], BF16)
    self.kvcT_d = self.scr("kvcT_d", [4, 64, S])
    self.vaug_d = self.scr("vaug_d", [S, 4 * 65], BF16)


def _gen_nsa_prep(self, l, st, pA, pB):
    kb, S, NT = self.kb, self.S, self.NT
    P = self.proj_tm[l]
    sb = lambda name, shape, dt=F32: self.sb(st, "np_" + name, shape, dt)
    idf = sb("idf", [128, 128])
    kb.dma(idf[:], self.ident_f[:, :], w=[idf.name])
    gq = sb("gq", [128, 12, 64])
    for h in range(8):
        kb.dma(gq[:, h, :], self.nsa_q_norm[l:l + 1, :].to_broadcast([128, 64]), w=[gq.name])
    for h in range(2):
        kb.dma(gq[:, 8 + h, :], self.nsa_k_norm[l, 1:2, :].to_broadcast([128, 64]), w=[gq.name])
        kb.dma(gq[:, 10 + h, :], self.nsa_k_norm[l, 2:3, :].to_broadcast([128, 64]), w=[gq.name])
    xin = [sb("xin%d" % j, [128, NSA_IN]) for j in range(2)]
    nin = sb("nin", [128, 768]); sq = sb("sq", [128, 768]); ss = sb("ss", [128, 12]); rs = sb("rs", [128, 12])
    qo = [sb("qo%d" % j, [64, 8, 128]) for j in range(2)]
    ko = [sb("ko%d" % j, [64, 4, 128], BF16) for j in range(2)]
    co = [sb("co%d" % j, [64, 4, 128]) for j in range(2)]
    va = [sb("va%d" % j, [128, 4, 65], BF16) for j in range(2)]
    for j in range(2):
        kb.memset("dve", va[j][:], 1.0, w=[va[j].name])
    h12 = lambda x: x.rearrange("p (h n) -> p h n", h=12)
    yield
    for t in range(NT):
        j = t % 2
        rows = slice(t * 128, (t + 1) * 128)
        x = xin[j]
        kb.dma(x[:], P[rows, 0:NSA_IN], w=[x.name])
        kb.cp("pool", nin[:, 0:512], x[:, 0:512], r=[x.name], w=[nin.name])
        kb.cp("pool", nin[:, 512:640], x[:, 768:896], r=[x.name], w=[nin.name])
        kb.cp("pool", nin[:, 640:768], x[:, 1024:1152], r=[x.name], w=[nin.name])
        kb.act(sq[:], nin[:], AF.Square, r=[nin.name], w=[sq.name])
        yield
        kb.add("dve", lambda e: e.tensor_reduce(ss[:], sq[:].rearrange("p (h n) -> p h n", h=12), AX.X, ALU.add),
               [sq.name], [ss.name])
        self.rsqrt(rs[:], ss[:], 1.0 / 64, EPS, [ss.name], [rs.name])
        yield
        kb.tt("dve", h12(nin[:]), h12(nin[:]), rs[:].unsqueeze(2).to_broadcast([128, 12, 64]), ALU.mult,
              r=[nin.name, rs.name], w=[nin.name])
        kb.tt("pool", nin[:], nin[:], gq[:].rearrange("p h n -> p (h n)"), ALU.mult, r=[nin.name, gq.name], w=[nin.name])
        yield
        for half in range(2):
            for blk in range(4):
                b8 = half * 4 + blk
                kb.tr(pA[0:64, blk * 128:(blk + 1) * 128], nin[:, b8 * 64:(b8 + 1) * 64], idf[:],
                      r=[nin.name, idf.name], w=[pA.name])
            kb.cp("act" if half else "dve", qo[j][:, half * 4:half * 4 + 4, :].rearrange("p h t -> p (h t)"), pA[0:64, :],
                  r=[pA.name], w=[qo[j].name])
            yield
        for blk in range(4):
            kb.tr(pB[0:64, blk * 128:(blk + 1) * 128], nin[:, 512 + blk * 64:512 + (blk + 1) * 64], idf[:],
                  r=[nin.name, idf.name], w=[pB.name])
        kb.cp("act", ko[j][:].rearrange("p h t -> p (h t)"), pB[0:64, :], r=[pB.name], w=[ko[j].name])
        yield
        for blk in range(4):
            kb.tr(pB[0:64, blk * 128:(blk + 1) * 128], x[:, 512 + blk * 64:512 + (blk + 1) * 64], idf[:],
                  r=[x.name, idf.name], w=[pB.name])
        kb.cp("dve", co[j][:].rearrange("p h t -> p (h t)"), pB[0:64, :], r=[pB.name], w=[co[j].name])
        yield
        kb.dma(self.qT_d[:, :, rows].rearrange("h d t -> d h t"), qo[j][:], r=[qo[j].name], q="pool")
        kb.dma(self.kswT_d[:, :, rows].rearrange("h d t -> d h t"), ko[j][:], r=[ko[j].name], q="pool")
        kb.dma(self.kvcT_d[:, :, rows].rearrange("h d t -> d h t"), co[j][:], r=[co[j].name], q="pool")
        kb.cp("pool", va[j][:, 0:2, 0:64], x[:, 896:1024].rearrange("p (g n) -> p g n", g=2), r=[x.name], w=[va[j].name])
        kb.cp("pool", va[j][:, 2:4, 0:64], x[:, 1152:1280].rearrange("p (g n) -> p g n", g=2), r=[x.name], w=[va[j].name])
        kb.dma(self.vaug_d[rows, :], va[j][:].rearrange("p g n -> p (g n)"), r=[va[j].name], q="pool")
        yield


def _phase_nsa_prep(self, l):
    with ExitStack() as st:
        pA = self.psb(st, "ps_npA"); pB = self.psb(st, "ps_npB")
        for _ in self.gen_nsa_prep(l, st, pA, pB):
            pass
        self.kb.flush()


Prog.gen_nsa_prep = _gen_nsa_prep
Prog.declare_nsa = _declare_nsa
Prog.phase_nsa_prep = _phase_nsa_prep


def _phase_nsa_attn(self, l):
    kb, S, NT, NB, n_cmp, nch = self.kb, self.S, self.NT, self.NB, self.n_cmp, self.nch
    P = self.proj_tm[l]
    SC = 0.125
    with ExitStack() as st:
        sb = lambda name, shape, dt=F32: self.sb(st, name, shape, dt)
        psS = [self.psb(st, "ps_S%d" % j) for j in range(2)]
        psO = [self.psb(st, "ps_O%d" % j) for j in range(2)]
        psI = self.psb(st, "ps_I"); psT = self.psb(st, "ps_T")
        psX = [self.psb(st, "ps_X%d" % j) for j in range(2)]
        idf = sb("idf", [128, 128]); idb = sb("idb", [128, 128], BF16); ones = sb("ones", [128, 128])
        kb.dma(idf[:], self.ident_f[:, :], w=[idf.name])
        kb.dma(idb[:], self.ident_bf[:, :], w=[idb.name])
        kb.dma(ones[:], self.ones_f[:, :], w=[ones.name])
        ks = sb("ks", [64, 2, S], BF16); kw = sb("kw", [64, 2, S], BF16)
        kb.dma(ks[:], self.kswT_d[0:2].rearrange("g d s -> d g s"), w=[ks.name])
        kb.dma(kw[:], self.kswT_d[2:4].rearrange("g d s -> d g s"), w=[kw.name])
        vv = sb("vv", [128, NT, 4 * 65], BF16)
        kb.dma(vv[:], self.vaug_d.rearrange("(c p) n -> p c n", p=128), w=[vv.name])
        ex = sb("ex", [128, S], BF16)
        kb.dma(ex[:], self.exall[:, :], w=[ex.name])
        cneg = sb("cneg", [128, 512], BF16); wneg = sb("wneg", [128, 512], BF16)
        kb.dma(cneg[:], self.causneg[:, :], w=[cneg.name])
        kb.dma(wneg[:], self.winneg[:, :], w=[wneg.name])
        cmk = sb("cmk", [128, 17, 128]); c2s = sb("c2s", [128, nch, NB])
        kb.dma(cmk[:], self.cmask[:, :, :], w=[cmk.name])
        kb.dma(c2s[:], self.cmp2slc[:, :, :], w=[c2s.name])
        kcT = sb("kcT", [64, 2, nch * 128]); vc = sb("vc", [128, nch, 2, 65])
        kb.memset("dve", kcT[:], 0.0, w=[kcT.name])
        kb.memset("dve", vc[:], 0.0, w=[vc.name])
        kb.memset("dve", vc[:, :, :, 64:65], 1.0, w=[vc.name])
        with ExitStack() as s2:
            sb2 = lambda name, shape, dt=F32: self.sb(s2, name, shape, dt)
            w1 = sb2("w1c", [64, 32, 256]); w2 = sb2("w2c", [128, 2, 64]); pos = sb2("pos", [32, 64]); posT = sb2("posT", [64, 32])
            tT = sb2("tT", [64, S]); hid = sb2("hid", [128, 2, 512]); cb = sb2("cb", [128, 2])
            kg0r = sb2("kg0r", [1, 64]); kg0 = sb2("kg0", [64, 1]); sqc = sb2("sqc", [64, 512]); rsc = sb2("rsc", [64, 512])
            kcr = sb2("kcr", [64, 512])
            kb.dma(kg0r[:], self.nsa_k_norm[l, 0:1, :], w=[kg0r.name])
            kb.tr(psT[0:64, 0:1], kg0r[:], idf[0:1, 0:1], r=[kg0r.name, idf.name], w=[psT.name])
            kb.cp("dve", kg0[:], psT[0:64, 0:1], r=[psT.name], w=[kg0.name])
            for jj in range(2):
                kb.dma(w1[:], self.nsa_cmp_w1[l, jj].rearrange("(l d) n -> d l n", d=64), w=[w1.name])
                kb.dma(w2[:], self.nsa_cmp_w2[l, jj].rearrange("(k p) n -> p k n", p=128), w=[w2.name])
                kb.dma(pos[:], self.nsa_cmp_pos[l, jj], w=[pos.name])
                kb.tr(psT[0:64, 0:32], pos[:], idf[0:32, 0:32], r=[pos.name, idf.name], w=[psT.name])
                kb.cp("dve", posT[:], psT[0:64, 0:32], r=[psT.name], w=[posT.name])
                for half in range(2):
                    for li in range(32):
                        kb.mm(psT[:, 64 + half:65 + half], w1[:, li, half * 128:(half + 1) * 128], posT[:, li:li + 1],
                              li == 0, li == 31, r=[w1.name, posT.name], w=[psT.name])
                kb.cp("dve", cb[:], psT[:, 64:66], r=[psT.name], w=[cb.name])
                for g in range(2):
                    kb.dma(tT[:], self.kvcT_d[jj * 2 + g], w=[tT.name])
                    for half in range(2):
                        for li in range(32):
                            kb.mm(psX[half][:, 0:n_cmp], w1[:, li, half * 128:(half + 1) * 128],
                                  tT[:, li:li + 16 * (n_cmp - 1) + 1:16], li == 0, li == 31,
                                  r=[w1.name, tT.name], w=[psX[half].name])
                        kb.act(hid[:, half, 0:n_cmp], psX[half][:, 0:n_cmp], AF.Silu, bias=cb[:, half:half + 1],
                               r=[psX[half].name, cb.name], w=[hid.name])
                    if jj == 0:
                        for half in range(2):
                            kb.mm(psT[0:64, 0:n_cmp], w2[:, half, :], hid[:, half, 0:n_cmp], half == 0, half == 1,
                                  r=[w2.name, hid.name], w=[psT.name])
                        kb.act(sqc[:, 0:n_cmp], psT[0:64, 0:n_cmp], AF.Square, r=[psT.name], w=[sqc.name])
                        kb.cp("dve", kcr[:, 0:n_cmp], psT[0:64, 0:n_cmp], r=[psT.name], w=[kcr.name])
                        kb.mm(psI[0:64, 0:n_cmp], ones[0:64, 0:64], sqc[:, 0:n_cmp], True, True,
                              r=[ones.name, sqc.name], w=[psI.name])
                        self.rsqrt(rsc[:, 0:n_cmp], psI[0:64, 0:n_cmp], 1.0 / 64, EPS, [psI.name], [rsc.name])
                        kb.stt("dve", kcT[:, g, 0:n_cmp], kcr[:, 0:n_cmp], kg0[:, 0:1], rsc[:, 0:n_cmp], ALU.mult, ALU.mult,
                               r=[kcr.name, kg0.name, rsc.name], w=[kcT.name])
                    else:
                        for ch in range(nch):
                            cw = min(128, n_cmp - ch * 128)
                            for half in range(2):
                                kb.mm(psT[0:cw, 0:64], hid[:, half, ch * 128:ch * 128 + cw], w2[:, half, :],
                                      half == 0, half == 1, r=[hid.name, w2.name], w=[psT.name])
                            kb.cp("dve", vc[0:cw, ch, g, 0:64], psT[0:cw, 0:64], r=[psT.name], w=[vc.name])
            kb.flush()
        Ob = [[psO[0], psX[0]], [psO[1], psX[1]]]
        Ib = [psI, psT]
        q32 = [sb("q32_%d" % j, [64, 4, 128]) for j in range(2)]
        q16 = [sb("q16_%d" % j, [64, 4, 128], BF16) for j in range(2)]
        gs = [sb("gs%d" % j, [128, 24]) for j in range(2)]
        e32 = [sb("e32_%d" % j, [128, 512]) for j in range(2)]
        e16 = [sb("e16_%d" % j, [128, 512], BF16) for j in range(4)]
        den = [sb("den%d" % j, [128, 4]) for j in range(2)]
        rden = [sb("rden%d" % j, [128, 4]) for j in range(2)]
        coef = [sb("coef%d" % j, [128, 4]) for j in range(2)]
        impm = [sb("impm%d" % j, [128, NB]) for j in range(2)]
        wk = [sb("wk%d" % j, [128, NB]) for j in range(2)]
        m1 = [sb("m1_%d" % j, [128, 8]) for j in range(2)]
        m2 = [sb("m2_%d" % j, [128, 8]) for j in range(2)]
        sel = [sb("sel%d" % j, [128, NB]) for j in range(2)]
        negT = [sb("negT%d" % j, [128, 512], BF16) for j in range(2)]
        oacc = [sb("oacc%d" % j, [128, MIX]) for j in range(2)]
        for g in range(2):
            kb.memset("dve", negT[g][:], 0.0, w=[negT[g].name])
        cnt = {"S": 0, "E": 0}

        def finish_branch(pso, g, br, oa, gsb, first):
            kb.ts("dve", den[g][:], pso[:, 64:260:65], 1e-30, None, ALU.max, r=[pso.name], w=[den[g].name])
            kb.add("dve", lambda e: e.reciprocal(rden[g][:], den[g][:]), [den[g].name], [rden[g].name])
            kb.tt("dve", coef[g][:], rden[g][:], gsb[:, g * 12 + br:g * 12 + 12:3], ALU.mult,
                  r=[rden[g].name, gsb.name], w=[coef[g].name])
            for h in range(4):
                osl = oa[:, (4 * g + h) * 64:(4 * g + h + 1) * 64]
                if first:
                    kb.ts("dve", osl, pso[:, h * 65:h * 65 + 64], coef[g][:, h:h + 1], None, ALU.mult,
                          r=[pso.name, coef[g].name], w=[oa.name + ":%d" % g])
                else:
                    kb.stt("dve", osl, pso[:, h * 65:h * 65 + 64], coef[g][:, h:h + 1], osl, ALU.mult, ALU.add,
                           r=[pso.name, coef[g].name, oa.name + ":%d" % g], w=[oa.name + ":%d" % g])

        def bg_gen(b, g, oa, gsb):
            rows = slice(b * 128, (b + 1) * 128)
            kb.dma(q32[g][:], self.qT_d[4 * g:4 * g + 4, :, rows].rearrange("h d t -> d h t"), w=[q32[g].name])
            kb.cp("pool", q16[g][:], q32[g][:], r=[q32[g].name], w=[q16[g].name])
            qf32 = q32[g][:].rearrange("p h t -> p (h t)")
            qf16 = q16[g][:].rearrange("p h t -> p (h t)")
            psI_ = Ib[g]
            yield
            pso = Ob[g][0]
            chunks = list(range(0, min(b // 16, nch - 1) + 1))
            for ci, kc in enumerate(chunks):
                pS = psS[cnt["S"] % 2]; e = e32[cnt["S"] % 2]; cnt["S"] += 1
                kb.mm(pS[:], kcT[:, g, kc * 128:(kc + 1) * 128], qf32, True, True,
                      r=[kcT.name, q32[g].name], w=[pS.name])
                kb.act(e[:], pS[:], AF.Exp, scale=SC, r=[pS.name], w=[e.name])
                rr = b - 16 * kc
                if rr <= 16:
                    kb.tt("dve", e[:].rearrange("p (h t) -> p h t", h=4), e[:].rearrange("p (h t) -> p h t", h=4),
                          cmk[:, rr, :].unsqueeze(1).to_broadcast([128, 4, 128]), ALU.mult,
                          r=[e.name, cmk.name], w=[e.name])
                for h in range(4):
                    kb.mm(pso[:, h * 65:(h + 1) * 65], e[:, h * 128:(h + 1) * 128], vc[:, kc, g, :],
                          ci == 0 and h == 0, False, r=[e.name, vc.name], w=[pso.name])
                for h in range(4):
                    kb.mm(psI_[:, h * 128:h * 128 + NB], e[:, h * 128:(h + 1) * 128], c2s[:, kc, :],
                          ci == 0 and h == 0, False, r=[e.name, c2s.name], w=[psI_.name])
                yield
            finish_branch(pso, g, 0, oa, gsb, True)
            yield
            if NB > 16:
                im = impm[g]
                for h in range(4):
                    if h == 0:
                        kb.ts("dve", im[:], psI_[:, 0:NB], rden[g][:, 0:1], None, ALU.mult,
                              r=[psI_.name, rden[g].name], w=[im.name])
                    else:
                        kb.stt("dve", im[:], psI_[:, h * 128:h * 128 + NB], rden[g][:, h:h + 1], im[:], ALU.mult, ALU.add,
                               r=[psI_.name, rden[g].name, im.name], w=[im.name])
                if 2 * b + 2 < NB:
                    kb.memset("pool", im[:, 2 * b + 2:NB], -1.0, w=[im.name])
                kb.memset("pool", im[0:64, 2 * b + 1:2 * b + 2], -1.0, w=[im.name])
                kb.ts("pool", im[:, 0:1], im[:, 0:1], 100.0, None, ALU.add, r=[im.name], w=[im.name])
                kb.ts("pool", im[:, 2 * b:2 * b + 1], im[:, 2 * b:2 * b + 1], 100.0, None, ALU.add,
                      r=[im.name], w=[im.name])
                if b > 0:
                    kb.ts("pool", im[0:64, 2 * b - 1:2 * b], im[0:64, 2 * b - 1:2 * b], 100.0, None, ALU.add,
                          r=[im.name], w=[im.name])
                kb.ts("pool", im[64:128, 2 * b + 1:2 * b + 2], im[64:128, 2 * b + 1:2 * b + 2], 100.0, None, ALU.add,
                      r=[im.name], w=[im.name])
                yield
                kb.add("dve", lambda e_: e_.max(m1[g][:], im[:]), [im.name], [m1[g].name])
                kb.add("dve", lambda e_: e_.match_replace(wk[g][:], m1[g][:], im[:], -1e9), [m1[g].name, im.name], [wk[g].name])
                kb.add("dve", lambda e_: e_.max(m2[g][:], wk[g][:]), [wk[g].name], [m2[g].name])
                kb.ts("dve", sel[g][:], im[:], m2[g][:, 7:8], None, ALU.is_ge, r=[im.name, m2[g].name], w=[sel[g].name])
                kb.ts("dve", sel[g][:], sel[g][:], -NEGM, NEGM, ALU.mult, ALU.add, r=[sel[g].name], w=[sel[g].name])
                yield
            pso = Ob[g][1]
            wch = list(range(max(0, b - 4), b + 1))
            for ci, kc in enumerate(wch):
                pS = psS[cnt["S"] % 2]; cnt["S"] += 1
                e = e16[cnt["E"] % 4]; cnt["E"] += 1
                diag = kc == b
                edge = kc == b - 4
                kb.mm(pS[:], kw[:, g, kc * 128:(kc + 1) * 128], qf16, True, not (diag or edge),
                      r=[kw.name, q16[g].name], w=[pS.name])
                if diag:
                    kb.mm(pS[:], idb[:], cneg[:], False, True, r=[idb.name, cneg.name], w=[pS.name])
                if edge:
                    kb.mm(pS[:], idb[:], wneg[:], False, True, r=[idb.name, wneg.name], w=[pS.name])
                kb.act(e[:], pS[:], AF.Exp, scale=SC, r=[pS.name], w=[e.name])
                for h in range(4):
                    kb.mm(pso[:, h * 65:(h + 1) * 65], e[:, h * 128:(h + 1) * 128],
                          vv[:, kc, (2 + g) * 65:(3 + g) * 65], ci == 0 and h == 0, False,
                          r=[e.name, vv.name], w=[pso.name])
                yield
            finish_branch(pso, g, 2, oa, gsb, False)
            if NB > 16:
                kb.tr(psI_[0:NB, 0:128], sel[g][:], idf[:], r=[sel[g].name, idf.name], w=[psI_.name])
                kb.cp("dve", negT[g][0:NB, :].rearrange("p (h t) -> p h t", h=4),
                      psI_[0:NB, 0:128].unsqueeze(1).to_broadcast([NB, 4, 128]), r=[psI_.name], w=[negT[g].name])
            yield
            pso = Ob[g][0]
            for kc in range(0, b + 1):
                pS = psS[cnt["S"] % 2]; cnt["S"] += 1
                e = e16[cnt["E"] % 4]; cnt["E"] += 1
                diag = kc == b
                kb.mm(pS[:], ks[:, g, kc * 128:(kc + 1) * 128], qf16, True, False, r=[ks.name, q16[g].name], w=[pS.name])
                kb.mm(pS[:], ex[0:NB, kc * 128:(kc + 1) * 128], negT[g][0:NB, :], False, not diag,
                      r=[ex.name, negT[g].name], w=[pS.name])
                if diag:
                    kb.mm(pS[:], idb[:], cneg[:], False, True, r=[idb.name, cneg.name], w=[pS.name])
                kb.act(e[:], pS[:], AF.Exp, scale=SC, r=[pS.name], w=[e.name])
                for h in range(4):
                    kb.mm(pso[:, h * 65:(h + 1) * 65], e[:, h * 128:(h + 1) * 128], vv[:, kc, g * 65:(g + 1) * 65],
                          kc == 0 and h == 0, False, r=[e.name, vv.name], w=[pso.name])
                yield
            finish_branch(pso, g, 1, oa, gsb, False)

        for b in range(NT):
            rows = slice(b * 128, (b + 1) * 128)
            oa = oacc[b % 2]
            gsb = gs[b % 2]
            kb.dma(gsb[:], P[rows, 1280:1304], w=[gsb.name])
            kb.act(gsb[:], gsb[:], AF.Sigmoid, r=[gsb.name], w=[gsb.name])
            gens = [bg_gen(b, 0, oa, gsb), bg_gen(b, 1, oa, gsb)]
            while gens:
                for gg in list(gens):
                    try:
                        next(gg)
                    except StopIteration:
                        gens.remove(gg)
            kb.dma(self.o_mix[l][0][rows, :], oa[:], r=[oa.name + ":0", oa.name + ":1"], q="pool")
        kb.flush()


Prog.phase_nsa_attn = _phase_nsa_attn
```

```python
import numpy as np
from contextlib import ExitStack
import concourse.bass as bass
import concourse.mybir as mybir
from concourse.bass_utils import run_bass_kernel_spmd

F32 = mybir.dt.float32
BF16 = mybir.dt.bfloat16
AF = mybir.ActivationFunctionType
ALU = mybir.AluOpType
AX = mybir.AxisListType

D = 1024
MIX = 512
NSA_IN = 1304
RWKV_IN = 1792
GDN_IN = 2056
D_IN = 8224
D_FF = 4096
DP = D_IN + 32
EPS = 1e-6
OFF_NSA = 0
OFF_RWKV = NSA_IN
OFF_GDN = NSA_IN + RWKV_IN
OFF_GATE = NSA_IN + RWKV_IN + GDN_IN
NEGM = -30000.0


class _Op:
    __slots__ = ("eng", "fn", "deps", "is_dma", "need_sig", "sem", "val", "pos")


class KB:
    ENGS = ("pe", "act", "dve", "pool", "sp")

    def __init__(self, nc, stack):
        self.nc = nc
        self.stack = stack
        self.esem = {e: stack.enter_context(nc.semaphore("es_" + e)) for e in ("pe", "act", "dve", "pool")}
        self.ecnt = {e: 0 for e in self.esem}
        self.dsem = {"sp": [stack.enter_context(nc.semaphore("dsp%d" % i)) for i in range(16)],
                     "pool": [stack.enter_context(nc.semaphore("dpl%d" % i)) for i in range(8)]}
        self.dcnt = {"sp": 0, "pool": 0}
        self.dlast = {"sp": {}, "pool": {}}
        self.seen = {e: {} for e in self.ENGS}
        self.begin()

    def begin(self):
        self.ops = {e: [] for e in self.ENGS}
        self.res = {}

    def add(self, eng, fn, reads=(), writes=(), dma=False):
        op = _Op()
        op.eng = eng
        op.fn = fn
        op.is_dma = dma
        op.need_sig = dma
        op.sem = None
        op.val = 0
        op.pos = len(self.ops[eng])
        deps = set()
        rl, wl = [], []
        reads = [x.split("__u")[0] for x in reads]
        writes = [x.split("__u")[0] for x in writes]
        for r in reads:
            (wl if r.startswith("ps") else rl).append(r)
        wl.extend(writes)
        for r in rl:
            st = self.res.setdefault(r, [None, []])
            if st[0] is not None:
                deps.add(st[0])
        for w in wl:
            st = self.res.setdefault(w, [None, []])
            if st[0] is not None:
                deps.add(st[0])
            deps.update(st[1])
        for r in rl:
            self.res[r][1].append(op)
        for w in wl:
            st = self.res[w]
            st[0] = op
            st[1] = []
        deps.discard(op)
        keep = []
        latest = {}
        for d in deps:
            if d.is_dma:
                keep.append(d)
                continue
            if d.eng == eng:
                if eng == "pe":
                    continue
                if op.pos - d.pos > 2:
                    continue
            cur = latest.get(d.eng)
            if cur is None or d.pos > cur.pos:
                latest[d.eng] = d
        for d in latest.values():
            d.need_sig = True
            keep.append(d)
        if dma:
            k = self.dcnt[eng]
            self.dcnt[eng] += 1
            P = len(self.dsem[eng])
            slot = k % P
            op.sem = self.dsem[eng][slot]
            op.val = 16 * (k // P + 1)
            prev = self.dlast[eng].get(slot)
            if prev is not None:
                keep.append(prev)
            self.dlast[eng][slot] = op
        op.deps = keep
        self.ops[eng].append(op)
        return op

    def flush(self):
        nc = self.nc
        for q in ("sp", "pool"):
            outstanding = [o for o in self.dlast[q].values()]
            if outstanding:
                op = self.add(q, None)
                op.deps = outstanding
        for e in ("pe", "act", "dve", "pool"):
            for op in self.ops[e]:
                if op.need_sig and not op.is_dma:
                    self.ecnt[e] += 1
                    op.sem = self.esem[e]
                    op.val = self.ecnt[e]
        ops = self.ops
        seen = self.seen

        def emit(e, eng):
            sn = seen[e]
            for op in ops[e]:
                for d in op.deps:
                    key = id(d.sem)
                    if sn.get(key, 0) >= d.val:
                        continue
                    eng.wait_ge(d.sem, d.val)
                    sn[key] = d.val
                if op.fn is None:
                    continue
                ins = op.fn(eng)
                if op.need_sig:
                    ins.then_inc(op.sem, 16 if op.is_dma else 1)

        with nc.Block() as block:
            if ops["pe"]:
                block.tensor(lambda t: emit("pe", t))
            if ops["act"]:
                block.scalar(lambda t: emit("act", t))
            if ops["dve"]:
                block.vector(lambda t: emit("dve", t))
            if ops["pool"]:
                block.gpsimd(lambda t: emit("pool", t))
            if ops["sp"]:
                block.sync(lambda t: emit("sp", t))
        self.begin()

    def dma(self, out, in_, r=(), w=(), q="sp"):
        return self.add(q, lambda e: e.dma_start(out=out, in_=in_), r, w, dma=True)

    def mm(self, out, lhsT, rhs, start, stop, r=(), w=()):
        return self.add("pe", lambda e: e.matmul(out, lhsT=lhsT, rhs=rhs, start=start, stop=stop), r, w)

    def tr(self, out, in_, ident, r=(), w=()):
        return self.add("pe", lambda e: e.transpose(out, in_, ident), r, w)

    def act(self, out, in_, func, r=(), w=(), bias=None, scale=1.0, accum=None):
        def fn(e):
            kw = {}
            if bias is not None:
                kw["bias"] = bias
            if accum is not None:
                kw["accum_out"] = accum
            return e.activation(out, in_, func, scale=scale, **kw)
        return self.add("act", fn, r, w)

    def ts(self, eng, out, in0, s1, s2, op0, op1=None, r=(), w=()):
        def fn(e):
            if op1 is None:
                return e.tensor_scalar(out, in0, s1, None, op0)
            return e.tensor_scalar(out, in0, s1, s2, op0, op1)
        return self.add(eng, fn, r, w)

    def tt(self, eng, out, in0, in1, op, r=(), w=()):
        return self.add(eng, lambda e: e.tensor_tensor(out, in0, in1, op), r, w)

    def stt(self, eng, out, in0, scalar, in1, op0, op1, r=(), w=()):
        return self.add(eng, lambda e: e.scalar_tensor_tensor(out, in0, scalar, in1, op0, op1), r, w)

    def cp(self, eng, out, in_, r=(), w=()):
        if eng == "act":
            return self.add(eng, lambda e: e.copy(out, in_), r, w)
        return self.add(eng, lambda e: e.tensor_copy(out, in_), r, w)

    def memset(self, eng, ap, val, w=()):
        return self.add(eng, lambda e: e.memset(ap, val), (), w)


class Prog:
    def __init__(self, S, n_layers=2, debug=False):
        self.S = S
        self.NT = S // 128
        self.L = n_layers
        self.debug = debug
        self.nc = bass.Bass("TRN2", target_bir_lowering=False)
        self.stack = ExitStack()
        self.kb = KB(self.nc, self.stack)
        self.din = {}
        self.dscr = {}

    def inp(self, name, shape, dt=F32):
        t = self.nc.dram_tensor(name, list(shape), dt, kind="ExternalInput")
        self.din[name] = t
        return t.ap()

    def outp(self, name, shape, dt=F32):
        return self.nc.dram_tensor(name, list(shape), dt, kind="ExternalOutput").ap()

    def scr(self, name, shape, dt=F32):
        if self.debug:
            return self.outp(name, shape, dt)
        return self.nc.dram_tensor(name, list(shape), dt, kind="Internal").ap()

    def sb(self, st, name, shape, dt=F32):
        self.uid = getattr(self, "uid", 0) + 1
        return st.enter_context(self.nc.sbuf_tensor("%s__u%d" % (name, self.uid), list(shape), dt))

    def psb(self, st, name, dt=F32, n=512):
        self.uid = getattr(self, "uid", 0) + 1
        return st.enter_context(self.nc.psum_tensor("%s__u%d" % (name, self.uid), [128, n], dt))

    def declare(self):
        L, S = self.L, self.S
        i = self.inp
        self.x_in = i("x", [S, D])
        self.out = self.outp("out", [S, D])
        self.norm_mix_g = i("norm_mix_g", [L, D])
        self.w_in = i("w_in", [L, D, DP])
        self.ident_bf = i("ident_bf", [128, 128], BF16)
        self.ident_f = i("ident_f", [128, 128])
        self.proj_tm = [self.scr("proj_tm%d" % l, [S, OFF_GATE]) for l in range(L)]
        self.proj_g = [self.scr("proj_g%d" % l, [S, DP - OFF_GATE]) for l in range(L)]

    def load_weight_bf16(self, st, name, src_ap, kc, ncols, chunk):
        kb = self.kb
        w = self.sb(st, name, [128, kc, ncols], BF16)
        src = src_ap.rearrange("(k p) n -> p k n", p=128)
        with ExitStack() as s2:
            stg = [self.sb(s2, "%s_stg%d" % (name, j), [128, kc, chunk], F32) for j in range(2)]
            engs = ["dve", "pool", "act"]
            n = 0
            for c0 in range(0, ncols, chunk):
                cw = min(chunk, ncols - c0)
                j = n % 2
                kb.dma(stg[j][:, :, 0:cw], src[:, :, c0:c0 + cw], w=[stg[j].name])
                kb.cp(engs[n % 3], w[:, :, c0:c0 + cw], stg[j][:, :, 0:cw], r=[stg[j].name], w=[name + ":%d" % n])
                n += 1
            kb.flush()
        return w

    def phase_proj(self, l, x_src):
        kb, S, NT = self.kb, self.S, self.NT
        with ExitStack() as st:
            wsb = self.load_weight_bf16(st, "w_in_sb", self.w_in[l], 8, DP, 516)
            gb = self.sb(st, "gmix", [128, D])
            idb = self.sb(st, "idb", [128, 128], BF16)
            kb.dma(gb[:], self.norm_mix_g[l:l + 1, :].to_broadcast([128, D]), w=["gmix"])
            kb.dma(idb[:], self.ident_bf[:, :], w=["idb"])
            xt = [self.sb(st, "xt%d" % j, [128, D]) for j in range(2)]
            junk = self.sb(st, "junk", [128, D])
            ss = [self.sb(st, "ss%d" % j, [128, 1]) for j in range(2)]
            rstd = [self.sb(st, "rstd%d" % j, [128, 1]) for j in range(2)]
            hb = [self.sb(st, "hb%d" % j, [128, D], BF16) for j in range(2)]
            hT = [self.sb(st, "hT%d" % j, [128, 8, 128], BF16) for j in range(2)]
            osb = [self.sb(st, "osb%d" % j, [128, 2048]) for j in range(2)]
            pst = self.psb(st, "ps_tr", BF16, 1024)
            pso = [self.psb(st, "ps_o%d" % j) for j in range(4)]
            nblk = 0
            nog = 0
            def prep(tt_, part):
                jj = tt_ % 2
                self.rms_to_hT(xt[jj], x_src[tt_ * 128:(tt_ + 1) * 128, :], junk, ss[jj], rstd[jj], gb, hb[jj], pst, idb,
                               hT[jj], part=part)
            prep(0, None)
            for t in range(NT):
                j = t % 2
                rows = slice(t * 128, (t + 1) * 128)
                for gi, (og, oe) in enumerate(((0, 2048), (2048, 4096), (4096, OFF_GATE), (OFF_GATE, OFF_GATE + 2048),
                                               (OFF_GATE + 2048, DP))):
                    if gi == 1 and t + 1 < NT:
                        prep(t + 1, 0)
                    o = osb[nog % 2]
                    nog += 1
                    ow = oe - og
                    for c0 in range(og, oe, 512):
                        cw = min(512, oe - c0)
                        ps = pso[nblk % 4]
                        for k in range(8):
                            kb.mm(ps[:, 0:cw], hT[j][:, k, :], wsb[:, k, c0:c0 + cw], k == 0, k == 7,
                                  r=[hT[j].name, "w_in_sb"], w=[ps.name])
                        kb.cp("act" if nblk % 2 else "dve", o[:, c0 - og:c0 - og + cw], ps[:, 0:cw],
                              r=[ps.name], w=[o.name])
                        nblk += 1
                    if og < OFF_GATE:
                        kb.dma(self.proj_tm[l][rows, og:oe], o[:, 0:ow], r=[o.name], q="pool")
                    else:
                        kb.dma(self.proj_g[l][rows, og - OFF_GATE:oe - OFF_GATE], o[:, 0:ow], r=[o.name], q="pool")
                if t + 1 < NT:
                    prep(t + 1, 1)
            kb.flush()

    def rms_to_hT(self, xt, x_rows, junk, ss, rstd, gb, hb, pst, idb, hT, part=None):
        kb = self.kb
        if part in (None, 0):
            kb.dma(xt[:], x_rows, w=[xt.name])
            kb.act(junk[:], xt[:], AF.Square, r=[xt.name], w=[junk.name, ss.name], accum=ss[:])
            self.rsqrt(rstd[:], ss[:], 1.0 / D, EPS, [ss.name], [rstd.name])
            kb.stt("dve", hb[:], xt[:], rstd[:, 0:1], gb[:], ALU.mult, ALU.mult,
                   r=[xt.name, rstd.name, gb.name], w=[hb.name])
        if part == 0:
            return
        for k in range(8):
            kb.tr(pst[:, k * 128:(k + 1) * 128], hb[:, k * 128:(k + 1) * 128], idb[:],
                  r=[hb.name, idb.name], w=[pst.name])
        kb.cp("act", hT[:].rearrange("p k t -> p (k t)"), pst[:], r=[pst.name], w=[hT.name])

    def rsqrt(self, out, in_, scale, eps, r, w):
        kb = self.kb
        kb.ts("dve", out, in_, scale, eps, ALU.mult, ALU.add, r=r, w=w)
        kb.add("act", lambda e: e.sqrt(out, out), w, w)
        kb.add("dve", lambda e: e.reciprocal(out, out), w, w)

    def declare_rest(self):
        L, S = self.L, self.S
        i = self.inp
        self.w_branch = i("w_branch", [L, 3, MIX, D])
        self.w_out = i("w_out", [L, D, D])
        self.norm_ffn_g = i("norm_ffn_g", [L, D])
        self.w_ff1 = i("w_ff1", [L, D, D_FF])
        self.w_ff2 = i("w_ff2", [L, D_FF, D])
        self.o_mix = [[self.scr("o_mix%d_%d" % (l, b), [S, MIX]) for b in range(3)] for l in range(L)]
        self.x_mid = [self.scr("x_mid%d" % l, [S, D]) for l in range(L)]
        self.x_lay = [self.scr("x_lay%d" % l, [S, D]) for l in range(L - 1)] + [self.out]

    def transpose_bf(self, src_bf, nk, pst, idb, dstT, eng="act"):
        kb = self.kb
        for k in range(nk):
            kb.tr(pst[:, k * 128:(k + 1) * 128], src_bf[:, k * 128:(k + 1) * 128], idb[:],
                  r=[src_bf.name, idb.name], w=[pst.name])
        kb.cp(eng, dstT.rearrange("p k t -> p (k t)"), pst[:, 0:nk * 128], r=[pst.name], w=[dstT.name])

    def phase_merge(self, l, x_src):
        kb, S, NT = self.kb, self.S, self.NT
        with ExitStack() as st:
            wb = [self.load_weight_bf16(st, "wbr%d" % b, self.w_branch[l, b], 4, D, 1024) for b in range(3)]
            wo = self.load_weight_bf16(st, "wout_sb", self.w_out[l], 8, D, 512)
            idb = self.sb(st, "idb", [128, 128], BF16)
            kb.dma(idb[:], self.ident_bf[:, :], w=["idb"])
            ot = [self.sb(st, "ot%d" % j, [128, MIX]) for j in range(2)]
            ob = [self.sb(st, "ob%d" % j, [128, MIX], BF16) for j in range(2)]
            oT = [self.sb(st, "oT%d" % j, [128, 4, 128], BF16) for j in range(2)]
            gt = [self.sb(st, "gt%d" % j, [128, D]) for j in range(2)]
            tmp = self.sb(st, "tmpm", [128, D])
            mrg = self.sb(st, "mrg", [128, D])
            mb = self.sb(st, "mrgb", [128, D], BF16)
            mT = self.sb(st, "mT", [128, 8, 128], BF16)
            xt = [self.sb(st, "xt%d" % j, [128, D]) for j in range(2)]
            xo = [self.sb(st, "xo%d" % j, [128, D]) for j in range(2)]
            pst = self.psb(st, "ps_tr", BF16, 1024)
            pso = [self.psb(st, "ps_o%d" % j) for j in range(4)]
            n = 0
            nb = 0
            for t in range(NT):
                rows = slice(t * 128, (t + 1) * 128)
                for b in range(3):
                    j = n % 2
                    n += 1
                    kb.dma(ot[j][:], self.o_mix[l][b][rows, :], w=[ot[j].name])
                    kb.dma(gt[j][:], self.proj_g[l][rows, b * D:(b + 1) * D], w=[gt[j].name])
                    kb.cp("pool", ob[j][:], ot[j][:], r=[ot[j].name], w=[ob[j].name])
                    kb.act(gt[j][:], gt[j][:], AF.Sigmoid, r=[gt[j].name], w=[gt[j].name])
                    self.transpose_bf(ob[j], 4, pst, idb, oT[j][:])
                    for c in range(2):
                        ps = pso[nb % 4]
                        nb += 1
                        for k in range(4):
                            kb.mm(ps[:], oT[j][:, k, :], wb[b][:, k, c * 512:(c + 1) * 512], k == 0, k == 3,
                                  r=[oT[j].name], w=[ps.name])
                        cs = slice(c * 512, (c + 1) * 512)
                        if b == 0:
                            kb.tt("dve", mrg[:, cs], ps[:], gt[j][:, cs], ALU.mult, r=[ps.name, gt[j].name], w=["mrg"])
                        else:
                            kb.tt("dve", tmp[:, cs], ps[:], gt[j][:, cs], ALU.mult, r=[ps.name, gt[j].name], w=["tmpm"])
                            kb.tt("pool", mrg[:, cs], mrg[:, cs], tmp[:, cs], ALU.add, r=["tmpm", "mrg"], w=["mrg"])
                kb.cp("act", mb[:], mrg[:], r=["mrg"], w=["mrgb"])
                self.transpose_bf(mb, 8, pst, idb, mT[:])
                j = t % 2
                kb.dma(xt[j][:], x_src[rows, :], w=[xt[j].name])
                for c in range(2):
                    ps = pso[nb % 4]
                    nb += 1
                    for k in range(8):
                        kb.mm(ps[:], mT[:, k, :], wo[:, k, c * 512:(c + 1) * 512], k == 0, k == 7, r=["mT"], w=[ps.name])
                    cs = slice(c * 512, (c + 1) * 512)
                    kb.tt("dve", xo[j][:, cs], ps[:], xt[j][:, cs], ALU.add, r=[ps.name, xt[j].name], w=[xo[j].name])
                kb.dma(self.x_mid[l][rows, :], xo[j][:], r=[xo[j].name], q="pool")
            kb.flush()

    def phase_ffn(self, l):
        kb, S, NT = self.kb, self.S, self.NT
        with ExitStack() as st:
            w1 = self.load_weight_bf16(st, "w1_sb", self.w_ff1[l], 8, D_FF, 512)
            w2 = self.load_weight_bf16(st, "w2_sb", self.w_ff2[l], 32, D, 128)
            gb = self.sb(st, "gffn", [128, D])
            idb = self.sb(st, "idb", [128, 128], BF16)
            kb.dma(gb[:], self.norm_ffn_g[l:l + 1, :].to_broadcast([128, D]), w=["gffn"])
            kb.dma(idb[:], self.ident_bf[:, :], w=["idb"])
            xt = [self.sb(st, "xt%d" % j, [128, D]) for j in range(2)]
            junk = self.sb(st, "junk", [128, D])
            ss = [self.sb(st, "ss%d" % j, [128, 1]) for j in range(2)]
            rstd = [self.sb(st, "rstd%d" % j, [128, 1]) for j in range(2)]
            hb = [self.sb(st, "hb%d" % j, [128, D], BF16) for j in range(2)]
            hT = [self.sb(st, "hT%d" % j, [128, 8, 128], BF16) for j in range(2)]
            fr = [self.sb(st, "fr%d" % j, [128, 512]) for j in range(2)]
            fb = self.sb(st, "fb", [128, D_FF], BF16)
            fT = self.sb(st, "fT", [128, 32, 128], BF16)
            xo = [self.sb(st, "xo%d" % j, [128, D]) for j in range(2)]
            pst = self.psb(st, "ps_tr", BF16, 1024)
            pso = [self.psb(st, "ps_o%d" % j) for j in range(4)]
            nb = 0
            def prep(tt_, part):
                jj = tt_ % 2
                self.rms_to_hT(xt[jj], self.x_mid[l][tt_ * 128:(tt_ + 1) * 128, :], junk, ss[jj], rstd[jj], gb, hb[jj], pst,
                               idb, hT[jj], part=part)
            prep(0, None)
            for t in range(NT):
                j = t % 2
                rows = slice(t * 128, (t + 1) * 128)
                for c in range(8):
                    ps = pso[nb % 4]
                    nb += 1
                    for k in range(8):
                        kb.mm(ps[:], hT[j][:, k, :], w1[:, k, c * 512:(c + 1) * 512], k == 0, k == 7,
                              r=[hT[j].name], w=[ps.name])
                    kb.act(fr[c % 2][:], ps[:], AF.Relu, r=[ps.name], w=[fr[c % 2].name])
                    kb.tt("dve", fb[:, c * 512:(c + 1) * 512], fr[c % 2][:], fr[c % 2][:], ALU.mult,
                          r=[fr[c % 2].name], w=["fb:%d" % c])
                    if c >= 2 and c % 2 == 0:
                        self.transpose_bf_slice(fb, (c - 2) * 4, 8, pst, idb, fT, rd=["fb:%d" % (c - 2), "fb:%d" % (c - 1)])
                    if c == 3 and t + 1 < NT:
                        prep(t + 1, 0)
                self.transpose_bf_slice(fb, 24, 8, pst, idb, fT, rd=["fb:6", "fb:7"])
                for c in range(2):
                    ps = pso[nb % 4]
                    nb += 1
                    for k in range(32):
                        kb.mm(ps[:], fT[:, k, :], w2[:, k, c * 512:(c + 1) * 512], k == 0, k == 31, r=["fT"], w=[ps.name])
                    cs = slice(c * 512, (c + 1) * 512)
                    kb.tt("dve", xo[j][:, cs], ps[:], xt[j][:, cs], ALU.add, r=[ps.name, xt[j].name], w=[xo[j].name])
                kb.dma(self.x_lay[l][rows, :], xo[j][:], r=[xo[j].name], q="pool")
                if t + 1 < NT:
                    prep(t + 1, 1)
            kb.flush()

    def transpose_bf_slice(self, src_bf, k0, nk, pst, idb, dstT, rd=None):
        kb = self.kb
        for k in range(nk):
            kb.tr(pst[:, k * 128:(k + 1) * 128], src_bf[:, (k0 + k) * 128:(k0 + k + 1) * 128], idb[:],
                  r=(rd if rd is not None else [src_bf.name]) + [idb.name], w=[pst.name])
        kb.cp("act", dstT[:, k0:k0 + nk, :].rearrange("p k t -> p (k t)"), pst[:, 0:nk * 128],
              r=[pst.name], w=[dstT.name])

    def declare_gdn(self):
        L = self.L
        i = self.inp
        self.gdn_conv_w = i("gdn_conv_w", [L, 4, 3 * MIX])
        self.gdn_a_log = i("gdn_a_log", [L, 4])
        self.gdn_dt_bias = i("gdn_dt_bias", [L, 4])
        self.gdn_norm_w = i("gdn_norm_w", [L, 128])
        self.ones_f = i("ones_f", [128, 128])
        self.triu_incl = i("triu_incl", [128, 128])
        self.triu_strict = i("triu_strict", [128, 128])

    def next_ps(self):
        self.psn = getattr(self, "psn", -1) + 1
        return self.pss[self.psn % len(self.pss)]

    def tr_f32(self, dst, src, idf, eng="act"):
        kb = self.kb
        n = src.shape[-1] // 128
        ps = self.next_ps()
        for k in range(n):
            kb.tr(ps[:, k * 128:(k + 1) * 128], src[:, k * 128:(k + 1) * 128], idf[:],
                  r=[src.name, idf.name], w=[ps.name])
        kb.cp(eng, dst, ps[:, 0:n * 128], r=[ps.name], w=[dst.name])

    def phase_gdn(self, l, side=False):
        kb, S, NT = self.kb, self.S, self.NT
        H, N = 4, 128
        P = self.proj_tm[l]
        c0 = OFF_GDN
        with ExitStack() as st:
            sb = lambda name, shape, dt=F32: self.sb(st, name, shape, dt)
            self.pss = [self.psb(st, "ps_g%d" % j) for j in range(6 if side else 8)]
            gen = None
            if side:
                gen = self.gen_nsa_prep(l, st, self.psb(st, "ps_npA"), self.psb(st, "ps_npB"))
            hook = (lambda: next(gen, None)) if gen is not None else (lambda: None)
            idf = sb("idf", [128, 128]); ones = sb("ones", [128, 128])
            tin = sb("tin", [128, 128]); tst = sb("tst", [128, 128])
            kb.dma(idf[:], self.ident_f[:, :], w=[idf.name])
            kb.dma(ones[:], self.ones_f[:, :], w=[ones.name])
            kb.dma(tin[:], self.triu_incl[:, :], w=[tin.name])
            kb.dma(tst[:], self.triu_strict[:, :], w=[tst.name])
            cw = sb("cw", [128, 4, 1536])
            kb.dma(cw[:].rearrange("p k c -> p (k c)"),
                   self.gdn_conv_w[l:l + 1].rearrange("o k c -> o (k c)").to_broadcast([128, 4 * 1536]), w=[cw.name])
            alog = sb("alog", [128, 4]); dtb = sb("dtb", [128, 4]); nw = sb("nw", [128, 128])
            kb.dma(alog[:], self.gdn_a_log[l:l + 1, :].to_broadcast([128, 4]), w=[alog.name])
            kb.dma(dtb[:], self.gdn_dt_bias[l:l + 1, :].to_broadcast([128, 4]), w=[dtb.name])
            kb.dma(nw[:], self.gdn_norm_w[l:l + 1, :].to_broadcast([128, 128]), w=[nw.name])
            nea = sb("nea", [128, 4])
            kb.act(nea[:], alog[:], AF.Exp, r=[alog.name], w=[nea.name])
            kb.ts("dve", nea[:], nea[:], -1.0, None, ALU.mult, r=[nea.name], w=[nea.name])
            X = [sb("X%d" % k, [128, 1536]) for k in range(4)]
            zt = sb("zt", [128, 512]); ab = sb("ab", [128, 8])
            y = sb("y", [128, 1536]); t1 = sb("t1", [128, 1536])
            ssq = sb("ssq", [128, 8]); rs = sb("rs", [128, 8])
            qkn = sb("qkn", [128, 1024])
            beta = sb("beta", [128, 4]); g = sb("g", [128, 4]); G = sb("G", [128, 4]); Gt = sb("Gt", [128, 4])
            eG = sb("eG", [128, 4]); neG = sb("neG", [128, 4]); eGC = sb("eGC", [128, 4]); eGr = sb("eGr", [128, 4])
            kbm = sb("kbm", [128, 512]); kam = sb("kam", [128, 512]); qgm = sb("qgm", [128, 512])
            khat = sb("khat", [128, 512]); vp = sb("vp", [128, 512])
            kT = sb("kT", [128, 512]); kbT = sb("kbT", [128, 512]); qT = sb("qT", [128, 512])
            aT = sb("aT", [128, 512]); rT = sb("rT", [128, 512])
            dg = sb("dg", [128, 512]); gam = sb("gam", [128, 512])
            Nm = [sb("Nm%d" % k, [128, 512]) for k in range(2)]
            Lm = [sb("Lm%d" % k, [128, 512]) for k in range(2)]
            Pm = [sb("Pm%d" % k, [128, 512]) for k in range(2)]
            MT = sb("MT", [128, 512]); N0k = sb("N0k", [128, 512])
            X1 = sb("X1", [128, 512]); UV = sb("UV", [128, 512]); Y = sb("Y", [128, 512])
            T0 = sb("T0", [128, 512])
            o = sb("o", [128, 512]); sz = sb("sz", [128, 512]); ysq = sb("ysq", [128, 512])
            yss = sb("yss", [128, 4]); yrs = sb("yrs", [128, 4])
            kb.memset("dve", T0[:], 0.0, w=[T0.name])
            hv = lambda a: a.rearrange("p (h n) -> p h n", h=H)
            for t in range(NT):
                r0 = t * 128
                for k in range(4):
                    sh = 3 - k
                    if r0 - sh < 0:
                        kb.memset("pool", X[k][:], 0.0, w=[X[k].name])
                        kb.dma(X[k][sh:128, :], P[0:128 - sh, c0:c0 + 1536], w=[X[k].name])
                    else:
                        kb.dma(X[k][:], P[r0 - sh:r0 - sh + 128, c0:c0 + 1536], w=[X[k].name])
                kb.dma(zt[:], P[r0:r0 + 128, c0 + 1536:c0 + 2048], w=[zt.name])
                kb.dma(ab[:], P[r0:r0 + 128, c0 + 2048:c0 + 2056], w=[ab.name])
                kb.tt("dve", y[:], X[0][:], cw[:, 0, :], ALU.mult, r=[X[0].name, cw.name], w=[y.name])
                for k in range(1, 4):
                    kb.tt("pool", t1[:], X[k][:], cw[:, k, :], ALU.mult, r=[X[k].name, cw.name], w=[t1.name])
                    kb.tt("dve", y[:], y[:], t1[:], ALU.add, r=[y.name, t1.name], w=[y.name])
                kb.act(y[:], y[:], AF.Silu, r=[y.name], w=[y.name])
                kb.act(t1[:, 0:1024], y[:, 0:1024], AF.Square, r=[y.name], w=[t1.name])
                kb.add("dve", lambda e: e.tensor_reduce(ssq[:], t1[:, 0:1024].rearrange("p (h n) -> p h n", h=8), AX.X, ALU.add),
                       [t1.name], [ssq.name])
                self.rsqrt(rs[:], ssq[:], 1.0, EPS, [ssq.name], [rs.name])
                kb.ts("dve", rs[:, 0:4], rs[:, 0:4], float(N) ** -0.5, None, ALU.mult, r=[rs.name], w=[rs.name])
                kb.tt("dve", qkn[:].rearrange("p (h n) -> p h n", h=8), y[:, 0:1024].rearrange("p (h n) -> p h n", h=8),
                      rs[:].unsqueeze(2).to_broadcast([128, 8, 128]), ALU.mult, r=[y.name, rs.name], w=[qkn.name])
                qn = qkn[:, 0:512]; kn = qkn[:, 512:1024]; v = y[:, 1024:1536]
                kb.act(beta[:], ab[:, 4:8], AF.Sigmoid, r=[ab.name], w=[beta.name])
                kb.tt("dve", g[:], ab[:, 0:4], dtb[:], ALU.add, r=[ab.name, dtb.name], w=[g.name])
                kb.act(g[:], g[:], AF.Exp, r=[g.name], w=[g.name])
                kb.act(g[:], g[:], AF.Ln, r=[g.name], w=[g.name], bias=1.0)
                kb.tt("dve", g[:], g[:], nea[:], ALU.mult, r=[g.name, nea.name], w=[g.name])
                ps = self.next_ps()
                kb.mm(ps[:, 0:4], tin[:], g[:], True, True, r=[tin.name, g.name], w=[ps.name])
                kb.mm(ps[:, 4:8], ones[:], g[:], True, True, r=[ones.name, g.name], w=[ps.name])
                kb.cp("dve", G[:], ps[:, 0:4], r=[ps.name], w=[G.name])
                kb.cp("dve", Gt[:], ps[:, 4:8], r=[ps.name], w=[Gt.name])
                kb.act(eG[:], G[:], AF.Exp, r=[G.name], w=[eG.name])
                kb.act(eGC[:], Gt[:], AF.Exp, r=[Gt.name], w=[eGC.name])
                kb.tt("dve", eGr[:], Gt[:], G[:], ALU.subtract, r=[Gt.name, G.name], w=[eGr.name])
                kb.act(eGr[:], eGr[:], AF.Exp, r=[eGr.name], w=[eGr.name])
                bb = lambda a: a[:].unsqueeze(2).to_broadcast([128, 4, 128])
                kb.tt("dve", hv(kbm[:]), hv(kn), bb(beta), ALU.mult, r=[qkn.name, beta.name], w=[kbm.name])
                kb.tt("pool", hv(vp[:]), hv(v), bb(beta), ALU.mult, r=[y.name, beta.name], w=[vp.name])
                kb.tt("dve", hv(kam[:]), hv(kbm[:]), bb(eG), ALU.mult, r=[kbm.name, eG.name], w=[kam.name])
                kb.ts("pool", kam[:], kam[:], -1.0, None, ALU.mult, r=[kam.name], w=[kam.name])
                kb.tt("dve", hv(qgm[:]), hv(qn), bb(eG), ALU.mult, r=[qkn.name, eG.name], w=[qgm.name])
                kb.tt("pool", hv(khat[:]), hv(kn), bb(eGr), ALU.mult, r=[qkn.name, eGr.name], w=[khat.name])
                hook()
                self.tr_f32(kT[:], kn, idf); self.tr_f32(kbT[:], kbm[:], idf, "dve")
                self.tr_f32(qT[:], qn, idf); self.tr_f32(aT[:], kam[:], idf, "dve"); self.tr_f32(rT[:], qgm[:], idf)
                for h in range(H):
                    kb.ts("dve", dg[:, h * 128:(h + 1) * 128], idf[:], G[:, h:h + 1], None, ALU.mult,
                          r=[idf.name, G.name], w=[dg.name])
                ps = self.next_ps()
                for h in range(H):
                    kb.mm(ps[:, h * 128:(h + 1) * 128], ones[:], dg[:, h * 128:(h + 1) * 128], True, True,
                          r=[ones.name, dg.name], w=[ps.name])
                for h in range(H):
                    kb.ts("dve", gam[:, h * 128:(h + 1) * 128], ps[:, h * 128:(h + 1) * 128], G[:, h:h + 1], 0.0,
                          ALU.subtract, ALU.min, r=[ps.name, G.name], w=[gam.name])
                kb.act(gam[:], gam[:], AF.Exp, r=[gam.name], w=[gam.name])
                ps = self.next_ps()
                ps2 = self.next_ps()
                for h in range(H):
                    hs = slice(h * 128, (h + 1) * 128)
                    kb.mm(ps[:, hs], kT[:, hs], kbT[:, hs], True, True, r=[kT.name, kbT.name], w=[ps.name])
                    kb.mm(ps2[:, hs], kT[:, hs], qT[:, hs], True, True, r=[kT.name, qT.name], w=[ps2.name])
                N0 = Nm[0]
                kb.tt("dve", N0[:], ps[:], gam[:], ALU.mult, r=[ps.name, gam.name], w=[N0.name])
                kb.stt("dve", hv(N0[:]), hv(N0[:]), -1.0, tst[:].unsqueeze(1).to_broadcast([128, 4, 128]), ALU.mult, ALU.mult,
                       r=[N0.name, tst.name], w=[N0.name])
                kb.tt("dve", MT[:], ps2[:], gam[:], ALU.mult, r=[ps2.name, gam.name], w=[MT.name])
                kb.cp("pool", N0k[:], N0[:], r=[N0.name], w=[N0k.name])
                L0 = Lm[0]
                self.tr_f32(L0[:], N0[:], idf)
                P0 = Pm[0]
                kb.tt("dve", hv(P0[:]), hv(N0[:]), idf[:].unsqueeze(1).to_broadcast([128, 4, 128]), ALU.add,
                      r=[N0.name, idf.name], w=[P0.name])
                cur = 0
                for lev in range(6):
                    Nc, Lc, Pc = Nm[cur], Lm[cur], Pm[cur]
                    Nn, Ln, Pn = Nm[1 - cur], Lm[1 - cur], Pm[1 - cur]
                    psl = self.next_ps(); psn = self.next_ps()
                    for h in range(H):
                        hs = slice(h * 128, (h + 1) * 128)
                        kb.mm(psl[:, hs], Nc[:, hs], Lc[:, hs], True, True, r=[Nc.name, Lc.name], w=[psl.name])
                    if lev < 5:
                        for h in range(H):
                            hs = slice(h * 128, (h + 1) * 128)
                            kb.mm(psn[:, hs], Lc[:, hs], Nc[:, hs], True, True, r=[Nc.name, Lc.name], w=[psn.name])
                    kb.cp("act", Ln[:], psl[:], r=[psl.name], w=[Ln.name])
                    if lev < 5:
                        kb.cp("dve", Nn[:], psn[:], r=[psn.name], w=[Nn.name])
                    psp = self.next_ps()
                    for h in range(H):
                        hs = slice(h * 128, (h + 1) * 128)
                        kb.mm(psp[:, hs], Ln[:, hs], Pc[:, hs], True, True, r=[Ln.name, Pc.name], w=[psp.name])
                    kb.tt("dve", Pn[:], psp[:], Pc[:], ALU.add, r=[psp.name, Pc.name], w=[Pn.name])
                    cur = 1 - cur
                    if lev == 0:
                        kb.tt("pool", hv(MT[:]), hv(MT[:]), tin[:].unsqueeze(1).to_broadcast([128, 4, 128]), ALU.mult,
                              r=[MT.name, tin.name], w=[MT.name])
                    elif lev == 1:
                        kb.act(sz[:], zt[:], AF.Silu, r=[zt.name], w=[sz.name])
                    hook()
                WT = Pm[cur]
                N0 = Nm[0]
                ps = self.next_ps()
                for h in range(H):
                    hs = slice(h * 128, (h + 1) * 128)
                    kb.mm(ps[:, hs], aT[:, hs], T0[:, hs], True, False, r=[aT.name, T0.name], w=[ps.name])
                    kb.mm(ps[:, hs], N0k[:, hs], vp[:, hs], False, True, r=[N0k.name, vp.name], w=[ps.name])
                kb.cp("act", X1[:], ps[:], r=[ps.name], w=[X1.name])
                ps = self.next_ps()
                for h in range(H):
                    hs = slice(h * 128, (h + 1) * 128)
                    kb.mm(ps[:, hs], WT[:, hs], X1[:, hs], True, True, r=[WT.name, X1.name], w=[ps.name])
                kb.tt("dve", UV[:], ps[:], vp[:], ALU.add, r=[ps.name, vp.name], w=[UV.name])
                ps = self.next_ps()
                ps2 = self.next_ps()
                for h in range(H):
                    hs = slice(h * 128, (h + 1) * 128)
                    kb.mm(ps[:, hs], rT[:, hs], T0[:, hs], True, False, r=[rT.name, T0.name], w=[ps.name])
                    kb.mm(ps[:, hs], MT[:, hs], UV[:, hs], False, True, r=[MT.name, UV.name], w=[ps.name])
                    kb.mm(ps2[:, hs], khat[:, hs], UV[:, hs], True, True, r=[khat.name, UV.name], w=[ps2.name])
                kb.cp("act", Y[:], ps[:], r=[ps.name], w=[Y.name])
                for h in range(H):
                    hs = slice(h * 128, (h + 1) * 128)
                    kb.stt("dve", T0[:, hs], T0[:, hs], eGC[:, h:h + 1], ps2[:, hs], ALU.mult, ALU.add,
                           r=[T0.name, eGC.name, ps2.name], w=[T0.name])
                kb.act(ysq[:], Y[:], AF.Square, r=[Y.name], w=[ysq.name])
                kb.add("dve", lambda e: e.tensor_reduce(yss[:], ysq[:].rearrange("p (h n) -> p h n", h=4), AX.X, ALU.add),
                       [ysq.name], [yss.name])
                self.rsqrt(yrs[:], yss[:], 1.0 / N, EPS, [yss.name], [yrs.name])
                kb.tt("dve", hv(o[:]), hv(Y[:]), bb(yrs), ALU.mult, r=[Y.name, yrs.name], w=[o.name])
                kb.tt("pool", hv(o[:]), hv(o[:]), nw[:].unsqueeze(1).to_broadcast([128, 4, 128]), ALU.mult,
                      r=[o.name, nw.name], w=[o.name])
                kb.tt("dve", o[:], o[:], sz[:], ALU.mult, r=[o.name, sz.name], w=[o.name])
                kb.dma(self.o_mix[l][2][r0:r0 + 128, :], o[:], r=[o.name], q="pool")
                hook(); hook()
            if gen is not None:
                for _ in gen:
                    pass
            kb.flush()

    def declare_rwkv(self):
        L, S = self.L, self.S
        i = self.inp
        for nm, shp in (("rwkv_mu", [L, RWKV_IN]), ("rwkv_w0", [L, MIX]), ("rwkv_w_up", [L, 64, MIX]),
                        ("rwkv_a0", [L, MIX]), ("rwkv_a_up", [L, 64, MIX]), ("rwkv_g_up", [L, 128, MIX]),
                        ("rwkv_k_k", [L, MIX]), ("rwkv_k_a", [L, MIX]), ("rwkv_r_k", [L, MIX]),
                        ("rwkv_ln_w", [L, MIX]), ("rwkv_ln_b", [L, MIX]), ("rwkv_v0", [1, MIX]),
                        ("rwkv_vres_up", [1, 32, MIX])):
            setattr(self, nm, i(nm, shp))
        self.vfirst = self.scr("vfirst", [S, MIX])

    def phase_rwkv(self, l):
        kb, S, NT = self.kb, self.S, self.NT
        H, N = 8, 64
        P = self.proj_tm[l]
        c0 = OFF_RWKV
        with ExitStack() as st:
            sb = lambda name, shape, dt=F32: self.sb(st, name, shape, dt)
            self.pss = [self.psb(st, "ps_r%d" % j) for j in range(8)]
            idf = sb("idf", [128, 128]); ones = sb("ones", [128, 128])
            tin = sb("tin", [128, 128]); tst = sb("tst", [128, 128])
            kb.dma(idf[:], self.ident_f[:, :], w=[idf.name])
            kb.dma(ones[:], self.ones_f[:, :], w=[ones.name])
            kb.dma(tin[:], self.triu_incl[:, :], w=[tin.name])
            kb.dma(tst[:], self.triu_strict[:, :], w=[tst.name])

            def bvec(name, src, n):
                tl = sb(name, [128, n])
                kb.dma(tl[:], src[l:l + 1, :].to_broadcast([128, n]), w=[tl.name])
                return tl
            mu = bvec("mu", self.rwkv_mu, RWKV_IN)
            w0 = bvec("w0", self.rwkv_w0, MIX); a0 = bvec("a0", self.rwkv_a0, MIX)
            k_k = bvec("k_k", self.rwkv_k_k, MIX); k_a = bvec("k_a", self.rwkv_k_a, MIX)
            r_k = bvec("r_k", self.rwkv_r_k, MIX); ln_w = bvec("ln_w", self.rwkv_ln_w, MIX)
            ln_b = bvec("ln_b", self.rwkv_ln_b, MIX)
            wup = sb("wup", [64, MIX]); aup = sb("aup", [64, MIX]); gup = sb("gup", [128, MIX])
            kb.dma(wup[:], self.rwkv_w_up[l], w=[wup.name])
            kb.dma(aup[:], self.rwkv_a_up[l], w=[aup.name])
            kb.dma(gup[:], self.rwkv_g_up[l], w=[gup.name])
            if l > 0:
                v0 = sb("v0", [128, MIX]); vup = sb("vup", [32, MIX])
                kb.dma(v0[:], self.rwkv_v0[0:1, :].to_broadcast([128, MIX]), w=[v0.name])
                kb.dma(vup[:], self.rwkv_vres_up[0], w=[vup.name])
                vf = sb("vf", [128, MIX]); xvd = sb("xvd", [128, 32]); xvdT = sb("xvdT", [32, 128])
            X0 = sb("X0", [128, RWKV_IN]); X1s = sb("X1s", [128, RWKV_IN]); c = sb("c", [128, RWKV_IN])
            sm = sb("sm", [128, 256]); smT = sb("smT", [128, 256])
            wdT = sb("wdT", [64, 128]); adT = sb("adT", [64, 128])
            ld = sb("ld", [128, MIX]); a = sb("a", [128, MIX]); g = sb("g", [128, MIX])
            kk = sb("kk", [128, MIX]); t1 = sb("t1", [128, MIX]); t2 = sb("t2", [128, MIX])
            ssq = sb("ssq", [128, 8]); rs = sb("rs", [128, 8])
            kp = sb("kp", [128, MIX]); bet = sb("bet", [128, MIX])
            G = sb("G", [128, MIX]); eG = sb("eG", [128, MIX]); enG = sb("enG", [128, MIX])
            eGr = sb("eGr", [128, MIX]); eGm = sb("eGm", [128, MIX])
            at = sb("at", [128, MIX]); bt = sb("bt", [128, MIX]); kt = sb("kt", [128, MIX]); rt = sb("rt", [128, MIX])
            bh = sb("bh", [128, MIX]); kh = sb("kh", [128, MIX])
            aT = sb("aT", [64, 1024]); bT = sb("bT", [64, 1024]); kT = sb("kT", [64, 1024]); rT = sb("rT", [64, 1024])
            PCT = sb("PCT", [64, 8])
            Nm = [sb("Nm%d" % k, [128, 1024]) for k in range(2)]
            Lm = [sb("Lm%d" % k, [128, 1024]) for k in range(2)]
            Pm = [sb("Pm%d" % k, [128, 1024]) for k in range(2)]
            LakT = sb("LakT", [128, 1024]); MrbT = sb("MrbT", [128, 1024]); MrkT = sb("MrkT", [128, 1024])
            X1 = sb("X1", [128, MIX]); U = sb("U", [128, MIX]); Y = sb("Y", [128, MIX])
            T0 = sb("T0", [64, MIX])
            mean = sb("mean", [128, 8]); var = sb("var", [128, 8]); rkv = sb("rkv", [128, 8])
            kb.memset("dve", T0[:], 0.0, w=[T0.name])
            hv = lambda x: x.rearrange("p (h n) -> p h n", h=H)
            b8 = lambda x: x[:].unsqueeze(2).to_broadcast([128, 8, 64])
            msk = lambda m: m[:].unsqueeze(1).to_broadcast([128, 8, 128])
            hm = lambda x: x.rearrange("p (h n) -> p h n", h=H)

            def tr64(dst, src):
                for half in range(2):
                    ps = self.next_ps()
                    for hh in range(4):
                        h = half * 4 + hh
                        kb.tr(ps[0:64, hh * 128:(hh + 1) * 128], src[:, h * 64:(h + 1) * 64], idf[:],
                              r=[src.name, idf.name], w=[ps.name])
                    kb.cp("act" if half else "dve", dst[:, half * 512:(half + 1) * 512], ps[0:64, :],
                          r=[ps.name], w=[dst.name])

            def mat8(dst, lT, rT_, mask):
                for half in range(2):
                    ps = self.next_ps()
                    for hh in range(4):
                        h = half * 4 + hh
                        kb.mm(ps[:, hh * 128:(hh + 1) * 128], lT[:, h * 128:(h + 1) * 128], rT_[:, h * 128:(h + 1) * 128],
                              True, True, r=[lT.name, rT_.name], w=[ps.name])
                    kb.tt("dve", dst[:, half * 512:(half + 1) * 512].rearrange("p (h n) -> p h n", h=4),
                          ps[:].rearrange("p (h n) -> p h n", h=4),
                          mask[:].unsqueeze(1).to_broadcast([128, 4, 128]), ALU.mult,
                          r=[ps.name, mask.name], w=[dst.name])

            for t in range(NT):
                r0 = t * 128
                kb.dma(X0[:], P[r0:r0 + 128, c0:c0 + RWKV_IN], w=[X0.name])
                if t == 0:
                    kb.memset("pool", X1s[:], 0.0, w=[X1s.name])
                    kb.dma(X1s[1:128, :], P[0:127, c0:c0 + RWKV_IN], w=[X1s.name])
                else:
                    kb.dma(X1s[:], P[r0 - 1:r0 + 127, c0:c0 + RWKV_IN], w=[X1s.name])
                kb.tt("dve", c[:], X1s[:], X0[:], ALU.subtract, r=[X0.name, X1s.name], w=[c.name])
                kb.tt("pool", c[:], c[:], mu[:], ALU.mult, r=[c.name, mu.name], w=[c.name])
                kb.tt("dve", c[:], c[:], X0[:], ALU.add, r=[c.name, X0.name], w=[c.name])
                r_ = c[:, 0:512]; k_ = c[:, 512:1024]; v_ = c[:, 1024:1536]
                kb.act(sm[:, 0:64], c[:, 1536:1600], AF.Tanh, r=[c.name], w=[sm.name])
                kb.cp("dve", sm[:, 64:128], c[:, 1600:1664], r=[c.name], w=[sm.name])
                kb.act(sm[:, 128:256], c[:, 1664:1792], AF.Sigmoid, r=[c.name], w=[sm.name])
                ps = self.next_ps()
                kb.tr(ps[0:64, 0:128], sm[:, 0:64], idf[:], r=[sm.name, idf.name], w=[ps.name])
                kb.tr(ps[0:64, 128:256], sm[:, 64:128], idf[:], r=[sm.name, idf.name], w=[ps.name])
                kb.tr(ps[:, 256:384], sm[:, 128:256], idf[:], r=[sm.name, idf.name], w=[ps.name])
                kb.cp("act", wdT[:], ps[0:64, 0:128], r=[ps.name], w=[wdT.name])
                kb.cp("dve", adT[:], ps[0:64, 128:256], r=[ps.name], w=[adT.name])
                kb.cp("act", smT[:, 0:128], ps[:, 256:384], r=[ps.name], w=[smT.name])
                psw = self.next_ps(); psa = self.next_ps(); psg = self.next_ps()
                kb.mm(psw[:], wdT[:], wup[:], True, True, r=[wdT.name, wup.name], w=[psw.name])
                kb.mm(psa[:], adT[:], aup[:], True, True, r=[adT.name, aup.name], w=[psa.name])
                kb.mm(psg[:], smT[:, 0:128], gup[:], True, True, r=[smT.name, gup.name], w=[psg.name])
                kb.tt("dve", ld[:], psw[:], w0[:], ALU.add, r=[psw.name, w0.name], w=[ld.name])
                kb.act(ld[:], ld[:], AF.Sigmoid, r=[ld.name], w=[ld.name])
                kb.ts("dve", ld[:], ld[:], -float(np.exp(-0.5)), None, ALU.mult, r=[ld.name], w=[ld.name])
                kb.tt("dve", a[:], psa[:], a0[:], ALU.add, r=[psa.name, a0.name], w=[a.name])
                kb.act(a[:], a[:], AF.Sigmoid, r=[a.name], w=[a.name])
                kb.cp("act", g[:], psg[:], r=[psg.name], w=[g.name])
                if l == 0:
                    kb.dma(self.vfirst[r0:r0 + 128, :], v_, r=[c.name], q="pool")
                else:
                    kb.dma(vf[:], self.vfirst[r0:r0 + 128, :], w=[vf.name])
                    kb.dma(xvd[:], self.proj_g[l][r0:r0 + 128, 3 * D:3 * D + 32], w=[xvd.name])
                    ps = self.next_ps()
                    kb.tr(ps[0:32, 0:128], xvd[:], idf[:], r=[xvd.name, idf.name], w=[ps.name])
                    kb.cp("act", xvdT[:], ps[0:32, 0:128], r=[ps.name], w=[xvdT.name])
                    ps = self.next_ps()
                    kb.mm(ps[:], xvdT[:], vup[:], True, True, r=[xvdT.name, vup.name], w=[ps.name])
                    kb.tt("dve", t1[:], ps[:], v0[:], ALU.add, r=[ps.name, v0.name], w=[t1.name])
                    kb.act(t1[:], t1[:], AF.Sigmoid, r=[t1.name], w=[t1.name])
                    kb.tt("dve", t2[:], vf[:], v_, ALU.subtract, r=[vf.name, c.name], w=[t2.name])
                    kb.tt("dve", t2[:], t2[:], t1[:], ALU.mult, r=[t2.name, t1.name], w=[t2.name])
                    kb.tt("dve", v_, v_, t2[:], ALU.add, r=[c.name, t2.name], w=[c.name])
                kb.tt("dve", kk[:], k_, k_k[:], ALU.mult, r=[c.name, k_k.name], w=[kk.name])
                kb.act(t1[:], kk[:], AF.Square, r=[kk.name], w=[t1.name])
                kb.add("dve", lambda e: e.tensor_reduce(ssq[:], t1[:].rearrange("p (h n) -> p h n", h=8), AX.X, ALU.add),
                       [t1.name], [ssq.name])
                self.rsqrt(rs[:], ssq[:], 1.0, EPS, [ssq.name], [rs.name])
                kb.tt("dve", hv(kk[:]), hv(kk[:]), b8(rs), ALU.mult, r=[kk.name, rs.name], w=[kk.name])
                kb.ts("dve", t2[:], a[:], -1.0, None, ALU.add, r=[a.name], w=[t2.name])
                kb.tt("dve", t2[:], t2[:], k_a[:], ALU.mult, r=[t2.name, k_a.name], w=[t2.name])
                kb.ts("dve", t2[:], t2[:], 1.0, None, ALU.add, r=[t2.name], w=[t2.name])
                kb.tt("dve", kp[:], t2[:], k_, ALU.mult, r=[t2.name, c.name], w=[kp.name])
                kb.tt("pool", bet[:], kk[:], a[:], ALU.mult, r=[kk.name, a.name], w=[bet.name])
                ps = self.next_ps(); ps2 = self.next_ps()
                kb.mm(ps[:], tin[:], ld[:], True, True, r=[tin.name, ld.name], w=[ps.name])
                kb.mm(ps2[:], ones[:], ld[:], True, True, r=[ones.name, ld.name], w=[ps2.name])
                kb.cp("dve", G[:], ps[:], r=[ps.name], w=[G.name])
                kb.tt("dve", eGr[:], ps2[:], G[:], ALU.subtract, r=[ps2.name, G.name], w=[eGr.name])
                kb.act(eGr[:], eGr[:], AF.Exp, r=[eGr.name], w=[eGr.name])
                kb.act(eG[:], G[:], AF.Exp, r=[G.name], w=[eG.name])
                kb.act(enG[:], G[:], AF.Exp, r=[G.name], w=[enG.name], scale=-1.0)
                kb.tt("pool", eGm[:], G[:], ld[:], ALU.subtract, r=[G.name, ld.name], w=[eGm.name])
                kb.act(eGm[:], eGm[:], AF.Exp, r=[eGm.name], w=[eGm.name])
                ps = self.next_ps()
                for h in range(H):
                    kb.mm(ps[0:64, h:h + 1], ld[:, h * 64:(h + 1) * 64], ones[:, 0:1], True, True,
                          r=[ld.name, ones.name], w=[ps.name])
                kb.act(PCT[:], ps[0:64, 0:8], AF.Exp, r=[ps.name], w=[PCT.name])
                kb.tt("dve", at[:], kk[:], eGm[:], ALU.mult, r=[kk.name, eGm.name], w=[at.name])
                kb.ts("pool", at[:], at[:], -1.0, None, ALU.mult, r=[at.name], w=[at.name])
                kb.tt("dve", bt[:], bet[:], enG[:], ALU.mult, r=[bet.name, enG.name], w=[bt.name])
                kb.tt("pool", kt[:], kp[:], enG[:], ALU.mult, r=[kp.name, enG.name], w=[kt.name])
                kb.tt("dve", rt[:], r_, eG[:], ALU.mult, r=[c.name, eG.name], w=[rt.name])
                kb.tt("pool", bh[:], bet[:], eGr[:], ALU.mult, r=[bet.name, eGr.name], w=[bh.name])
                kb.tt("dve", kh[:], kp[:], eGr[:], ALU.mult, r=[kp.name, eGr.name], w=[kh.name])
                tr64(aT, at); tr64(bT, bt); tr64(kT, kt); tr64(rT, rt)
                N0 = Nm[0]
                mat8(N0, bT, aT, tst)
                L0 = Lm[0]
                for half in range(2):
                    self.tr_f32(L0[:, half * 512:(half + 1) * 512], N0[:, half * 512:(half + 1) * 512], idf)
                P0 = Pm[0]
                kb.tt("dve", hm(P0[:]), hm(N0[:]), msk(idf), ALU.add, r=[N0.name, idf.name], w=[P0.name])
                cur = 0
                for lev in range(6):
                    Nc, Lc, Pc = Nm[cur], Lm[cur], Pm[cur]
                    Nn, Ln, Pn = Nm[1 - cur], Lm[1 - cur], Pm[1 - cur]
                    for half in range(2):
                        hsl = slice(half * 512, (half + 1) * 512)
                        psl = self.next_ps()
                        for hh in range(4):
                            hs = slice(half * 512 + hh * 128, half * 512 + (hh + 1) * 128)
                            kb.mm(psl[:, hh * 128:(hh + 1) * 128], Nc[:, hs], Lc[:, hs], True, True,
                                  r=[Nc.name, Lc.name], w=[psl.name])
                        kb.cp("act", Ln[:, hsl], psl[:], r=[psl.name], w=[Ln.name])
                        if lev < 5:
                            psn = self.next_ps()
                            for hh in range(4):
                                hs = slice(half * 512 + hh * 128, half * 512 + (hh + 1) * 128)
                                kb.mm(psn[:, hh * 128:(hh + 1) * 128], Lc[:, hs], Nc[:, hs], True, True,
                                      r=[Nc.name, Lc.name], w=[psn.name])
                            kb.cp("dve", Nn[:, hsl], psn[:], r=[psn.name], w=[Nn.name])
                    for half in range(2):
                        hsl = slice(half * 512, (half + 1) * 512)
                        psp = self.next_ps()
                        for hh in range(4):
                            hs = slice(half * 512 + hh * 128, half * 512 + (hh + 1) * 128)
                            kb.mm(psp[:, hh * 128:(hh + 1) * 128], Ln[:, hs], Pc[:, hs], True, True,
                                  r=[Ln.name, Pc.name], w=[psp.name])
                        kb.tt("dve", Pn[:, hsl], psp[:], Pc[:, hsl], ALU.add, r=[psp.name, Pc.name], w=[Pn.name])
                    cur = 1 - cur
                    if lev == 0:
                        mat8(LakT, kT, aT, tst)
                    elif lev == 1:
                        mat8(MrbT, bT, rT, tin)
                    elif lev == 2:
                        mat8(MrkT, kT, rT, tin)
                    elif lev == 3:
                        kb.tt("pool", t2[:], r_, kp[:], ALU.mult, r=[c.name, kp.name], w=[t2.name])
                        kb.tt("pool", t2[:], t2[:], r_k[:], ALU.mult, r=[t2.name, r_k.name], w=[t2.name])
                        kb.add("dve", lambda e: e.tensor_reduce(rkv[:], t2[:].rearrange("p (h n) -> p h n", h=8), AX.X, ALU.add),
                               [t2.name], [rkv.name])
                        kb.tt("dve", hv(t2[:]), hv(v_), b8(rkv), ALU.mult, r=[c.name, rkv.name], w=[t2.name])
                WT = Pm[cur]
                ps = self.next_ps()
                for h in range(H):
                    hs = slice(h * 64, (h + 1) * 64); ms = slice(h * 128, (h + 1) * 128)
                    kb.mm(ps[:, hs], aT[:, ms], T0[:, hs], True, False, r=[aT.name, T0.name], w=[ps.name])
                    kb.mm(ps[:, hs], LakT[:, ms], c[:, 1024 + h * 64:1024 + (h + 1) * 64], False, True,
                          r=[LakT.name, c.name], w=[ps.name])
                kb.cp("act", X1[:], ps[:], r=[ps.name], w=[X1.name])
                ps = self.next_ps()
                for h in range(H):
                    hs = slice(h * 64, (h + 1) * 64); ms = slice(h * 128, (h + 1) * 128)
                    kb.mm(ps[:, hs], WT[:, ms], X1[:, hs], True, True, r=[WT.name, X1.name], w=[ps.name])
                kb.cp("dve", U[:], ps[:], r=[ps.name], w=[U.name])
                ps = self.next_ps(); ps2 = self.next_ps()
                for h in range(H):
                    hs = slice(h * 64, (h + 1) * 64); ms = slice(h * 128, (h + 1) * 128)
                    vs = c[:, 1024 + h * 64:1024 + (h + 1) * 64]
                    kb.mm(ps[:, hs], rT[:, ms], T0[:, hs], True, False, r=[rT.name, T0.name], w=[ps.name])
                    kb.mm(ps[:, hs], MrbT[:, ms], U[:, hs], False, False, r=[MrbT.name, U.name], w=[ps.name])
                    kb.mm(ps[:, hs], MrkT[:, ms], vs, False, True, r=[MrkT.name, c.name], w=[ps.name])
                    kb.mm(ps2[0:64, hs], bh[:, hs], U[:, hs], True, False, r=[bh.name, U.name], w=[ps2.name])
                    kb.mm(ps2[0:64, hs], kh[:, hs], vs, False, True, r=[kh.name, c.name], w=[ps2.name])
                kb.cp("act", Y[:], ps[:], r=[ps.name], w=[Y.name])
                for h in range(H):
                    hs = slice(h * 64, (h + 1) * 64)
                    kb.stt("dve", T0[:, hs], T0[:, hs], PCT[:, h:h + 1], ps2[0:64, hs], ALU.mult, ALU.add,
                           r=[T0.name, PCT.name, ps2.name], w=[T0.name])
                kb.add("dve", lambda e: e.tensor_reduce(mean[:], Y[:].rearrange("p (h n) -> p h n", h=8), AX.X, ALU.add),
                       [Y.name], [mean.name])
                kb.ts("dve", mean[:], mean[:], 1.0 / N, None, ALU.mult, r=[mean.name], w=[mean.name])
                kb.tt("dve", hv(Y[:]), hv(Y[:]), b8(mean), ALU.subtract, r=[Y.name, mean.name], w=[Y.name])
                kb.act(t1[:], Y[:], AF.Square, r=[Y.name], w=[t1.name])
                kb.add("dve", lambda e: e.tensor_reduce(var[:], t1[:].rearrange("p (h n) -> p h n", h=8), AX.X, ALU.add),
                       [t1.name], [var.name])
                self.rsqrt(var[:], var[:], 1.0 / N, 64e-5, [var.name], [var.name])
                kb.tt("dve", hv(Y[:]), hv(Y[:]), b8(var), ALU.mult, r=[Y.name, var.name], w=[Y.name])
                kb.tt("pool", Y[:], Y[:], ln_w[:], ALU.mult, r=[Y.name, ln_w.name], w=[Y.name])
                kb.tt("dve", Y[:], Y[:], ln_b[:], ALU.add, r=[Y.name, ln_b.name], w=[Y.name])
                kb.tt("dve", Y[:], Y[:], t2[:], ALU.add, r=[Y.name, t2.name], w=[Y.name])
                kb.tt("dve", Y[:], Y[:], g[:], ALU.mult, r=[Y.name, g.name], w=[Y.name])
                kb.dma(self.o_mix[l][1][r0:r0 + 128, :], Y[:], r=[Y.name], q="pool")
            kb.flush()

    def build(self):
        self.declare(); self.declare_rest(); self.declare_gdn(); self.declare_rwkv(); self.declare_nsa()
        x = self.x_in
        for l in range(self.L):
            self.phase_proj(l, x)
            self.phase_gdn(l, side=True)
            self.phase_nsa_attn(l)
            self.phase_rwkv(l)
            self.phase_merge(l, x)
            self.phase_ffn(l)
            x = self.x_lay[l]
        return self.nc


def _consts():
    import ml_dtypes
    f = np.float32
    return {"ident_bf": np.eye(128, dtype=ml_dtypes.bfloat16), "ident_f": np.eye(128, dtype=f),
            "ones_f": np.ones((128, 128), f), "triu_incl": np.triu(np.ones((128, 128), f)),
            "triu_strict": np.triu(np.ones((128, 128), f), 1)}


def kernel(**inputs):
    x = np.asarray(inputs["x"], np.float32)
    B, S, _ = x.shape
    L = inputs["w_in"].shape[0]
    prog = Prog(S, n_layers=L)
    nc = prog.build()
    w_in = np.asarray(inputs["w_in"], np.float32)
    ext = np.zeros((L, D, 32), np.float32)
    ext[1:] = np.asarray(inputs["rwkv_vres_down"], np.float32)
    shared = dict(_consts())
    shared.update(_nsa_consts(S))
    shared["w_in"] = np.ascontiguousarray(np.concatenate([w_in, ext], axis=2))
    for k in prog.din:
        if k not in shared and k != "x":
            shared[k] = np.ascontiguousarray(np.asarray(inputs[k], np.float32))
    in_maps = [dict(shared, x=np.ascontiguousarray(x[b])) for b in range(B)]
    res = run_bass_kernel_spmd(nc, in_maps, core_ids=list(range(B)))
    return np.stack([np.asarray(r["out"], np.float32) for r in res.results], axis=0)


def _nsa_consts(S):
    import ml_dtypes
    f = np.float32
    NB = S // 64
    n_cmp = (S - 32) // 16 + 1
    nch = (n_cmp + 127) // 128
    ex = np.zeros((128, S), f)
    ex[np.arange(S) // 64, np.arange(S)] = 1.0
    k = np.arange(128)[:, None]
    q = np.arange(128)[None, :]
    caus = np.where(k > q, NEGM, 0.0).astype(f)
    win = np.where(k <= q, NEGM, 0.0).astype(f)
    rr = np.arange(17)[None, :, None]
    cm = ((16 * k[:, :, None] + 31) <= (128 * rr + q[:, None, :])).astype(f)
    cs = np.arange(nch * 128) * 16
    ss = np.arange(NB) * 64
    ov = np.clip(np.minimum(cs[:, None] + 32, ss[None, :] + 64) - np.maximum(cs[:, None], ss[None, :]), 0, None) / 32.0
    ov[n_cmp:] = 0.0
    c2s = ov.reshape(nch, 128, NB).transpose(1, 0, 2).astype(f)
    return {"exall": ex.astype(ml_dtypes.bfloat16),
            "causneg": np.tile(caus, (1, 4)).astype(ml_dtypes.bfloat16),
            "winneg": np.tile(win, (1, 4)).astype(ml_dtypes.bfloat16),
            "cmask": np.ascontiguousarray(cm), "cmp2slc": np.ascontiguousarray(c2s)}


def _declare_nsa(self):
    L, S = self.L, self.S
    i = self.inp
    self.NB = S // 64
    self.n_cmp = (S - 32) // 16 + 1
    self.nch = (self.n_cmp + 127) // 128
    self.nsa_q_norm = i("nsa_q_norm", [L, 64])
    self.nsa_k_norm = i("nsa_k_norm", [L, 3, 64])
    self.nsa_cmp_pos = i("nsa_cmp_pos", [L, 2, 32, 64])
    self.nsa_cmp_w1 = i("nsa_cmp_w1", [L, 2, 2048, 256])
    self.nsa_cmp_w2 = i("nsa_cmp_w2", [L, 2, 256, 64])
    self.exall = i("exall", [128, S], BF16)
    self.causneg = i("causneg", [128, 512], BF16)
    self.winneg = i("winneg", [128, 512], BF16)
    self.cmask = i("cmask", [128, 17, 128])
    self.cmp2slc = i("cmp2slc", [128, self.nch, self.NB])
    self.qT_d = self.scr("qT_d", [8, 64, S])
    self.kswT_d = self.scr("kswT_d", [4, 64, S], BF16)
    self.kvcT_d = self.scr("kvcT_d", [4, 64, S])
    self.vaug_d = self.scr("vaug_d", [S, 4 * 65], BF16)


def _gen_nsa_prep(self, l, st, pA, pB):
    kb, S, NT = self.kb, self.S, self.NT
    P = self.proj_tm[l]
    sb = lambda name, shape, dt=F32: self.sb(st, "np_" + name, shape, dt)
    idf = sb("idf", [128, 128])
    kb.dma(idf[:], self.ident_f[:, :], w=[idf.name])
    gq = sb("gq", [128, 12, 64])
    for h in range(8):
        kb.dma(gq[:, h, :], self.nsa_q_norm[l:l + 1, :].to_broadcast([128, 64]), w=[gq.name])
    for h in range(2):
        kb.dma(gq[:, 8 + h, :], self.nsa_k_norm[l, 1:2, :].to_broadcast([128, 64]), w=[gq.name])
        kb.dma(gq[:, 10 + h, :], self.nsa_k_norm[l, 2:3, :].to_broadcast([128, 64]), w=[gq.name])
    xin = [sb("xin%d" % j, [128, NSA_IN]) for j in range(2)]
    nin = sb("nin", [128, 768]); sq = sb("sq", [128, 768]); ss = sb("ss", [128, 12]); rs = sb("rs", [128, 12])
    qo = [sb("qo%d" % j, [64, 8, 128]) for j in range(2)]
    ko = [sb("ko%d" % j, [64, 4, 128], BF16) for j in range(2)]
    co = [sb("co%d" % j, [64, 4, 128]) for j in range(2)]
    va = [sb("va%d" % j, [128, 4, 65], BF16) for j in range(2)]
    for j in range(2):
        kb.memset("dve", va[j][:], 1.0, w=[va[j].name])
    h12 = lambda x: x.rearrange("p (h n) -> p h n", h=12)
    yield
    for t in range(NT):
        j = t % 2
        rows = slice(t * 128, (t + 1) * 128)
        x = xin[j]
        kb.dma(x[:], P[rows, 0:NSA_IN], w=[x.name])
        kb.cp("pool", nin[:, 0:512], x[:, 0:512], r=[x.name], w=[nin.name])
        kb.cp("pool", nin[:, 512:640], x[:, 768:896], r=[x.name], w=[nin.name])
        kb.cp("pool", nin[:, 640:768], x[:, 1024:1152], r=[x.name], w=[nin.name])
        kb.act(sq[:], nin[:], AF.Square, r=[nin.name], w=[sq.name])
        yield
        kb.add("dve", lambda e: e.tensor_reduce(ss[:], sq[:].rearrange("p (h n) -> p h n", h=12), AX.X, ALU.add),
               [sq.name], [ss.name])
        self.rsqrt(rs[:], ss[:], 1.0 / 64, EPS, [ss.name], [rs.name])
        yield
        kb.tt("dve", h12(nin[:]), h12(nin[:]), rs[:].unsqueeze(2).to_broadcast([128, 12, 64]), ALU.mult,
              r=[nin.name, rs.name], w=[nin.name])
        kb.tt("pool", nin[:], nin[:], gq[:].rearrange("p h n -> p (h n)"), ALU.mult, r=[nin.name, gq.name], w=[nin.name])
        yield
        for half in range(2):
            for blk in range(4):
                b8 = half * 4 + blk
                kb.tr(pA[0:64, blk * 128:(blk + 1) * 128], nin[:, b8 * 64:(b8 + 1) * 64], idf[:],
                      r=[nin.name, idf.name], w=[pA.name])
            kb.cp("act" if half else "dve", qo[j][:, half * 4:half * 4 + 4, :].rearrange("p h t -> p (h t)"), pA[0:64, :],
                  r=[pA.name], w=[qo[j].name])
            yield
        for blk in range(4):
            kb.tr(pB[0:64, blk * 128:(blk + 1) * 128], nin[:, 512 + blk * 64:512 + (blk + 1) * 64], idf[:],
                  r=[nin.name, idf.name], w=[pB.name])
        kb.cp("act", ko[j][:].rearrange("p h t -> p (h t)"), pB[0:64, :], r=[pB.name], w=[ko[j].name])
        yield
        for blk in range(4):
            kb.tr(pB[0:64, blk * 128:(blk + 1) * 128], x[:, 512 + blk * 64:512 + (blk + 1) * 64], idf[:],
                  r=[x.name, idf.name], w=[pB.name])
        kb.cp("dve", co[j][:].rearrange("p h t -> p (h t)"), pB[0:64, :], r=[pB.name], w=[co[j].name])
        yield
        kb.dma(self.qT_d[:, :, rows].rearrange("h d t -> d h t"), qo[j][:], r=[qo[j].name], q="pool")
        kb.dma(self.kswT_d[:, :, rows].rearrange("h d t -> d h t"), ko[j][:], r=[ko[j].name], q="pool")
        kb.dma(self.kvcT_d[:, :, rows].rearrange("h d t -> d h t"), co[j][:], r=[co[j].name], q="pool")
        kb.cp("pool", va[j][:, 0:2, 0:64], x[:, 896:1024].rearrange("p (g n) -> p g n", g=2), r=[x.name], w=[va[j].name])
        kb.cp("pool", va[j][:, 2:4, 0:64], x[:, 1152:1280].rearrange("p (g n) -> p g n", g=2), r=[x.name], w=[va[j].name])
        kb.dma(self.vaug_d[rows, :], va[j][:].rearrange("p g n -> p (g n)"), r=[va[j].name], q="pool")
        yield


def _phase_nsa_prep(self, l):
    with ExitStack() as st:
        pA = self.psb(st, "ps_npA"); pB = self.psb(st, "ps_npB")
        for _ in self.gen_nsa_prep(l, st, pA, pB):
            pass
        self.kb.flush()


Prog.gen_nsa_prep = _gen_nsa_prep
Prog.declare_nsa = _declare_nsa
Prog.phase_nsa_prep = _phase_nsa_prep


def _phase_nsa_attn(self, l):
    kb, S, NT, NB, n_cmp, nch = self.kb, self.S, self.NT, self.NB, self.n_cmp, self.nch
    P = self.proj_tm[l]
    SC = 0.125
    with ExitStack() as st:
        sb = lambda name, shape, dt=F32: self.sb(st, name, shape, dt)
        psS = [self.psb(st, "ps_S%d" % j) for j in range(2)]
        psO = [self.psb(st, "ps_O%d" % j) for j in range(2)]
        psI = self.psb(st, "ps_I"); psT = self.psb(st, "ps_T")
        psX = [self.psb(st, "ps_X%d" % j) for j in range(2)]
        idf = sb("idf", [128, 128]); idb = sb("idb", [128, 128], BF16); ones = sb("ones", [128, 128])
        kb.dma(idf[:], self.ident_f[:, :], w=[idf.name])
        kb.dma(idb[:], self.ident_bf[:, :], w=[idb.name])
        kb.dma(ones[:], self.ones_f[:, :], w=[ones.name])
        ks = sb("ks", [64, 2, S], BF16); kw = sb("kw", [64, 2, S], BF16)
        kb.dma(ks[:], self.kswT_d[0:2].rearrange("g d s -> d g s"), w=[ks.name])
        kb.dma(kw[:], self.kswT_d[2:4].rearrange("g d s -> d g s"), w=[kw.name])
        vv = sb("vv", [128, NT, 4 * 65], BF16)
        kb.dma(vv[:], self.vaug_d.rearrange("(c p) n -> p c n", p=128), w=[vv.name])
        ex = sb("ex", [128, S], BF16)
        kb.dma(ex[:], self.exall[:, :], w=[ex.name])
        cneg = sb("cneg", [128, 512], BF16); wneg = sb("wneg", [128, 512], BF16)
        kb.dma(cneg[:], self.causneg[:, :], w=[cneg.name])
        kb.dma(wneg[:], self.winneg[:, :], w=[wneg.name])
        cmk = sb("cmk", [128, 17, 128]); c2s = sb("c2s", [128, nch, NB])
        kb.dma(cmk[:], self.cmask[:, :, :], w=[cmk.name])
        kb.dma(c2s[:], self.cmp2slc[:, :, :], w=[c2s.name])
        kcT = sb("kcT", [64, 2, nch * 128]); vc = sb("vc", [128, nch, 2, 65])
        kb.memset("dve", kcT[:], 0.0, w=[kcT.name])
        kb.memset("dve", vc[:], 0.0, w=[vc.name])
        kb.memset("dve", vc[:, :, :, 64:65], 1.0, w=[vc.name])
        with ExitStack() as s2:
            sb2 = lambda name, shape, dt=F32: self.sb(s2, name, shape, dt)
            w1 = sb2("w1c", [64, 32, 256]); w2 = sb2("w2c", [128, 2, 64]); pos = sb2("pos", [32, 64]); posT = sb2("posT", [64, 32])
            tT = sb2("tT", [64, S]); hid = sb2("hid", [128, 2, 512]); cb = sb2("cb", [128, 2])
            kg0r = sb2("kg0r", [1, 64]); kg0 = sb2("kg0", [64, 1]); sqc = sb2("sqc", [64, 512]); rsc = sb2("rsc", [64, 512])
            kcr = sb2("kcr", [64, 512])
            kb.dma(kg0r[:], self.nsa_k_norm[l, 0:1, :], w=[kg0r.name])
            kb.tr(psT[0:64, 0:1], kg0r[:], idf[0:1, 0:1], r=[kg0r.name, idf.name], w=[psT.name])
            kb.cp("dve", kg0[:], psT[0:64, 0:1], r=[psT.name], w=[kg0.name])
            for jj in range(2):
                kb.dma(w1[:], self.nsa_cmp_w1[l, jj].rearrange("(l d) n -> d l n", d=64), w=[w1.name])
                kb.dma(w2[:], self.nsa_cmp_w2[l, jj].rearrange("(k p) n -> p k n", p=128), w=[w2.name])
                kb.dma(pos[:], self.nsa_cmp_pos[l, jj], w=[pos.name])
                kb.tr(psT[0:64, 0:32], pos[:], idf[0:32, 0:32], r=[pos.name, idf.name], w=[psT.name])
                kb.cp("dve", posT[:], psT[0:64, 0:32], r=[psT.name], w=[posT.name])
                for half in range(2):
                    for li in range(32):
                        kb.mm(psT[:, 64 + half:65 + half], w1[:, li, half * 128:(half + 1) * 128], posT[:, li:li + 1],
                              li == 0, li == 31, r=[w1.name, posT.name], w=[psT.name])
                kb.cp("dve", cb[:], psT[:, 64:66], r=[psT.name], w=[cb.name])
                for g in range(2):
                    kb.dma(tT[:], self.kvcT_d[jj * 2 + g], w=[tT.name])
                    for half in range(2):
                        for li in range(32):
                            kb.mm(psX[half][:, 0:n_cmp], w1[:, li, half * 128:(half + 1) * 128],
                                  tT[:, li:li + 16 * (n_cmp - 1) + 1:16], li == 0, li == 31,
                                  r=[w1.name, tT.name], w=[psX[half].name])
                        kb.act(hid[:, half, 0:n_cmp], psX[half][:, 0:n_cmp], AF.Silu, bias=cb[:, half:half + 1],
                               r=[psX[half].name, cb.name], w=[hid.name])
                    if jj == 0:
                        for half in range(2):
                            kb.mm(psT[0:64, 0:n_cmp], w2[:, half, :], hid[:, half, 0:n_cmp], half == 0, half == 1,
                                  r=[w2.name, hid.name], w=[psT.name])
                        kb.act(sqc[:, 0:n_cmp], psT[0:64, 0:n_cmp], AF.Square, r=[psT.name], w=[sqc.name])
                        kb.cp("dve", kcr[:, 0:n_cmp], psT[0:64, 0:n_cmp], r=[psT.name], w=[kcr.name])
                        kb.mm(psI[0:64, 0:n_cmp], ones[0:64, 0:64], sqc[:, 0:n_cmp], True, True,
                              r=[ones.name, sqc.name], w=[psI.name])
                        self.rsqrt(rsc[:, 0:n_cmp], psI[0:64, 0:n_cmp], 1.0 / 64, EPS, [psI.name], [rsc.name])
                        kb.stt("dve", kcT[:, g, 0:n_cmp], kcr[:, 0:n_cmp], kg0[:, 0:1], rsc[:, 0:n_cmp], ALU.mult, ALU.mult,
                               r=[kcr.name, kg0.name, rsc.name], w=[kcT.name])
                    else:
                        for ch in range(nch):
                            cw = min(128, n_cmp - ch * 128)
                            for half in range(2):
                                kb.mm(psT[0:cw, 0:64], hid[:, half, ch * 128:ch * 128 + cw], w2[:, half, :],
                                      half == 0, half == 1, r=[hid.name, w2.name], w=[psT.name])
                            kb.cp("dve", vc[0:cw, ch, g, 0:64], psT[0:cw, 0:64], r=[psT.name], w=[vc.name])
            kb.flush()
        Ob = [[psO[0], psX[0]], [psO[1], psX[1]]]
        Ib = [psI, psT]
        q32 = [sb("q32_%d" % j, [64, 4, 128]) for j in range(2)]
        q16 = [sb("q16_%d" % j, [64, 4, 128], BF16) for j in range(2)]
        gs = [sb("gs%d" % j, [128, 24]) for j in range(2)]
        e32 = [sb("e32_%d" % j, [128, 512]) for j in range(2)]
        e16 = [sb("e16_%d" % j, [128, 512], BF16) for j in range(4)]
        den = [sb("den%d" % j, [128, 4]) for j in range(2)]
        rden = [sb("rden%d" % j, [128, 4]) for j in range(2)]
        coef = [sb("coef%d" % j, [128, 4]) for j in range(2)]
        impm = [sb("impm%d" % j, [128, NB]) for j in range(2)]
        wk = [sb("wk%d" % j, [128, NB]) for j in range(2)]
        m1 = [sb("m1_%d" % j, [128, 8]) for j in range(2)]
        m2 = [sb("m2_%d" % j, [128, 8]) for j in range(2)]
        sel = [sb("sel%d" % j, [128, NB]) for j in range(2)]
        negT = [sb("negT%d" % j, [128, 512], BF16) for j in range(2)]
        oacc = [sb("oacc%d" % j, [128, MIX]) for j in range(2)]
        for g in range(2):
            kb.memset("dve", negT[g][:], 0.0, w=[negT[g].name])
        cnt = {"S": 0, "E": 0}

        def finish_branch(pso, g, br, oa, gsb, first):
            kb.ts("dve", den[g][:], pso[:, 64:260:65], 1e-30, None, ALU.max, r=[pso.name], w=[den[g].name])
            kb.add("dve", lambda e: e.reciprocal(rden[g][:], den[g][:]), [den[g].name], [rden[g].name])
            kb.tt("dve", coef[g][:], rden[g][:], gsb[:, g * 12 + br:g * 12 + 12:3], ALU.mult,
                  r=[rden[g].name, gsb.name], w=[coef[g].name])
            for h in range(4):
                osl = oa[:, (4 * g + h) * 64:(4 * g + h + 1) * 64]
                if first:
                    kb.ts("dve", osl, pso[:, h * 65:h * 65 + 64], coef[g][:, h:h + 1], None, ALU.mult,
                          r=[pso.name, coef[g].name], w=[oa.name + ":%d" % g])
                else:
                    kb.stt("dve", osl, pso[:, h * 65:h * 65 + 64], coef[g][:, h:h + 1], osl, ALU.mult, ALU.add,
                           r=[pso.name, coef[g].name, oa.name + ":%d" % g], w=[oa.name + ":%d" % g])

        def bg_gen(b, g, oa, gsb):
            rows = slice(b * 128, (b + 1) * 128)
            kb.dma(q32[g][:], self.qT_d[4 * g:4 * g + 4, :, rows].rearrange("h d t -> d h t"), w=[q32[g].name])
            kb.cp("pool", q16[g][:], q32[g][:], r=[q32[g].name], w=[q16[g].name])
            qf32 = q32[g][:].rearrange("p h t -> p (h t)")
            qf16 = q16[g][:].rearrange("p h t -> p (h t)")
            psI_ = Ib[g]
            yield
            pso = Ob[g][0]
            chunks = list(range(0, min(b // 16, nch - 1) + 1))
            for ci, kc in enumerate(chunks):
                pS = psS[cnt["S"] % 2]; e = e32[cnt["S"] % 2]; cnt["S"] += 1
                kb.mm(pS[:], kcT[:, g, kc * 128:(kc + 1) * 128], qf32, True, True,
                      r=[kcT.name, q32[g].name], w=[pS.name])
                kb.act(e[:], pS[:], AF.Exp, scale=SC, r=[pS.name], w=[e.name])
                rr = b - 16 * kc
                if rr <= 16:
                    kb.tt("dve", e[:].rearrange("p (h t) -> p h t", h=4), e[:].rearrange("p (h t) -> p h t", h=4),
                          cmk[:, rr, :].unsqueeze(1).to_broadcast([128, 4, 128]), ALU.mult,
                          r=[e.name, cmk.name], w=[e.name])
                for h in range(4):
                    kb.mm(pso[:, h * 65:(h + 1) * 65], e[:, h * 128:(h + 1) * 128], vc[:, kc, g, :],
                          ci == 0 and h == 0, False, r=[e.name, vc.name], w=[pso.name])
                for h in range(4):
                    kb.mm(psI_[:, h * 128:h * 128 + NB], e[:, h * 128:(h + 1) * 128], c2s[:, kc, :],
                          ci == 0 and h == 0, False, r=[e.name, c2s.name], w=[psI_.name])
                yield
            finish_branch(pso, g, 0, oa, gsb, True)
            yield
            if NB > 16:
                im = impm[g]
                for h in range(4):
                    if h == 0:
                        kb.ts("dve", im[:], psI_[:, 0:NB], rden[g][:, 0:1], None, ALU.mult,
                              r=[psI_.name, rden[g].name], w=[im.name])
                    else:
                        kb.stt("dve", im[:], psI_[:, h * 128:h * 128 + NB], rden[g][:, h:h + 1], im[:], ALU.mult, ALU.add,
                               r=[psI_.name, rden[g].name, im.name], w=[im.name])
                if 2 * b + 2 < NB:
                    kb.memset("pool", im[:, 2 * b + 2:NB], -1.0, w=[im.name])
                kb.memset("pool", im[0:64, 2 * b + 1:2 * b + 2], -1.0, w=[im.name])
                kb.ts("pool", im[:, 0:1], im[:, 0:1], 100.0, None, ALU.add, r=[im.name], w=[im.name])
                kb.ts("pool", im[:, 2 * b:2 * b + 1], im[:, 2 * b:2 * b + 1], 100.0, None, ALU.add,
                      r=[im.name], w=[im.name])
                if b > 0:
                    kb.ts("pool", im[0:64, 2 * b - 1:2 * b], im[0:64, 2 * b - 1:2 * b], 100.0, None, ALU.add,
                          r=[im.name], w=[im.name])
                kb.ts("pool", im[64:128, 2 * b + 1:2 * b + 2], im[64:128, 2 * b + 1:2 * b + 2], 100.0, None, ALU.add,
                      r=[im.name], w=[im.name])
                yield
                kb.add("dve", lambda e_: e_.max(m1[g][:], im[:]), [im.name], [m1[g].name])
                kb.add("dve", lambda e_: e_.match_replace(wk[g][:], m1[g][:], im[:], -1e9), [m1[g].name, im.name], [wk[g].name])
                kb.add("dve", lambda e_: e_.max(m2[g][:], wk[g][:]), [wk[g].name], [m2[g].name])
                kb.ts("dve", sel[g][:], im[:], m2[g][:, 7:8], None, ALU.is_ge, r=[im.name, m2[g].name], w=[sel[g].name])
                kb.ts("dve", sel[g][:], sel[g][:], -NEGM, NEGM, ALU.mult, ALU.add, r=[sel[g].name], w=[sel[g].name])
                yield
            pso = Ob[g][1]
            wch = list(range(max(0, b - 4), b + 1))
            for ci, kc in enumerate(wch):
                pS = psS[cnt["S"] % 2]; cnt["S"] += 1
                e = e16[cnt["E"] % 4]; cnt["E"] += 1
                diag = kc == b
                edge = kc == b - 4
                kb.mm(pS[:], kw[:, g, kc * 128:(kc + 1) * 128], qf16, True, not (diag or edge),
                      r=[kw.name, q16[g].name], w=[pS.name])
                if diag:
                    kb.mm(pS[:], idb[:], cneg[:], False, True, r=[idb.name, cneg.name], w=[pS.name])
                if edge:
                    kb.mm(pS[:], idb[:], wneg[:], False, True, r=[idb.name, wneg.name], w=[pS.name])
                kb.act(e[:], pS[:], AF.Exp, scale=SC, r=[pS.name], w=[e.name])
                for h in range(4):
                    kb.mm(pso[:, h * 65:(h + 1) * 65], e[:, h * 128:(h + 1) * 128],
                          vv[:, kc, (2 + g) * 65:(3 + g) * 65], ci == 0 and h == 0, False,
                          r=[e.name, vv.name], w=[pso.name])
                yield
            finish_branch(pso, g, 2, oa, gsb, False)
            if NB > 16:
                kb.tr(psI_[0:NB, 0:128], sel[g][:], idf[:], r=[sel[g].name, idf.name], w=[psI_.name])
                kb.cp("dve", negT[g][0:NB, :].rearrange("p (h t) -> p h t", h=4),
                      psI_[0:NB, 0:128].unsqueeze(1).to_broadcast([NB, 4, 128]), r=[psI_.name], w=[negT[g].name])
            yield
            pso = Ob[g][0]
            for kc in range(0, b + 1):
                pS = psS[cnt["S"] % 2]; cnt["S"] += 1
                e = e16[cnt["E"] % 4]; cnt["E"] += 1
                diag = kc == b
                kb.mm(pS[:], ks[:, g, kc * 128:(kc + 1) * 128], qf16, True, False, r=[ks.name, q16[g].name], w=[pS.name])
                kb.mm(pS[:], ex[0:NB, kc * 128:(kc + 1) * 128], negT[g][0:NB, :], False, not diag,
                      r=[ex.name, negT[g].name], w=[pS.name])
                if diag:
                    kb.mm(pS[:], idb[:], cneg[:], False, True, r=[idb.name, cneg.name], w=[pS.name])
                kb.act(e[:], pS[:], AF.Exp, scale=SC, r=[pS.name], w=[e.name])
                for h in range(4):
                    kb.mm(pso[:, h * 65:(h + 1) * 65], e[:, h * 128:(h + 1) * 128], vv[:, kc, g * 65:(g + 1) * 65],
                          kc == 0 and h == 0, False, r=[e.name, vv.name], w=[pso.name])
                yield
            finish_branch(pso, g, 1, oa, gsb, False)

        for b in range(NT):
            rows = slice(b * 128, (b + 1) * 128)
            oa = oacc[b % 2]
            gsb = gs[b % 2]
            kb.dma(gsb[:], P[rows, 1280:1304], w=[gsb.name])
            kb.act(gsb[:], gsb[:], AF.Sigmoid, r=[gsb.name], w=[gsb.name])
            gens = [bg_gen(b, 0, oa, gsb), bg_gen(b, 1, oa, gsb)]
            while gens:
                for gg in list(gens):
                    try:
                        next(gg)
                    except StopIteration:
                        gens.remove(gg)
            kb.dma(self.o_mix[l][0][rows, :], oa[:], r=[oa.name + ":0", oa.name + ":1"], q="pool")
        kb.flush()


Prog.phase_nsa_attn = _phase_nsa_attn
```
